# Optimizing a Trainium2 kernel written in Bass

```python
import math
import jax, jax.numpy as jnp
from jax import lax
import numpy as np

D_MODEL = 1024
BATCH = 8
SEQ = 8192
DEPTH = 1

HEAD_DIM = 64
N_ATTN_HEADS = 8
N_DELTA_HEADS = 8
ATTN_WIDTH = N_ATTN_HEADS * HEAD_DIM
DELTA_WIDTH = N_DELTA_HEADS * HEAD_DIM
MIX_WIDTH = ATTN_WIDTH + DELTA_WIDTH
DILATED_BRANCHES = ((128, 1), (512, 4), (2048, 16))
PAD_UNIT = 2048
N_BUCKETS = 32
MAX_DISTANCE = 2048
CONV_WIDTH = 4
CHUNK = 64
D_FF = (8 * D_MODEL + 3 * 256 - 1) // (3 * 256) * 256
IN_WIDTH = 3 * ATTN_WIDTH + 4 * DELTA_WIDTH + 2 * N_DELTA_HEADS
EPS = 1e-6
NEG_INF = -1e30

kernel_name = 'hybrid_dilated_attn_gated_deltanet_block'


def _rmsnorm(x, g):
    x32 = x.astype(jnp.float32)
    y = x32 * lax.rsqrt(jnp.mean(x32 * x32, axis=-1, keepdims=True) + EPS)
    return (y * g.astype(jnp.float32)).astype(x.dtype)


def _l2norm(x):
    return x * lax.rsqrt(jnp.sum(x * x, axis=-1, keepdims=True) + EPS)


def _t5_bucket(distance):
    max_exact = N_BUCKETS // 2
    dist_f = jnp.maximum(distance, 1).astype(jnp.float32)
    large = max_exact + (jnp.log(dist_f / max_exact) / math.log(MAX_DISTANCE / max_exact)
                         * (N_BUCKETS - max_exact)).astype(jnp.int32)
    return jnp.where(distance < max_exact, distance, jnp.minimum(large, N_BUCKETS - 1))


def _dilated_branch(q, k, v, rel_bias, window, dilation):
    b, h, p, dh = q.shape
    band = window // dilation
    length = p // dilation
    n_blocks = length // band

    def to_blocks(t):
        t = t.reshape(b, h, length, dilation, dh).transpose(0, 1, 3, 2, 4)
        return t.reshape(b, h, dilation, n_blocks, band, dh)

    qb, kb, vb = to_blocks(q), to_blocks(k), to_blocks(v)

    def with_prev(t):
        prev = jnp.pad(t, ((0, 0), (0, 0), (0, 0), (1, 0), (0, 0), (0, 0)))[:, :, :, :-1]
        return jnp.concatenate([prev, t], axis=-2)

    kw, vw = with_prev(kb), with_prev(vb)
    qi = jnp.arange(band)[:, None]
    kj = jnp.arange(2 * band)[None, :]
    steps = qi + band - kj
    in_window = (steps >= 0) & (steps <= band)
    not_before_start = (jnp.arange(n_blocks)[:, None, None] > 0) | (kj >= band)[None]
    valid = in_window[None] & not_before_start
    bias = rel_bias.astype(jnp.float32)[_t5_bucket(jnp.maximum(steps, 0) * dilation)]
    bias = bias.transpose(2, 0, 1)
    s = jnp.einsum('bhrnqd,bhrnkd->bhrnqk', qb * dh ** -0.5, kw) + bias[None, :, None, None]
    s = jnp.where(valid, s, NEG_INF)
    m = jnp.max(s, axis=-1, keepdims=True)
    e = jnp.exp(s - m)
    denom = jnp.sum(e, axis=-1, keepdims=True)
    o = jnp.einsum('bhrnqk,bhrnkd->bhrnqd', e, vw) / denom
    lse = (m + jnp.log(denom))[..., 0]
    o = o.reshape(b, h, dilation, length, dh).transpose(0, 1, 3, 2, 4).reshape(b, h, p, dh)
    lse = lse.reshape(b, h, dilation, length).transpose(0, 1, 3, 2).reshape(b, h, p)
    return o, lse


def _dilated_attention(q, k, v, rel_bias):
    b, s, h, dh = q.shape
    p = (s + PAD_UNIT - 1) // PAD_UNIT * PAD_UNIT

    def to_bhpd(t):
        t = jnp.pad(t.astype(jnp.float32), ((0, 0), (0, p - s), (0, 0), (0, 0)))
        return t.transpose(0, 2, 1, 3)

    q, k, v = to_bhpd(q), to_bhpd(k), to_bhpd(v)
    outs, lses = [], []
    for window, dilation in DILATED_BRANCHES:
        o_i, lse_i = _dilated_branch(q, k, v, rel_bias, window, dilation)
        outs.append(o_i)
        lses.append(lse_i)
    w = jax.nn.softmax(jnp.stack(lses), axis=0)
    o = jnp.sum(w[..., None] * jnp.stack(outs), axis=0)
    return o[:, :, :s].transpose(0, 2, 1, 3).reshape(b, s, h * dh)


def _causal_conv(x, w):
    return lax.conv_general_dilated(x, w[:, None, :], window_strides=(1,),
                                    padding=((CONV_WIDTH - 1, 0),),
                                    dimension_numbers=('NWC', 'WIO', 'NWC'),
                                    feature_group_count=x.shape[-1])


def _chunk_gated_delta_rule(q, k, v, g, beta):
    b, s, h, dk = q.shape
    dv = v.shape[-1]
    nc = s // CHUNK

    def chunks(t):
        return t.reshape(b, nc, CHUNK, h, t.shape[-1]).transpose(1, 0, 3, 2, 4)

    qc, kc, vc = chunks(q), chunks(k), chunks(v)
    gcum = jnp.cumsum(g.reshape(b, nc, CHUNK, h).transpose(1, 0, 3, 2), axis=-1)
    bc = beta.reshape(b, nc, CHUNK, h).transpose(1, 0, 3, 2)
    causal = jnp.tril(jnp.ones((CHUNK, CHUNK), dtype=bool))
    strict = jnp.tril(jnp.ones((CHUNK, CHUNK), dtype=bool), k=-1)
    diff = gcum[..., :, None] - gcum[..., None, :]
    decay = jnp.where(causal, jnp.exp(jnp.where(causal, diff, 0.0)), 0.0)
    k_beta = kc * bc[..., None]
    a_mat = jnp.where(strict, jnp.einsum('nbhcd,nbhed->nbhce', k_beta, kc) * decay, 0.0)
    rhs = jnp.concatenate([vc * bc[..., None], k_beta * jnp.exp(gcum)[..., None]], axis=-1)
    sol = lax.linalg.triangular_solve(a_mat + jnp.eye(CHUNK, dtype=a_mat.dtype), rhs,
                                      left_side=True, lower=True, unit_diagonal=True)
    u, w = sol[..., :dv], sol[..., dv:]
    qk = jnp.where(causal, jnp.einsum('nbhcd,nbhed->nbhce', qc, kc) * decay, 0.0)

    def step(state, xs):
        q_i, k_i, u_i, w_i, g_i, qk_i = xs
        v_new = u_i - jnp.einsum('bhck,bhkv->bhcv', w_i, state)
        o_i = (jnp.einsum('bhck,bhkv->bhcv', q_i * jnp.exp(g_i)[..., None], state)
               + jnp.einsum('bhce,bhev->bhcv', qk_i, v_new))
        g_last = g_i[..., -1]
        k_dec = k_i * jnp.exp(g_last[..., None] - g_i)[..., None]
        state = state * jnp.exp(g_last)[..., None, None] + jnp.einsum('bhck,bhcv->bhkv', k_dec, v_new)
        return state, o_i

    state0 = jnp.zeros((b, h, dk, dv), jnp.float32)
    _, o = lax.scan(step, state0, (qc, kc, u, w, gcum, qk))
    return o.transpose(1, 0, 3, 2, 4).reshape(b, s, h, dv)


def _gated_deltanet(q, k, v, z, b_logit, a_logit, conv_w, a_log, dt_bias, norm_g):
    bsz, s, _ = q.shape
    qkv = jax.nn.silu(_causal_conv(jnp.concatenate([q, k, v], axis=-1), conv_w))
    q, k, v = jnp.split(qkv.astype(jnp.float32), 3, axis=-1)
    shp = (bsz, s, N_DELTA_HEADS, HEAD_DIM)
    q = _l2norm(q.reshape(shp)) * HEAD_DIM ** -0.5
    k = _l2norm(k.reshape(shp))
    v = v.reshape(shp)
    beta = jax.nn.sigmoid(b_logit.astype(jnp.float32))
    g = -jnp.exp(a_log.astype(jnp.float32)) * jax.nn.softplus(
        a_logit.astype(jnp.float32) + dt_bias.astype(jnp.float32))
    o = _chunk_gated_delta_rule(q, k, v, g, beta)
    o = _rmsnorm(o, norm_g) * jax.nn.silu(z.astype(jnp.float32).reshape(shp))
    return o.reshape(bsz, s, DELTA_WIDTH).astype(z.dtype)


def setup_inputs(seed: int = 0) -> dict:
    key = jax.random.key(seed)
    ks = jax.random.split(key, 20)
    f32 = jnp.float32

    def nrm(k, shape, scale):
        return jax.random.normal(k, shape, f32) * scale

    dt = jnp.exp(jax.random.uniform(ks[9], (DEPTH, N_DELTA_HEADS), f32,
                                    minval=math.log(1e-3), maxval=math.log(1e-1)))
    return {
        'x': nrm(ks[0], (BATCH, SEQ, D_MODEL), 1.0),
        'c': nrm(ks[1], (BATCH, D_MODEL), 1.0),
        'w_ada': nrm(ks[2], (DEPTH, D_MODEL, 6 * D_MODEL), 0.5 * D_MODEL ** -0.5),
        'b_ada': nrm(ks[3], (DEPTH, 6 * D_MODEL), 0.02),
        'norm_attn_g': 1.0 + nrm(ks[4], (DEPTH, D_MODEL), 0.05),
        'w_in': nrm(ks[5], (DEPTH, D_MODEL, IN_WIDTH), D_MODEL ** -0.5),
        'rel_bias': nrm(ks[6], (N_BUCKETS, N_ATTN_HEADS), 0.5),
        'conv_w': nrm(ks[7], (DEPTH, CONV_WIDTH, 3 * DELTA_WIDTH), CONV_WIDTH ** -0.5),
        'a_log': jnp.log(jax.random.uniform(ks[8], (DEPTH, N_DELTA_HEADS), f32, minval=1.0, maxval=16.0)),
        'dt_bias': dt + jnp.log(-jnp.expm1(-dt)),
        'delta_norm_g': 1.0 + nrm(ks[10], (DEPTH, HEAD_DIM), 0.05),
        'w_out': nrm(ks[11], (DEPTH, MIX_WIDTH, D_MODEL), MIX_WIDTH ** -0.5),
        'norm_ffn_g': 1.0 + nrm(ks[12], (DEPTH, D_MODEL), 0.05),
        'w_gate': nrm(ks[13], (DEPTH, D_MODEL, D_FF), D_MODEL ** -0.5),
        'w_up': nrm(ks[14], (DEPTH, D_MODEL, D_FF), D_MODEL ** -0.5),
        'w_down': nrm(ks[15], (DEPTH, D_FF, D_MODEL), D_FF ** -0.5),
        'final_norm_g': 1.0 + nrm(ks[16], (D_MODEL,), 0.05),
    }


def reference(x, c, w_ada, b_ada, norm_attn_g, w_in, rel_bias, conv_w, a_log, dt_bias,
              delta_norm_g, w_out, norm_ffn_g, w_gate, w_up, w_down, final_norm_g):
    bsz, s, _ = x.shape
    split_points = np.cumsum([ATTN_WIDTH] * 3 + [DELTA_WIDTH] * 4 + [N_DELTA_HEADS])
    c_act = jax.nn.silu(c)
    for l in range(DEPTH):
        mod = c_act @ w_ada[l] + b_ada[l]
        sh1, sc1, g1, sh2, sc2, g2 = [m[:, None, :] for m in jnp.split(mod, 6, axis=-1)]
        h = _rmsnorm(x, norm_attn_g[l]) * (1.0 + sc1) + sh1
        proj = h @ w_in[l]
        q_a, k_a, v_a, q_d, k_d, v_d, z_d, b_d, a_d = jnp.split(proj, split_points, axis=-1)
        hs = (bsz, s, N_ATTN_HEADS, HEAD_DIM)
        y_attn = _dilated_attention(q_a.reshape(hs), k_a.reshape(hs), v_a.reshape(hs), rel_bias).astype(x.dtype)
        y_delta = _gated_deltanet(q_d, k_d, v_d, z_d, b_d, a_d, conv_w[l], a_log[l], dt_bias[l], delta_norm_g[l])
        y = jnp.concatenate([y_attn, y_delta], axis=-1) @ w_out[l]
        x = x + g1 * y
        h = _rmsnorm(x, norm_ffn_g[l]) * (1.0 + sc2) + sh2
        y = (jax.nn.silu(h @ w_gate[l]) * (h @ w_up[l])) @ w_down[l]
        x = x + g2 * y
    return _rmsnorm(x, final_norm_g)
```

```python
from contextlib import ExitStack
import numpy as np
import concourse.bass as bass
import concourse.mybir as mybir
from concourse.bass_utils import run_bass_kernel_spmd

F32 = mybir.dt.float32
BF16 = mybir.dt.bfloat16
ALU = mybir.AluOpType
AF = mybir.ActivationFunctionType

ENGS = ("pe", "act", "dve", "pool", "sp")
EPOCH = 20000


class _Op:
    __slots__ = ("eng", "fn", "deps", "dma", "semkey", "idx", "needs_inc", "sem", "val")

    def __init__(self, eng, fn, dma, semkey):
        self.eng, self.fn, self.dma, self.semkey = eng, fn, dma, semkey
        self.deps = []
        self.needs_inc = False
        self.sem = None
        self.val = 0


class Sched:
    def __init__(self, nc):
        self.nc = nc
        self.ops = {e: [] for e in ENGS}
        self.last_w = {}
        self.readers = {}
        self.last_dma_on_sem = {}
        self.n = 0

    def op(self, eng, fn, reads=(), writes=(), dma=False, semkey=None):
        o = _Op(eng, fn, dma, semkey)
        o.idx = self.n
        self.n += 1
        deps = {}

        def add(p):
            if p is None or p is o:
                return
            if (not p.dma) and (not dma) and p.eng == "pe" and eng == "pe":
                return
            deps[id(p)] = p

        for k in reads:
            add(self.last_w.get(k))
        for k in writes:
            add(self.last_w.get(k))
            for r in self.readers.get(k, ()):
                add(r)
        if dma:
            assert semkey is not None
            add(self.last_dma_on_sem.get(semkey))
            self.last_dma_on_sem[semkey] = o
        o.deps = list(deps.values())
        for p in o.deps:
            p.needs_inc = True
        for k in reads:
            self.readers.setdefault(k, []).append(o)
        for k in writes:
            self.last_w[k] = o
            self.readers[k] = []
        self.ops[eng].append(o)
        return o

    def emit(self, stack, final_wait_ops=()):
        nc = self.nc
        for o in final_wait_ops:
            o.needs_inc = True
        sems = {}

        def getsem(name):
            if name not in sems:
                sems[name] = stack.enter_context(nc.semaphore(name))
            return sems[name]

        dma_cnt = {}
        for e in ENGS:
            cnt = 0
            for o in self.ops[e]:
                if o.dma:
                    c = dma_cnt.get(o.semkey, 0) + 1
                    dma_cnt[o.semkey] = c
                    o.sem = getsem("d_" + str(o.semkey))
                    o.val = 16 * c
                    o.needs_inc = True
                elif o.needs_inc:
                    ep, v = divmod(cnt, EPOCH)
                    o.sem = getsem("c_%s_%d" % (e, ep))
                    o.val = v + 1
                    cnt += 1
        self.nsems = len(sems)
        block = stack.enter_context(nc.Block())
        engmap = {"pe": block.tensor, "act": block.scalar, "dve": block.vector,
                  "pool": block.gpsimd, "sp": block.sync}
        for e in ENGS:
            ops = self.ops[e]
            fw = [o for o in final_wait_ops] if e == "sp" else []

            def body(engine, ops=ops, fw=fw):
                waited = {}
                for o in ops:
                    for p in o.deps:
                        key = id(p.sem)
                        if waited.get(key, 0) >= p.val:
                            continue
                        engine.wait_ge(p.sem, p.val)
                        waited[key] = p.val
                    ins = o.fn(engine)
                    if o.needs_inc:
                        ins.then_inc(o.sem, 16 if o.dma else 1)
                for p in fw:
                    key = id(p.sem)
                    if waited.get(key, 0) >= p.val:
                        continue
                    engine.wait_ge(p.sem, p.val)
                    waited[key] = p.val

            engmap[e](body)


D = 1024
INW = 3600
DFF = 2816
EPS = 1e-6


class Phase:
    def __init__(self, nc, semstack, name):
        self.nc, self.semstack, self.name = nc, semstack, name
        self.st = ExitStack()
        self.S = Sched(nc)
        self.nb = 0
        self.pb = None
        self.cnt = 0

    def __enter__(self):
        self.st.__enter__()
        self.pb = [self.st.enter_context(self.nc.psum_tensor("%s_pb%d" % (self.name, i), [128, 512], F32))
                   for i in range(8)]
        return self

    def sb(self, name, shape, dt):
        return self.st.enter_context(self.nc.sbuf_tensor(self.name + "_" + name, shape, dt))

    def bank(self):
        i = self.nb % 8
        self.nb += 1
        return i

    def finish(self):
        S = self.S
        fw = [o for e in ENGS for o in S.ops[e] if o.dma]
        for e in ("pe", "act", "dve", "pool"):
            if S.ops[e]:
                fw.append(S.ops[e][-1])
        nc = self.nc
        ph = self

        class _SemStack:
            def enter_context(self_inner, cm):
                return cm

        _emit(S, nc, self.semstack, self.st, fw, self.name)

    def __exit__(self, *a):
        r = self.st.__exit__(*a)
        return r


def _emit(S, nc, semstack, blockstack, final_wait_ops, pname):
    for o in final_wait_ops:
        o.needs_inc = True
    sems = {}

    def getsem(name):
        name = pname + "_" + "".join(ch if ch.isalnum() else "_" for ch in name)
        if name not in sems:
            sems[name] = semstack.enter_context(nc.semaphore(name))
        return sems[name]

    dma_cnt = {}
    for e in ENGS:
        cnt = 0
        for o in S.ops[e]:
            if o.dma:
                c = dma_cnt.get(o.semkey, 0) + 1
                dma_cnt[o.semkey] = c
                o.sem = getsem("d_" + str(o.semkey))
                o.val = 16 * c
                o.needs_inc = True
            elif o.needs_inc:
                ep, v = divmod(cnt, EPOCH)
                o.sem = getsem("c_%s_%d" % (e, ep))
                o.val = v + 1
                cnt += 1
    S.nsems = len(sems)
    block = blockstack.enter_context(nc.Block())
    engmap = {"pe": block.tensor, "act": block.scalar, "dve": block.vector,
              "pool": block.gpsimd, "sp": block.sync}
    for e in ENGS:
        ops = S.ops[e]
        fw = list(final_wait_ops) if e == "sp" else []

        def body(engine, ops=ops, fw=fw):
            waited = {}
            for o in ops:
                for p in o.deps:
                    key = id(p.sem)
                    if waited.get(key, 0) >= p.val:
                        continue
                    engine.wait_ge(p.sem, p.val)
                    waited[key] = p.val
                ins = o.fn(engine)
                if o.needs_inc:
                    ins.then_inc(o.sem, 16 if o.dma else 1)
            for p in fw:
                key = id(p.sem)
                if waited.get(key, 0) >= p.val:
                    continue
                engine.wait_ge(p.sem, p.val)
                waited[key] = p.val

        engmap[e](body)


def _kw(**k):
    return k


def build_program(S, dbg=False, upto=9):
    nc = bass.Bass("TRN2", target_bir_lowering=False)
    NG = S // 512
    OUTK = "ExternalOutput" if dbg else "Internal"

    def din(name, shape, dt=F32):
        return nc.dram_tensor(name, shape, dt, kind="ExternalInput").ap()

    def dsc(name, shape, dt):
        return nc.dram_tensor(name, shape, dt, kind=OUTK).ap()

    x = din("x", [S, D])
    c_col = din("c_col", [128, 8])
    w_ada = din("w_ada", [D, 6 * D])
    bada_col = din("bada_col", [128, 48])
    bada_row = din("bada_row", [6, D])
    gattn_col = din("gattn_col", [128, 8])
    gffn_col = din("gffn_col", [128, 8])
    w_in = din("w_in", [D, INW])
    cw_col = din("cw_col", [128, 48])
    ident_in = din("ident", [128, 128])
    bones_in = din("bones", [128, 128])
    tb_in = din("tb", [128, 24 * 256])
    out = nc.dram_tensor("out", [S, D], F32, kind="ExternalOutput").ap()

    MODC = dsc("MODC", [128, 32], F32)
    GROW = dsc("GROW", [2, 128, D], F32)
    QT = dsc("QT", [512, S], BF16)
    KT = dsc("KT", [512, S], BF16)
    VV = dsc("VV", [S, 512], BF16)
    QD = dsc("QD", [512, S], BF16)
    KD = dsc("KD", [512, S], BF16)
    VD = dsc("VD", [512, S], BF16)
    ZS = dsc("ZS", [512, S], BF16)
    BA = dsc("BA", [S, 16], F32)
    MIXT = dsc("MIXT", [D, S], BF16)

    semstack = ExitStack()
    semstack.__enter__()

    with Phase(nc, semstack, "p0") as P:
        S_ = P.S
        ccol = P.sb("ccol", [128, 8], F32)
        sbf = P.sb("sbf", [128, 8], BF16)
        sbc = P.sb("sbc", [128, 8, 128], BF16)
        bcol = P.sb("bcol", [128, 48], F32)
        gcol = P.sb("gcol", [128, 16], F32)
        modc = P.sb("modc", [128, 32], F32)
        wa = [P.sb("wa%d" % i, [128, 8, D], BF16) for i in range(2)]
        brow = [P.sb("brow%d" % i, [128, D], F32) for i in range(2)]
        grow = [P.sb("grow%d" % i, [128, D], F32) for i in range(2)]
        S_.op("sp", lambda e: e.dma_start(out=ccol[:], in_=c_col), writes=["ccol"], dma=True, semkey="ccol")
        S_.op("sp", lambda e: e.dma_start(out=bcol[:], in_=bada_col), writes=["bcol"], dma=True, semkey="bcol")
        S_.op("sp", lambda e: e.dma_start(out=gcol[:, 0:8], in_=gattn_col), writes=["gcol"], dma=True, semkey="gcol")
        S_.op("sp", lambda e: e.dma_start(out=gcol[:, 8:16], in_=gffn_col), writes=["gcol"], dma=True, semkey="gcol")
        S_.op("act", lambda e: e.activation(out=sbf[:], in_=ccol[:], func=AF.Silu), reads=["ccol"], writes=["sbf"])
        S_.op("dve", lambda e: e.tensor_copy(out=sbc[:], in_=sbf[:].unsqueeze(2).to_broadcast([128, 8, 128])),
              reads=["sbf"], writes=["sbc"])
        wav = w_ada.rearrange("(k p) f -> p k f", p=128)
        colidx = {0: 0, 1: 1, 3: 2, 4: 3}
        for j in range(6):
            sl = j % 2
            S_.op("pool", lambda e, j=j, sl=sl: e.dma_start(out=wa[sl][:], in_=wav[:, :, j * D:(j + 1) * D]),
                  writes=[("wa", sl)], dma=True, semkey="wa%d" % sl)
            if j in colidx:
                jj = colidx[j]
                b = P.bank()
                for fcn in range(8):
                    for k in range(8):
                        S_.op("pe", lambda e, b=b, fcn=fcn, k=k, sl=sl: e.matmul(
                            P.pb[b][:, fcn:fcn + 1], lhsT=wa[sl][:, k, fcn * 128:(fcn + 1) * 128], rhs=sbf[:, k:k + 1],
                            start=(k == 0), stop=(k == 7)), reads=[("wa", sl), "sbf"], writes=[("pb", b)])
                S_.op("dve", lambda e, b=b, jj=jj, j=j: e.tensor_tensor(
                    out=modc[:, jj * 8:(jj + 1) * 8], in0=P.pb[b][:, 0:8], in1=bcol[:, j * 8:(j + 1) * 8], op=ALU.add),
                    reads=[("pb", b), "bcol"], writes=["modc"])
            else:
                gi = 0 if j == 2 else 1
                S_.op("sp", lambda e, j=j, gi=gi: e.dma_start(out=brow[gi][:], in_=bada_row[j:j + 1, :].partition_broadcast(128)),
                      writes=[("brow", gi)], dma=True, semkey="brow%d" % gi)
                for half in range(2):
                    b = P.bank()
                    for k in range(8):
                        S_.op("pe", lambda e, b=b, k=k, sl=sl, half=half: e.matmul(
                            P.pb[b][:], lhsT=sbc[:, k, :], rhs=wa[sl][:, k, half * 512:(half + 1) * 512],
                            start=(k == 0), stop=(k == 7)), reads=[("wa", sl), "sbc"], writes=[("pb", b)])
                    S_.op("dve", lambda e, b=b, gi=gi, half=half: e.tensor_tensor(
                        out=grow[gi][:, half * 512:(half + 1) * 512], in0=P.pb[b][:], in1=brow[gi][:, half * 512:(half + 1) * 512],
                        op=ALU.add), reads=[("pb", b), ("brow", gi)], writes=[("grow", gi)])
                S_.op("sp", lambda e, gi=gi: e.dma_start(out=GROW[gi], in_=grow[gi][:]), reads=[("grow", gi)],
                      dma=True, semkey="grow%d" % gi)
        for jj, go in ((1, 0), (3, 8)):
            S_.op("dve", lambda e, jj=jj, go=go: e.scalar_tensor_tensor(
                out=modc[:, jj * 8:(jj + 1) * 8], in0=modc[:, jj * 8:(jj + 1) * 8], scalar=1.0, in1=gcol[:, go:go + 8],
                op0=ALU.add, op1=ALU.mult), reads=["modc", "gcol"], writes=["modc"])
        S_.op("sp", lambda e: e.dma_start(out=MODC, in_=modc[:]), reads=["modc"], dma=True, semkey="modc")
        P.finish()
    nc.all_engine_barrier()
    if upto < 1:
        return nc, semstack

    with Phase(nc, semstack, "p1") as P:
        S_ = P.S
        ident = P.sb("ident", [128, 128], BF16)
        bones = P.sb("bones", [128, 128], BF16)
        modc = P.sb("modc", [128, 32], F32)
        cw = P.sb("cw", [128, 48], F32)
        epsT = P.sb("eps", [128, 1], F32)
        win = P.sb("win", [128, 8, INW], BF16)
        S_.op("pool", lambda e: e.dma_start(out=ident[:], in_=ident_in), writes=["ident"], dma=True, semkey="ident")
        S_.op("pool", lambda e: e.dma_start(out=bones[:], in_=bones_in), writes=["bones"], dma=True, semkey="bones")
        S_.op("sp", lambda e: e.dma_start(out=modc[:], in_=MODC), writes=["modc"], dma=True, semkey="modc")
        S_.op("sp", lambda e: e.dma_start(out=cw[:], in_=cw_col), writes=["cw"], dma=True, semkey="cw")
        S_.op("dve", lambda e: e.memset(epsT[:], EPS), writes=["eps"])
        winv = w_in.rearrange("(k p) f -> p k f", p=128)
        for k in range(8):
            S_.op("pool", lambda e, k=k: e.dma_start(out=win[:, k, :], in_=winv[:, k, :]), writes=[("win", k)],
                  dma=True, semkey="win%d" % k)
        WIN = [("win", k) for k in range(8)]
        xt = [P.sb("xt%d" % i, [128, 4, D], F32) for i in range(2)]
        junk = P.sb("junk", [128, D], BF16)
        ss = P.sb("ss", [128, 4], F32)
        rstd = P.sb("rstd", [128, 4], F32)
        xs = P.sb("xs", [128, 4, D], BF16)
        hT = P.sb("hT", [128, 8, 512], BF16)
        stq = [P.sb("stq%d" % i, [128, 4, 512], BF16) for i in range(2)]
        cin = P.sb("cin", [128, 12, 515], F32)
        acc = P.sb("acc", [128, 512], F32)
        slu = P.sb("slu", [128, 512], F32)
        sq = P.sb("sq", [128, 512], BF16)
        rs = P.sb("rs", [128, 512], F32)
        stb = P.sb("stb", [128, 4, 16], F32)
        xv = x.rearrange("(g j p) d -> g p j d", j=4, p=128)
        S_.op("pool", lambda e: e.memset(cin[:], 0.0), writes=["cin"])
        nst = [0]

        def stage():
            i = nst[0] % 2
            nst[0] += 1
            return i

        for g in range(NG):
            xs_ = g % 2
            S_.op("sp", lambda e, g=g, xs_=xs_: e.dma_start(out=xt[xs_][:], in_=xv[g]), writes=[("xt", xs_)],
                  dma=True, semkey="xt%d" % xs_)
            S_.op("pool", lambda e: e.memset(ss[:], 0.0), writes=["ss"])
            for j in range(4):
                S_.op("act", lambda e, j=j, xs_=xs_: e.activation(out=junk[:], in_=xt[xs_][:, j, :], func=AF.Square,
                                                                accum_out=ss[:, j:j + 1]),
                      reads=[("xt", xs_), "ss"], writes=["ss", "junk"])
            S_.op("act", lambda e: e.activation(out=rstd[:], in_=ss[:], func=AF.Sqrt, bias=epsT[:], scale=1.0 / D),
                  reads=["ss", "eps"], writes=["rstd"])
            S_.op("dve", lambda e: e.reciprocal(out=rstd[:], in_=rstd[:]), reads=["rstd"], writes=["rstd"])
            for j in range(4):
                S_.op("dve", lambda e, j=j, xs_=xs_: e.tensor_scalar(out=xs[:, j, :], in0=xt[xs_][:, j, :],
                                                                   scalar1=rstd[:, j:j + 1], scalar2=None, op0=ALU.mult),
                      reads=[("xt", xs_), "rstd"], writes=[("xs", j)])
            for c2 in range(4):
                b = P.bank()
                pbf = P.pb[b][:].bitcast(BF16)
                for cc in range(2):
                    c = c2 * 2 + cc
                    for j in range(4):
                        S_.op("pe", lambda e, pbf=pbf, cc=cc, c=c, j=j: e.transpose(
                            pbf[:, cc * 512 + j * 128: cc * 512 + (j + 1) * 128], xs[:, j, c * 128:(c + 1) * 128], ident[:]),
                            reads=[("xs", j), "ident"], writes=[("pb", b)])
                for cc in range(2):
                    c = c2 * 2 + cc
                    S_.op("act", lambda e, pbf=pbf, cc=cc, c=c: e.activation(
                        out=hT[:, c, :], in_=pbf[:, cc * 512:(cc + 1) * 512], func=AF.Identity,
                        bias=modc[:, c:c + 1], scale=modc[:, 8 + c:9 + c]),
                        reads=[("pb", b), "modc"], writes=[("hT", c)])
            HT = [("hT", c) for c in range(8)]

            def proj_fm(fc):
                b = P.bank()
                for k in range(8):
                    S_.op("pe", lambda e, b=b, k=k, fc=fc: e.matmul(
                        P.pb[b][:], lhsT=win[:, k, fc * 128:(fc + 1) * 128], rhs=hT[:, k, :], start=(k == 0), stop=(k == 7)),
                        reads=[("win", k), ("hT", k)], writes=[("pb", b)])
                return b

            for base, dst, nm in ((0, QT, "q"), (4, KT, "k")):
                si = stage()
                for i in range(4):
                    b = proj_fm(base + i)
                    eng = "act" if i % 2 == 0 else "dve"
                    if eng == "act":
                        S_.op("act", lambda e, b=b, si=si, i=i: e.activation(out=stq[si][:, i, :], in_=P.pb[b][:], func=AF.Identity),
                              reads=[("pb", b)], writes=[("stq", si, i)])
                    else:
                        S_.op("dve", lambda e, b=b, si=si, i=i: e.tensor_copy(out=stq[si][:, i, :], in_=P.pb[b][:]),
                              reads=[("pb", b)], writes=[("stq", si, i)])
                S_.op("sp", lambda e, dst=dst, si=si, g=g: e.dma_start(
                    out=dst.rearrange("(c p) s -> p c s", p=128)[:, :, g * 512:(g + 1) * 512], in_=stq[si][:]),
                    reads=[("stq", si, i) for i in range(4)], dma=True, semkey="stq%d" % si)
            si = stage()
            for j in range(4):
                b = P.bank()
                for k in range(8):
                    S_.op("pe", lambda e, b=b, k=k, j=j: e.matmul(
                        P.pb[b][:], lhsT=hT[:, k, j * 128:(j + 1) * 128], rhs=win[:, k, 1024:1536], start=(k == 0), stop=(k == 7)),
                        reads=[("win", k), ("hT", k)], writes=[("pb", b)])
                S_.op("act", lambda e, b=b, si=si, j=j: e.activation(out=stq[si][:, j, :], in_=P.pb[b][:], func=AF.Identity),
                      reads=[("pb", b)], writes=[("stq", si, j)])
            S_.op("sp", lambda e, si=si, g=g: e.dma_start(
                out=VV.rearrange("(g j p) f -> g p j f", j=4, p=128)[g], in_=stq[si][:]),
                reads=[("stq", si, i) for i in range(4)], dma=True, semkey="stq%d" % si)
            for grp, dst in ((0, QD), (1, KD), (2, VD)):
                si = stage()
                for i in range(4):
                    ci = grp * 4 + i
                    b = proj_fm(12 + ci)
                    S_.op("act", lambda e, b=b, ci=ci: e.activation(out=cin[:, ci, 3:515], in_=P.pb[b][:], func=AF.Identity),
                          reads=[("pb", b)], writes=[("cin", ci)])
                    S_.op("dve", lambda e, ci=ci: e.tensor_scalar(out=acc[:], in0=cin[:, ci, 0:512], scalar1=cw[:, ci * 4:ci * 4 + 1],
                                                                scalar2=None, op0=ALU.mult),
                          reads=[("cin", ci), "cw"], writes=["acc"])
                    for t in range(1, 4):
                        S_.op("dve", lambda e, ci=ci, t=t: e.scalar_tensor_tensor(
                            out=acc[:], in0=cin[:, ci, t:t + 512], scalar=cw[:, ci * 4 + t:ci * 4 + t + 1], in1=acc[:],
                            op0=ALU.mult, op1=ALU.add), reads=[("cin", ci), "cw", "acc"], writes=["acc"])
                    S_.op("dve", lambda e, ci=ci: e.tensor_copy(out=cin[:, ci, 0:3], in_=cin[:, ci, 512:515]),
                          reads=[("cin", ci)], writes=[("cin", ci)])
                    if grp == 2:
                        S_.op("act", lambda e, si=si, i=i: e.activation(out=stq[si][:, i, :], in_=acc[:], func=AF.Silu),
                              reads=["acc"], writes=[("stq", si, i)])
                    else:
                        S_.op("act", lambda e: e.activation(out=slu[:], in_=acc[:], func=AF.Silu), reads=["acc"], writes=["slu"])
                        S_.op("act", lambda e: e.activation(out=sq[:], in_=slu[:], func=AF.Square), reads=["slu"], writes=["sq"])
                        b2 = P.bank()
                        S_.op("pe", lambda e, b2=b2: e.matmul(P.pb[b2][:], lhsT=bones[:], rhs=sq[:], start=True, stop=True),
                              reads=["bones", "sq"], writes=[("pb", b2)])
                        S_.op("act", lambda e, b2=b2: e.activation(out=rs[:], in_=P.pb[b2][:], func=AF.Sqrt, bias=epsT[:], scale=1.0),
                              reads=[("pb", b2), "eps"], writes=["rs"])
                        S_.op("dve", lambda e: e.reciprocal(out=rs[:], in_=rs[:]), reads=["rs"], writes=["rs"])
                        scl = 0.125 if grp == 0 else 1.0
                        S_.op("dve", lambda e, si=si, i=i, scl=scl: e.scalar_tensor_tensor(
                            out=stq[si][:, i, :], in0=slu[:], scalar=scl, in1=rs[:], op0=ALU.mult, op1=ALU.mult),
                            reads=["slu", "rs"], writes=[("stq", si, i)])
                S_.op("sp", lambda e, dst=dst, si=si, g=g: e.dma_start(
                    out=dst.rearrange("(c p) s -> p c s", p=128)[:, :, g * 512:(g + 1) * 512], in_=stq[si][:]),
                    reads=[("stq", si, i) for i in range(4)], dma=True, semkey="stq%d" % si)
            si = stage()
            for i in range(4):
                b = proj_fm(24 + i)
                S_.op("act", lambda e, b=b, si=si, i=i: e.activation(out=stq[si][:, i, :], in_=P.pb[b][:], func=AF.Silu),
                      reads=[("pb", b)], writes=[("stq", si, i)])
            S_.op("sp", lambda e, si=si, g=g: e.dma_start(
                out=ZS.rearrange("(c p) s -> p c s", p=128)[:, :, g * 512:(g + 1) * 512], in_=stq[si][:]),
                reads=[("stq", si, i) for i in range(4)], dma=True, semkey="stq%d" % si)
            b = P.bank()
            for j in range(4):
                for k in range(8):
                    S_.op("pe", lambda e, b=b, k=k, j=j: e.matmul(
                        P.pb[b][:, j * 16:(j + 1) * 16], lhsT=hT[:, k, j * 128:(j + 1) * 128], rhs=win[:, k, 3584:3600],
                        start=(k == 0), stop=(k == 7)), reads=[("win", k), ("hT", k)], writes=[("pb", b)])
            S_.op("dve", lambda e, b=b: e.tensor_copy(out=stb[:].rearrange("p j f -> p (j f)"), in_=P.pb[b][:, 0:64]),
                  reads=[("pb", b)], writes=["stb"])
            S_.op("sp", lambda e, g=g: e.dma_start(out=BA.rearrange("(g j p) f -> g p j f", j=4, p=128)[g], in_=stb[:]),
                  reads=["stb"], dma=True, semkey="stb")
        P.finish()
    nc.all_engine_barrier()
    if upto < 2:
        return nc, semstack
    _phase2(nc, semstack, S, QT, KT, VV, tb_in, MIXT)
    nc.all_engine_barrier()
    if upto < 3:
        return nc, semstack
    cst = dict(identf=din("identf", [128, 128]), trif=din("trif", [128, 128]), bonesf=din("bonesf", [128, 128]),
               maskL=din("maskL", [128, 128]), maskU=din("maskU", [128, 128]), ident=ident_in,
               alog=din("alog_row", [1, 8]), dtb=din("dtb_row", [1, 8]), dng=din("dng_row", [1, 64]))
    _phase3(nc, semstack, S, QD, KD, VD, ZS, BA, MIXT, cst)
    nc.all_engine_barrier()
    if upto < 4:
        return nc, semstack
    w_out = din("w_out", [D, D])
    w_gate = din("w_gate", [D, DFF])
    w_up = din("w_up", [D, DFF])
    w_down = din("w_down", [DFF, D])
    fg_row = din("fg_row", [1, D])
    X1 = dsc("X1", [S, D], F32)
    H2T = dsc("H2T", [D, S], BF16)
    _phase4a(nc, semstack, S, x, MIXT, w_out, GROW, MODC, ident_in, X1, H2T)
    nc.all_engine_barrier()
    if upto < 5:
        return nc, semstack
    _phase4b(nc, semstack, S, X1, H2T, w_gate, w_up, w_down, GROW, fg_row, out)
    return nc, semstack


P2DBG = {'mode': 9, 'strided': True}


def _phase2(nc, semstack, S, QT, KT, VV, tb_in, MIXT):
    NSB = S // 2048
    import os
    mode = int(os.environ.get('P2MODE', '9'))
    with Phase(nc, semstack, "p2") as P:
        S_ = P.S
        EB = P.sb("EB", [128, 24 * 256], BF16)
        ones = P.sb("ones", [128, 64], BF16)
        qt = P.sb("qt", [64, 8, 2048], BF16)
        kt = [P.sb("kt%d" % i, [64, 8, 2048], BF16) for i in range(2)]
        v1 = P.sb("v1", [128, 16, 512], BF16)
        v1p = P.sb("v1p", [128, 1, 512], BF16)
        v2 = P.sb("v2", [128, 16, 512], BF16)
        v2p = P.sb("v2p", [128, 4, 512], BF16)
        v3 = [P.sb("v3_%d" % i, [128, 16, 512], BF16) for i in range(2)]
        Et = [P.sb("E%d" % i, [128, 512], BF16) for i in range(2)]
        PT = [P.sb("PT%d" % i, [128, 512], BF16) for i in range(2)]
        accn = P.sb("accn", [128, 2048], F32)
        accd = P.sb("accd", [128, 2048], F32)
        mst = P.sb("mst", [128, 2048], BF16)
        S_.op("pool", lambda e: e.dma_start(out=EB[:], in_=tb_in), writes=["EB"], dma=True, semkey="tb")
        S_.op("act", lambda e: e.activation(out=EB[:], in_=EB[:], func=AF.Exp), reads=["EB"], writes=["EB"])
        S_.op("pool", lambda e: e.memset(ones[:], 1.0), writes=["ones"])
        cnt = [0, 0, 0]
        for N in range(NSB):
            cur, prv = N % 2, (N + 1) % 2
            t0 = N * 2048
            S_.op("sp", lambda e, t0=t0: e.dma_start(out=qt[:], in_=QT.rearrange("(c p) s -> p c s", p=64)[:, :, t0:t0 + 2048]),
                  writes=["qt"], dma=True, semkey="qt")
            S_.op("sp", lambda e, t0=t0, cur=cur: e.dma_start(out=kt[cur][:], in_=KT.rearrange("(c p) s -> p c s", p=64)[:, :, t0:t0 + 2048]),
                  writes=[("kt", cur)], dma=True, semkey="kt%d" % cur)
            Vsb = VV[t0:t0 + 2048, :]
            S_.op("sp", lambda e, Vsb=Vsb: e.dma_start(out=v1[:], in_=Vsb.rearrange("(n p) f -> p n f", p=128)),
                  writes=["v1"], dma=True, semkey="v1")
            for n_ in range(4):
                S_.op("sp", lambda e, Vsb=Vsb, n_=n_: e.dma_start(
                    out=v2[:, n_ * 4:(n_ + 1) * 4, :],
                    in_=Vsb[n_ * 512:(n_ + 1) * 512, :].rearrange("(p r) f -> p r f", r=4)),
                    writes=["v2"], dma=True, semkey="v2")
            S_.op("sp", lambda e, Vsb=Vsb, cur=cur: e.dma_start(out=v3[cur][:], in_=Vsb.rearrange("(p r) f -> p r f", r=16)),
                  writes=[("v3", cur)], dma=True, semkey="v3_%d" % cur)
            for hp in range(4 if mode >= 1 else 0):
                for br, d in enumerate((1, 4, 16)[:(3 if mode >= 2 else 1)]):
                    for gq in range(4):
                        nbk = 3 + cnt[1] % 2
                        dbk = 5 + cnt[1] % 2
                        cnt[1] += 1
                        for jj in range(4):
                            if br == 0:
                                n = 4 * gq + jj
                                qs, st = n * 128, 1
                                vcur = (v1, n, "v1")
                                if n >= 1:
                                    pk = (cur, (n - 1) * 128, (v1, n - 1, "v1"))
                                elif N >= 1:
                                    pk = (prv, 15 * 128, (v1p, 0, "v1p"))
                                else:
                                    pk = None
                            elif br == 1:
                                n_, r = gq, jj
                                qs, st = n_ * 512 + r, 4
                                vcur = (v2, n_ * 4 + r, "v2")
                                if n_ >= 1:
                                    pk = (cur, (n_ - 1) * 512 + r, (v2, (n_ - 1) * 4 + r, "v2"))
                                elif N >= 1:
                                    pk = (prv, 3 * 512 + r, (v2p, r, "v2p"))
                                else:
                                    pk = None
                            else:
                                r = 4 * gq + jj
                                qs, st = r, 16
                                vcur = (v3[cur], r, ("v3", cur))
                                pk = (prv, r, (v3[prv], r, ("v3", prv))) if N >= 1 else None
                            sbk = cnt[0] % 3
                            ei = cnt[0] % 2
                            cnt[0] += 1
                            blks = ([(0,) + pk] if pk else []) + [(1, cur, qs, vcur)]
                            for hl in range(2):
                                for (blk, slot, ks, _v) in blks:
                                    S_.op("pe", lambda e, sbk=sbk, hl=hl, blk=blk, slot=slot, ks=ks, qs=qs, st=st, hp=hp: e.matmul(
                                        P.pb[sbk][:, hl * 256 + blk * 128: hl * 256 + (blk + 1) * 128],
                                        lhsT=kt[slot][:, 2 * hp + hl, ks:ks + 127 * st + 1:st],
                                        rhs=qt[:, 2 * hp + hl, qs:qs + 127 * st + 1:st],
                                        start=True, stop=True),
                                        reads=[("kt", slot), "qt"], writes=[("pb", sbk)])
                            c0 = 0 if pk else 128
                            for hl in range(2):
                                S_.op("act", lambda e, sbk=sbk, ei=ei, hl=hl, c0=c0: e.activation(
                                    out=Et[ei][:, hl * 256 + c0:(hl + 1) * 256], in_=P.pb[sbk][:, hl * 256 + c0:(hl + 1) * 256],
                                    func=AF.Exp, scale=0.125), reads=[("pb", sbk)], writes=[("E", ei)])
                            eoff = (br * 8 + 2 * hp) * 256
                            eng = "dve" if cnt[0] % 2 == 0 else "pool"
                            for hl in range(2):
                                S_.op(eng, lambda e, ei=ei, hl=hl, c0=c0, eoff=eoff: e.tensor_tensor(
                                    out=PT[ei][:, hl * 256 + c0:(hl + 1) * 256], in0=Et[ei][:, hl * 256 + c0:(hl + 1) * 256],
                                    in1=EB[:, eoff + hl * 256 + c0: eoff + (hl + 1) * 256], op=ALU.mult),
                                    reads=[("E", ei), "EB"], writes=[("PT", ei)])
                            for hl in range(2 if mode >= 3 else 0):
                                h = 2 * hp + hl
                                for bi, (blk, slot, ks, (vt, vi, vkey)) in enumerate(blks):
                                    fl = _kw(start=(bi == 0), stop=(bi == len(blks) - 1), tile_position=(0, hl * 64))
                                    S_.op("pe", lambda e, nbk=nbk, hl=hl, jj=jj, vt=vt, vi=vi, h=h, ei=ei, blk=blk, fl=fl: e.matmul(
                                        P.pb[nbk][hl * 64:(hl + 1) * 64, jj * 128:(jj + 1) * 128],
                                        lhsT=vt[:, vi, h * 64:(h + 1) * 64],
                                        rhs=PT[ei][:, hl * 256 + blk * 128: hl * 256 + (blk + 1) * 128], **fl),
                                        reads=[vkey, ("PT", ei)], writes=[("pb", nbk)])
                                    S_.op("pe", lambda e, dbk=dbk, hl=hl, jj=jj, ei=ei, blk=blk, fl=fl: e.matmul(
                                        P.pb[dbk][hl * 64:(hl + 1) * 64, jj * 128:(jj + 1) * 128],
                                        lhsT=ones[:, 0:64],
                                        rhs=PT[ei][:, hl * 256 + blk * 128: hl * 256 + (blk + 1) * 128], **fl),
                                        reads=["ones", ("PT", ei)], writes=[("pb", dbk)])
                        for bk, acc, akey in (((nbk, accn, "accn"), (dbk, accd, "accd")) if mode >= 4 else ()):
                            if br == 0:
                                S_.op("act", lambda e, bk=bk, acc=acc, gq=gq: e.activation(
                                    out=acc[:, gq * 512:(gq + 1) * 512], in_=P.pb[bk][:], func=AF.Identity),
                                    reads=[("pb", bk)], writes=[akey])
                            else:
                                if br == 1:
                                    oap = acc[:, gq * 512:(gq + 1) * 512].rearrange("p (i r) -> p r i", r=4)
                                else:
                                    oap = acc[:].rearrange("p (i r) -> p r i", r=16)[:, 4 * gq:4 * gq + 4, :]
                                S_.op("dve", lambda e, bk=bk, oap=oap: e.tensor_tensor(
                                    out=oap, in0=P.pb[bk][:].rearrange("p (r i) -> p r i", r=4), in1=oap, op=ALU.add),
                                    reads=[("pb", bk), akey], writes=[akey])
                if mode < 5:
                    continue
                S_.op("dve", lambda e: e.reciprocal(out=accd[:], in_=accd[:]), reads=["accd"], writes=["accd"])
                S_.op("dve", lambda e: e.tensor_tensor(out=mst[:], in0=accn[:], in1=accd[:], op=ALU.mult),
                      reads=["accn", "accd"], writes=["mst"])
                S_.op("sp", lambda e, hp=hp, t0=t0: e.dma_start(out=MIXT[hp * 128:(hp + 1) * 128, t0:t0 + 2048], in_=mst[:]),
                      reads=["mst"], dma=True, semkey="mst")
            if N + 1 < NSB:
                S_.op("pool", lambda e: e.tensor_copy(out=v1p[:, 0, :], in_=v1[:, 15, :]), reads=["v1"], writes=["v1p"])
                S_.op("pool", lambda e: e.tensor_copy(out=v2p[:], in_=v2[:, 12:16, :]), reads=["v2"], writes=["v2p"])
        P.finish()


def host_consts(inputs):
    import math
    rel_bias = np.asarray(inputs["rel_bias"], np.float32)
    k = np.arange(128)[:, None]
    q = np.arange(128)[None, :]
    steps_prev = q + 128 - k
    steps_cur = q - k
    tb = np.full((128, 3, 8, 2, 128), -30000.0, np.float32)

    def bucket(dist):
        dist = np.asarray(dist, np.int64)
        max_exact = 16
        dist_f = np.maximum(dist, 1).astype(np.float32)
        lg = (np.log(dist_f / np.float32(max_exact)) / np.float32(math.log(2048 / max_exact))
              * np.float32(32 - max_exact)).astype(np.float32)
        large = max_exact + lg.astype(np.int32)
        return np.where(dist < max_exact, dist, np.minimum(large, 31)).astype(np.int64)

    for br, d in enumerate((1, 4, 16)):
        for blk, steps in ((0, steps_prev), (1, steps_cur)):
            valid = (steps >= 0) & (steps <= 128)
            bk = bucket(np.maximum(steps, 0) * d)
            for h in range(8):
                vals = rel_bias[bk, h]
                tb[:, br, h, blk, :] = np.where(valid, vals, np.float32(-30000.0))
    return {"tb": np.ascontiguousarray(tb.reshape(128, 24 * 256))}


def _phase4a(nc, semstack, S, x, MIXT, w_out, GROW, MODC, ident_in, X1, H2T):
    NG = S // 512
    with Phase(nc, semstack, "p4a") as P:
        S_ = P.S
        ident = P.sb("ident", [128, 128], BF16)
        modc = P.sb("modc", [128, 32], F32)
        epsT = P.sb("eps", [128, 1], F32)
        g1 = P.sb("g1", [128, D], F32)
        wo = P.sb("wo", [128, 8, D], BF16)
        S_.op("pool", lambda e: e.dma_start(out=ident[:], in_=ident_in), writes=["ident"], dma=True, semkey="ident")
        S_.op("sp", lambda e: e.dma_start(out=modc[:], in_=MODC), writes=["modc"], dma=True, semkey="modc")
        S_.op("sp", lambda e: e.dma_start(out=g1[:], in_=GROW[0]), writes=["g1"], dma=True, semkey="g1")
        S_.op("pool", lambda e: e.dma_start(out=wo[:], in_=w_out.rearrange("(k p) f -> p k f", p=128)), writes=["wo"],
              dma=True, semkey="wo")
        S_.op("dve", lambda e: e.memset(epsT[:], EPS), writes=["eps"])
        for k in range(8):
            S_.op("dve", lambda e, k=k: e.tensor_tensor(out=wo[:, k, :], in0=wo[:, k, :], in1=g1[:], op=ALU.mult),
                  reads=["wo", "g1"], writes=["wo"])
        xt = [P.sb("xt%d" % i, [128, 4, D], F32) for i in range(2)]
        mt = [P.sb("mt%d" % i, [128, 8, 512], BF16) for i in range(2)]
        junk = P.sb("junk", [128, D], BF16)
        ss = P.sb("ss", [128, 4], F32)
        rstd = P.sb("rstd", [128, 4], F32)
        xs = P.sb("xs", [128, 4, D], BF16)
        hst = [P.sb("hst%d" % i, [128, 8, 512], BF16) for i in range(2)]
        xv = x.rearrange("(g j p) d -> g p j d", j=4, p=128)
        x1v = X1.rearrange("(g j p) d -> g p j d", j=4, p=128)
        for g in range(NG):
            sl = g % 2
            S_.op("sp", lambda e, g=g, sl=sl: e.dma_start(out=xt[sl][:], in_=xv[g]), writes=[("xt", sl)], dma=True, semkey="xt%d" % sl)
            S_.op("sp", lambda e, g=g, sl=sl: e.dma_start(out=mt[sl][:], in_=MIXT.rearrange("(c p) s -> p c s", p=128)[:, :, g * 512:(g + 1) * 512]),
                  writes=[("mt", sl)], dma=True, semkey="mt%d" % sl)
            for j in range(4):
                for half in range(2):
                    b = P.bank()
                    for k in range(8):
                        S_.op("pe", lambda e, b=b, k=k, j=j, half=half, sl=sl: e.matmul(
                            P.pb[b][:], lhsT=mt[sl][:, k, j * 128:(j + 1) * 128], rhs=wo[:, k, half * 512:(half + 1) * 512],
                            start=(k == 0), stop=(k == 7)), reads=[("mt", sl), "wo"], writes=[("pb", b)])
                    S_.op("dve", lambda e, b=b, j=j, half=half, sl=sl: e.tensor_tensor(
                        out=xt[sl][:, j, half * 512:(half + 1) * 512], in0=P.pb[b][:], in1=xt[sl][:, j, half * 512:(half + 1) * 512],
                        op=ALU.add), reads=[("pb", b), ("xt", sl)], writes=[("xt", sl)])
            S_.op("sp", lambda e, g=g, sl=sl: e.dma_start(out=x1v[g], in_=xt[sl][:]), reads=[("xt", sl)], dma=True, semkey="xt%d" % sl)
            S_.op("pool", lambda e: e.memset(ss[:], 0.0), writes=["ss"])
            for j in range(4):
                S_.op("act", lambda e, j=j, sl=sl: e.activation(out=junk[:], in_=xt[sl][:, j, :], func=AF.Square, accum_out=ss[:, j:j + 1]),
                      reads=[("xt", sl), "ss"], writes=["ss", "junk"])
            S_.op("act", lambda e: e.activation(out=rstd[:], in_=ss[:], func=AF.Sqrt, bias=epsT[:], scale=1.0 / D),
                  reads=["ss", "eps"], writes=["rstd"])
            S_.op("dve", lambda e: e.reciprocal(out=rstd[:], in_=rstd[:]), reads=["rstd"], writes=["rstd"])
            for j in range(4):
                S_.op("dve", lambda e, j=j, sl=sl: e.tensor_scalar(out=xs[:, j, :], in0=xt[sl][:, j, :], scalar1=rstd[:, j:j + 1],
                                                                 scalar2=None, op0=ALU.mult),
                      reads=[("xt", sl), "rstd"], writes=[("xs", j)])
            for c2 in range(4):
                b = P.bank()
                pbf = P.pb[b][:].bitcast(BF16)
                for cc in range(2):
                    c = c2 * 2 + cc
                    for j in range(4):
                        S_.op("pe", lambda e, pbf=pbf, cc=cc, c=c, j=j: e.transpose(
                            pbf[:, cc * 512 + j * 128: cc * 512 + (j + 1) * 128], xs[:, j, c * 128:(c + 1) * 128], ident[:]),
                            reads=[("xs", j), "ident"], writes=[("pb", b)])
                for cc in range(2):
                    c = c2 * 2 + cc
                    S_.op("act", lambda e, pbf=pbf, cc=cc, c=c, sl=sl: e.activation(
                        out=hst[sl][:, c, :], in_=pbf[:, cc * 512:(cc + 1) * 512], func=AF.Identity,
                        bias=modc[:, 16 + c:17 + c], scale=modc[:, 24 + c:25 + c]),
                        reads=[("pb", b), "modc"], writes=[("hst", sl)])
            S_.op("sp", lambda e, g=g, sl=sl: e.dma_start(out=H2T.rearrange("(c p) s -> p c s", p=128)[:, :, g * 512:(g + 1) * 512],
                                                        in_=hst[sl][:]), reads=[("hst", sl)], dma=True, semkey="hst%d" % sl)
        P.finish()


def _phase4b(nc, semstack, S, X1, H2T, w_gate, w_up, w_down, GROW, fg_row, out):
    G = 256
    NG = S // G
    NF = DFF // 128
    with Phase(nc, semstack, "p4b") as P:
        S_ = P.S
        epsT = P.sb("eps", [128, 1], F32)
        fg = P.sb("fg", [128, D], F32)
        wg = P.sb("wg", [128, 8, DFF], BF16)
        wu = P.sb("wu", [128, 8, DFF], BF16)
        wd = P.sb("wd", [128, NF, D], BF16)
        S_.op("dve", lambda e: e.memset(epsT[:], EPS), writes=["eps"])
        S_.op("sp", lambda e: e.dma_start(out=fg[:], in_=fg_row.partition_broadcast(128)), writes=["fg"], dma=True, semkey="fg")
        for k in range(8):
            S_.op("pool", lambda e, k=k: e.dma_start(out=wg[:, k, :], in_=w_gate[k * 128:(k + 1) * 128, :]), writes=["wg"], dma=True, semkey="wg")
            S_.op("pool", lambda e, k=k: e.dma_start(out=wu[:, k, :], in_=w_up[k * 128:(k + 1) * 128, :]), writes=["wu"], dma=True, semkey="wu")
        xq = [P.sb("xq%d" % i, [128, 2, D], F32) for i in range(2)]
        g2 = xq[1]
        S_.op("sp", lambda e: e.dma_start(out=g2[:, 0, :], in_=GROW[1]), writes=[("xq", 1)], dma=True, semkey="xq1")
        S_.op("pool", lambda e: e.dma_start(out=wd[:], in_=w_down.rearrange("(k p) f -> p k f", p=128)), writes=["wd"], dma=True, semkey="wd")
        for k in range(NF):
            S_.op("dve", lambda e, k=k: e.tensor_tensor(out=wd[:, k, :], in0=wd[:, k, :], in1=g2[:, 0, :], op=ALU.mult),
                  reads=["wd", ("xq", 1)], writes=["wd"])
        ht = [P.sb("ht%d" % i, [128, 8, G], BF16) for i in range(2)]
        aT = P.sb("aT", [128, NF, G], BF16)
        sg = [P.sb("sg%d" % i, [128, G], BF16) for i in range(2)]
        junk = P.sb("junk", [128, D], BF16)
        ss = P.sb("ss", [128, 2], F32)
        rstd = P.sb("rstd", [128, 2], F32)
        x1v = X1.rearrange("(g j p) d -> g p j d", j=2, p=128)
        ov = out.rearrange("(g j p) d -> g p j d", j=2, p=128)
        for g in range(NG):
            sl = g % 2
            S_.op("sp", lambda e, g=g, sl=sl: e.dma_start(out=xq[sl][:], in_=x1v[g]), writes=[("xq", sl)], dma=True, semkey="xq%d" % sl)
            S_.op("sp", lambda e, g=g, sl=sl: e.dma_start(out=ht[sl][:], in_=H2T.rearrange("(c p) s -> p c s", p=128)[:, :, g * G:(g + 1) * G]),
                  writes=[("ht", sl)], dma=True, semkey="ht%d" % sl)
            for fc in range(NF):
                ba, bb = P.bank(), P.bank()
                for (bk, w, wk) in ((ba, wg, "wg"), (bb, wu, "wu")):
                    for k in range(8):
                        S_.op("pe", lambda e, bk=bk, w=w, k=k, fc=fc, sl=sl: e.matmul(
                            P.pb[bk][:, 0:G], lhsT=w[:, k, fc * 128:(fc + 1) * 128], rhs=ht[sl][:, k, :], start=(k == 0), stop=(k == 7)),
                            reads=[wk, ("ht", sl)], writes=[("pb", bk)])
                si = fc % 2
                S_.op("act", lambda e, ba=ba, si=si: e.activation(out=sg[si][:], in_=P.pb[ba][:, 0:G], func=AF.Silu),
                      reads=[("pb", ba)], writes=[("sg", si)])
                S_.op("dve", lambda e, bb=bb, si=si, fc=fc: e.tensor_tensor(out=aT[:, fc, :], in0=P.pb[bb][:, 0:G], in1=sg[si][:], op=ALU.mult),
                      reads=[("pb", bb), ("sg", si)], writes=["aT"])
            for j in range(2):
                for half in range(2):
                    b = P.bank()
                    for k in range(NF):
                        S_.op("pe", lambda e, b=b, k=k, j=j, half=half: e.matmul(
                            P.pb[b][:], lhsT=aT[:, k, j * 128:(j + 1) * 128], rhs=wd[:, k, half * 512:(half + 1) * 512],
                            start=(k == 0), stop=(k == NF - 1)), reads=["aT", "wd"], writes=[("pb", b)])
                    S_.op("dve", lambda e, b=b, j=j, half=half, sl=sl: e.tensor_tensor(
                        out=xq[sl][:, j, half * 512:(half + 1) * 512], in0=P.pb[b][:], in1=xq[sl][:, j, half * 512:(half + 1) * 512],
                        op=ALU.add), reads=[("pb", b), ("xq", sl)], writes=[("xq", sl)])
            S_.op("pool", lambda e: e.memset(ss[:], 0.0), writes=["ss"])
            for j in range(2):
                S_.op("act", lambda e, j=j, sl=sl: e.activation(out=junk[:], in_=xq[sl][:, j, :], func=AF.Square, accum_out=ss[:, j:j + 1]),
                      reads=[("xq", sl), "ss"], writes=["ss", "junk"])
            S_.op("act", lambda e: e.activation(out=rstd[:], in_=ss[:], func=AF.Sqrt, bias=epsT[:], scale=1.0 / D),
                  reads=["ss", "eps"], writes=["rstd"])
            S_.op("dve", lambda e: e.reciprocal(out=rstd[:], in_=rstd[:]), reads=["rstd"], writes=["rstd"])
            for j in range(2):
                S_.op("dve", lambda e, j=j, sl=sl: e.scalar_tensor_tensor(
                    out=xq[sl][:, j, :], in0=xq[sl][:, j, :], scalar=rstd[:, j:j + 1], in1=fg[:], op0=ALU.mult, op1=ALU.mult),
                    reads=[("xq", sl), "rstd", "fg"], writes=[("xq", sl)])
            S_.op("sp", lambda e, g=g, sl=sl: e.dma_start(out=ov[g], in_=xq[sl][:]), reads=[("xq", sl)], dma=True, semkey="xq%d" % sl)
        P.finish()


def _phase3(nc, semstack, S, QD, KD, VD, ZS, BA, MIXT, cst):
    NG = S // 512
    with Phase(nc, semstack, "p3") as P:
        S_ = P.S
        sb = P.sb
        ident = sb("ident", [128, 128], BF16)
        identf = sb("identf", [128, 128], F32)
        trif = sb("trif", [128, 128], F32)
        bonesf = sb("bonesf", [128, 128], F32)
        onesf = sb("onesf", [128, 128], F32)
        maskL = sb("maskL", [128, 128], F32)
        maskU = sb("maskU", [128, 128], F32)
        negA = sb("negA", [128, 8], F32)
        dtb = sb("dtb", [128, 8], F32)
        dng = sb("dng", [128, 64], F32)
        one1 = sb("one1", [128, 1], F32)
        epsT = sb("eps", [128, 1], F32)
        S_.op("pool", lambda e: e.dma_start(out=ident[:], in_=cst["ident"]), writes=["ident"], dma=True, semkey="ident")
        for nm, t in (("identf", identf), ("trif", trif), ("bonesf", bonesf), ("maskL", maskL), ("maskU", maskU)):
            S_.op("sp", lambda e, nm=nm, t=t: e.dma_start(out=t[:], in_=cst[nm]), writes=[nm], dma=True, semkey=nm)
        S_.op("sp", lambda e: e.dma_start(out=negA[:], in_=cst["alog"].partition_broadcast(128)), writes=["negA"], dma=True, semkey="negA")
        S_.op("sp", lambda e: e.dma_start(out=dtb[:], in_=cst["dtb"].partition_broadcast(128)), writes=["dtb"], dma=True, semkey="dtb")
        S_.op("sp", lambda e: e.dma_start(out=dng[:], in_=cst["dng"].partition_broadcast(128)), writes=["dng"], dma=True, semkey="dng")
        S_.op("pool", lambda e: e.memset(onesf[:], 1.0), writes=["onesf"])
        S_.op("pool", lambda e: e.memset(one1[:], 1.0), writes=["one1"])
        S_.op("pool", lambda e: e.memset(epsT[:], EPS), writes=["eps"])
        S_.op("act", lambda e: e.activation(out=negA[:], in_=negA[:], func=AF.Exp), reads=["negA"], writes=["negA"])
        S_.op("dve", lambda e: e.tensor_scalar(out=negA[:], in0=negA[:], scalar1=-1.0, scalar2=None, op0=ALU.mult),
              reads=["negA"], writes=["negA"])
        kd = [sb("kd%d" % i, [128, 4, 512], BF16) for i in range(2)]
        qd = [sb("qd%d" % i, [64, 8, 512], BF16) for i in range(2)]
        kd8 = [sb("kd8_%d" % i, [64, 8, 512], BF16) for i in range(2)]
        vd = [sb("vd%d" % i, [128, 4, 512], BF16) for i in range(2)]
        zs = [sb("zs%d" % i, [128, 4, 512], BF16) for i in range(2)]
        ba = [sb("ba%d" % i, [128, 4, 16], F32) for i in range(2)]
        mst = [sb("mst%d" % i, [128, 4, 512], BF16) for i in range(2)]
        y16 = sb("y16", [128, 16], F32)
        u16 = sb("u16", [128, 16], F32)
        beta = sb("beta", [128, 8], F32)
        gg = sb("gg", [128, 8], F32)
        gcl = sb("gcl", [128, 16], F32)
        eg = sb("eg", [128, 8], F32)
        kdsc = sb("kdsc", [128, 8], F32)
        be = sb("be", [128, 8], F32)
        offL = sb("offL", [128, 8], F32)
        decS = sb("decS", [64, 16], F32)
        kbg = sb("kbg", [128, 512], BF16)
        kdec = sb("kdec", [128, 512], BF16)
        bv = sb("bv", [128, 512], BF16)
        DG = sb("DG", [128, 8, 128], F32)
        tL = sb("tL", [128, 8, 128], F32)
        tU = sb("tU", [128, 8, 128], F32)
        Lb = sb("Lb", [128, 8, 128], F32)
        TTb = sb("TTb", [128, 8, 128], BF16)
        Ui = sb("Ui", [128, 8, 128], BF16)
        qkdT = sb("qkdT", [128, 8, 128], BF16)
        X = [sb("X%d" % i, [128, 8, 128], F32) for i in range(2)]
        Y = [sb("Y%d" % i, [128, 8, 128], F32) for i in range(2)]
        Q = [sb("Q%d" % i, [128, 8, 128], F32) for i in range(2)]
        uu = sb("uu", [128, 512], F32)
        wT = sb("wT", [64, 8, 128], BF16)
        vnew = sb("vnew", [128, 512], BF16)
        o1s = sb("o1s", [128, 512], F32)
        otm = sb("otm", [128, 512], F32)
        sqo = sb("sqo", [128, 512], F32)
        ssq = sb("ssq", [128, 8], F32)
        onb = sb("onb", [128, 512], BF16)
        Sf = sb("Sf", [64, 512], F32)
        S1 = sb("S1", [64, 512], F32)
        Sbf = sb("Sbf", [64, 512], BF16)
        S_.op("pool", lambda e: e.memset(Sf[:], 0.0), writes=["Sf"])
        S_.op("pool", lambda e: e.memset(Sbf[:], 0.0), writes=["Sbf"])

        def bc8(t):
            return lambda n: t.unsqueeze(2).to_broadcast([128, 8, n])

        for g in range(NG):
            sl = g % 2
            for nm, t, src, pp in (("kd", kd, KD, 128), ("qd", qd, QD, 64), ("kd8", kd8, KD, 64), ("vd", vd, VD, 128), ("zs", zs, ZS, 128)):
                S_.op("sp", lambda e, t=t, src=src, g=g, sl=sl, pp=pp: e.dma_start(
                    out=t[sl][:], in_=src.rearrange("(c p) s -> p c s", p=pp)[:, :, g * 512:(g + 1) * 512]),
                    writes=[(nm, sl)], dma=True, semkey="%s%d" % (nm, sl))
            S_.op("sp", lambda e, g=g, sl=sl: e.dma_start(out=ba[sl][:], in_=BA.rearrange("(g j p) f -> g p j f", j=4, p=128)[g]),
                  writes=[("ba", sl)], dma=True, semkey="ba%d" % sl)
            for j in range(4):
                c0 = j * 128
                S_.op("dve", lambda e, j=j, sl=sl: e.tensor_scalar(out=y16[:, 0:8], in0=ba[sl][:, j, 0:8], scalar1=-1.0, scalar2=None, op0=ALU.mult),
                      reads=[("ba", sl)], writes=["y16a"])
                S_.op("dve", lambda e, j=j, sl=sl: e.tensor_tensor(out=y16[:, 8:16], in0=ba[sl][:, j, 8:16], in1=dtb[:], op=ALU.add),
                      reads=[("ba", sl), "dtb"], writes=["y16b"])
                S_.op("act", lambda e: e.activation(out=u16[:], in_=y16[:], func=AF.Exp), reads=["y16a", "y16b"], writes=["u16"])
                S_.op("act", lambda e: e.activation(out=u16[:], in_=u16[:], func=AF.Ln, bias=one1[:], scale=1.0), reads=["u16", "one1"], writes=["u16"])
                S_.op("act", lambda e: e.activation(out=beta[:], in_=u16[:, 0:8], func=AF.Exp, scale=-1.0), reads=["u16"], writes=["beta"])
                S_.op("dve", lambda e: e.tensor_tensor(out=gg[:], in0=u16[:, 8:16], in1=negA[:], op=ALU.mult), reads=["u16", "negA"], writes=["gg"])
                b = P.bank()
                S_.op("pe", lambda e, b=b: e.matmul(P.pb[b][:, 0:8], lhsT=trif[:], rhs=gg[:], start=True, stop=True),
                      reads=["trif", "gg"], writes=[("pb", b)])
                S_.op("pe", lambda e, b=b: e.matmul(P.pb[b][:, 8:16], lhsT=bonesf[:], rhs=gg[:], start=True, stop=True),
                      reads=["bonesf", "gg"], writes=[("pb", b)])
                for ch in range(2):
                    S_.op("pe", lambda e, b=b, ch=ch: e.matmul(
                        P.pb[b][0:64, 16 + ch * 8:24 + ch * 8], lhsT=bonesf[:, ch * 64:(ch + 1) * 64],
                        rhs=gg[:], start=True, stop=True), reads=["bonesf", "gg"], writes=[("pb", b)])
                S_.op("dve", lambda e, b=b: e.tensor_copy(out=gcl[:], in_=P.pb[b][:, 0:16]), reads=[("pb", b)], writes=["gcl"])
                S_.op("act", lambda e, b=b: e.activation(out=decS[:], in_=P.pb[b][0:64, 16:32], func=AF.Exp), reads=[("pb", b)], writes=["decS"])
                S_.op("act", lambda e: e.activation(out=eg[:], in_=gcl[:, 0:8], func=AF.Exp), reads=["gcl"], writes=["eg"])
                S_.op("dve", lambda e: e.tensor_tensor(out=kdsc[:], in0=gcl[:, 8:16], in1=gcl[:, 0:8], op=ALU.subtract), reads=["gcl"], writes=["kdsc"])
                S_.op("act", lambda e: e.activation(out=kdsc[:], in_=kdsc[:], func=AF.Exp), reads=["kdsc"], writes=["kdsc"])
                S_.op("dve", lambda e: e.tensor_tensor(out=be[:], in0=beta[:], in1=eg[:], op=ALU.mult), reads=["beta", "eg"], writes=["be"])
                S_.op("dve", lambda e: e.tensor_tensor(out=offL[:], in0=gcl[:, 0:8], in1=u16[:, 0:8], op=ALU.subtract), reads=["gcl", "u16"], writes=["offL"])
                bk_ = P.bank()
                bv_ = bk_
                for (off, src, key) in ((0, kd, "kd"), (512, vd, "vd")):
                    pbf = P.pb[bk_][:].bitcast(BF16)
                    for hp in range(4):
                        S_.op("pe", lambda e, pbf=pbf, hp=hp, src=src, sl=sl, c0=c0, off=off: e.transpose(
                            pbf[:, off + hp * 128:off + (hp + 1) * 128], src[sl][:, hp, c0:c0 + 128], ident[:]),
                            reads=[(key, sl), "ident"], writes=[("pb", bk_)])
                pk = P.pb[bk_][:].bitcast(BF16)[:, 0:512].rearrange("p (h d) -> p h d", h=8)
                pv = P.pb[bv_][:].bitcast(BF16)[:, 512:1024].rearrange("p (h d) -> p h d", h=8)
                S_.op("dve", lambda e, pk=pk: e.tensor_tensor(out=kbg[:].rearrange("p (h d) -> p h d", h=8), in0=pk, in1=bc8(be[:])(64), op=ALU.mult),
                      reads=[("pb", bk_), "be"], writes=["kbg"])
                S_.op("dve", lambda e, pk=pk: e.tensor_tensor(out=kdec[:].rearrange("p (h d) -> p h d", h=8), in0=pk, in1=bc8(kdsc[:])(64), op=ALU.mult),
                      reads=[("pb", bk_), "kdsc"], writes=["kdec"])
                S_.op("dve", lambda e, pv=pv: e.tensor_tensor(out=bv[:].rearrange("p (h d) -> p h d", h=8), in0=pv, in1=bc8(beta[:])(64), op=ALU.mult),
                      reads=[("pb", bv_), "beta"], writes=["bv"])
                bKK = [P.bank(), P.bank()]
                bQK = [P.bank(), P.bank()]
                for h in range(8):
                    hp, hl = h // 2, h % 2
                    S_.op("pe", lambda e, h=h, sl=sl, c0=c0, bKK=bKK: e.matmul(
                        P.pb[bKK[h // 4]][:, (h % 4) * 128:(h % 4 + 1) * 128], lhsT=kd8[sl][:, h, c0:c0 + 128],
                        rhs=kd8[sl][:, h, c0:c0 + 128], start=True, stop=True),
                        reads=[("kd8", sl)], writes=[("pb", bKK[h // 4])])
                    S_.op("pe", lambda e, h=h, sl=sl, c0=c0, bQK=bQK: e.matmul(
                        P.pb[bQK[h // 4]][:, (h % 4) * 128:(h % 4 + 1) * 128], lhsT=kd8[sl][:, h, c0:c0 + 128],
                        rhs=qd[sl][:, h, c0:c0 + 128], start=True, stop=True),
                        reads=[("kd8", sl), ("qd", sl)], writes=[("pb", bQK[h // 4])])
                S_.op("pool", lambda e: e.tensor_tensor(out=DG[:], in0=identf[:].unsqueeze(1).to_broadcast([128, 8, 128]),
                                                       in1=bc8(gcl[:, 0:8])(128), op=ALU.mult), reads=["identf", "gcl"], writes=["DG"])
                for hh in range(2):
                    bG = P.bank()
                    hs = slice(hh * 4, hh * 4 + 4)
                    S_.op("pe", lambda e, bG=bG, hs=hs: e.matmul(P.pb[bG][:], lhsT=onesf[:], rhs=DG[:, hs, :].rearrange("p h c -> p (h c)"),
                                                                 start=True, stop=True), reads=["onesf", "DG"], writes=[("pb", bG)])
                    pG = P.pb[bG][:].rearrange("p (h c) -> p h c", h=4)
                    S_.op("dve", lambda e, pG=pG, hs=hs: e.scalar_tensor_tensor(
                        out=tL[:, hs, :], in0=pG, scalar=-1.0, in1=offL[:, hs].unsqueeze(2).to_broadcast([128, 4, 128]),
                        op0=ALU.mult, op1=ALU.add), reads=[("pb", bG), "offL"], writes=[("tL", hh)])
                    S_.op("dve", lambda e, pG=pG, hs=hs: e.tensor_tensor(
                        out=tU[:, hs, :], in0=pG, in1=gcl[:, hs].unsqueeze(2).to_broadcast([128, 4, 128]), op=ALU.subtract),
                        reads=[("pb", bG), "gcl"], writes=[("tU", hh)])
                    S_.op("pool", lambda e, hs=hs: e.tensor_tensor(out=tL[:, hs, :], in0=tL[:, hs, :],
                                                                   in1=maskL[:].unsqueeze(1).to_broadcast([128, 4, 128]), op=ALU.add),
                          reads=[("tL", hh), "maskL"], writes=[("tL", hh)])
                    S_.op("pool", lambda e, hs=hs: e.tensor_tensor(out=tU[:, hs, :], in0=tU[:, hs, :],
                                                                   in1=maskU[:].unsqueeze(1).to_broadcast([128, 4, 128]), op=ALU.add),
                          reads=[("tU", hh), "maskU"], writes=[("tU", hh)])
                    S_.op("act", lambda e, hs=hs: e.activation(out=Lb[:, hs, :], in_=tL[:, hs, :], func=AF.Exp), reads=[("tL", hh)], writes=[("Lb", hh)])
                    S_.op("act", lambda e, hs=hs: e.activation(out=Ui[:, hs, :], in_=tU[:, hs, :], func=AF.Exp), reads=[("tU", hh)], writes=[("Ui", hh)])
                    S_.op("dve", lambda e, hh=hh, hs=hs, bKK=bKK: e.scalar_tensor_tensor(
                        out=X[0][:, hs, :], in0=P.pb[bKK[hh]][:].rearrange("p (h c) -> p h c", h=4), scalar=-1.0, in1=Lb[:, hs, :],
                        op0=ALU.mult, op1=ALU.mult), reads=[("pb", bKK[hh]), ("Lb", hh)], writes=[("X", 0)])
                    S_.op("dve", lambda e, hh=hh, hs=hs, bQK=bQK: e.tensor_tensor(
                        out=qkdT[:, hs, :], in0=P.pb[bQK[hh]][:].rearrange("p (h c) -> p h c", h=4), in1=Ui[:, hs, :], op=ALU.mult),
                        reads=[("pb", bQK[hh]), ("Ui", hh)], writes=["qkdT"])
                bB = [P.bank(), P.bank()]
                for h in range(8):
                    S_.op("pe", lambda e, bB=bB, h=h: e.transpose(P.pb[bB[h // 4]][:, (h % 4) * 128:(h % 4 + 1) * 128], X[0][:, h, :], identf[:]),
                          reads=[("X", 0), "identf"], writes=[("pb", bB[h // 4])])
                for hh in range(2):
                    S_.op("act", lambda e, bB=bB, hh=hh: e.activation(out=Y[0][:, hh * 4:hh * 4 + 4, :].rearrange("p h c -> p (h c)"), in_=P.pb[bB[hh]][:],
                                                                     func=AF.Identity), reads=[("pb", bB[hh])], writes=[("Y", 0)])
                S_.op("pool", lambda e: e.tensor_tensor(out=Q[0][:], in0=Y[0][:], in1=identf[:].unsqueeze(1).to_broadcast([128, 8, 128]), op=ALU.add),
                      reads=[("Y", 0), "identf"], writes=[("Q", 0)])
                for lv in range(1, 6):
                    a, n = (lv - 1) % 2, lv % 2
                    bX = [P.bank(), P.bank()]
                    for h in range(8):
                        S_.op("pe", lambda e, h=h, a=a, bX=bX: e.matmul(P.pb[bX[h // 4]][:, (h % 4) * 128:(h % 4 + 1) * 128],
                                                                     lhsT=Y[a][:, h, :], rhs=X[a][:, h, :], start=True, stop=True),
                              reads=[("X", a), ("Y", a)], writes=[("pb", bX[h // 4])])
                    for hh in range(2):
                        S_.op("act", lambda e, hh=hh, n=n, bX=bX: e.activation(
                            out=X[n][:, hh * 4:hh * 4 + 4, :].rearrange("p h c -> p (h c)"), in_=P.pb[bX[hh]][:], func=AF.Identity),
                            reads=[("pb", bX[hh])], writes=[("X", n)])
                    if lv < 5:
                        bY = [P.bank(), P.bank()]
                        for h in range(8):
                            S_.op("pe", lambda e, h=h, a=a, bY=bY: e.matmul(P.pb[bY[h // 4]][:, (h % 4) * 128:(h % 4 + 1) * 128],
                                                                         lhsT=X[a][:, h, :], rhs=Y[a][:, h, :], start=True, stop=True),
                                  reads=[("X", a), ("Y", a)], writes=[("pb", bY[h // 4])])
                        for hh in range(2):
                            S_.op("dve", lambda e, hh=hh, n=n, bY=bY: e.tensor_copy(
                                out=Y[n][:, hh * 4:hh * 4 + 4, :].rearrange("p h c -> p (h c)"), in_=P.pb[bY[hh]][:]),
                                reads=[("pb", bY[hh])], writes=[("Y", n)])
                    bQ = [P.bank(), P.bank()]
                    for h in range(8):
                        S_.op("pe", lambda e, h=h, a=a, n=n, bQ=bQ: e.matmul(P.pb[bQ[h // 4]][:, (h % 4) * 128:(h % 4 + 1) * 128],
                                                                          lhsT=X[n][:, h, :], rhs=Q[a][:, h, :], start=True, stop=True),
                              reads=[("X", n), ("Q", a)], writes=[("pb", bQ[h // 4])])
                    for hh in range(2):
                        S_.op("dve", lambda e, hh=hh, n=n, a=a, bQ=bQ: e.tensor_tensor(
                            out=Q[n][:, hh * 4:hh * 4 + 4, :].rearrange("p h c -> p (h c)"), in0=P.pb[bQ[hh]][:],
                            in1=Q[a][:, hh * 4:hh * 4 + 4, :].rearrange("p h c -> p (h c)"), op=ALU.add),
                            reads=[("pb", bQ[hh]), ("Q", a)], writes=[("Q", n)])
                S_.op("act", lambda e: e.activation(out=TTb[:].rearrange("p h c -> p (h c)"), in_=Q[1][:].rearrange("p h c -> p (h c)"), func=AF.Identity),
                      reads=[("Q", 1)], writes=["TTb"])
                TT = TTb
                bu = P.bank()
                bw = [P.bank(), P.bank()]
                for h in range(8):
                    S_.op("pe", lambda e, h=h, bu=bu: e.matmul(P.pb[bu][:, h * 64:(h + 1) * 64], lhsT=TT[:, h, :], rhs=bv[:, h * 64:(h + 1) * 64],
                                                             start=True, stop=True), reads=["TTb", "bv"], writes=[("pb", bu)])
                    S_.op("pe", lambda e, h=h, bw=bw: e.matmul(
                        P.pb[bw[h // 4]][0:64, (h % 4) * 128:(h % 4 + 1) * 128], lhsT=kbg[:, h * 64:(h + 1) * 64], rhs=TT[:, h, :],
                        start=True, stop=True), reads=["TTb", "kbg"], writes=[("pb", bw[h // 4])])
                S_.op("act", lambda e, bu=bu: e.activation(out=uu[:], in_=P.pb[bu][:], func=AF.Identity), reads=[("pb", bu)], writes=["uu"])
                for hh in range(2):
                    S_.op("dve", lambda e, bw=bw, hh=hh: e.tensor_copy(out=wT[:, hh * 4:hh * 4 + 4, :].rearrange("p a c -> p (a c)"), in_=P.pb[bw[hh]][0:64, :]),
                          reads=[("pb", bw[hh])], writes=["wT"])
                for ch in range(2):
                    p0 = ch * 64
                    bvn, bo1, bo2, bs = P.bank(), P.bank(), P.bank(), P.bank()
                    for h in range(8):
                        S_.op("pe", lambda e, h=h, p0=p0, ch=ch, bvn=bvn: e.matmul(
                            P.pb[bvn][p0:p0 + 64, h * 64:(h + 1) * 64], lhsT=wT[:, h, ch * 64:(ch + 1) * 64],
                            rhs=Sbf[:, h * 64:(h + 1) * 64], start=True, stop=True, tile_position=(0, p0)),
                            reads=["wT", "Sbf"], writes=[("pb", bvn)])
                    for h in range(8):
                        S_.op("pe", lambda e, h=h, p0=p0, ch=ch, bo1=bo1, sl=sl, c0=c0: e.matmul(
                            P.pb[bo1][p0:p0 + 64, h * 64:(h + 1) * 64], lhsT=qd[sl][:, h, c0 + p0:c0 + p0 + 64],
                            rhs=Sbf[:, h * 64:(h + 1) * 64], start=True, stop=True, tile_position=(0, p0)),
                            reads=[("qd", sl), "Sbf"], writes=[("pb", bo1)])
                    S_.op("dve", lambda e, p0=p0, bvn=bvn: e.tensor_tensor(out=vnew[p0:p0 + 64, :], in0=uu[p0:p0 + 64, :], in1=P.pb[bvn][p0:p0 + 64, :],
                                                                        op=ALU.subtract), reads=["uu", ("pb", bvn)], writes=[("vnew", ch)])
                    S_.op("dve", lambda e, p0=p0, bo1=bo1: e.tensor_tensor(
                        out=o1s[p0:p0 + 64, :].rearrange("p (h d) -> p h d", h=8), in0=P.pb[bo1][p0:p0 + 64, :].rearrange("p (h d) -> p h d", h=8),
                        in1=eg[p0:p0 + 64, :].unsqueeze(2).to_broadcast([64, 8, 64]), op=ALU.mult),
                        reads=["eg", ("pb", bo1)], writes=[("o1s", ch)])
                    S_.op("pool", lambda e, ch=ch: e.tensor_tensor(
                        out=S1[:].rearrange("p (a d) -> p a d", a=8), in0=Sf[:].rearrange("p (a d) -> p a d", a=8),
                        in1=decS[:, ch * 8:ch * 8 + 8].unsqueeze(2).to_broadcast([64, 8, 64]), op=ALU.mult),
                        reads=["Sf", "decS"], writes=["S1"])
                    for h in range(8):
                        S_.op("pe", lambda e, h=h, p0=p0, ch=ch, bs=bs: e.matmul(
                            P.pb[bs][0:64, h * 64:(h + 1) * 64], lhsT=kdec[p0:p0 + 64, h * 64:(h + 1) * 64],
                            rhs=vnew[p0:p0 + 64, h * 64:(h + 1) * 64], start=True, stop=True, tile_position=(p0, 0)),
                            reads=["kdec", ("vnew", ch)], writes=[("pb", bs)])
                    for h in range(8):
                        S_.op("pe", lambda e, h=h, p0=p0, ch=ch, bo2=bo2: e.matmul(
                            P.pb[bo2][p0:p0 + 64, h * 64:(h + 1) * 64], lhsT=qkdT[p0:p0 + 64, h, ch * 64:(ch + 1) * 64],
                            rhs=vnew[p0:p0 + 64, h * 64:(h + 1) * 64], start=True, stop=True, tile_position=(p0, p0)),
                            reads=["qkdT", ("vnew", ch)], writes=[("pb", bo2)])
                    S_.op("dve", lambda e, bs=bs: e.tensor_tensor(out=Sbf[:], in0=S1[:], in1=P.pb[bs][0:64, :], op=ALU.add),
                          reads=["S1", ("pb", bs)], writes=["Sbf"])
                    S_.op("dve", lambda e, bs=bs: e.tensor_tensor(out=Sf[:], in0=S1[:], in1=P.pb[bs][0:64, :], op=ALU.add),
                          reads=["S1", ("pb", bs)], writes=["Sf"])
                    S_.op("dve", lambda e, p0=p0, bo2=bo2: e.tensor_tensor(out=otm[p0:p0 + 64, :], in0=o1s[p0:p0 + 64, :], in1=P.pb[bo2][p0:p0 + 64, :],
                                                                        op=ALU.add), reads=[("o1s", ch), ("pb", bo2)], writes=[("otm", ch)])
                OT = [("otm", 0), ("otm", 1)]
                S_.op("pool", lambda e: e.tensor_tensor(out=sqo[:], in0=otm[:], in1=otm[:], op=ALU.mult), reads=OT, writes=["sqo"])
                S_.op("dve", lambda e: e.tensor_reduce(out=ssq[:], in_=sqo[:].rearrange("p (h d) -> p h d", h=8), axis=mybir.AxisListType.X, op=ALU.add),
                      reads=["sqo"], writes=["ssq"])
                S_.op("act", lambda e: e.activation(out=ssq[:], in_=ssq[:], func=AF.Sqrt, bias=epsT[:], scale=1.0 / 64), reads=["ssq", "eps"], writes=["ssq"])
                S_.op("dve", lambda e: e.reciprocal(out=ssq[:], in_=ssq[:]), reads=["ssq"], writes=["ssq"])
                S_.op("dve", lambda e: e.tensor_tensor(out=sqo[:].rearrange("p (h d) -> p h d", h=8), in0=otm[:].rearrange("p (h d) -> p h d", h=8),
                                                      in1=bc8(ssq[:])(64), op=ALU.mult), reads=OT + ["ssq"], writes=["sqo"])
                S_.op("pool", lambda e: e.tensor_tensor(out=onb[:].rearrange("p (h d) -> p h d", h=8), in0=sqo[:].rearrange("p (h d) -> p h d", h=8),
                                                       in1=dng[:].unsqueeze(1).to_broadcast([128, 8, 64]), op=ALU.mult), reads=["sqo", "dng"], writes=["onb"])
                bo = P.bank()
                pO = P.pb[bo][:].bitcast(BF16)
                for hp in range(8):
                    S_.op("pe", lambda e, pO=pO, hp=hp: e.transpose(pO[:, hp * 128:(hp + 1) * 128], onb[:, (hp % 4) * 128:(hp % 4 + 1) * 128], ident[:]),
                          reads=["onb", "ident"], writes=[("pb", bo)])
                S_.op("dve", lambda e, pO=pO, sl=sl, c0=c0: e.tensor_tensor(
                    out=mst[sl][:, :, c0:c0 + 128], in0=pO[:, 0:512].rearrange("p (a c) -> p a c", a=4), in1=zs[sl][:, :, c0:c0 + 128], op=ALU.mult),
                    reads=[("pb", bo), ("zs", sl)], writes=[("mst", sl)])
            S_.op("sp", lambda e, g=g, sl=sl: e.dma_start(out=MIXT.rearrange("(c p) s -> p c s", p=128)[:, 4:8, g * 512:(g + 1) * 512], in_=mst[sl][:]),
                  reads=[("mst", sl)], dma=True, semkey="mst%d" % sl)
        P.finish()


def _host_all(inputs, b, S):
    f = np.float32
    col = lambda v: np.ascontiguousarray(np.asarray(v, f).reshape(-1, 128).T)
    d = {}
    d["x"] = np.ascontiguousarray(np.asarray(inputs["x"][b, :S], f))
    d["c_col"] = col(inputs["c"][b])
    d["w_ada"] = np.asarray(inputs["w_ada"][0], f)
    d["bada_col"] = col(inputs["b_ada"][0])
    d["bada_row"] = np.ascontiguousarray(np.asarray(inputs["b_ada"][0], f).reshape(6, 1024))
    d["gattn_col"] = col(inputs["norm_attn_g"][0])
    d["gffn_col"] = col(inputs["norm_ffn_g"][0])
    d["w_in"] = np.asarray(inputs["w_in"][0], f)
    cw = np.asarray(inputs["conv_w"][0], f)
    d["cw_col"] = np.ascontiguousarray(cw.T.reshape(12, 128, 4).transpose(1, 0, 2).reshape(128, 48))
    d["ident"] = np.eye(128, dtype=f)
    bo = np.zeros((128, 128), f)
    bo[:64, :64] = 1
    bo[64:, 64:] = 1
    d["bones"] = bo
    d["identf"] = np.eye(128, dtype=f)
    d["bonesf"] = bo
    idx = np.arange(128)
    same = (idx[:, None] // 64) == (idx[None, :] // 64)
    d["trif"] = (same & (idx[:, None] <= idx[None, :])).astype(f)
    d["maskL"] = np.where(same & (idx[None, :] < idx[:, None]), 0.0, -30000.0).astype(f)
    d["maskU"] = np.where(same & (idx[None, :] >= idx[:, None]), 0.0, -30000.0).astype(f)
    d["alog_row"] = np.asarray(inputs["a_log"][0], f).reshape(1, 8)
    d["dtb_row"] = np.asarray(inputs["dt_bias"][0], f).reshape(1, 8)
    d["dng_row"] = np.asarray(inputs["delta_norm_g"][0], f).reshape(1, 64)
    d["w_out"] = np.asarray(inputs["w_out"][0], f)
    d["w_gate"] = np.asarray(inputs["w_gate"][0], f)
    d["w_up"] = np.asarray(inputs["w_up"][0], f)
    d["w_down"] = np.asarray(inputs["w_down"][0], f)
    d["fg_row"] = np.asarray(inputs["final_norm_g"], f).reshape(1, 1024)
    return d


def kernel(**inputs):
    S = inputs["x"].shape[1]
    B = inputs["x"].shape[0]
    nc, semstack = build_program(S, dbg=False)
    consts = host_consts(inputs)
    in_maps = []
    for b in range(B):
        d = _host_all(inputs, b, S)
        d.update(consts)
        in_maps.append(d)
    res = run_bass_kernel_spmd(nc, in_maps, core_ids=list(range(B)))
    return np.stack([np.asarray(r["out"], np.float32) for r in res.results], axis=0)
```

```python
from contextlib import ExitStack
import numpy as np
import concourse.bass as bass
import concourse.mybir as mybir
from concourse.bass_utils import run_bass_kernel_spmd

F32 = mybir.dt.float32
BF16 = mybir.dt.bfloat16
ALU = mybir.AluOpType
AF = mybir.ActivationFunctionType

ENGS = ("pe", "act", "dve", "pool", "sp")
EPOCH = 20000


class _Op:
    __slots__ = ("eng", "fn", "deps", "dma", "semkey", "idx", "needs_inc", "sem", "val")

    def __init__(self, eng, fn, dma, semkey):
        self.eng, self.fn, self.dma, self.semkey = eng, fn, dma, semkey
        self.deps = []
        self.needs_inc = False
        self.sem = None
        self.val = 0


class Sched:
    def __init__(self, nc):
        self.nc = nc
        self.ops = {e: [] for e in ENGS}
        self.last_w = {}
        self.readers = {}
        self.last_dma_on_sem = {}
        self.n = 0

    def op(self, eng, fn, reads=(), writes=(), dma=False, semkey=None):
        o = _Op(eng, fn, dma, semkey)
        o.idx = self.n
        self.n += 1
        deps = {}

        def add(p):
            if p is None or p is o:
                return
            if (not p.dma) and (not dma) and p.eng == "pe" and eng == "pe":
                return
            deps[id(p)] = p

        for k in reads:
            add(self.last_w.get(k))
        for k in writes:
            add(self.last_w.get(k))
            for r in self.readers.get(k, ()):
                add(r)
        if dma:
            assert semkey is not None
            add(self.last_dma_on_sem.get(semkey))
            self.last_dma_on_sem[semkey] = o
        o.deps = list(deps.values())
        for p in o.deps:
            p.needs_inc = True
        for k in reads:
            self.readers.setdefault(k, []).append(o)
        for k in writes:
            self.last_w[k] = o
            self.readers[k] = []
        self.ops[eng].append(o)
        return o

    def emit(self, stack, final_wait_ops=()):
        nc = self.nc
        for o in final_wait_ops:
            o.needs_inc = True
        sems = {}

        def getsem(name):
            if name not in sems:
                sems[name] = stack.enter_context(nc.semaphore(name))
            return sems[name]

        dma_cnt = {}
        for e in ENGS:
            cnt = 0
            for o in self.ops[e]:
                if o.dma:
                    c = dma_cnt.get(o.semkey, 0) + 1
                    dma_cnt[o.semkey] = c
                    o.sem = getsem("d_" + str(o.semkey))
                    o.val = 16 * c
                    o.needs_inc = True
                elif o.needs_inc:
                    ep, v = divmod(cnt, EPOCH)
                    o.sem = getsem("c_%s_%d" % (e, ep))
                    o.val = v + 1
                    cnt += 1
        self.nsems = len(sems)
        block = stack.enter_context(nc.Block())
        engmap = {"pe": block.tensor, "act": block.scalar, "dve": block.vector,
                  "pool": block.gpsimd, "sp": block.sync}
        for e in ENGS:
            ops = self.ops[e]
            fw = [o for o in final_wait_ops] if e == "sp" else []

            def body(engine, ops=ops, fw=fw):
                waited = {}
                for o in ops:
                    for p in o.deps:
                        key = id(p.sem)
                        if waited.get(key, 0) >= p.val:
                            continue
                        engine.wait_ge(p.sem, p.val)
                        waited[key] = p.val
                    ins = o.fn(engine)
                    if o.needs_inc:
                        ins.then_inc(o.sem, 16 if o.dma else 1)
                for p in fw:
                    key = id(p.sem)
                    if waited.get(key, 0) >= p.val:
                        continue
                    engine.wait_ge(p.sem, p.val)
                    waited[key] = p.val

            engmap[e](body)


D = 1024
INW = 3600
DFF = 2816
EPS = 1e-6


class Phase:
    def __init__(self, nc, semstack, name):
        self.nc, self.semstack, self.name = nc, semstack, name
        self.st = ExitStack()
        self.S = Sched(nc)
        self.nb = 0
        self.pb = None
        self.cnt = 0

    def __enter__(self):
        self.st.__enter__()
        self.pb = [self.st.enter_context(self.nc.psum_tensor("%s_pb%d" % (self.name, i), [128, 512], F32))
                   for i in range(8)]
        return self

    def sb(self, name, shape, dt):
        return self.st.enter_context(self.nc.sbuf_tensor(self.name + "_" + name, shape, dt))

    def bank(self):
        i = self.nb % 8
        self.nb += 1
        return i

    def finish(self):
        S = self.S
        fw = [o for e in ENGS for o in S.ops[e] if o.dma]
        for e in ("pe", "act", "dve", "pool"):
            if S.ops[e]:
                fw.append(S.ops[e][-1])
        nc = self.nc
        ph = self

        class _SemStack:
            def enter_context(self_inner, cm):
                return cm

        _emit(S, nc, self.semstack, self.st, fw, self.name)

    def __exit__(self, *a):
        r = self.st.__exit__(*a)
        return r


def _emit(S, nc, semstack, blockstack, final_wait_ops, pname):
    for o in final_wait_ops:
        o.needs_inc = True
    sems = {}

    def getsem(name):
        name = pname + "_" + "".join(ch if ch.isalnum() else "_" for ch in name)
        if name not in sems:
            sems[name] = blockstack.enter_context(nc.semaphore(name))
        return sems[name]

    dma_cnt = {}
    for e in ENGS:
        cnt = 0
        for o in S.ops[e]:
            if o.dma:
                c = dma_cnt.get(o.semkey, 0) + 1
                dma_cnt[o.semkey] = c
                o.sem = getsem("d_" + str(o.semkey))
                o.val = 16 * c
                o.needs_inc = True
            elif o.needs_inc:
                ep, v = divmod(cnt, EPOCH)
                o.sem = getsem("c_%s_%d" % (e, ep))
                o.val = v + 1
                cnt += 1
    S.nsems = len(sems)
    for sm in sems.values():
        nc.sync.sem_clear(sm)
    nc.all_engine_barrier()
    block = blockstack.enter_context(nc.Block())
    engmap = {"pe": block.tensor, "act": block.scalar, "dve": block.vector,
              "pool": block.gpsimd, "sp": block.sync}
    for e in ENGS:
        ops = S.ops[e]
        fw = list(final_wait_ops) if e == "sp" else []

        def body(engine, ops=ops, fw=fw):
            waited = {}
            for o in ops:
                for p in o.deps:
                    key = id(p.sem)
                    if waited.get(key, 0) >= p.val:
                        continue
                    engine.wait_ge(p.sem, p.val)
                    waited[key] = p.val
                ins = o.fn(engine)
                if o.needs_inc:
                    ins.then_inc(o.sem, 16 if o.dma else 1)
            for p in fw:
                key = id(p.sem)
                if waited.get(key, 0) >= p.val:
                    continue
                engine.wait_ge(p.sem, p.val)
                waited[key] = p.val

        engmap[e](body)


def _kw(**k):
    return k


def build_program(S, dbg=False, upto=9):
    nc = bass.Bass("TRN2", target_bir_lowering=False)
    NG = S // 512
    OUTK = "ExternalOutput" if dbg else "Internal"

    def din(name, shape, dt=F32):
        return nc.dram_tensor(name, shape, dt, kind="ExternalInput").ap()

    def dsc(name, shape, dt):
        return nc.dram_tensor(name, shape, dt, kind=OUTK).ap()

    x = din("x", [S, D])
    c_col = din("c_col", [128, 8])
    w_ada = din("w_ada", [D, 6 * D])
    bada_col = din("bada_col", [128, 48])
    bada_row = din("bada_row", [6, D])
    gattn_col = din("gattn_col", [128, 8])
    gffn_col = din("gffn_col", [128, 8])
    w_in = din("w_in", [D, INW])
    cw_col = din("cw_col", [128, 48])
    ident_in = din("ident", [128, 128])
    bones_in = din("bones", [128, 128])
    tb_in = din("tb", [128, 24 * 256])
    out = nc.dram_tensor("out", [S, D], F32, kind="ExternalOutput").ap()

    MODC = dsc("MODC", [128, 32], F32)
    GROW = dsc("GROW", [2, 128, D], F32)
    QT = dsc("QT", [512, S], BF16)
    KT = dsc("KT", [512, S], BF16)
    VV = dsc("VV", [S, 512], BF16)
    QD = dsc("QD", [512, S], BF16)
    KD = dsc("KD", [512, S], BF16)
    VD = dsc("VD", [512, S], BF16)
    ZS = dsc("ZS", [512, S], BF16)
    BA = dsc("BA", [S, 16], F32)
    MIXT = dsc("MIXT", [D, S], BF16)

    semstack = ExitStack()
    semstack.__enter__()

    with Phase(nc, semstack, "p0") as P:
        S_ = P.S
        ccol = P.sb("ccol", [128, 8], F32)
        sbf = P.sb("sbf", [128, 8], BF16)
        sbc = P.sb("sbc", [128, 8, 128], BF16)
        bcol = P.sb("bcol", [128, 48], F32)
        gcol = P.sb("gcol", [128, 16], F32)
        modc = P.sb("modc", [128, 32], F32)
        wa = [P.sb("wa%d" % i, [128, 8, D], BF16) for i in range(2)]
        brow = [P.sb("brow%d" % i, [128, D], F32) for i in range(2)]
        grow = [P.sb("grow%d" % i, [128, D], F32) for i in range(2)]
        S_.op("sp", lambda e: e.dma_start(out=ccol[:], in_=c_col), writes=["ccol"], dma=True, semkey="ccol")
        S_.op("sp", lambda e: e.dma_start(out=bcol[:], in_=bada_col), writes=["bcol"], dma=True, semkey="bcol")
        S_.op("sp", lambda e: e.dma_start(out=gcol[:, 0:8], in_=gattn_col), writes=["gcol"], dma=True, semkey="gcol")
        S_.op("sp", lambda e: e.dma_start(out=gcol[:, 8:16], in_=gffn_col), writes=["gcol"], dma=True, semkey="gcol")
        S_.op("act", lambda e: e.activation(out=sbf[:], in_=ccol[:], func=AF.Silu), reads=["ccol"], writes=["sbf"])
        S_.op("dve", lambda e: e.tensor_copy(out=sbc[:], in_=sbf[:].unsqueeze(2).to_broadcast([128, 8, 128])),
              reads=["sbf"], writes=["sbc"])
        wav = w_ada.rearrange("(k p) f -> p k f", p=128)
        colidx = {0: 0, 1: 1, 3: 2, 4: 3}
        for j in range(6):
            sl = j % 2
            S_.op("pool", lambda e, j=j, sl=sl: e.dma_start(out=wa[sl][:], in_=wav[:, :, j * D:(j + 1) * D]),
                  writes=[("wa", sl)], dma=True, semkey="wa%d" % sl)
            if j in colidx:
                jj = colidx[j]
                b = P.bank()
                for fcn in range(8):
                    for k in range(8):
                        S_.op("pe", lambda e, b=b, fcn=fcn, k=k, sl=sl: e.matmul(
                            P.pb[b][:, fcn:fcn + 1], lhsT=wa[sl][:, k, fcn * 128:(fcn + 1) * 128], rhs=sbf[:, k:k + 1],
                            start=(k == 0), stop=(k == 7)), reads=[("wa", sl), "sbf"], writes=[("pb", b)])
                S_.op("dve", lambda e, b=b, jj=jj, j=j: e.tensor_tensor(
                    out=modc[:, jj * 8:(jj + 1) * 8], in0=P.pb[b][:, 0:8], in1=bcol[:, j * 8:(j + 1) * 8], op=ALU.add),
                    reads=[("pb", b), "bcol"], writes=["modc"])
            else:
                gi = 0 if j == 2 else 1
                S_.op("sp", lambda e, j=j, gi=gi: e.dma_start(out=brow[gi][:], in_=bada_row[j:j + 1, :].partition_broadcast(128)),
                      writes=[("brow", gi)], dma=True, semkey="brow%d" % gi)
                for half in range(2):
                    b = P.bank()
                    for k in range(8):
                        S_.op("pe", lambda e, b=b, k=k, sl=sl, half=half: e.matmul(
                            P.pb[b][:], lhsT=sbc[:, k, :], rhs=wa[sl][:, k, half * 512:(half + 1) * 512],
                            start=(k == 0), stop=(k == 7)), reads=[("wa", sl), "sbc"], writes=[("pb", b)])
                    S_.op("dve", lambda e, b=b, gi=gi, half=half: e.tensor_tensor(
                        out=grow[gi][:, half * 512:(half + 1) * 512], in0=P.pb[b][:], in1=brow[gi][:, half * 512:(half + 1) * 512],
                        op=ALU.add), reads=[("pb", b), ("brow", gi)], writes=[("grow", gi)])
                S_.op("sp", lambda e, gi=gi: e.dma_start(out=GROW[gi], in_=grow[gi][:]), reads=[("grow", gi)],
                      dma=True, semkey="grow%d" % gi)
        for jj, go in ((1, 0), (3, 8)):
            S_.op("dve", lambda e, jj=jj, go=go: e.scalar_tensor_tensor(
                out=modc[:, jj * 8:(jj + 1) * 8], in0=modc[:, jj * 8:(jj + 1) * 8], scalar=1.0, in1=gcol[:, go:go + 8],
                op0=ALU.add, op1=ALU.mult), reads=["modc", "gcol"], writes=["modc"])
        S_.op("sp", lambda e: e.dma_start(out=MODC, in_=modc[:]), reads=["modc"], dma=True, semkey="modc")
        P.finish()
    nc.all_engine_barrier()
    if upto < 1:
        return nc, semstack

    with Phase(nc, semstack, "p1") as P:
        S_ = P.S
        ident = P.sb("ident", [128, 128], BF16)
        bones = P.sb("bones", [128, 128], BF16)
        modc = P.sb("modc", [128, 32], F32)
        cw = P.sb("cw", [128, 48], F32)
        epsT = P.sb("eps", [128, 1], F32)
        win = P.sb("win", [128, 8, INW], BF16)
        S_.op("pool", lambda e: e.dma_start(out=ident[:], in_=ident_in), writes=["ident"], dma=True, semkey="ident")
        S_.op("pool", lambda e: e.dma_start(out=bones[:], in_=bones_in), writes=["bones"], dma=True, semkey="bones")
        S_.op("sp", lambda e: e.dma_start(out=modc[:], in_=MODC), writes=["modc"], dma=True, semkey="modc")
        S_.op("sp", lambda e: e.dma_start(out=cw[:], in_=cw_col), writes=["cw"], dma=True, semkey="cw")
        S_.op("dve", lambda e: e.memset(epsT[:], EPS), writes=["eps"])
        winv = w_in.rearrange("(k p) f -> p k f", p=128)
        for k in range(8):
            S_.op("pool", lambda e, k=k: e.dma_start(out=win[:, k, :], in_=winv[:, k, :]), writes=[("win", k)],
                  dma=True, semkey="win%d" % k)
        WIN = [("win", k) for k in range(8)]
        xt = [P.sb("xt%d" % i, [128, 4, D], F32) for i in range(2)]
        junk = P.sb("junk", [128, D], BF16)
        ss2 = [P.sb("ss%d" % i, [128, 4], F32) for i in range(2)]
        rstd2 = [P.sb("rstd%d" % i, [128, 4], F32) for i in range(2)]
        xs2 = [P.sb("xs%d" % i, [128, 4, D], BF16) for i in range(2)]
        hT2 = [P.sb("hT%d" % i, [128, 8, 512], BF16) for i in range(2)]
        NSTQ = 6
        stq = [P.sb("stq%d" % i, [128, 4, 512], BF16) for i in range(NSTQ)]
        cin = P.sb("cin", [128, 12, 515], F32)
        NROT = 5
        acc3 = [P.sb("acc%d" % i, [128, 512], F32) for i in range(NROT)]
        slu3 = [P.sb("slu%d" % i, [128, 512], F32) for i in range(NROT)]
        sq3 = [P.sb("sq%d" % i, [128, 512], BF16) for i in range(NROT)]
        rs3 = [P.sb("rs%d" % i, [128, 512], F32) for i in range(NROT)]
        rot = [0]
        stb = P.sb("stb", [128, 4, 16], F32)
        xv = x.rearrange("(g j p) d -> g p j d", j=4, p=128)
        S_.op("pool", lambda e: e.memset(cin[:], 0.0), writes=["cin"])
        nst = [0]

        def stage():
            i = nst[0] % NSTQ
            nst[0] += 1
            return i

        NROT2 = NROT

        def norm_item(g):
            xs_ = g % 2
            ss, rstd, xs, hT = ss2[xs_], rstd2[xs_], xs2[xs_], hT2[xs_]
            KSS, KRS = ("ss", xs_), ("rstd", xs_)
            S_.op("sp", lambda e: e.dma_start(out=xt[xs_][:], in_=xv[g]), writes=[("xt", xs_)], dma=True, semkey="xt%d" % xs_)
            S_.op("pool", lambda e: e.memset(ss[:], 0.0), writes=[KSS])
            yield
            for j in range(4):
                S_.op("act", lambda e, j=j: e.activation(out=junk[:], in_=xt[xs_][:, j, :], func=AF.Square, accum_out=ss[:, j:j + 1]),
                      reads=[("xt", xs_), KSS], writes=[KSS, "junk"])
            S_.op("act", lambda e: e.activation(out=rstd[:], in_=ss[:], func=AF.Sqrt, bias=epsT[:], scale=1.0 / D),
                  reads=[KSS, "eps"], writes=[KRS])
            yield
            S_.op("dve", lambda e: e.reciprocal(out=rstd[:], in_=rstd[:]), reads=[KRS], writes=[KRS])
            for j in range(4):
                S_.op("dve", lambda e, j=j: e.tensor_scalar(out=xs[:, j, :], in0=xt[xs_][:, j, :], scalar1=rstd[:, j:j + 1], scalar2=None, op0=ALU.mult),
                      reads=[("xt", xs_), KRS], writes=[("xs", xs_, j)])
            yield
            pend = None
            for c2 in range(5):
                if c2 < 4:
                    b = P.bank()
                    pbf = P.pb[b][:].bitcast(BF16)
                    for cc in range(2):
                        c = c2 * 2 + cc
                        for j in range(4):
                            S_.op("pe", lambda e, pbf=pbf, cc=cc, c=c, j=j: e.transpose(
                                pbf[:, cc * 512 + j * 128: cc * 512 + (j + 1) * 128], xs[:, j, c * 128:(c + 1) * 128], ident[:]),
                                reads=[("xs", xs_, j), "ident"], writes=[("pb", b)])
                if pend is not None:
                    pb_, pbf_, pc2 = pend
                    for cc in range(2):
                        c = pc2 * 2 + cc
                        S_.op("act", lambda e, pbf_=pbf_, cc=cc, c=c: e.activation(
                            out=hT[:, c, :], in_=pbf_[:, cc * 512:(cc + 1) * 512], func=AF.Identity,
                            bias=modc[:, c:c + 1], scale=modc[:, 8 + c:9 + c]),
                            reads=[("pb", pb_), "modc"], writes=[("hT", xs_, c)])
                pend = (b, pbf, c2) if c2 < 4 else None
                yield

        def proj_mm(g, fc):
            xs_ = g % 2
            hT = hT2[xs_]
            b = P.bank()
            for k in range(8):
                S_.op("pe", lambda e, k=k: e.matmul(
                    P.pb[b][:], lhsT=win[:, k, fc * 128:(fc + 1) * 128], rhs=hT[:, k, :], start=(k == 0), stop=(k == 7)),
                    reads=[("win", k), ("hT", xs_, k)], writes=[("pb", b)])
            return b

        def store(dst, si, g):
            S_.op("sp", lambda e: e.dma_start(
                out=dst.rearrange("(c p) s -> p c s", p=128)[:, :, g * 512:(g + 1) * 512], in_=stq[si][:]),
                reads=[("stq", si, i) for i in range(4)], dma=True, semkey="stq%d" % si)

        def qk_item(g, base, dst, si, i):
            b = proj_mm(g, base + i)
            yield
            if i % 2 == 0:
                S_.op("act", lambda e: e.activation(out=stq[si][:, i, :], in_=P.pb[b][:], func=AF.Identity),
                      reads=[("pb", b)], writes=[("stq", si, i)])
            else:
                S_.op("dve", lambda e: e.tensor_copy(out=stq[si][:, i, :], in_=P.pb[b][:]),
                      reads=[("pb", b)], writes=[("stq", si, i)])
            if i == 3:
                store(dst, si, g)

        def v_item(g, si, j):
            xs_ = g % 2
            hT = hT2[xs_]
            b = P.bank()
            for k in range(8):
                S_.op("pe", lambda e, k=k: e.matmul(
                    P.pb[b][:], lhsT=hT[:, k, j * 128:(j + 1) * 128], rhs=win[:, k, 1024:1536], start=(k == 0), stop=(k == 7)),
                    reads=[("win", k), ("hT", xs_, k)], writes=[("pb", b)])
            yield
            S_.op("act", lambda e: e.activation(out=stq[si][:, j, :], in_=P.pb[b][:], func=AF.Identity),
                  reads=[("pb", b)], writes=[("stq", si, j)])
            if j == 3:
                S_.op("sp", lambda e: e.dma_start(out=VV.rearrange("(g j p) f -> g p j f", j=4, p=128)[g], in_=stq[si][:]),
                      reads=[("stq", si, i) for i in range(4)], dma=True, semkey="stq%d" % si)

        def z_item(g, si, i):
            b = proj_mm(g, 24 + i)
            yield
            S_.op("act", lambda e: e.activation(out=stq[si][:, i, :], in_=P.pb[b][:], func=AF.Silu),
                  reads=[("pb", b)], writes=[("stq", si, i)])
            if i == 3:
                store(ZS, si, g)

        def ba_item(g):
            xs_ = g % 2
            hT = hT2[xs_]
            b = P.bank()
            for j in range(4):
                for k in range(8):
                    S_.op("pe", lambda e, k=k, j=j: e.matmul(
                        P.pb[b][:, j * 16:(j + 1) * 16], lhsT=hT[:, k, j * 128:(j + 1) * 128], rhs=win[:, k, 3584:3600],
                        start=(k == 0), stop=(k == 7)), reads=[("win", k), ("hT", xs_, k)], writes=[("pb", b)])
            yield
            S_.op("dve", lambda e: e.tensor_copy(out=stb[:].rearrange("p j f -> p (j f)"), in_=P.pb[b][:, 0:64]),
                  reads=[("pb", b)], writes=["stb"])
            S_.op("sp", lambda e: e.dma_start(out=BA.rearrange("(g j p) f -> g p j f", j=4, p=128)[g], in_=stb[:]),
                  reads=["stb"], dma=True, semkey="stb")

        def delta_item(g, grp, dst, si, i):
            ci = grp * 4 + i
            b = proj_mm(g, 12 + ci)
            yield
            S_.op("act", lambda e: e.activation(out=cin[:, ci, 3:515], in_=P.pb[b][:], func=AF.Identity),
                  reads=[("pb", b)], writes=[("cin", ci)])
            yield
            ri = rot[0] % NROT2
            rot[0] += 1
            acc, slu, sq, rs = acc3[ri], slu3[ri], sq3[ri], rs3[ri]
            KA, KSL, KSQ, KR = ("acc", ri), ("slu", ri), ("sq", ri), ("rs", ri)
            S_.op("dve", lambda e: e.tensor_scalar(out=acc[:], in0=cin[:, ci, 0:512], scalar1=cw[:, ci * 4:ci * 4 + 1], scalar2=None, op0=ALU.mult),
                  reads=[("cin", ci), "cw"], writes=[KA])
            for t in range(1, 4):
                S_.op("dve", lambda e, t=t: e.scalar_tensor_tensor(
                    out=acc[:], in0=cin[:, ci, t:t + 512], scalar=cw[:, ci * 4 + t:ci * 4 + t + 1], in1=acc[:],
                    op0=ALU.mult, op1=ALU.add), reads=[("cin", ci), "cw", KA], writes=[KA])
            S_.op("pool", lambda e: e.tensor_copy(out=cin[:, ci, 0:3], in_=cin[:, ci, 512:515]), reads=[("cin", ci)], writes=[("cin", ci)])
            yield
            if grp == 2:
                S_.op("act", lambda e: e.activation(out=stq[si][:, i, :], in_=acc[:], func=AF.Silu), reads=[KA], writes=[("stq", si, i)])
                if i == 3:
                    store(dst, si, g)
                return
            S_.op("act", lambda e: e.activation(out=slu[:], in_=acc[:], func=AF.Silu), reads=[KA], writes=[KSL])
            S_.op("pool", lambda e: e.tensor_tensor(out=sq[:], in0=slu[:], in1=slu[:], op=ALU.mult), reads=[KSL], writes=[KSQ])
            yield
            b2 = P.bank()
            S_.op("pe", lambda e: e.matmul(P.pb[b2][:], lhsT=bones[:], rhs=sq[:], start=True, stop=True), reads=["bones", KSQ], writes=[("pb", b2)])
            yield
            S_.op("act", lambda e: e.activation(out=rs[:], in_=P.pb[b2][:], func=AF.Sqrt, bias=epsT[:], scale=1.0), reads=[("pb", b2), "eps"], writes=[KR])
            yield
            S_.op("dve", lambda e: e.reciprocal(out=rs[:], in_=rs[:]), reads=[KR], writes=[KR])
            scl = 0.125 if grp == 0 else 1.0
            S_.op("dve", lambda e: e.scalar_tensor_tensor(out=stq[si][:, i, :], in0=slu[:], scalar=scl, in1=rs[:], op0=ALU.mult, op1=ALU.mult),
                  reads=[KSL, KR], writes=[("stq", si, i)])
            if i == 3:
                store(dst, si, g)

        def p1_items():
            for g in range(NG):
                if g + 1 < NG:
                    yield norm_item(g + 1)
                si = stage()
                for i in range(4):
                    yield qk_item(g, 0, QT, si, i)
                si = stage()
                for i in range(4):
                    yield qk_item(g, 4, KT, si, i)
                si = stage()
                for j in range(4):
                    yield v_item(g, si, j)
                for grp, dst in ((0, QD), (1, KD), (2, VD)):
                    si = stage()
                    for i in range(4):
                        yield delta_item(g, grp, dst, si, i)
                si = stage()
                for i in range(4):
                    yield z_item(g, si, i)
                yield ba_item(g)

        for _ in norm_item(0):
            pass
        run_skewed(p1_items())
        P.finish()
    nc.all_engine_barrier()
    if upto < 2:
        return nc, semstack
    _phase2(nc, semstack, S, QT, KT, VV, tb_in, MIXT)
    nc.all_engine_barrier()
    if upto < 3:
        return nc, semstack
    cst = dict(identf=din("identf", [128, 128]), trif=din("trif", [128, 128]), bonesf=din("bonesf", [128, 128]),
               maskL=din("maskL", [128, 128]), maskU=din("maskU", [128, 128]), ident=ident_in,
               alog=din("alog_row", [1, 8]), dtb=din("dtb_row", [1, 8]), dng=din("dng_row", [1, 64]))
    _phase3(nc, semstack, S, QD, KD, VD, ZS, BA, MIXT, cst)
    nc.all_engine_barrier()
    if upto < 4:
        return nc, semstack
    w_out = din("w_out", [D, D])
    w_gate = din("w_gate", [D, DFF])
    w_up = din("w_up", [D, DFF])
    w_down = din("w_down", [DFF, D])
    fg_row = din("fg_row", [1, D])
    X1 = dsc("X1", [S, D], F32)
    H2T = dsc("H2T", [D, S], BF16)
    _phase4a(nc, semstack, S, x, MIXT, w_out, GROW, MODC, ident_in, X1, H2T)
    nc.all_engine_barrier()
    if upto < 5:
        return nc, semstack
    _phase4b(nc, semstack, S, X1, H2T, w_gate, w_up, w_down, GROW, fg_row, out)
    return nc, semstack


P2DBG = {'mode': 9, 'strided': True}


def run_skewed(items):
    live = []
    it = iter(items)
    while True:
        nxt = next(it, None)
        if nxt is not None:
            live.append(nxt)
        if not live:
            break
        for gen in list(live):
            try:
                next(gen)
            except StopIteration:
                live.remove(gen)


def _phase2(nc, semstack, S, QT, KT, VV, tb_in, MIXT):
    NSB = S // 2048
    import os
    mode = int(os.environ.get('P2MODE', '9'))
    with Phase(nc, semstack, "p2") as P:
        S_ = P.S
        EB = P.sb("EB", [128, 24 * 256], BF16)
        ones = P.sb("ones", [128, 64], BF16)
        qt = P.sb("qt", [64, 8, 2048], BF16)
        kt = [P.sb("kt%d" % i, [64, 8, 2048], BF16) for i in range(2)]
        v1 = P.sb("v1", [128, 16, 512], BF16)
        v1p = P.sb("v1p", [128, 1, 512], BF16)
        v2 = P.sb("v2", [128, 16, 512], BF16)
        v2p = P.sb("v2p", [128, 4, 512], BF16)
        v3 = [P.sb("v3_%d" % i, [128, 16, 512], BF16) for i in range(2)]
        Et = [P.sb("E%d" % i, [128, 512], BF16) for i in range(4)]
        PT = [P.sb("PT%d" % i, [128, 512], BF16) for i in range(4)]
        accn = P.sb("accn", [128, 2048], F32)
        accd = P.sb("accd", [128, 2048], F32)
        mst = P.sb("mst", [128, 2048], BF16)
        S_.op("pool", lambda e: e.dma_start(out=EB[:], in_=tb_in), writes=["EB"], dma=True, semkey="tb")
        S_.op("act", lambda e: e.activation(out=EB[:], in_=EB[:], func=AF.Exp), reads=["EB"], writes=["EB"])
        S_.op("pool", lambda e: e.memset(ones[:], 1.0), writes=["ones"])
        cnt = [0, 0, 0]
        for N in range(NSB):
            cur, prv = N % 2, (N + 1) % 2
            t0 = N * 2048
            S_.op("sp", lambda e, t0=t0: e.dma_start(out=qt[:], in_=QT.rearrange("(c p) s -> p c s", p=64)[:, :, t0:t0 + 2048]),
                  writes=["qt"], dma=True, semkey="qt")
            S_.op("sp", lambda e, t0=t0, cur=cur: e.dma_start(out=kt[cur][:], in_=KT.rearrange("(c p) s -> p c s", p=64)[:, :, t0:t0 + 2048]),
                  writes=[("kt", cur)], dma=True, semkey="kt%d" % cur)
            Vsb = VV[t0:t0 + 2048, :]
            S_.op("sp", lambda e, Vsb=Vsb: e.dma_start(out=v1[:], in_=Vsb.rearrange("(n p) f -> p n f", p=128)),
                  writes=["v1"], dma=True, semkey="v1")
            for n_ in range(4):
                S_.op("sp", lambda e, Vsb=Vsb, n_=n_: e.dma_start(
                    out=v2[:, n_ * 4:(n_ + 1) * 4, :],
                    in_=Vsb[n_ * 512:(n_ + 1) * 512, :].rearrange("(p r) f -> p r f", r=4)),
                    writes=["v2"], dma=True, semkey="v2")
            S_.op("sp", lambda e, Vsb=Vsb, cur=cur: e.dma_start(out=v3[cur][:], in_=Vsb.rearrange("(p r) f -> p r f", r=16)),
                  writes=[("v3", cur)], dma=True, semkey="v3_%d" % cur)
            def unit(hp, br, gq, jj, nbk, dbk, N=N, cur=cur, prv=prv):
                if br == 0:
                    n = 4 * gq + jj
                    qs, st = n * 128, 1
                    vcur = (v1, n, "v1")
                    if n >= 1:
                        pk = (cur, (n - 1) * 128, (v1, n - 1, "v1"))
                    elif N >= 1:
                        pk = (prv, 15 * 128, (v1p, 0, "v1p"))
                    else:
                        pk = None
                elif br == 1:
                    n_, r = gq, jj
                    qs, st = n_ * 512 + r, 4
                    vcur = (v2, n_ * 4 + r, "v2")
                    if n_ >= 1:
                        pk = (cur, (n_ - 1) * 512 + r, (v2, (n_ - 1) * 4 + r, "v2"))
                    elif N >= 1:
                        pk = (prv, 3 * 512 + r, (v2p, r, "v2p"))
                    else:
                        pk = None
                else:
                    r = 4 * gq + jj
                    qs, st = r, 16
                    vcur = (v3[cur], r, ("v3", cur))
                    pk = (prv, r, (v3[prv], r, ("v3", prv))) if N >= 1 else None
                sbk = cnt[0] % 3
                ei = cnt[0] % 4
                cnt[0] += 1
                blks = ([(0,) + pk] if pk else []) + [(1, cur, qs, vcur)]
                for hl in range(2):
                    for (blk, slot, ks, _v) in blks:
                        S_.op("pe", lambda e, sbk=sbk, hl=hl, blk=blk, slot=slot, ks=ks, qs=qs, st=st, hp=hp: e.matmul(
                            P.pb[sbk][:, hl * 256 + blk * 128: hl * 256 + (blk + 1) * 128],
                            lhsT=kt[slot][:, 2 * hp + hl, ks:ks + 127 * st + 1:st],
                            rhs=qt[:, 2 * hp + hl, qs:qs + 127 * st + 1:st],
                            start=True, stop=True),
                            reads=[("kt", slot), "qt"], writes=[("pb", sbk)])
                yield
                c0 = 0 if pk else 128
                vw = lambda ap, c0=c0: ap.rearrange("p (h c) -> p h c", h=2)[:, :, c0:256]
                S_.op("act", lambda e, sbk=sbk, ei=ei, vw=vw: e.activation(
                    out=vw(Et[ei][:]), in_=vw(P.pb[sbk][:]), func=AF.Exp, scale=0.125),
                    reads=[("pb", sbk)], writes=[("E", ei)])
                yield
                eoff = (br * 8 + 2 * hp) * 256
                eng = "dve" if ei % 2 == 0 else "pool"
                S_.op(eng, lambda e, ei=ei, eoff=eoff, vw=vw: e.tensor_tensor(
                    out=vw(PT[ei][:]), in0=vw(Et[ei][:]), in1=vw(EB[:, eoff:eoff + 512]), op=ALU.mult),
                    reads=[("E", ei), "EB"], writes=[("PT", ei)])
                yield
                for hl in range(2):
                    h = 2 * hp + hl
                    for bi, (blk, slot, ks, (vt, vi, vkey)) in enumerate(blks):
                        fl = _kw(start=(bi == 0), stop=(bi == len(blks) - 1), tile_position=(0, hl * 64))
                        S_.op("pe", lambda e, nbk=nbk, hl=hl, jj=jj, vt=vt, vi=vi, h=h, ei=ei, blk=blk, fl=fl: e.matmul(
                            P.pb[nbk][hl * 64:(hl + 1) * 64, jj * 128:(jj + 1) * 128],
                            lhsT=vt[:, vi, h * 64:(h + 1) * 64],
                            rhs=PT[ei][:, hl * 256 + blk * 128: hl * 256 + (blk + 1) * 128], **fl),
                            reads=[vkey, ("PT", ei)], writes=[("pb", nbk)])
                        S_.op("pe", lambda e, dbk=dbk, hl=hl, jj=jj, ei=ei, blk=blk, fl=fl: e.matmul(
                            P.pb[dbk][hl * 64:(hl + 1) * 64, jj * 128:(jj + 1) * 128],
                            lhsT=ones[:, 0:64],
                            rhs=PT[ei][:, hl * 256 + blk * 128: hl * 256 + (blk + 1) * 128], **fl),
                            reads=["ones", ("PT", ei)], writes=[("pb", dbk)])
                if jj < 3:
                    return
                yield
                for bk, acc, akey in ((nbk, accn, "accn"), (dbk, accd, "accd")):
                    if br == 0:
                        S_.op("act", lambda e, bk=bk, acc=acc, gq=gq: e.activation(
                            out=acc[:, gq * 512:(gq + 1) * 512], in_=P.pb[bk][:], func=AF.Identity),
                            reads=[("pb", bk)], writes=[akey])
                    else:
                        if br == 1:
                            oap = acc[:, gq * 512:(gq + 1) * 512].rearrange("p (i r) -> p r i", r=4)
                        else:
                            oap = acc[:].rearrange("p (i r) -> p r i", r=16)[:, 4 * gq:4 * gq + 4, :]
                        S_.op("dve", lambda e, bk=bk, oap=oap: e.tensor_tensor(
                            out=oap, in0=P.pb[bk][:].rearrange("p (r i) -> p r i", r=4), in1=oap, op=ALU.add),
                            reads=[("pb", bk), akey], writes=[akey])

            def finalize(hp, t0=t0):
                for _ in range(6):
                    yield
                S_.op("dve", lambda e: e.reciprocal(out=accd[:], in_=accd[:]), reads=["accd"], writes=["accd"])
                S_.op("dve", lambda e: e.tensor_tensor(out=mst[:], in0=accn[:], in1=accd[:], op=ALU.mult),
                      reads=["accn", "accd"], writes=["mst"])
                S_.op("sp", lambda e, hp=hp, t0=t0: e.dma_start(out=MIXT[hp * 128:(hp + 1) * 128, t0:t0 + 2048], in_=mst[:]),
                      reads=["mst"], dma=True, semkey="mst")

            def items():
                for hp in range(4):
                    for br in range(3):
                        for gq in range(4):
                            nbk = 3 + cnt[1] % 2
                            dbk = 5 + cnt[1] % 2
                            cnt[1] += 1
                            for jj in range(4):
                                yield unit(hp, br, gq, jj, nbk, dbk)
                    yield finalize(hp)

            run_skewed(items())
            if N + 1 < NSB:
                S_.op("pool", lambda e: e.tensor_copy(out=v1p[:, 0, :], in_=v1[:, 15, :]), reads=["v1"], writes=["v1p"])
                S_.op("pool", lambda e: e.tensor_copy(out=v2p[:], in_=v2[:, 12:16, :]), reads=["v2"], writes=["v2p"])
        P.finish()


def host_consts(inputs):
    import math
    rel_bias = np.asarray(inputs["rel_bias"], np.float32)
    k = np.arange(128)[:, None]
    q = np.arange(128)[None, :]
    steps_prev = q + 128 - k
    steps_cur = q - k
    tb = np.full((128, 3, 8, 2, 128), -30000.0, np.float32)

    def bucket(dist):
        dist = np.asarray(dist, np.int64)
        max_exact = 16
        dist_f = np.maximum(dist, 1).astype(np.float32)
        lg = (np.log(dist_f / np.float32(max_exact)) / np.float32(math.log(2048 / max_exact))
              * np.float32(32 - max_exact)).astype(np.float32)
        large = max_exact + lg.astype(np.int32)
        return np.where(dist < max_exact, dist, np.minimum(large, 31)).astype(np.int64)

    for br, d in enumerate((1, 4, 16)):
        for blk, steps in ((0, steps_prev), (1, steps_cur)):
            valid = (steps >= 0) & (steps <= 128)
            bk = bucket(np.maximum(steps, 0) * d)
            for h in range(8):
                vals = rel_bias[bk, h]
                tb[:, br, h, blk, :] = np.where(valid, vals, np.float32(-30000.0))
    return {"tb": np.ascontiguousarray(tb.reshape(128, 24 * 256))}


def _phase4a(nc, semstack, S, x, MIXT, w_out, GROW, MODC, ident_in, X1, H2T):
    NG = S // 512
    with Phase(nc, semstack, "p4a") as P:
        S_ = P.S
        ident = P.sb("ident", [128, 128], BF16)
        modc = P.sb("modc", [128, 32], F32)
        epsT = P.sb("eps", [128, 1], F32)
        g1 = P.sb("g1", [128, D], F32)
        wo = P.sb("wo", [128, 8, D], BF16)
        S_.op("pool", lambda e: e.dma_start(out=ident[:], in_=ident_in), writes=["ident"], dma=True, semkey="ident")
        S_.op("sp", lambda e: e.dma_start(out=modc[:], in_=MODC), writes=["modc"], dma=True, semkey="modc")
        S_.op("sp", lambda e: e.dma_start(out=g1[:], in_=GROW[0]), writes=["g1"], dma=True, semkey="g1")
        S_.op("pool", lambda e: e.dma_start(out=wo[:], in_=w_out.rearrange("(k p) f -> p k f", p=128)), writes=["wo"],
              dma=True, semkey="wo")
        S_.op("dve", lambda e: e.memset(epsT[:], EPS), writes=["eps"])
        for k in range(8):
            S_.op("dve", lambda e, k=k: e.tensor_tensor(out=wo[:, k, :], in0=wo[:, k, :], in1=g1[:], op=ALU.mult),
                  reads=["wo", "g1"], writes=["wo"])
        xt = [P.sb("xt%d" % i, [128, 4, D], F32) for i in range(2)]
        mt = [P.sb("mt%d" % i, [128, 8, 512], BF16) for i in range(2)]
        junk = P.sb("junk", [128, D], BF16)
        ss = P.sb("ss", [128, 4], F32)
        rstd = P.sb("rstd", [128, 4], F32)
        xs = P.sb("xs", [128, 4, D], BF16)
        hst = [P.sb("hst%d" % i, [128, 8, 512], BF16) for i in range(2)]
        xv = x.rearrange("(g j p) d -> g p j d", j=4, p=128)
        x1v = X1.rearrange("(g j p) d -> g p j d", j=4, p=128)
        for g in range(NG):
            sl = g % 2
            S_.op("sp", lambda e, g=g, sl=sl: e.dma_start(out=xt[sl][:], in_=xv[g]), writes=[("xt", sl)], dma=True, semkey="xt%d" % sl)
            S_.op("sp", lambda e, g=g, sl=sl: e.dma_start(out=mt[sl][:], in_=MIXT.rearrange("(c p) s -> p c s", p=128)[:, :, g * 512:(g + 1) * 512]),
                  writes=[("mt", sl)], dma=True, semkey="mt%d" % sl)
            for j in range(4):
                for half in range(2):
                    b = P.bank()
                    for k in range(8):
                        S_.op("pe", lambda e, b=b, k=k, j=j, half=half, sl=sl: e.matmul(
                            P.pb[b][:], lhsT=mt[sl][:, k, j * 128:(j + 1) * 128], rhs=wo[:, k, half * 512:(half + 1) * 512],
                            start=(k == 0), stop=(k == 7)), reads=[("mt", sl), "wo"], writes=[("pb", b)])
                    S_.op("dve", lambda e, b=b, j=j, half=half, sl=sl: e.tensor_tensor(
                        out=xt[sl][:, j, half * 512:(half + 1) * 512], in0=P.pb[b][:], in1=xt[sl][:, j, half * 512:(half + 1) * 512],
                        op=ALU.add), reads=[("pb", b), ("xt", sl)], writes=[("xt", sl)])
            S_.op("sp", lambda e, g=g, sl=sl: e.dma_start(out=x1v[g], in_=xt[sl][:]), reads=[("xt", sl)], dma=True, semkey="xt%d" % sl)
            S_.op("pool", lambda e: e.memset(ss[:], 0.0), writes=["ss"])
            for j in range(4):
                S_.op("act", lambda e, j=j, sl=sl: e.activation(out=junk[:], in_=xt[sl][:, j, :], func=AF.Square, accum_out=ss[:, j:j + 1]),
                      reads=[("xt", sl), "ss"], writes=["ss", "junk"])
            S_.op("act", lambda e: e.activation(out=rstd[:], in_=ss[:], func=AF.Sqrt, bias=epsT[:], scale=1.0 / D),
                  reads=["ss", "eps"], writes=["rstd"])
            S_.op("dve", lambda e: e.reciprocal(out=rstd[:], in_=rstd[:]), reads=["rstd"], writes=["rstd"])
            for j in range(4):
                S_.op("dve", lambda e, j=j, sl=sl: e.tensor_scalar(out=xs[:, j, :], in0=xt[sl][:, j, :], scalar1=rstd[:, j:j + 1],
                                                                 scalar2=None, op0=ALU.mult),
                      reads=[("xt", sl), "rstd"], writes=[("xs", j)])
            for c2 in range(4):
                b = P.bank()
                pbf = P.pb[b][:].bitcast(BF16)
                for cc in range(2):
                    c = c2 * 2 + cc
                    for j in range(4):
                        S_.op("pe", lambda e, pbf=pbf, cc=cc, c=c, j=j: e.transpose(
                            pbf[:, cc * 512 + j * 128: cc * 512 + (j + 1) * 128], xs[:, j, c * 128:(c + 1) * 128], ident[:]),
                            reads=[("xs", j), "ident"], writes=[("pb", b)])
                for cc in range(2):
                    c = c2 * 2 + cc
                    S_.op("act", lambda e, pbf=pbf, cc=cc, c=c, sl=sl: e.activation(
                        out=hst[sl][:, c, :], in_=pbf[:, cc * 512:(cc + 1) * 512], func=AF.Identity,
                        bias=modc[:, 16 + c:17 + c], scale=modc[:, 24 + c:25 + c]),
                        reads=[("pb", b), "modc"], writes=[("hst", sl)])
            S_.op("sp", lambda e, g=g, sl=sl: e.dma_start(out=H2T.rearrange("(c p) s -> p c s", p=128)[:, :, g * 512:(g + 1) * 512],
                                                        in_=hst[sl][:]), reads=[("hst", sl)], dma=True, semkey="hst%d" % sl)
        P.finish()


def _phase4b(nc, semstack, S, X1, H2T, w_gate, w_up, w_down, GROW, fg_row, out):
    G = 512
    NJ = G // 128
    NG = S // G
    NF = DFF // 128
    with Phase(nc, semstack, "p4b") as P:
        S_ = P.S
        epsT = P.sb("eps", [128, 1], F32)
        fg = P.sb("fg", [128, D], F32)
        wg = P.sb("wg", [128, 8, DFF], BF16)
        wu = P.sb("wu", [128, 8, DFF], BF16)
        wd = P.sb("wd", [128, NF, D], BF16)
        S_.op("dve", lambda e: e.memset(epsT[:], EPS), writes=["eps"])
        S_.op("sp", lambda e: e.dma_start(out=fg[:], in_=fg_row.partition_broadcast(128)), writes=["fg"], dma=True, semkey="fg")
        for k in range(8):
            S_.op("pool", lambda e, k=k: e.dma_start(out=wg[:, k, :], in_=w_gate[k * 128:(k + 1) * 128, :]), writes=["wg"], dma=True, semkey="wg")
            S_.op("pool", lambda e, k=k: e.dma_start(out=wu[:, k, :], in_=w_up[k * 128:(k + 1) * 128, :]), writes=["wu"], dma=True, semkey="wu")
        xq = [P.sb("xq%d" % i, [128, NJ, D], F32) for i in range(2)]
        g2 = xq[1]
        S_.op("sp", lambda e: e.dma_start(out=g2[:, 0, :], in_=GROW[1]), writes=[("xq", 1)], dma=True, semkey="xq1")
        S_.op("pool", lambda e: e.dma_start(out=wd[:], in_=w_down.rearrange("(k p) f -> p k f", p=128)), writes=["wd"], dma=True, semkey="wd")
        for k in range(NF):
            S_.op("dve", lambda e, k=k: e.tensor_tensor(out=wd[:, k, :], in0=wd[:, k, :], in1=g2[:, 0, :], op=ALU.mult),
                  reads=["wd", ("xq", 1)], writes=["wd"])
        ht = [P.sb("ht%d" % i, [128, 8, G], BF16) for i in range(2)]
        aT = P.sb("aT", [128, NF, G], BF16)
        ss = P.sb("ss", [128, NJ], F32)
        rstd = P.sb("rstd", [128, NJ], F32)
        x1v = X1.rearrange("(g j p) d -> g p j d", j=NJ, p=128)
        ov = out.rearrange("(g j p) d -> g p j d", j=NJ, p=128)
        for g in range(NG):
            sl = g % 2
            S_.op("sp", lambda e, g=g, sl=sl: e.dma_start(out=xq[sl][:], in_=x1v[g]), writes=[("xq", sl)], dma=True, semkey="xq%d" % sl)
            S_.op("sp", lambda e, g=g, sl=sl: e.dma_start(out=ht[sl][:], in_=H2T.rearrange("(c p) s -> p c s", p=128)[:, :, g * G:(g + 1) * G]),
                  writes=[("ht", sl)], dma=True, semkey="ht%d" % sl)
            for fc in range(NF):
                ba, bb = P.bank(), P.bank()
                for (bk, w, wk) in ((ba, wg, "wg"), (bb, wu, "wu")):
                    for k in range(8):
                        S_.op("pe", lambda e, bk=bk, w=w, k=k, fc=fc, sl=sl: e.matmul(
                            P.pb[bk][:, 0:G], lhsT=w[:, k, fc * 128:(fc + 1) * 128], rhs=ht[sl][:, k, :], start=(k == 0), stop=(k == 7)),
                            reads=[wk, ("ht", sl)], writes=[("pb", bk)])
                S_.op("act", lambda e, ba=ba, fc=fc: e.activation(out=aT[:, fc, :], in_=P.pb[ba][:, 0:G], func=AF.Silu),
                      reads=[("pb", ba)], writes=[("aT", fc)])
                S_.op("dve", lambda e, bb=bb, fc=fc: e.tensor_tensor(out=aT[:, fc, :], in0=P.pb[bb][:, 0:G], in1=aT[:, fc, :], op=ALU.mult),
                      reads=[("pb", bb), ("aT", fc)], writes=[("aT", fc)])
            for j in range(NJ):
                for half in range(2):
                    b = P.bank()
                    for k in range(NF):
                        S_.op("pe", lambda e, b=b, k=k, j=j, half=half: e.matmul(
                            P.pb[b][:], lhsT=aT[:, k, j * 128:(j + 1) * 128], rhs=wd[:, k, half * 512:(half + 1) * 512],
                            start=(k == 0), stop=(k == NF - 1)), reads=[("aT", k), "wd"], writes=[("pb", b)])
                    S_.op("dve", lambda e, b=b, j=j, half=half, sl=sl: e.tensor_tensor(
                        out=xq[sl][:, j, half * 512:(half + 1) * 512], in0=P.pb[b][:], in1=xq[sl][:, j, half * 512:(half + 1) * 512],
                        op=ALU.add), reads=[("pb", b), ("xq", sl)], writes=[("xq", sl)])
            S_.op("pool", lambda e: e.memset(ss[:], 0.0), writes=["ss"])
            for j in range(NJ):
                S_.op("act", lambda e, j=j, sl=sl: e.activation(out=aT[:, 0:2, :].rearrange("p a c -> p (a c)"), in_=xq[sl][:, j, :], func=AF.Square, accum_out=ss[:, j:j + 1]),
                      reads=[("xq", sl), "ss"], writes=["ss", ("aT", 0), ("aT", 1)])
            S_.op("act", lambda e: e.activation(out=rstd[:], in_=ss[:], func=AF.Sqrt, bias=epsT[:], scale=1.0 / D),
                  reads=["ss", "eps"], writes=["rstd"])
            S_.op("dve", lambda e: e.reciprocal(out=rstd[:], in_=rstd[:]), reads=["rstd"], writes=["rstd"])
            for j in range(NJ):
                S_.op("dve", lambda e, j=j, sl=sl: e.scalar_tensor_tensor(
                    out=xq[sl][:, j, :], in0=xq[sl][:, j, :], scalar=rstd[:, j:j + 1], in1=fg[:], op0=ALU.mult, op1=ALU.mult),
                    reads=[("xq", sl), "rstd", "fg"], writes=[("xq", sl)])
            S_.op("sp", lambda e, g=g, sl=sl: e.dma_start(out=ov[g], in_=xq[sl][:]), reads=[("xq", sl)], dma=True, semkey="xq%d" % sl)
        P.finish()


def _phase3(nc, semstack, S, QD, KD, VD, ZS, BA, MIXT, cst):
    NG = S // 512
    with Phase(nc, semstack, "p3") as P:
        S_ = P.S
        sb = P.sb
        ident = sb("ident", [128, 128], BF16)
        identf = sb("identf", [128, 128], F32)
        trif = sb("trif", [128, 128], F32)
        bonesf = sb("bonesf", [128, 128], F32)
        onesf = sb("onesf", [128, 128], F32)
        maskL = sb("maskL", [128, 128], F32)
        maskU = sb("maskU", [128, 128], F32)
        negA = sb("negA", [128, 8], F32)
        dtb = sb("dtb", [128, 8], F32)
        dng = sb("dng", [128, 64], F32)
        one1 = sb("one1", [128, 1], F32)
        epsT = sb("eps", [128, 1], F32)
        S_.op("pool", lambda e: e.dma_start(out=ident[:], in_=cst["ident"]), writes=["ident"], dma=True, semkey="ident")
        for nm, t in (("identf", identf), ("trif", trif), ("bonesf", bonesf), ("maskL", maskL), ("maskU", maskU)):
            S_.op("sp", lambda e, nm=nm, t=t: e.dma_start(out=t[:], in_=cst[nm]), writes=[nm], dma=True, semkey=nm)
        S_.op("sp", lambda e: e.dma_start(out=negA[:], in_=cst["alog"].partition_broadcast(128)), writes=["negA"], dma=True, semkey="negA")
        S_.op("sp", lambda e: e.dma_start(out=dtb[:], in_=cst["dtb"].partition_broadcast(128)), writes=["dtb"], dma=True, semkey="dtb")
        S_.op("sp", lambda e: e.dma_start(out=dng[:], in_=cst["dng"].partition_broadcast(128)), writes=["dng"], dma=True, semkey="dng")
        S_.op("pool", lambda e: e.memset(onesf[:], 1.0), writes=["onesf"])
        S_.op("pool", lambda e: e.memset(one1[:], 1.0), writes=["one1"])
        S_.op("pool", lambda e: e.memset(epsT[:], EPS), writes=["eps"])
        S_.op("act", lambda e: e.activation(out=negA[:], in_=negA[:], func=AF.Exp), reads=["negA"], writes=["negA"])
        S_.op("dve", lambda e: e.tensor_scalar(out=negA[:], in0=negA[:], scalar1=-1.0, scalar2=None, op0=ALU.mult),
              reads=["negA"], writes=["negA"])
        kd = [sb("kd%d" % i, [128, 4, 512], BF16) for i in range(2)]
        qd = [sb("qd%d" % i, [64, 8, 512], BF16) for i in range(2)]
        kd8 = [sb("kd8_%d" % i, [64, 8, 512], BF16) for i in range(2)]
        vd = [sb("vd%d" % i, [128, 4, 512], BF16) for i in range(2)]
        zs = [sb("zs%d" % i, [128, 4, 512], BF16) for i in range(2)]
        ba = [sb("ba%d" % i, [128, 4, 16], F32) for i in range(2)]
        mst = [sb("mst%d" % i, [128, 4, 512], BF16) for i in range(2)]
        y16 = sb("y16", [128, 16], F32)
        u16 = sb("u16", [128, 16], F32)
        beta = sb("beta", [128, 8], F32)
        gg = sb("gg", [128, 8], F32)
        gcl = sb("gcl", [128, 16], F32)
        eg = sb("eg", [128, 8], F32)
        kdsc = sb("kdsc", [128, 8], F32)
        be = sb("be", [128, 8], F32)
        offL = sb("offL", [128, 8], F32)
        decS = sb("decS", [64, 16], F32)
        kbg = sb("kbg", [128, 512], BF16)
        kdec = sb("kdec", [128, 512], BF16)
        bv = sb("bv", [128, 512], BF16)
        DG = sb("DG", [128, 8, 128], F32)
        tL = sb("tL", [128, 8, 128], F32)
        tU = sb("tU", [128, 8, 128], F32)
        Lb = sb("Lb", [128, 8, 128], F32)
        TTb = sb("TTb", [128, 8, 128], BF16)
        Ui = sb("Ui", [128, 8, 128], BF16)
        qkdT = sb("qkdT", [128, 8, 128], BF16)
        X = [sb("X%d" % i, [128, 8, 128], F32) for i in range(2)]
        Y = [sb("Y%d" % i, [128, 8, 128], F32) for i in range(2)]
        Q = [sb("Q%d" % i, [128, 8, 128], F32) for i in range(2)]
        uu = sb("uu", [128, 512], F32)
        wT = sb("wT", [64, 8, 128], BF16)
        vnew = sb("vnew", [128, 512], BF16)
        o1s = sb("o1s", [128, 512], F32)
        otm = sb("otm", [128, 512], F32)
        sqo = sb("sqo", [128, 512], F32)
        ssq = sb("ssq", [128, 8], F32)
        onb = sb("onb", [128, 512], BF16)
        Sf = sb("Sf", [64, 512], F32)
        S1 = sb("S1", [64, 512], F32)
        Sbf = sb("Sbf", [64, 512], BF16)
        S_.op("pool", lambda e: e.memset(Sf[:], 0.0), writes=["Sf"])
        S_.op("pool", lambda e: e.memset(Sbf[:], 0.0), writes=["Sbf"])

        def bc8(t):
            return lambda n: t.unsqueeze(2).to_broadcast([128, 8, n])

        for g in range(NG):
            sl = g % 2
            for nm, t, src, pp in (("kd", kd, KD, 128), ("qd", qd, QD, 64), ("kd8", kd8, KD, 64), ("vd", vd, VD, 128), ("zs", zs, ZS, 128)):
                S_.op("sp", lambda e, t=t, src=src, g=g, sl=sl, pp=pp: e.dma_start(
                    out=t[sl][:], in_=src.rearrange("(c p) s -> p c s", p=pp)[:, :, g * 512:(g + 1) * 512]),
                    writes=[(nm, sl)], dma=True, semkey="%s%d" % (nm, sl))
            S_.op("sp", lambda e, g=g, sl=sl: e.dma_start(out=ba[sl][:], in_=BA.rearrange("(g j p) f -> g p j f", j=4, p=128)[g]),
                  writes=[("ba", sl)], dma=True, semkey="ba%d" % sl)
            for j in range(4):
                c0 = j * 128
                S_.op("dve", lambda e, j=j, sl=sl: e.tensor_scalar(out=y16[:, 0:8], in0=ba[sl][:, j, 0:8], scalar1=-1.0, scalar2=None, op0=ALU.mult),
                      reads=[("ba", sl)], writes=["y16a"])
                S_.op("dve", lambda e, j=j, sl=sl: e.tensor_tensor(out=y16[:, 8:16], in0=ba[sl][:, j, 8:16], in1=dtb[:], op=ALU.add),
                      reads=[("ba", sl), "dtb"], writes=["y16b"])
                S_.op("act", lambda e: e.activation(out=u16[:], in_=y16[:], func=AF.Exp), reads=["y16a", "y16b"], writes=["u16"])
                S_.op("act", lambda e: e.activation(out=u16[:], in_=u16[:], func=AF.Ln, bias=one1[:], scale=1.0), reads=["u16", "one1"], writes=["u16"])
                S_.op("act", lambda e: e.activation(out=beta[:], in_=u16[:, 0:8], func=AF.Exp, scale=-1.0), reads=["u16"], writes=["beta"])
                S_.op("dve", lambda e: e.tensor_tensor(out=gg[:], in0=u16[:, 8:16], in1=negA[:], op=ALU.mult), reads=["u16", "negA"], writes=["gg"])
                b = P.bank()
                S_.op("pe", lambda e, b=b: e.matmul(P.pb[b][:, 0:8], lhsT=trif[:], rhs=gg[:], start=True, stop=True),
                      reads=["trif", "gg"], writes=[("pb", b)])
                S_.op("pe", lambda e, b=b: e.matmul(P.pb[b][:, 8:16], lhsT=bonesf[:], rhs=gg[:], start=True, stop=True),
                      reads=["bonesf", "gg"], writes=[("pb", b)])
                for ch in range(2):
                    S_.op("pe", lambda e, b=b, ch=ch: e.matmul(
                        P.pb[b][0:64, 16 + ch * 8:24 + ch * 8], lhsT=bonesf[:, ch * 64:(ch + 1) * 64],
                        rhs=gg[:], start=True, stop=True), reads=["bonesf", "gg"], writes=[("pb", b)])
                S_.op("dve", lambda e, b=b: e.tensor_copy(out=gcl[:], in_=P.pb[b][:, 0:16]), reads=[("pb", b)], writes=["gcl"])
                S_.op("act", lambda e, b=b: e.activation(out=decS[:], in_=P.pb[b][0:64, 16:32], func=AF.Exp), reads=[("pb", b)], writes=["decS"])
                S_.op("act", lambda e: e.activation(out=eg[:], in_=gcl[:, 0:8], func=AF.Exp), reads=["gcl"], writes=["eg"])
                S_.op("dve", lambda e: e.tensor_tensor(out=kdsc[:], in0=gcl[:, 8:16], in1=gcl[:, 0:8], op=ALU.subtract), reads=["gcl"], writes=["kdsc"])
                S_.op("act", lambda e: e.activation(out=kdsc[:], in_=kdsc[:], func=AF.Exp), reads=["kdsc"], writes=["kdsc"])
                S_.op("dve", lambda e: e.tensor_tensor(out=be[:], in0=beta[:], in1=eg[:], op=ALU.mult), reads=["beta", "eg"], writes=["be"])
                S_.op("dve", lambda e: e.tensor_tensor(out=offL[:], in0=gcl[:, 0:8], in1=u16[:, 0:8], op=ALU.subtract), reads=["gcl", "u16"], writes=["offL"])
                bk_ = P.bank()
                bv_ = bk_
                for (off, src, key) in ((0, kd, "kd"), (512, vd, "vd")):
                    pbf = P.pb[bk_][:].bitcast(BF16)
                    for hp in range(4):
                        S_.op("pe", lambda e, pbf=pbf, hp=hp, src=src, sl=sl, c0=c0, off=off: e.transpose(
                            pbf[:, off + hp * 128:off + (hp + 1) * 128], src[sl][:, hp, c0:c0 + 128], ident[:]),
                            reads=[(key, sl), "ident"], writes=[("pb", bk_)])
                pk = P.pb[bk_][:].bitcast(BF16)[:, 0:512].rearrange("p (h d) -> p h d", h=8)
                pv = P.pb[bv_][:].bitcast(BF16)[:, 512:1024].rearrange("p (h d) -> p h d", h=8)
                S_.op("dve", lambda e, pk=pk: e.tensor_tensor(out=kbg[:].rearrange("p (h d) -> p h d", h=8), in0=pk, in1=bc8(be[:])(64), op=ALU.mult),
                      reads=[("pb", bk_), "be"], writes=["kbg"])
                S_.op("dve", lambda e, pk=pk: e.tensor_tensor(out=kdec[:].rearrange("p (h d) -> p h d", h=8), in0=pk, in1=bc8(kdsc[:])(64), op=ALU.mult),
                      reads=[("pb", bk_), "kdsc"], writes=["kdec"])
                S_.op("dve", lambda e, pv=pv: e.tensor_tensor(out=bv[:].rearrange("p (h d) -> p h d", h=8), in0=pv, in1=bc8(beta[:])(64), op=ALU.mult),
                      reads=[("pb", bv_), "beta"], writes=["bv"])
                bKK = [P.bank(), P.bank()]
                bQK = [P.bank(), P.bank()]
                for h in range(8):
                    hp, hl = h // 2, h % 2
                    S_.op("pe", lambda e, h=h, sl=sl, c0=c0, bKK=bKK: e.matmul(
                        P.pb[bKK[h // 4]][:, (h % 4) * 128:(h % 4 + 1) * 128], lhsT=kd8[sl][:, h, c0:c0 + 128],
                        rhs=kd8[sl][:, h, c0:c0 + 128], start=True, stop=True),
                        reads=[("kd8", sl)], writes=[("pb", bKK[h // 4])])
                    S_.op("pe", lambda e, h=h, sl=sl, c0=c0, bQK=bQK: e.matmul(
                        P.pb[bQK[h // 4]][:, (h % 4) * 128:(h % 4 + 1) * 128], lhsT=kd8[sl][:, h, c0:c0 + 128],
                        rhs=qd[sl][:, h, c0:c0 + 128], start=True, stop=True),
                        reads=[("kd8", sl), ("qd", sl)], writes=[("pb", bQK[h // 4])])
                S_.op("pool", lambda e: e.tensor_tensor(out=DG[:], in0=identf[:].unsqueeze(1).to_broadcast([128, 8, 128]),
                                                       in1=bc8(gcl[:, 0:8])(128), op=ALU.mult), reads=["identf", "gcl"], writes=["DG"])
                for hh in range(2):
                    bG = P.bank()
                    hs = slice(hh * 4, hh * 4 + 4)
                    S_.op("pe", lambda e, bG=bG, hs=hs: e.matmul(P.pb[bG][:], lhsT=onesf[:], rhs=DG[:, hs, :].rearrange("p h c -> p (h c)"),
                                                                 start=True, stop=True), reads=["onesf", "DG"], writes=[("pb", bG)])
                    pG = P.pb[bG][:].rearrange("p (h c) -> p h c", h=4)
                    S_.op("dve", lambda e, pG=pG, hs=hs: e.scalar_tensor_tensor(
                        out=tL[:, hs, :], in0=pG, scalar=-1.0, in1=offL[:, hs].unsqueeze(2).to_broadcast([128, 4, 128]),
                        op0=ALU.mult, op1=ALU.add), reads=[("pb", bG), "offL"], writes=[("tL", hh)])
                    S_.op("dve", lambda e, pG=pG, hs=hs: e.tensor_tensor(
                        out=tU[:, hs, :], in0=pG, in1=gcl[:, hs].unsqueeze(2).to_broadcast([128, 4, 128]), op=ALU.subtract),
                        reads=[("pb", bG), "gcl"], writes=[("tU", hh)])
                    S_.op("pool", lambda e, hs=hs: e.tensor_tensor(out=tL[:, hs, :], in0=tL[:, hs, :],
                                                                   in1=maskL[:].unsqueeze(1).to_broadcast([128, 4, 128]), op=ALU.add),
                          reads=[("tL", hh), "maskL"], writes=[("tL", hh)])
                    S_.op("pool", lambda e, hs=hs: e.tensor_tensor(out=tU[:, hs, :], in0=tU[:, hs, :],
                                                                   in1=maskU[:].unsqueeze(1).to_broadcast([128, 4, 128]), op=ALU.add),
                          reads=[("tU", hh), "maskU"], writes=[("tU", hh)])
                    S_.op("act", lambda e, hs=hs: e.activation(out=Lb[:, hs, :], in_=tL[:, hs, :], func=AF.Exp), reads=[("tL", hh)], writes=[("Lb", hh)])
                    S_.op("act", lambda e, hs=hs: e.activation(out=Ui[:, hs, :], in_=tU[:, hs, :], func=AF.Exp), reads=[("tU", hh)], writes=[("Ui", hh)])
                    S_.op("dve", lambda e, hh=hh, hs=hs, bKK=bKK: e.scalar_tensor_tensor(
                        out=X[0][:, hs, :], in0=P.pb[bKK[hh]][:].rearrange("p (h c) -> p h c", h=4), scalar=-1.0, in1=Lb[:, hs, :],
                        op0=ALU.mult, op1=ALU.mult), reads=[("pb", bKK[hh]), ("Lb", hh)], writes=[("X", 0)])
                    S_.op("dve", lambda e, hh=hh, hs=hs, bQK=bQK: e.tensor_tensor(
                        out=qkdT[:, hs, :], in0=P.pb[bQK[hh]][:].rearrange("p (h c) -> p h c", h=4), in1=Ui[:, hs, :], op=ALU.mult),
                        reads=[("pb", bQK[hh]), ("Ui", hh)], writes=["qkdT"])
                bB = [P.bank(), P.bank()]
                for h in range(8):
                    S_.op("pe", lambda e, bB=bB, h=h: e.transpose(P.pb[bB[h // 4]][:, (h % 4) * 128:(h % 4 + 1) * 128], X[0][:, h, :], identf[:]),
                          reads=[("X", 0), "identf"], writes=[("pb", bB[h // 4])])
                for hh in range(2):
                    S_.op("act", lambda e, bB=bB, hh=hh: e.activation(out=Y[0][:, hh * 4:hh * 4 + 4, :].rearrange("p h c -> p (h c)"), in_=P.pb[bB[hh]][:],
                                                                     func=AF.Identity), reads=[("pb", bB[hh])], writes=[("Y", 0)])
                S_.op("pool", lambda e: e.tensor_tensor(out=Q[0][:], in0=Y[0][:], in1=identf[:].unsqueeze(1).to_broadcast([128, 8, 128]), op=ALU.add),
                      reads=[("Y", 0), "identf"], writes=[("Q", 0)])
                for lv in range(1, 6):
                    a, n = (lv - 1) % 2, lv % 2
                    bX = [P.bank(), P.bank()]
                    for h in range(8):
                        S_.op("pe", lambda e, h=h, a=a, bX=bX: e.matmul(P.pb[bX[h // 4]][:, (h % 4) * 128:(h % 4 + 1) * 128],
                                                                     lhsT=Y[a][:, h, :], rhs=X[a][:, h, :], start=True, stop=True),
                              reads=[("X", a), ("Y", a)], writes=[("pb", bX[h // 4])])
                    for hh in range(2):
                        S_.op("act", lambda e, hh=hh, n=n, bX=bX: e.activation(
                            out=X[n][:, hh * 4:hh * 4 + 4, :].rearrange("p h c -> p (h c)"), in_=P.pb[bX[hh]][:], func=AF.Identity),
                            reads=[("pb", bX[hh])], writes=[("X", n)])
                    if lv < 5:
                        bY = [P.bank(), P.bank()]
                        for h in range(8):
                            S_.op("pe", lambda e, h=h, a=a, bY=bY: e.matmul(P.pb[bY[h // 4]][:, (h % 4) * 128:(h % 4 + 1) * 128],
                                                                         lhsT=X[a][:, h, :], rhs=Y[a][:, h, :], start=True, stop=True),
                                  reads=[("X", a), ("Y", a)], writes=[("pb", bY[h // 4])])
                        for hh in range(2):
                            S_.op("dve", lambda e, hh=hh, n=n, bY=bY: e.tensor_copy(
                                out=Y[n][:, hh * 4:hh * 4 + 4, :].rearrange("p h c -> p (h c)"), in_=P.pb[bY[hh]][:]),
                                reads=[("pb", bY[hh])], writes=[("Y", n)])
                    bQ = [P.bank(), P.bank()]
                    for h in range(8):
                        S_.op("pe", lambda e, h=h, a=a, n=n, bQ=bQ: e.matmul(P.pb[bQ[h // 4]][:, (h % 4) * 128:(h % 4 + 1) * 128],
                                                                          lhsT=X[n][:, h, :], rhs=Q[a][:, h, :], start=True, stop=True),
                              reads=[("X", n), ("Q", a)], writes=[("pb", bQ[h // 4])])
                    for hh in range(2):
                        S_.op("dve", lambda e, hh=hh, n=n, a=a, bQ=bQ: e.tensor_tensor(
                            out=Q[n][:, hh * 4:hh * 4 + 4, :].rearrange("p h c -> p (h c)"), in0=P.pb[bQ[hh]][:],
                            in1=Q[a][:, hh * 4:hh * 4 + 4, :].rearrange("p h c -> p (h c)"), op=ALU.add),
                            reads=[("pb", bQ[hh]), ("Q", a)], writes=[("Q", n)])
                S_.op("act", lambda e: e.activation(out=TTb[:].rearrange("p h c -> p (h c)"), in_=Q[1][:].rearrange("p h c -> p (h c)"), func=AF.Identity),
                      reads=[("Q", 1)], writes=["TTb"])
                TT = TTb
                bu = P.bank()
                bw = [P.bank(), P.bank()]
                for h in range(8):
                    S_.op("pe", lambda e, h=h, bu=bu: e.matmul(P.pb[bu][:, h * 64:(h + 1) * 64], lhsT=TT[:, h, :], rhs=bv[:, h * 64:(h + 1) * 64],
                                                             start=True, stop=True), reads=["TTb", "bv"], writes=[("pb", bu)])
                    S_.op("pe", lambda e, h=h, bw=bw: e.matmul(
                        P.pb[bw[h // 4]][0:64, (h % 4) * 128:(h % 4 + 1) * 128], lhsT=kbg[:, h * 64:(h + 1) * 64], rhs=TT[:, h, :],
                        start=True, stop=True), reads=["TTb", "kbg"], writes=[("pb", bw[h // 4])])
                S_.op("act", lambda e, bu=bu: e.activation(out=uu[:], in_=P.pb[bu][:], func=AF.Identity), reads=[("pb", bu)], writes=["uu"])
                for hh in range(2):
                    S_.op("dve", lambda e, bw=bw, hh=hh: e.tensor_copy(out=wT[:, hh * 4:hh * 4 + 4, :].rearrange("p a c -> p (a c)"), in_=P.pb[bw[hh]][0:64, :]),
                          reads=[("pb", bw[hh])], writes=["wT"])
                for ch in range(2):
                    p0 = ch * 64
                    bvn, bo1, bo2, bs = P.bank(), P.bank(), P.bank(), P.bank()
                    for h in range(8):
                        S_.op("pe", lambda e, h=h, p0=p0, ch=ch, bvn=bvn: e.matmul(
                            P.pb[bvn][p0:p0 + 64, h * 64:(h + 1) * 64], lhsT=wT[:, h, ch * 64:(ch + 1) * 64],
                            rhs=Sbf[:, h * 64:(h + 1) * 64], start=True, stop=True, tile_position=(0, p0)),
                            reads=["wT", "Sbf"], writes=[("pb", bvn)])
                    for h in range(8):
                        S_.op("pe", lambda e, h=h, p0=p0, ch=ch, bo1=bo1, sl=sl, c0=c0: e.matmul(
                            P.pb[bo1][p0:p0 + 64, h * 64:(h + 1) * 64], lhsT=qd[sl][:, h, c0 + p0:c0 + p0 + 64],
                            rhs=Sbf[:, h * 64:(h + 1) * 64], start=True, stop=True, tile_position=(0, p0)),
                            reads=[("qd", sl), "Sbf"], writes=[("pb", bo1)])
                    S_.op("dve", lambda e, p0=p0, bvn=bvn: e.tensor_tensor(out=vnew[p0:p0 + 64, :], in0=uu[p0:p0 + 64, :], in1=P.pb[bvn][p0:p0 + 64, :],
                                                                        op=ALU.subtract), reads=["uu", ("pb", bvn)], writes=[("vnew", ch)])
                    S_.op("dve", lambda e, p0=p0, bo1=bo1: e.tensor_tensor(
                        out=o1s[p0:p0 + 64, :].rearrange("p (h d) -> p h d", h=8), in0=P.pb[bo1][p0:p0 + 64, :].rearrange("p (h d) -> p h d", h=8),
                        in1=eg[p0:p0 + 64, :].unsqueeze(2).to_broadcast([64, 8, 64]), op=ALU.mult),
                        reads=["eg", ("pb", bo1)], writes=[("o1s", ch)])
                    S_.op("pool", lambda e, ch=ch: e.tensor_tensor(
                        out=S1[:].rearrange("p (a d) -> p a d", a=8), in0=Sf[:].rearrange("p (a d) -> p a d", a=8),
                        in1=decS[:, ch * 8:ch * 8 + 8].unsqueeze(2).to_broadcast([64, 8, 64]), op=ALU.mult),
                        reads=["Sf", "decS"], writes=["S1"])
                    for h in range(8):
                        S_.op("pe", lambda e, h=h, p0=p0, ch=ch, bs=bs: e.matmul(
                            P.pb[bs][0:64, h * 64:(h + 1) * 64], lhsT=kdec[p0:p0 + 64, h * 64:(h + 1) * 64],
                            rhs=vnew[p0:p0 + 64, h * 64:(h + 1) * 64], start=True, stop=True, tile_position=(p0, 0)),
                            reads=["kdec", ("vnew", ch)], writes=[("pb", bs)])
                    for h in range(8):
                        S_.op("pe", lambda e, h=h, p0=p0, ch=ch, bo2=bo2: e.matmul(
                            P.pb[bo2][p0:p0 + 64, h * 64:(h + 1) * 64], lhsT=qkdT[p0:p0 + 64, h, ch * 64:(ch + 1) * 64],
                            rhs=vnew[p0:p0 + 64, h * 64:(h + 1) * 64], start=True, stop=True, tile_position=(p0, p0)),
                            reads=["qkdT", ("vnew", ch)], writes=[("pb", bo2)])
                    S_.op("dve", lambda e, bs=bs: e.tensor_tensor(out=Sbf[:], in0=S1[:], in1=P.pb[bs][0:64, :], op=ALU.add),
                          reads=["S1", ("pb", bs)], writes=["Sbf"])
                    S_.op("dve", lambda e, bs=bs: e.tensor_tensor(out=Sf[:], in0=S1[:], in1=P.pb[bs][0:64, :], op=ALU.add),
                          reads=["S1", ("pb", bs)], writes=["Sf"])
                    S_.op("dve", lambda e, p0=p0, bo2=bo2: e.tensor_tensor(out=otm[p0:p0 + 64, :], in0=o1s[p0:p0 + 64, :], in1=P.pb[bo2][p0:p0 + 64, :],
                                                                        op=ALU.add), reads=[("o1s", ch), ("pb", bo2)], writes=[("otm", ch)])
                OT = [("otm", 0), ("otm", 1)]
                S_.op("pool", lambda e: e.tensor_tensor(out=sqo[:], in0=otm[:], in1=otm[:], op=ALU.mult), reads=OT, writes=["sqo"])
                S_.op("dve", lambda e: e.tensor_reduce(out=ssq[:], in_=sqo[:].rearrange("p (h d) -> p h d", h=8), axis=mybir.AxisListType.X, op=ALU.add),
                      reads=["sqo"], writes=["ssq"])
                S_.op("act", lambda e: e.activation(out=ssq[:], in_=ssq[:], func=AF.Sqrt, bias=epsT[:], scale=1.0 / 64), reads=["ssq", "eps"], writes=["ssq"])
                S_.op("dve", lambda e: e.reciprocal(out=ssq[:], in_=ssq[:]), reads=["ssq"], writes=["ssq"])
                S_.op("dve", lambda e: e.tensor_tensor(out=sqo[:].rearrange("p (h d) -> p h d", h=8), in0=otm[:].rearrange("p (h d) -> p h d", h=8),
                                                      in1=bc8(ssq[:])(64), op=ALU.mult), reads=OT + ["ssq"], writes=["sqo"])
                S_.op("pool", lambda e: e.tensor_tensor(out=onb[:].rearrange("p (h d) -> p h d", h=8), in0=sqo[:].rearrange("p (h d) -> p h d", h=8),
                                                       in1=dng[:].unsqueeze(1).to_broadcast([128, 8, 64]), op=ALU.mult), reads=["sqo", "dng"], writes=["onb"])
                bo = P.bank()
                pO = P.pb[bo][:].bitcast(BF16)
                for hp in range(8):
                    S_.op("pe", lambda e, pO=pO, hp=hp: e.transpose(pO[:, hp * 128:(hp + 1) * 128], onb[:, (hp % 4) * 128:(hp % 4 + 1) * 128], ident[:]),
                          reads=["onb", "ident"], writes=[("pb", bo)])
                S_.op("dve", lambda e, pO=pO, sl=sl, c0=c0: e.tensor_tensor(
                    out=mst[sl][:, :, c0:c0 + 128], in0=pO[:, 0:512].rearrange("p (a c) -> p a c", a=4), in1=zs[sl][:, :, c0:c0 + 128], op=ALU.mult),
                    reads=[("pb", bo), ("zs", sl)], writes=[("mst", sl)])
            S_.op("sp", lambda e, g=g, sl=sl: e.dma_start(out=MIXT.rearrange("(c p) s -> p c s", p=128)[:, 4:8, g * 512:(g + 1) * 512], in_=mst[sl][:]),
                  reads=[("mst", sl)], dma=True, semkey="mst%d" % sl)
        P.finish()


def _host_all(inputs, b, S):
    f = np.float32
    col = lambda v: np.ascontiguousarray(np.asarray(v, f).reshape(-1, 128).T)
    d = {}
    d["x"] = np.ascontiguousarray(np.asarray(inputs["x"][b, :S], f))
    d["c_col"] = col(inputs["c"][b])
    d["w_ada"] = np.asarray(inputs["w_ada"][0], f)
    d["bada_col"] = col(inputs["b_ada"][0])
    d["bada_row"] = np.ascontiguousarray(np.asarray(inputs["b_ada"][0], f).reshape(6, 1024))
    d["gattn_col"] = col(inputs["norm_attn_g"][0])
    d["gffn_col"] = col(inputs["norm_ffn_g"][0])
    d["w_in"] = np.asarray(inputs["w_in"][0], f)
    cw = np.asarray(inputs["conv_w"][0], f)
    d["cw_col"] = np.ascontiguousarray(cw.T.reshape(12, 128, 4).transpose(1, 0, 2).reshape(128, 48))
    d["ident"] = np.eye(128, dtype=f)
    bo = np.zeros((128, 128), f)
    bo[:64, :64] = 1
    bo[64:, 64:] = 1
    d["bones"] = bo
    d["identf"] = np.eye(128, dtype=f)
    d["bonesf"] = bo
    idx = np.arange(128)
    same = (idx[:, None] // 64) == (idx[None, :] // 64)
    d["trif"] = (same & (idx[:, None] <= idx[None, :])).astype(f)
    d["maskL"] = np.where(same & (idx[None, :] < idx[:, None]), 0.0, -30000.0).astype(f)
    d["maskU"] = np.where(same & (idx[None, :] >= idx[:, None]), 0.0, -30000.0).astype(f)
    d["alog_row"] = np.asarray(inputs["a_log"][0], f).reshape(1, 8)
    d["dtb_row"] = np.asarray(inputs["dt_bias"][0], f).reshape(1, 8)
    d["dng_row"] = np.asarray(inputs["delta_norm_g"][0], f).reshape(1, 64)
    d["w_out"] = np.asarray(inputs["w_out"][0], f)
    d["w_gate"] = np.asarray(inputs["w_gate"][0], f)
    d["w_up"] = np.asarray(inputs["w_up"][0], f)
    d["w_down"] = np.asarray(inputs["w_down"][0], f)
    d["fg_row"] = np.asarray(inputs["final_norm_g"], f).reshape(1, 1024)
    return d


def kernel(**inputs):
    S = inputs["x"].shape[1]
    B = inputs["x"].shape[0]
    nc, semstack = build_program(S, dbg=False)
    consts = host_consts(inputs)
    in_maps = []
    for b in range(B):
        d = _host_all(inputs, b, S)
        d.update(consts)
        in_maps.append(d)
    res = run_bass_kernel_spmd(nc, in_maps, core_ids=list(range(B)))
    return np.stack([np.asarray(r["out"], np.float32) for r in res.results], axis=0)
```

```python
from contextlib import ExitStack
import numpy as np
import concourse.bass as bass
import concourse.mybir as mybir
from concourse.bass_utils import run_bass_kernel_spmd

F32 = mybir.dt.float32
BF16 = mybir.dt.bfloat16
ALU = mybir.AluOpType
AF = mybir.ActivationFunctionType

ENGS = ("pe", "act", "dve", "pool", "sp")
EPOCH = 20000


class _Op:
    __slots__ = ("eng", "fn", "deps", "dma", "semkey", "idx", "needs_inc", "sem", "val")

    def __init__(self, eng, fn, dma, semkey):
        self.eng, self.fn, self.dma, self.semkey = eng, fn, dma, semkey
        self.deps = []
        self.needs_inc = False
        self.sem = None
        self.val = 0


class Sched:
    def __init__(self, nc):
        self.nc = nc
        self.ops = {e: [] for e in ENGS}
        self.last_w = {}
        self.readers = {}
        self.last_dma_on_sem = {}
        self.n = 0

    def op(self, eng, fn, reads=(), writes=(), dma=False, semkey=None):
        o = _Op(eng, fn, dma, semkey)
        o.idx = self.n
        self.n += 1
        deps = {}

        def add(p):
            if p is None or p is o:
                return
            if (not p.dma) and (not dma) and p.eng == "pe" and eng == "pe":
                return
            deps[id(p)] = p

        for k in reads:
            add(self.last_w.get(k))
        for k in writes:
            add(self.last_w.get(k))
            for r in self.readers.get(k, ()):
                add(r)
        if dma:
            assert semkey is not None
            add(self.last_dma_on_sem.get(semkey))
            self.last_dma_on_sem[semkey] = o
        o.deps = list(deps.values())
        for p in o.deps:
            p.needs_inc = True
        for k in reads:
            self.readers.setdefault(k, []).append(o)
        for k in writes:
            self.last_w[k] = o
            self.readers[k] = []
        self.ops[eng].append(o)
        return o

    def emit(self, stack, final_wait_ops=()):
        nc = self.nc
        for o in final_wait_ops:
            o.needs_inc = True
        sems = {}

        def getsem(name):
            if name not in sems:
                sems[name] = stack.enter_context(nc.semaphore(name))
            return sems[name]

        dma_cnt = {}
        for e in ENGS:
            cnt = 0
            for o in self.ops[e]:
                if o.dma:
                    c = dma_cnt.get(o.semkey, 0) + 1
                    dma_cnt[o.semkey] = c
                    o.sem = getsem("d_" + str(o.semkey))
                    o.val = 16 * c
                    o.needs_inc = True
                elif o.needs_inc:
                    ep, v = divmod(cnt, EPOCH)
                    o.sem = getsem("c_%s_%d" % (e, ep))
                    o.val = v + 1
                    cnt += 1
        self.nsems = len(sems)
        block = stack.enter_context(nc.Block())
        engmap = {"pe": block.tensor, "act": block.scalar, "dve": block.vector,
                  "pool": block.gpsimd, "sp": block.sync}
        for e in ENGS:
            ops = self.ops[e]
            fw = [o for o in final_wait_ops] if e == "sp" else []

            def body(engine, ops=ops, fw=fw):
                waited = {}
                for o in ops:
                    for p in o.deps:
                        key = id(p.sem)
                        if waited.get(key, 0) >= p.val:
                            continue
                        engine.wait_ge(p.sem, p.val)
                        waited[key] = p.val
                    ins = o.fn(engine)
                    if o.needs_inc:
                        ins.then_inc(o.sem, 16 if o.dma else 1)
                for p in fw:
                    key = id(p.sem)
                    if waited.get(key, 0) >= p.val:
                        continue
                    engine.wait_ge(p.sem, p.val)
                    waited[key] = p.val

            engmap[e](body)


D = 1024
INW = 3600
DFF = 2816
EPS = 1e-6


class Phase:
    def __init__(self, nc, semstack, name):
        self.nc, self.semstack, self.name = nc, semstack, name
        self.st = ExitStack()
        self.S = Sched(nc)
        self.nb = 0
        self.pb = None
        self.cnt = 0

    def __enter__(self):
        self.st.__enter__()
        self.pb = [self.st.enter_context(self.nc.psum_tensor("%s_pb%d" % (self.name, i), [128, 512], F32))
                   for i in range(8)]
        return self

    def sb(self, name, shape, dt):
        return self.st.enter_context(self.nc.sbuf_tensor(self.name + "_" + name, shape, dt))

    def bank(self):
        i = self.nb % 8
        self.nb += 1
        return i

    def finish(self):
        S = self.S
        fw = [o for e in ENGS for o in S.ops[e] if o.dma]
        for e in ("pe", "act", "dve", "pool"):
            if S.ops[e]:
                fw.append(S.ops[e][-1])
        nc = self.nc
        ph = self

        class _SemStack:
            def enter_context(self_inner, cm):
                return cm

        _emit(S, nc, self.semstack, self.st, fw, self.name)

    def __exit__(self, *a):
        r = self.st.__exit__(*a)
        return r


def _emit(S, nc, semstack, blockstack, final_wait_ops, pname):
    for o in final_wait_ops:
        o.needs_inc = True
    sems = {}

    def getsem(name):
        name = pname + "_" + "".join(ch if ch.isalnum() else "_" for ch in name)
        if name not in sems:
            sems[name] = blockstack.enter_context(nc.semaphore(name))
        return sems[name]

    dma_cnt = {}
    for e in ENGS:
        cnt = 0
        for o in S.ops[e]:
            if o.dma:
                c = dma_cnt.get(o.semkey, 0) + 1
                dma_cnt[o.semkey] = c
                o.sem = getsem("d_" + str(o.semkey))
                o.val = 16 * c
                o.needs_inc = True
            elif o.needs_inc:
                ep, v = divmod(cnt, EPOCH)
                o.sem = getsem("c_%s_%d" % (e, ep))
                o.val = v + 1
                cnt += 1
    S.nsems = len(sems)
    for sm in sems.values():
        nc.sync.sem_clear(sm)
    nc.all_engine_barrier()
    block = blockstack.enter_context(nc.Block())
    engmap = {"pe": block.tensor, "act": block.scalar, "dve": block.vector,
              "pool": block.gpsimd, "sp": block.sync}
    for e in ENGS:
        ops = S.ops[e]
        fw = list(final_wait_ops) if e == "sp" else []

        def body(engine, ops=ops, fw=fw):
            waited = {}
            for o in ops:
                for p in o.deps:
                    key = id(p.sem)
                    if waited.get(key, 0) >= p.val:
                        continue
                    engine.wait_ge(p.sem, p.val)
                    waited[key] = p.val
                ins = o.fn(engine)
                if o.needs_inc:
                    ins.then_inc(o.sem, 16 if o.dma else 1)
            for p in fw:
                key = id(p.sem)
                if waited.get(key, 0) >= p.val:
                    continue
                engine.wait_ge(p.sem, p.val)
                waited[key] = p.val

        engmap[e](body)


def _kw(**k):
    return k


def build_program(S, dbg=False, upto=9):
    nc = bass.Bass("TRN2", target_bir_lowering=False)
    NG = S // 512
    OUTK = "ExternalOutput" if dbg else "Internal"

    def din(name, shape, dt=F32):
        return nc.dram_tensor(name, shape, dt, kind="ExternalInput").ap()

    def dsc(name, shape, dt):
        return nc.dram_tensor(name, shape, dt, kind=OUTK).ap()

    x = din("x", [S, D])
    c_col = din("c_col", [128, 8])
    w_ada = din("w_ada", [D, 6 * D])
    bada_col = din("bada_col", [128, 48])
    bada_row = din("bada_row", [6, D])
    gattn_col = din("gattn_col", [128, 8])
    gffn_col = din("gffn_col", [128, 8])
    w_in = din("w_in", [D, INW])
    cw_col = din("cw_col", [128, 48])
    ident_in = din("ident", [128, 128])
    bones_in = din("bones", [128, 128])
    tb_in = din("tb", [128, 24 * 256])
    out = nc.dram_tensor("out", [S, D], F32, kind="ExternalOutput").ap()

    MODC = dsc("MODC", [128, 32], F32)
    GROW = dsc("GROW", [2, 128, D], F32)
    QT = dsc("QT", [512, S], BF16)
    KT = dsc("KT", [512, S], BF16)
    VV = dsc("VV", [S, 512], BF16)
    QD = dsc("QD", [512, S], BF16)
    KD = dsc("KD", [512, S], BF16)
    VD = dsc("VD", [512, S], BF16)
    ZS = dsc("ZS", [512, S], BF16)
    BA = dsc("BA", [S, 16], F32)
    MIXT = dsc("MIXT", [D, S], BF16)

    semstack = ExitStack()
    semstack.__enter__()

    with Phase(nc, semstack, "p0") as P:
        S_ = P.S
        ccol = P.sb("ccol", [128, 8], F32)
        sbf = P.sb("sbf", [128, 8], BF16)
        sbc = P.sb("sbc", [128, 8, 128], BF16)
        bcol = P.sb("bcol", [128, 48], F32)
        gcol = P.sb("gcol", [128, 16], F32)
        modc = P.sb("modc", [128, 32], F32)
        wa = [P.sb("wa%d" % i, [128, 8, D], BF16) for i in range(2)]
        brow = [P.sb("brow%d" % i, [128, D], F32) for i in range(2)]
        grow = [P.sb("grow%d" % i, [128, D], F32) for i in range(2)]
        S_.op("sp", lambda e: e.dma_start(out=ccol[:], in_=c_col), writes=["ccol"], dma=True, semkey="ccol")
        S_.op("sp", lambda e: e.dma_start(out=bcol[:], in_=bada_col), writes=["bcol"], dma=True, semkey="bcol")
        S_.op("sp", lambda e: e.dma_start(out=gcol[:, 0:8], in_=gattn_col), writes=["gcol"], dma=True, semkey="gcol")
        S_.op("sp", lambda e: e.dma_start(out=gcol[:, 8:16], in_=gffn_col), writes=["gcol"], dma=True, semkey="gcol")
        S_.op("act", lambda e: e.activation(out=sbf[:], in_=ccol[:], func=AF.Silu), reads=["ccol"], writes=["sbf"])
        S_.op("dve", lambda e: e.tensor_copy(out=sbc[:], in_=sbf[:].unsqueeze(2).to_broadcast([128, 8, 128])),
              reads=["sbf"], writes=["sbc"])
        wav = w_ada.rearrange("(k p) f -> p k f", p=128)
        colidx = {0: 0, 1: 1, 3: 2, 4: 3}
        for j in range(6):
            sl = j % 2
            S_.op("pool", lambda e, j=j, sl=sl: e.dma_start(out=wa[sl][:], in_=wav[:, :, j * D:(j + 1) * D]),
                  writes=[("wa", sl)], dma=True, semkey="wa%d" % sl)
            if j in colidx:
                jj = colidx[j]
                b = P.bank()
                for fcn in range(8):
                    for k in range(8):
                        S_.op("pe", lambda e, b=b, fcn=fcn, k=k, sl=sl: e.matmul(
                            P.pb[b][:, fcn:fcn + 1], lhsT=wa[sl][:, k, fcn * 128:(fcn + 1) * 128], rhs=sbf[:, k:k + 1],
                            start=(k == 0), stop=(k == 7)), reads=[("wa", sl), "sbf"], writes=[("pb", b)])
                S_.op("dve", lambda e, b=b, jj=jj, j=j: e.tensor_tensor(
                    out=modc[:, jj * 8:(jj + 1) * 8], in0=P.pb[b][:, 0:8], in1=bcol[:, j * 8:(j + 1) * 8], op=ALU.add),
                    reads=[("pb", b), "bcol"], writes=["modc"])
            else:
                gi = 0 if j == 2 else 1
                S_.op("sp", lambda e, j=j, gi=gi: e.dma_start(out=brow[gi][:], in_=bada_row[j:j + 1, :].partition_broadcast(128)),
                      writes=[("brow", gi)], dma=True, semkey="brow%d" % gi)
                for half in range(2):
                    b = P.bank()
                    for k in range(8):
                        S_.op("pe", lambda e, b=b, k=k, sl=sl, half=half: e.matmul(
                            P.pb[b][:], lhsT=sbc[:, k, :], rhs=wa[sl][:, k, half * 512:(half + 1) * 512],
                            start=(k == 0), stop=(k == 7)), reads=[("wa", sl), "sbc"], writes=[("pb", b)])
                    S_.op("dve", lambda e, b=b, gi=gi, half=half: e.tensor_tensor(
                        out=grow[gi][:, half * 512:(half + 1) * 512], in0=P.pb[b][:], in1=brow[gi][:, half * 512:(half + 1) * 512],
                        op=ALU.add), reads=[("pb", b), ("brow", gi)], writes=[("grow", gi)])
                S_.op("sp", lambda e, gi=gi: e.dma_start(out=GROW[gi], in_=grow[gi][:]), reads=[("grow", gi)],
                      dma=True, semkey="grow%d" % gi)
        for jj, go in ((1, 0), (3, 8)):
            S_.op("dve", lambda e, jj=jj, go=go: e.scalar_tensor_tensor(
                out=modc[:, jj * 8:(jj + 1) * 8], in0=modc[:, jj * 8:(jj + 1) * 8], scalar=1.0, in1=gcol[:, go:go + 8],
                op0=ALU.add, op1=ALU.mult), reads=["modc", "gcol"], writes=["modc"])
        S_.op("sp", lambda e: e.dma_start(out=MODC, in_=modc[:]), reads=["modc"], dma=True, semkey="modc")
        P.finish()
    nc.all_engine_barrier()
    if upto < 1:
        return nc, semstack

    with Phase(nc, semstack, "p1") as P:
        S_ = P.S
        ident = P.sb("ident", [128, 128], BF16)
        bones = P.sb("bones", [128, 128], BF16)
        modc = P.sb("modc", [128, 32], F32)
        cw = P.sb("cw", [128, 48], F32)
        epsT = P.sb("eps", [128, 1], F32)
        win = P.sb("win", [128, 8, INW], BF16)
        S_.op("pool", lambda e: e.dma_start(out=ident[:], in_=ident_in), writes=["ident"], dma=True, semkey="ident")
        S_.op("pool", lambda e: e.dma_start(out=bones[:], in_=bones_in), writes=["bones"], dma=True, semkey="bones")
        S_.op("sp", lambda e: e.dma_start(out=modc[:], in_=MODC), writes=["modc"], dma=True, semkey="modc")
        S_.op("sp", lambda e: e.dma_start(out=cw[:], in_=cw_col), writes=["cw"], dma=True, semkey="cw")
        S_.op("dve", lambda e: e.memset(epsT[:], EPS), writes=["eps"])
        winv = w_in.rearrange("(k p) f -> p k f", p=128)
        for k in range(8):
            S_.op("pool", lambda e, k=k: e.dma_start(out=win[:, k, :], in_=winv[:, k, :]), writes=[("win", k)],
                  dma=True, semkey="win%d" % k)
        WIN = [("win", k) for k in range(8)]
        xt = [P.sb("xt%d" % i, [128, 4, D], F32) for i in range(2)]
        junk = P.sb("junk", [128, D], BF16)
        ss2 = [P.sb("ss%d" % i, [128, 4], F32) for i in range(2)]
        rstd2 = [P.sb("rstd%d" % i, [128, 4], F32) for i in range(2)]
        xs2 = [P.sb("xs%d" % i, [128, 4, D], BF16) for i in range(2)]
        hT2 = [P.sb("hT%d" % i, [128, 8, 512], BF16) for i in range(2)]
        NSTQ = 6
        stq = [P.sb("stq%d" % i, [128, 4, 512], BF16) for i in range(NSTQ)]
        cin = P.sb("cin", [128, 12, 515], F32)
        NROT = 5
        acc3 = [P.sb("acc%d" % i, [128, 512], F32) for i in range(NROT)]
        slu3 = [P.sb("slu%d" % i, [128, 512], F32) for i in range(NROT)]
        sq3 = [P.sb("sq%d" % i, [128, 512], BF16) for i in range(NROT)]
        rs3 = [P.sb("rs%d" % i, [128, 512], F32) for i in range(NROT)]
        rot = [0]
        stb = P.sb("stb", [128, 4, 16], F32)
        xv = x.rearrange("(g j p) d -> g p j d", j=4, p=128)
        S_.op("pool", lambda e: e.memset(cin[:], 0.0), writes=["cin"])
        nst = [0]

        def stage():
            i = nst[0] % NSTQ
            nst[0] += 1
            return i

        NROT2 = NROT

        def norm_item(g):
            xs_ = g % 2
            ss, rstd, xs, hT = ss2[xs_], rstd2[xs_], xs2[xs_], hT2[xs_]
            KSS, KRS = ("ss", xs_), ("rstd", xs_)
            S_.op("sp", lambda e: e.dma_start(out=xt[xs_][:], in_=xv[g]), writes=[("xt", xs_)], dma=True, semkey="xt%d" % xs_)
            S_.op("pool", lambda e: e.memset(ss[:], 0.0), writes=[KSS])
            yield
            for j in range(4):
                S_.op("act", lambda e, j=j: e.activation(out=junk[:], in_=xt[xs_][:, j, :], func=AF.Square, accum_out=ss[:, j:j + 1]),
                      reads=[("xt", xs_), KSS], writes=[KSS, "junk"])
            S_.op("act", lambda e: e.activation(out=rstd[:], in_=ss[:], func=AF.Sqrt, bias=epsT[:], scale=1.0 / D),
                  reads=[KSS, "eps"], writes=[KRS])
            yield
            S_.op("dve", lambda e: e.reciprocal(out=rstd[:], in_=rstd[:]), reads=[KRS], writes=[KRS])
            for j in range(4):
                S_.op("dve", lambda e, j=j: e.tensor_scalar(out=xs[:, j, :], in0=xt[xs_][:, j, :], scalar1=rstd[:, j:j + 1], scalar2=None, op0=ALU.mult),
                      reads=[("xt", xs_), KRS], writes=[("xs", xs_, j)])
            yield
            pend = None
            for c2 in range(5):
                if c2 < 4:
                    b = P.bank()
                    pbf = P.pb[b][:].bitcast(BF16)
                    for cc in range(2):
                        c = c2 * 2 + cc
                        for j in range(4):
                            S_.op("pe", lambda e, pbf=pbf, cc=cc, c=c, j=j: e.transpose(
                                pbf[:, cc * 512 + j * 128: cc * 512 + (j + 1) * 128], xs[:, j, c * 128:(c + 1) * 128], ident[:]),
                                reads=[("xs", xs_, j), "ident"], writes=[("pb", b)])
                if pend is not None:
                    pb_, pbf_, pc2 = pend
                    for cc in range(2):
                        c = pc2 * 2 + cc
                        S_.op("act", lambda e, pbf_=pbf_, cc=cc, c=c: e.activation(
                            out=hT[:, c, :], in_=pbf_[:, cc * 512:(cc + 1) * 512], func=AF.Identity,
                            bias=modc[:, c:c + 1], scale=modc[:, 8 + c:9 + c]),
                            reads=[("pb", pb_), "modc"], writes=[("hT", xs_, c)])
                pend = (b, pbf, c2) if c2 < 4 else None
                yield

        def proj_mm(g, fc):
            xs_ = g % 2
            hT = hT2[xs_]
            b = P.bank()
            for k in range(8):
                S_.op("pe", lambda e, k=k: e.matmul(
                    P.pb[b][:], lhsT=win[:, k, fc * 128:(fc + 1) * 128], rhs=hT[:, k, :], start=(k == 0), stop=(k == 7)),
                    reads=[("win", k), ("hT", xs_, k)], writes=[("pb", b)])
            return b

        def store(dst, si, g):
            S_.op("sp", lambda e: e.dma_start(
                out=dst.rearrange("(c p) s -> p c s", p=128)[:, :, g * 512:(g + 1) * 512], in_=stq[si][:]),
                reads=[("stq", si, i) for i in range(4)], dma=True, semkey="stq%d" % si)

        def qk_item(g, base, dst, si, i):
            b = proj_mm(g, base + i)
            yield
            if i % 2 == 0:
                S_.op("act", lambda e: e.activation(out=stq[si][:, i, :], in_=P.pb[b][:], func=AF.Identity),
                      reads=[("pb", b)], writes=[("stq", si, i)])
            else:
                S_.op("dve", lambda e: e.tensor_copy(out=stq[si][:, i, :], in_=P.pb[b][:]),
                      reads=[("pb", b)], writes=[("stq", si, i)])
            if i == 3:
                store(dst, si, g)

        def v_item(g, si, j):
            xs_ = g % 2
            hT = hT2[xs_]
            b = P.bank()
            for k in range(8):
                S_.op("pe", lambda e, k=k: e.matmul(
                    P.pb[b][:], lhsT=hT[:, k, j * 128:(j + 1) * 128], rhs=win[:, k, 1024:1536], start=(k == 0), stop=(k == 7)),
                    reads=[("win", k), ("hT", xs_, k)], writes=[("pb", b)])
            yield
            S_.op("act", lambda e: e.activation(out=stq[si][:, j, :], in_=P.pb[b][:], func=AF.Identity),
                  reads=[("pb", b)], writes=[("stq", si, j)])
            if j == 3:
                S_.op("sp", lambda e: e.dma_start(out=VV.rearrange("(g j p) f -> g p j f", j=4, p=128)[g], in_=stq[si][:]),
                      reads=[("stq", si, i) for i in range(4)], dma=True, semkey="stq%d" % si)

        def z_item(g, si, i):
            b = proj_mm(g, 24 + i)
            yield
            S_.op("act", lambda e: e.activation(out=stq[si][:, i, :], in_=P.pb[b][:], func=AF.Silu),
                  reads=[("pb", b)], writes=[("stq", si, i)])
            if i == 3:
                store(ZS, si, g)

        def ba_item(g):
            xs_ = g % 2
            hT = hT2[xs_]
            b = P.bank()
            for j in range(4):
                for k in range(8):
                    S_.op("pe", lambda e, k=k, j=j: e.matmul(
                        P.pb[b][:, j * 16:(j + 1) * 16], lhsT=hT[:, k, j * 128:(j + 1) * 128], rhs=win[:, k, 3584:3600],
                        start=(k == 0), stop=(k == 7)), reads=[("win", k), ("hT", xs_, k)], writes=[("pb", b)])
            yield
            S_.op("dve", lambda e: e.tensor_copy(out=stb[:].rearrange("p j f -> p (j f)"), in_=P.pb[b][:, 0:64]),
                  reads=[("pb", b)], writes=["stb"])
            S_.op("sp", lambda e: e.dma_start(out=BA.rearrange("(g j p) f -> g p j f", j=4, p=128)[g], in_=stb[:]),
                  reads=["stb"], dma=True, semkey="stb")

        def delta_item(g, grp, dst, si, i):
            ci = grp * 4 + i
            b = proj_mm(g, 12 + ci)
            yield
            S_.op("act", lambda e: e.activation(out=cin[:, ci, 3:515], in_=P.pb[b][:], func=AF.Identity),
                  reads=[("pb", b)], writes=[("cin", ci)])
            yield
            ri = rot[0] % NROT2
            rot[0] += 1
            acc, slu, sq, rs = acc3[ri], slu3[ri], sq3[ri], rs3[ri]
            KA, KSL, KSQ, KR = ("acc", ri), ("slu", ri), ("sq", ri), ("rs", ri)
            S_.op("dve", lambda e: e.tensor_scalar(out=acc[:], in0=cin[:, ci, 0:512], scalar1=cw[:, ci * 4:ci * 4 + 1], scalar2=None, op0=ALU.mult),
                  reads=[("cin", ci), "cw"], writes=[KA])
            for t in range(1, 4):
                S_.op("dve", lambda e, t=t: e.scalar_tensor_tensor(
                    out=acc[:], in0=cin[:, ci, t:t + 512], scalar=cw[:, ci * 4 + t:ci * 4 + t + 1], in1=acc[:],
                    op0=ALU.mult, op1=ALU.add), reads=[("cin", ci), "cw", KA], writes=[KA])
            S_.op("pool", lambda e: e.tensor_copy(out=cin[:, ci, 0:3], in_=cin[:, ci, 512:515]), reads=[("cin", ci)], writes=[("cin", ci)])
            yield
            if grp == 2:
                S_.op("act", lambda e: e.activation(out=stq[si][:, i, :], in_=acc[:], func=AF.Silu), reads=[KA], writes=[("stq", si, i)])
                if i == 3:
                    store(dst, si, g)
                return
            S_.op("act", lambda e: e.activation(out=slu[:], in_=acc[:], func=AF.Silu), reads=[KA], writes=[KSL])
            S_.op("pool", lambda e: e.tensor_tensor(out=sq[:], in0=slu[:], in1=slu[:], op=ALU.mult), reads=[KSL], writes=[KSQ])
            yield
            b2 = P.bank()
            S_.op("pe", lambda e: e.matmul(P.pb[b2][:], lhsT=bones[:], rhs=sq[:], start=True, stop=True), reads=["bones", KSQ], writes=[("pb", b2)])
            yield
            S_.op("act", lambda e: e.activation(out=rs[:], in_=P.pb[b2][:], func=AF.Sqrt, bias=epsT[:], scale=1.0), reads=[("pb", b2), "eps"], writes=[KR])
            yield
            S_.op("dve", lambda e: e.reciprocal(out=rs[:], in_=rs[:]), reads=[KR], writes=[KR])
            scl = 0.125 if grp == 0 else 1.0
            S_.op("dve", lambda e: e.scalar_tensor_tensor(out=stq[si][:, i, :], in0=slu[:], scalar=scl, in1=rs[:], op0=ALU.mult, op1=ALU.mult),
                  reads=[KSL, KR], writes=[("stq", si, i)])
            if i == 3:
                store(dst, si, g)

        def p1_items():
            for g in range(NG):
                if g + 1 < NG:
                    yield norm_item(g + 1)
                si = stage()
                for i in range(4):
                    yield qk_item(g, 0, QT, si, i)
                si = stage()
                for i in range(4):
                    yield qk_item(g, 4, KT, si, i)
                si = stage()
                for j in range(4):
                    yield v_item(g, si, j)
                for grp, dst in ((0, QD), (1, KD), (2, VD)):
                    si = stage()
                    for i in range(4):
                        yield delta_item(g, grp, dst, si, i)
                si = stage()
                for i in range(4):
                    yield z_item(g, si, i)
                yield ba_item(g)

        for _ in norm_item(0):
            pass
        run_skewed(p1_items())
        P.finish()
    nc.all_engine_barrier()
    if upto < 2:
        return nc, semstack
    _phase2(nc, semstack, S, QT, KT, VV, tb_in, MIXT)
    nc.all_engine_barrier()
    if upto < 3:
        return nc, semstack
    cst = dict(identf=din("identf", [128, 128]), trif=din("trif", [128, 128]), bonesf=din("bonesf", [128, 128]),
               maskL=din("maskL", [128, 128]), maskU=din("maskU", [128, 128]), ident=ident_in,
               alog=din("alog_row", [1, 8]), dtb=din("dtb_row", [1, 8]), dng=din("dng_row", [1, 64]))
    _phase3(nc, semstack, S, QD, KD, VD, ZS, BA, MIXT, cst)
    nc.all_engine_barrier()
    if upto < 4:
        return nc, semstack
    w_out = din("w_out", [D, D])
    w_gate = din("w_gate", [D, DFF])
    w_up = din("w_up", [D, DFF])
    w_down = din("w_down", [DFF, D])
    fg_row = din("fg_row", [1, D])
    X1 = dsc("X1", [S, D], F32)
    H2T = dsc("H2T", [D, S], BF16)
    _phase4a(nc, semstack, S, x, MIXT, w_out, GROW, MODC, ident_in, X1, H2T)
    nc.all_engine_barrier()
    if upto < 5:
        return nc, semstack
    _phase4b(nc, semstack, S, X1, H2T, w_gate, w_up, w_down, GROW, fg_row, out)
    return nc, semstack


P2DBG = {'mode': 9, 'strided': True}


def run_skewed(items):
    live = []
    it = iter(items)
    while True:
        nxt = next(it, None)
        if nxt is not None:
            live.append(nxt)
        if not live:
            break
        for gen in list(live):
            try:
                next(gen)
            except StopIteration:
                live.remove(gen)


def _phase2(nc, semstack, S, QT, KT, VV, tb_in, MIXT):
    NSB = S // 2048
    import os
    mode = int(os.environ.get('P2MODE', '9'))
    with Phase(nc, semstack, "p2") as P:
        S_ = P.S
        EB = P.sb("EB", [128, 24 * 256], BF16)
        ones = P.sb("ones", [128, 64], BF16)
        qt = P.sb("qt", [64, 8, 2048], BF16)
        kt = [P.sb("kt%d" % i, [64, 8, 2048], BF16) for i in range(2)]
        v1 = P.sb("v1", [128, 16, 512], BF16)
        v1p = P.sb("v1p", [128, 1, 512], BF16)
        v2 = P.sb("v2", [128, 16, 512], BF16)
        v2p = P.sb("v2p", [128, 4, 512], BF16)
        v3 = [P.sb("v3_%d" % i, [128, 16, 512], BF16) for i in range(2)]
        Et = [P.sb("E%d" % i, [128, 512], BF16) for i in range(4)]
        PT = [P.sb("PT%d" % i, [128, 512], BF16) for i in range(4)]
        accn = P.sb("accn", [128, 2048], F32)
        accd = P.sb("accd", [128, 2048], F32)
        mst = P.sb("mst", [128, 2048], BF16)
        S_.op("pool", lambda e: e.dma_start(out=EB[:], in_=tb_in), writes=["EB"], dma=True, semkey="tb")
        S_.op("act", lambda e: e.activation(out=EB[:], in_=EB[:], func=AF.Exp), reads=["EB"], writes=["EB"])
        S_.op("pool", lambda e: e.memset(ones[:], 1.0), writes=["ones"])
        cnt = [0, 0, 0]
        for N in range(NSB):
            cur, prv = N % 2, (N + 1) % 2
            t0 = N * 2048
            S_.op("sp", lambda e, t0=t0: e.dma_start(out=qt[:], in_=QT.rearrange("(c p) s -> p c s", p=64)[:, :, t0:t0 + 2048]),
                  writes=["qt"], dma=True, semkey="qt")
            S_.op("sp", lambda e, t0=t0, cur=cur: e.dma_start(out=kt[cur][:], in_=KT.rearrange("(c p) s -> p c s", p=64)[:, :, t0:t0 + 2048]),
                  writes=[("kt", cur)], dma=True, semkey="kt%d" % cur)
            Vsb = VV[t0:t0 + 2048, :]
            S_.op("sp", lambda e, Vsb=Vsb: e.dma_start(out=v1[:], in_=Vsb.rearrange("(n p) f -> p n f", p=128)),
                  writes=["v1"], dma=True, semkey="v1")
            for n_ in range(4):
                S_.op("sp", lambda e, Vsb=Vsb, n_=n_: e.dma_start(
                    out=v2[:, n_ * 4:(n_ + 1) * 4, :],
                    in_=Vsb[n_ * 512:(n_ + 1) * 512, :].rearrange("(p r) f -> p r f", r=4)),
                    writes=["v2"], dma=True, semkey="v2")
            S_.op("sp", lambda e, Vsb=Vsb, cur=cur: e.dma_start(out=v3[cur][:], in_=Vsb.rearrange("(p r) f -> p r f", r=16)),
                  writes=[("v3", cur)], dma=True, semkey="v3_%d" % cur)
            def unit(hp, br, gq, jj, nbk, dbk, N=N, cur=cur, prv=prv):
                if br == 0:
                    n = 4 * gq + jj
                    qs, st = n * 128, 1
                    vcur = (v1, n, "v1")
                    if n >= 1:
                        pk = (cur, (n - 1) * 128, (v1, n - 1, "v1"))
                    elif N >= 1:
                        pk = (prv, 15 * 128, (v1p, 0, "v1p"))
                    else:
                        pk = None
                elif br == 1:
                    n_, r = gq, jj
                    qs, st = n_ * 512 + r, 4
                    vcur = (v2, n_ * 4 + r, "v2")
                    if n_ >= 1:
                        pk = (cur, (n_ - 1) * 512 + r, (v2, (n_ - 1) * 4 + r, "v2"))
                    elif N >= 1:
                        pk = (prv, 3 * 512 + r, (v2p, r, "v2p"))
                    else:
                        pk = None
                else:
                    r = 4 * gq + jj
                    qs, st = r, 16
                    vcur = (v3[cur], r, ("v3", cur))
                    pk = (prv, r, (v3[prv], r, ("v3", prv))) if N >= 1 else None
                sbk = cnt[0] % 3
                ei = cnt[0] % 4
                cnt[0] += 1
                blks = ([(0,) + pk] if pk else []) + [(1, cur, qs, vcur)]
                for hl in range(2):
                    for (blk, slot, ks, _v) in blks:
                        S_.op("pe", lambda e, sbk=sbk, hl=hl, blk=blk, slot=slot, ks=ks, qs=qs, st=st, hp=hp: e.matmul(
                            P.pb[sbk][:, hl * 256 + blk * 128: hl * 256 + (blk + 1) * 128],
                            lhsT=kt[slot][:, 2 * hp + hl, ks:ks + 127 * st + 1:st],
                            rhs=qt[:, 2 * hp + hl, qs:qs + 127 * st + 1:st],
                            start=True, stop=True),
                            reads=[("kt", slot), "qt"], writes=[("pb", sbk)])
                yield
                c0 = 0 if pk else 128
                vw = lambda ap, c0=c0: ap.rearrange("p (h c) -> p h c", h=2)[:, :, c0:256]
                S_.op("act", lambda e, sbk=sbk, ei=ei, vw=vw: e.activation(
                    out=vw(Et[ei][:]), in_=vw(P.pb[sbk][:]), func=AF.Exp, scale=0.125),
                    reads=[("pb", sbk)], writes=[("E", ei)])
                yield
                eoff = (br * 8 + 2 * hp) * 256
                eng = "dve" if ei % 2 == 0 else "pool"
                S_.op(eng, lambda e, ei=ei, eoff=eoff, vw=vw: e.tensor_tensor(
                    out=vw(PT[ei][:]), in0=vw(Et[ei][:]), in1=vw(EB[:, eoff:eoff + 512]), op=ALU.mult),
                    reads=[("E", ei), "EB"], writes=[("PT", ei)])
                yield
                for hl in range(2):
                    h = 2 * hp + hl
                    for bi, (blk, slot, ks, (vt, vi, vkey)) in enumerate(blks):
                        fl = _kw(start=(bi == 0), stop=(bi == len(blks) - 1), tile_position=(0, hl * 64))
                        S_.op("pe", lambda e, nbk=nbk, hl=hl, jj=jj, vt=vt, vi=vi, h=h, ei=ei, blk=blk, fl=fl: e.matmul(
                            P.pb[nbk][hl * 64:(hl + 1) * 64, jj * 128:(jj + 1) * 128],
                            lhsT=vt[:, vi, h * 64:(h + 1) * 64],
                            rhs=PT[ei][:, hl * 256 + blk * 128: hl * 256 + (blk + 1) * 128], **fl),
                            reads=[vkey, ("PT", ei)], writes=[("pb", nbk)])
                        S_.op("pe", lambda e, dbk=dbk, hl=hl, jj=jj, ei=ei, blk=blk, fl=fl: e.matmul(
                            P.pb[dbk][hl * 64:(hl + 1) * 64, jj * 128:(jj + 1) * 128],
                            lhsT=ones[:, 0:64],
                            rhs=PT[ei][:, hl * 256 + blk * 128: hl * 256 + (blk + 1) * 128], **fl),
                            reads=["ones", ("PT", ei)], writes=[("pb", dbk)])
                if jj < 3:
                    return
                yield
                for bk, acc, akey in ((nbk, accn, "accn"), (dbk, accd, "accd")):
                    if br == 0:
                        S_.op("act", lambda e, bk=bk, acc=acc, gq=gq: e.activation(
                            out=acc[:, gq * 512:(gq + 1) * 512], in_=P.pb[bk][:], func=AF.Identity),
                            reads=[("pb", bk)], writes=[akey])
                    else:
                        if br == 1:
                            oap = acc[:, gq * 512:(gq + 1) * 512].rearrange("p (i r) -> p r i", r=4)
                        else:
                            oap = acc[:].rearrange("p (i r) -> p r i", r=16)[:, 4 * gq:4 * gq + 4, :]
                        S_.op("dve", lambda e, bk=bk, oap=oap: e.tensor_tensor(
                            out=oap, in0=P.pb[bk][:].rearrange("p (r i) -> p r i", r=4), in1=oap, op=ALU.add),
                            reads=[("pb", bk), akey], writes=[akey])

            def finalize(hp, t0=t0):
                for _ in range(6):
                    yield
                S_.op("dve", lambda e: e.reciprocal(out=accd[:], in_=accd[:]), reads=["accd"], writes=["accd"])
                S_.op("dve", lambda e: e.tensor_tensor(out=mst[:], in0=accn[:], in1=accd[:], op=ALU.mult),
                      reads=["accn", "accd"], writes=["mst"])
                S_.op("sp", lambda e, hp=hp, t0=t0: e.dma_start(out=MIXT[hp * 128:(hp + 1) * 128, t0:t0 + 2048], in_=mst[:]),
                      reads=["mst"], dma=True, semkey="mst")

            def items():
                for hp in range(4):
                    for br in range(3):
                        for gq in range(4):
                            nbk = 3 + cnt[1] % 2
                            dbk = 5 + cnt[1] % 2
                            cnt[1] += 1
                            for jj in range(4):
                                yield unit(hp, br, gq, jj, nbk, dbk)
                    yield finalize(hp)

            run_skewed(items())
            if N + 1 < NSB:
                S_.op("pool", lambda e: e.tensor_copy(out=v1p[:, 0, :], in_=v1[:, 15, :]), reads=["v1"], writes=["v1p"])
                S_.op("pool", lambda e: e.tensor_copy(out=v2p[:], in_=v2[:, 12:16, :]), reads=["v2"], writes=["v2p"])
        P.finish()


def host_consts(inputs):
    import math
    rel_bias = np.asarray(inputs["rel_bias"], np.float32)
    k = np.arange(128)[:, None]
    q = np.arange(128)[None, :]
    steps_prev = q + 128 - k
    steps_cur = q - k
    tb = np.full((128, 3, 8, 2, 128), -30000.0, np.float32)

    def bucket(dist):
        dist = np.asarray(dist, np.int64)
        max_exact = 16
        dist_f = np.maximum(dist, 1).astype(np.float32)
        lg = (np.log(dist_f / np.float32(max_exact)) / np.float32(math.log(2048 / max_exact))
              * np.float32(32 - max_exact)).astype(np.float32)
        large = max_exact + lg.astype(np.int32)
        return np.where(dist < max_exact, dist, np.minimum(large, 31)).astype(np.int64)

    for br, d in enumerate((1, 4, 16)):
        for blk, steps in ((0, steps_prev), (1, steps_cur)):
            valid = (steps >= 0) & (steps <= 128)
            bk = bucket(np.maximum(steps, 0) * d)
            for h in range(8):
                vals = rel_bias[bk, h]
                tb[:, br, h, blk, :] = np.where(valid, vals, np.float32(-30000.0))
    return {"tb": np.ascontiguousarray(tb.reshape(128, 24 * 256))}


def _phase4a(nc, semstack, S, x, MIXT, w_out, GROW, MODC, ident_in, X1, H2T):
    NG = S // 512
    with Phase(nc, semstack, "p4a") as P:
        S_ = P.S
        ident = P.sb("ident", [128, 128], BF16)
        modc = P.sb("modc", [128, 32], F32)
        epsT = P.sb("eps", [128, 1], F32)
        g1 = P.sb("g1", [128, D], F32)
        wo = P.sb("wo", [128, 8, D], BF16)
        S_.op("pool", lambda e: e.dma_start(out=ident[:], in_=ident_in), writes=["ident"], dma=True, semkey="ident")
        S_.op("sp", lambda e: e.dma_start(out=modc[:], in_=MODC), writes=["modc"], dma=True, semkey="modc")
        S_.op("sp", lambda e: e.dma_start(out=g1[:], in_=GROW[0]), writes=["g1"], dma=True, semkey="g1")
        S_.op("pool", lambda e: e.dma_start(out=wo[:], in_=w_out.rearrange("(k p) f -> p k f", p=128)), writes=["wo"],
              dma=True, semkey="wo")
        S_.op("dve", lambda e: e.memset(epsT[:], EPS), writes=["eps"])
        for k in range(8):
            S_.op("dve", lambda e, k=k: e.tensor_tensor(out=wo[:, k, :], in0=wo[:, k, :], in1=g1[:], op=ALU.mult),
                  reads=["wo", "g1"], writes=["wo"])
        xt = [P.sb("xt%d" % i, [128, 4, D], F32) for i in range(2)]
        mt = [P.sb("mt%d" % i, [128, 8, 512], BF16) for i in range(2)]
        junk = P.sb("junk", [128, D], BF16)
        ss = P.sb("ss", [128, 4], F32)
        rstd = P.sb("rstd", [128, 4], F32)
        xs = P.sb("xs", [128, 4, D], BF16)
        hst = [P.sb("hst%d" % i, [128, 8, 512], BF16) for i in range(2)]
        xv = x.rearrange("(g j p) d -> g p j d", j=4, p=128)
        x1v = X1.rearrange("(g j p) d -> g p j d", j=4, p=128)
        for g in range(NG):
            sl = g % 2
            S_.op("sp", lambda e, g=g, sl=sl: e.dma_start(out=xt[sl][:], in_=xv[g]), writes=[("xt", sl)], dma=True, semkey="xt%d" % sl)
            S_.op("sp", lambda e, g=g, sl=sl: e.dma_start(out=mt[sl][:], in_=MIXT.rearrange("(c p) s -> p c s", p=128)[:, :, g * 512:(g + 1) * 512]),
                  writes=[("mt", sl)], dma=True, semkey="mt%d" % sl)
            for j in range(4):
                for half in range(2):
                    b = P.bank()
                    for k in range(8):
                        S_.op("pe", lambda e, b=b, k=k, j=j, half=half, sl=sl: e.matmul(
                            P.pb[b][:], lhsT=mt[sl][:, k, j * 128:(j + 1) * 128], rhs=wo[:, k, half * 512:(half + 1) * 512],
                            start=(k == 0), stop=(k == 7)), reads=[("mt", sl), "wo"], writes=[("pb", b)])
                    S_.op("dve", lambda e, b=b, j=j, half=half, sl=sl: e.tensor_tensor(
                        out=xt[sl][:, j, half * 512:(half + 1) * 512], in0=P.pb[b][:], in1=xt[sl][:, j, half * 512:(half + 1) * 512],
                        op=ALU.add), reads=[("pb", b), ("xt", sl)], writes=[("xt", sl)])
            S_.op("sp", lambda e, g=g, sl=sl: e.dma_start(out=x1v[g], in_=xt[sl][:]), reads=[("xt", sl)], dma=True, semkey="xt%d" % sl)
            S_.op("pool", lambda e: e.memset(ss[:], 0.0), writes=["ss"])
            for j in range(4):
                S_.op("act", lambda e, j=j, sl=sl: e.activation(out=junk[:], in_=xt[sl][:, j, :], func=AF.Square, accum_out=ss[:, j:j + 1]),
                      reads=[("xt", sl), "ss"], writes=["ss", "junk"])
            S_.op("act", lambda e: e.activation(out=rstd[:], in_=ss[:], func=AF.Sqrt, bias=epsT[:], scale=1.0 / D),
                  reads=["ss", "eps"], writes=["rstd"])
            S_.op("dve", lambda e: e.reciprocal(out=rstd[:], in_=rstd[:]), reads=["rstd"], writes=["rstd"])
            for j in range(4):
                S_.op("dve", lambda e, j=j, sl=sl: e.tensor_scalar(out=xs[:, j, :], in0=xt[sl][:, j, :], scalar1=rstd[:, j:j + 1],
                                                                 scalar2=None, op0=ALU.mult),
                      reads=[("xt", sl), "rstd"], writes=[("xs", j)])
            for c2 in range(4):
                b = P.bank()
                pbf = P.pb[b][:].bitcast(BF16)
                for cc in range(2):
                    c = c2 * 2 + cc
                    for j in range(4):
                        S_.op("pe", lambda e, pbf=pbf, cc=cc, c=c, j=j: e.transpose(
                            pbf[:, cc * 512 + j * 128: cc * 512 + (j + 1) * 128], xs[:, j, c * 128:(c + 1) * 128], ident[:]),
                            reads=[("xs", j), "ident"], writes=[("pb", b)])
                for cc in range(2):
                    c = c2 * 2 + cc
                    S_.op("act", lambda e, pbf=pbf, cc=cc, c=c, sl=sl: e.activation(
                        out=hst[sl][:, c, :], in_=pbf[:, cc * 512:(cc + 1) * 512], func=AF.Identity,
                        bias=modc[:, 16 + c:17 + c], scale=modc[:, 24 + c:25 + c]),
                        reads=[("pb", b), "modc"], writes=[("hst", sl)])
            S_.op("sp", lambda e, g=g, sl=sl: e.dma_start(out=H2T.rearrange("(c p) s -> p c s", p=128)[:, :, g * 512:(g + 1) * 512],
                                                        in_=hst[sl][:]), reads=[("hst", sl)], dma=True, semkey="hst%d" % sl)
        P.finish()


def _phase4b(nc, semstack, S, X1, H2T, w_gate, w_up, w_down, GROW, fg_row, out):
    G = 512
    NJ = G // 128
    NG = S // G
    NF = DFF // 128
    with Phase(nc, semstack, "p4b") as P:
        S_ = P.S
        epsT = P.sb("eps", [128, 1], F32)
        fg = P.sb("fg", [128, D], F32)
        wg = P.sb("wg", [128, 8, DFF], BF16)
        wu = P.sb("wu", [128, 8, DFF], BF16)
        wd = P.sb("wd", [128, NF, D], BF16)
        S_.op("dve", lambda e: e.memset(epsT[:], EPS), writes=["eps"])
        S_.op("sp", lambda e: e.dma_start(out=fg[:], in_=fg_row.partition_broadcast(128)), writes=["fg"], dma=True, semkey="fg")
        for k in range(8):
            S_.op("pool", lambda e, k=k: e.dma_start(out=wg[:, k, :], in_=w_gate[k * 128:(k + 1) * 128, :]), writes=["wg"], dma=True, semkey="wg")
            S_.op("pool", lambda e, k=k: e.dma_start(out=wu[:, k, :], in_=w_up[k * 128:(k + 1) * 128, :]), writes=["wu"], dma=True, semkey="wu")
        xq = [P.sb("xq%d" % i, [128, NJ, D], F32) for i in range(2)]
        g2 = xq[1]
        S_.op("sp", lambda e: e.dma_start(out=g2[:, 0, :], in_=GROW[1]), writes=[("xq", 1)], dma=True, semkey="xq1")
        S_.op("pool", lambda e: e.dma_start(out=wd[:], in_=w_down.rearrange("(k p) f -> p k f", p=128)), writes=["wd"], dma=True, semkey="wd")
        for k in range(NF):
            S_.op("dve", lambda e, k=k: e.tensor_tensor(out=wd[:, k, :], in0=wd[:, k, :], in1=g2[:, 0, :], op=ALU.mult),
                  reads=["wd", ("xq", 1)], writes=["wd"])
        ht = [P.sb("ht%d" % i, [128, 8, G], BF16) for i in range(2)]
        aT = P.sb("aT", [128, NF, G], BF16)
        ss = P.sb("ss", [128, NJ], F32)
        rstd = P.sb("rstd", [128, NJ], F32)
        x1v = X1.rearrange("(g j p) d -> g p j d", j=NJ, p=128)
        ov = out.rearrange("(g j p) d -> g p j d", j=NJ, p=128)
        for g in range(NG):
            sl = g % 2
            S_.op("sp", lambda e, g=g, sl=sl: e.dma_start(out=xq[sl][:], in_=x1v[g]), writes=[("xq", sl)], dma=True, semkey="xq%d" % sl)
            S_.op("sp", lambda e, g=g, sl=sl: e.dma_start(out=ht[sl][:], in_=H2T.rearrange("(c p) s -> p c s", p=128)[:, :, g * G:(g + 1) * G]),
                  writes=[("ht", sl)], dma=True, semkey="ht%d" % sl)
            for fc in range(NF):
                ba, bb = P.bank(), P.bank()
                for (bk, w, wk) in ((ba, wg, "wg"), (bb, wu, "wu")):
                    for k in range(8):
                        S_.op("pe", lambda e, bk=bk, w=w, k=k, fc=fc, sl=sl: e.matmul(
                            P.pb[bk][:, 0:G], lhsT=w[:, k, fc * 128:(fc + 1) * 128], rhs=ht[sl][:, k, :], start=(k == 0), stop=(k == 7)),
                            reads=[wk, ("ht", sl)], writes=[("pb", bk)])
                S_.op("act", lambda e, ba=ba, fc=fc: e.activation(out=aT[:, fc, :], in_=P.pb[ba][:, 0:G], func=AF.Silu),
                      reads=[("pb", ba)], writes=[("aT", fc)])
                S_.op("dve", lambda e, bb=bb, fc=fc: e.tensor_tensor(out=aT[:, fc, :], in0=P.pb[bb][:, 0:G], in1=aT[:, fc, :], op=ALU.mult),
                      reads=[("pb", bb), ("aT", fc)], writes=[("aT", fc)])
            for j in range(NJ):
                for half in range(2):
                    b = P.bank()
                    for k in range(NF):
                        S_.op("pe", lambda e, b=b, k=k, j=j, half=half: e.matmul(
                            P.pb[b][:], lhsT=aT[:, k, j * 128:(j + 1) * 128], rhs=wd[:, k, half * 512:(half + 1) * 512],
                            start=(k == 0), stop=(k == NF - 1)), reads=[("aT", k), "wd"], writes=[("pb", b)])
                    S_.op("dve", lambda e, b=b, j=j, half=half, sl=sl: e.tensor_tensor(
                        out=xq[sl][:, j, half * 512:(half + 1) * 512], in0=P.pb[b][:], in1=xq[sl][:, j, half * 512:(half + 1) * 512],
                        op=ALU.add), reads=[("pb", b), ("xq", sl)], writes=[("xq", sl)])
            S_.op("pool", lambda e: e.memset(ss[:], 0.0), writes=["ss"])
            for j in range(NJ):
                S_.op("act", lambda e, j=j, sl=sl: e.activation(out=aT[:, 0:2, :].rearrange("p a c -> p (a c)"), in_=xq[sl][:, j, :], func=AF.Square, accum_out=ss[:, j:j + 1]),
                      reads=[("xq", sl), "ss"], writes=["ss", ("aT", 0), ("aT", 1)])
            S_.op("act", lambda e: e.activation(out=rstd[:], in_=ss[:], func=AF.Sqrt, bias=epsT[:], scale=1.0 / D),
                  reads=["ss", "eps"], writes=["rstd"])
            S_.op("dve", lambda e: e.reciprocal(out=rstd[:], in_=rstd[:]), reads=["rstd"], writes=["rstd"])
            for j in range(NJ):
                S_.op("dve", lambda e, j=j, sl=sl: e.scalar_tensor_tensor(
                    out=xq[sl][:, j, :], in0=xq[sl][:, j, :], scalar=rstd[:, j:j + 1], in1=fg[:], op0=ALU.mult, op1=ALU.mult),
                    reads=[("xq", sl), "rstd", "fg"], writes=[("xq", sl)])
            S_.op("sp", lambda e, g=g, sl=sl: e.dma_start(out=ov[g], in_=xq[sl][:]), reads=[("xq", sl)], dma=True, semkey="xq%d" % sl)
        P.finish()


def _phase3(nc, semstack, S, QD, KD, VD, ZS, BA, MIXT, cst):
    NG = S // 512
    with Phase(nc, semstack, "p3") as P:
        S_ = P.S
        sb = P.sb
        ident = sb("ident", [128, 128], BF16)
        identf = sb("identf", [128, 128], F32)
        trif = sb("trif", [128, 128], F32)
        bonesf = sb("bonesf", [128, 128], F32)
        onesf = sb("onesf", [128, 128], F32)
        maskL = sb("maskL", [128, 128], F32)
        maskU = sb("maskU", [128, 128], F32)
        negA = sb("negA", [128, 8], F32)
        dtb = sb("dtb", [128, 8], F32)
        dng = sb("dng", [128, 64], F32)
        one1 = sb("one1", [128, 1], F32)
        epsT = sb("eps", [128, 1], F32)
        S_.op("pool", lambda e: e.dma_start(out=ident[:], in_=cst["ident"]), writes=["ident"], dma=True, semkey="ident")
        for nm, t in (("identf", identf), ("trif", trif), ("bonesf", bonesf), ("maskL", maskL), ("maskU", maskU)):
            S_.op("sp", lambda e, nm=nm, t=t: e.dma_start(out=t[:], in_=cst[nm]), writes=[nm], dma=True, semkey=nm)
        S_.op("sp", lambda e: e.dma_start(out=negA[:], in_=cst["alog"].partition_broadcast(128)), writes=["negA"], dma=True, semkey="negA")
        S_.op("sp", lambda e: e.dma_start(out=dtb[:], in_=cst["dtb"].partition_broadcast(128)), writes=["dtb"], dma=True, semkey="dtb")
        S_.op("sp", lambda e: e.dma_start(out=dng[:], in_=cst["dng"].partition_broadcast(128)), writes=["dng"], dma=True, semkey="dng")
        S_.op("pool", lambda e: e.memset(onesf[:], 1.0), writes=["onesf"])
        S_.op("pool", lambda e: e.memset(one1[:], 1.0), writes=["one1"])
        S_.op("pool", lambda e: e.memset(epsT[:], EPS), writes=["eps"])
        S_.op("act", lambda e: e.activation(out=negA[:], in_=negA[:], func=AF.Exp), reads=["negA"], writes=["negA"])
        S_.op("dve", lambda e: e.tensor_scalar(out=negA[:], in0=negA[:], scalar1=-1.0, scalar2=None, op0=ALU.mult),
              reads=["negA"], writes=["negA"])
        kd = [sb("kd%d" % i, [128, 4, 512], BF16) for i in range(2)]
        qd = [sb("qd%d" % i, [64, 8, 512], BF16) for i in range(2)]
        kd8 = [sb("kd8_%d" % i, [64, 8, 512], BF16) for i in range(2)]
        vd = [sb("vd%d" % i, [128, 4, 512], BF16) for i in range(2)]
        zs = [sb("zs%d" % i, [128, 4, 512], BF16) for i in range(2)]
        ba = [sb("ba%d" % i, [128, 4, 16], F32) for i in range(2)]
        mst = [sb("mst%d" % i, [128, 4, 512], BF16) for i in range(2)]
        y16 = sb("y16", [128, 16], F32)
        u16 = sb("u16", [128, 16], F32)
        beta = sb("beta", [128, 8], F32)
        gg = sb("gg", [128, 8], F32)
        gcl = sb("gcl", [128, 16], F32)
        eg2 = [sb("eg%d" % i, [128, 8], F32) for i in range(2)]
        kdsc = sb("kdsc", [128, 8], F32)
        be = sb("be", [128, 8], F32)
        offL = sb("offL", [128, 8], F32)
        decS2 = [sb("decS%d" % i, [64, 16], F32) for i in range(2)]
        kbg = sb("kbg", [128, 512], BF16)
        kdec2 = [sb("kdec%d" % i, [128, 512], BF16) for i in range(2)]
        bv = sb("bv", [128, 512], BF16)
        DG = sb("DG", [128, 8, 128], F32)
        tL = sb("tL", [128, 8, 128], F32)
        tU = sb("tU", [128, 8, 128], F32)
        Lb = sb("Lb", [128, 8, 128], F32)
        TTb = sb("TTb", [128, 8, 128], BF16)
        Ui = sb("Ui", [128, 8, 128], BF16)
        qkdT2 = [sb("qkdT%d" % i, [128, 8, 128], BF16) for i in range(2)]
        X = [sb("X%d" % i, [128, 8, 128], F32) for i in range(2)]
        Y = [sb("Y%d" % i, [128, 8, 128], F32) for i in range(2)]
        Q = [sb("Q%d" % i, [128, 8, 128], F32) for i in range(2)]
        uu2 = [sb("uu%d" % i, [128, 512], F32) for i in range(2)]
        wT2 = [sb("wT%d" % i, [64, 8, 128], BF16) for i in range(2)]
        vnew = sb("vnew", [128, 512], BF16)
        o1s = sb("o1s", [128, 512], F32)
        otm = sb("otm", [128, 512], F32)
        sqo = sb("sqo", [128, 512], F32)
        ssq = sb("ssq", [128, 8], F32)
        onb = sb("onb", [128, 512], BF16)
        Sf = sb("Sf", [64, 512], F32)
        S1 = sb("S1", [64, 512], F32)
        Sbf = sb("Sbf", [64, 512], BF16)
        S_.op("pool", lambda e: e.memset(Sf[:], 0.0), writes=["Sf"])
        S_.op("pool", lambda e: e.memset(Sbf[:], 0.0), writes=["Sbf"])

        def bc8(t):
            return lambda n: t.unsqueeze(2).to_broadcast([128, 8, n])

        bcnt = [0, 0]

        def bankA():
            bcnt[0] += 1
            return (bcnt[0] - 1) % 5

        def bankB():
            bcnt[1] += 1
            return 5 + (bcnt[1] - 1) % 3

        def A_tile(g, sl, j, c0, par):
            uu, wT, kdec, qkdT, eg, decS = uu2[par], wT2[par], kdec2[par], qkdT2[par], eg2[par], decS2[par]
            K_uu = ("uu", par)
            K_wT = ("wT", par)
            K_kdec = ("kdec", par)
            K_qkdT = ("qkdT", par)
            K_eg = ("eg", par)
            K_decS = ("decS", par)
            if j == 0:
                for nm, t, src, pp in (("kd", kd, KD, 128), ("qd", qd, QD, 64), ("kd8", kd8, KD, 64), ("vd", vd, VD, 128), ("zs", zs, ZS, 128)):
                    S_.op("sp", lambda e, t=t, src=src, g=g, sl=sl, pp=pp: e.dma_start(
                        out=t[sl][:], in_=src.rearrange("(c p) s -> p c s", p=pp)[:, :, g * 512:(g + 1) * 512]),
                        writes=[(nm, sl)], dma=True, semkey="%s%d" % (nm, sl))
                S_.op("sp", lambda e, g=g, sl=sl: e.dma_start(out=ba[sl][:], in_=BA.rearrange("(g j p) f -> g p j f", j=4, p=128)[g]),
                      writes=[("ba", sl)], dma=True, semkey="ba%d" % sl)
            S_.op("dve", lambda e, j=j, sl=sl: e.tensor_scalar(out=y16[:, 0:8], in0=ba[sl][:, j, 0:8], scalar1=-1.0, scalar2=None, op0=ALU.mult),
                  reads=[("ba", sl)], writes=["y16a"])
            yield
            S_.op("dve", lambda e, j=j, sl=sl: e.tensor_tensor(out=y16[:, 8:16], in0=ba[sl][:, j, 8:16], in1=dtb[:], op=ALU.add),
                  reads=[("ba", sl), "dtb"], writes=["y16b"])
            yield
            S_.op("act", lambda e: e.activation(out=u16[:], in_=y16[:], func=AF.Exp), reads=["y16a", "y16b"], writes=["u16"])
            yield
            S_.op("act", lambda e: e.activation(out=u16[:], in_=u16[:], func=AF.Ln, bias=one1[:], scale=1.0), reads=["u16", "one1"], writes=["u16"])
            yield
            S_.op("act", lambda e: e.activation(out=beta[:], in_=u16[:, 0:8], func=AF.Exp, scale=-1.0), reads=["u16"], writes=["beta"])
            yield
            S_.op("dve", lambda e: e.tensor_tensor(out=gg[:], in0=u16[:, 8:16], in1=negA[:], op=ALU.mult), reads=["u16", "negA"], writes=["gg"])
            yield
            b = bankA()
            yield
            S_.op("pe", lambda e, b=b: e.matmul(P.pb[b][:, 0:8], lhsT=trif[:], rhs=gg[:], start=True, stop=True),
                  reads=["trif", "gg"], writes=[("pb", b)])
            yield
            S_.op("pe", lambda e, b=b: e.matmul(P.pb[b][:, 8:16], lhsT=bonesf[:], rhs=gg[:], start=True, stop=True),
                  reads=["bonesf", "gg"], writes=[("pb", b)])
            yield
            for ch in range(2):
                S_.op("pe", lambda e, b=b, ch=ch: e.matmul(
                    P.pb[b][0:64, 16 + ch * 8:24 + ch * 8], lhsT=bonesf[:, ch * 64:(ch + 1) * 64],
                    rhs=gg[:], start=True, stop=True), reads=["bonesf", "gg"], writes=[("pb", b)])
            yield
            S_.op("dve", lambda e, b=b: e.tensor_copy(out=gcl[:], in_=P.pb[b][:, 0:16]), reads=[("pb", b)], writes=["gcl"])
            yield
            S_.op("act", lambda e, b=b: e.activation(out=decS[:], in_=P.pb[b][0:64, 16:32], func=AF.Exp), reads=[("pb", b)], writes=[K_decS])
            yield
            S_.op("act", lambda e: e.activation(out=eg[:], in_=gcl[:, 0:8], func=AF.Exp), reads=["gcl"], writes=[K_eg])
            yield
            S_.op("dve", lambda e: e.tensor_tensor(out=kdsc[:], in0=gcl[:, 8:16], in1=gcl[:, 0:8], op=ALU.subtract), reads=["gcl"], writes=["kdsc"])
            yield
            S_.op("act", lambda e: e.activation(out=kdsc[:], in_=kdsc[:], func=AF.Exp), reads=["kdsc"], writes=["kdsc"])
            yield
            S_.op("dve", lambda e: e.tensor_tensor(out=be[:], in0=beta[:], in1=eg[:], op=ALU.mult), reads=["beta", K_eg], writes=["be"])
            yield
            S_.op("dve", lambda e: e.tensor_tensor(out=offL[:], in0=gcl[:, 0:8], in1=u16[:, 0:8], op=ALU.subtract), reads=["gcl", "u16"], writes=["offL"])
            yield
            bk_ = bankA()
            yield
            bv_ = bk_
            yield
            for (off, src, key) in ((0, kd, "kd"), (512, vd, "vd")):
                pbf = P.pb[bk_][:].bitcast(BF16)
                for hp in range(4):
                    S_.op("pe", lambda e, pbf=pbf, hp=hp, src=src, sl=sl, c0=c0, off=off: e.transpose(
                        pbf[:, off + hp * 128:off + (hp + 1) * 128], src[sl][:, hp, c0:c0 + 128], ident[:]),
                        reads=[(key, sl), "ident"], writes=[("pb", bk_)])
            yield
            pk = P.pb[bk_][:].bitcast(BF16)[:, 0:512].rearrange("p (h d) -> p h d", h=8)
            yield
            pv = P.pb[bv_][:].bitcast(BF16)[:, 512:1024].rearrange("p (h d) -> p h d", h=8)
            yield
            S_.op("dve", lambda e, pk=pk: e.tensor_tensor(out=kbg[:].rearrange("p (h d) -> p h d", h=8), in0=pk, in1=bc8(be[:])(64), op=ALU.mult),
                  reads=[("pb", bk_), "be"], writes=["kbg"])
            yield
            S_.op("dve", lambda e, pk=pk: e.tensor_tensor(out=kdec[:].rearrange("p (h d) -> p h d", h=8), in0=pk, in1=bc8(kdsc[:])(64), op=ALU.mult),
                  reads=[("pb", bk_), "kdsc"], writes=[K_kdec])
            yield
            S_.op("dve", lambda e, pv=pv: e.tensor_tensor(out=bv[:].rearrange("p (h d) -> p h d", h=8), in0=pv, in1=bc8(beta[:])(64), op=ALU.mult),
                  reads=[("pb", bv_), "beta"], writes=["bv"])
            yield
            bKK = [bankA(), bankA()]
            yield
            bQK = [bankA(), bankA()]
            yield
            for h in range(8):
                hp, hl = h // 2, h % 2
                S_.op("pe", lambda e, h=h, sl=sl, c0=c0, bKK=bKK: e.matmul(
                    P.pb[bKK[h // 4]][:, (h % 4) * 128:(h % 4 + 1) * 128], lhsT=kd8[sl][:, h, c0:c0 + 128],
                    rhs=kd8[sl][:, h, c0:c0 + 128], start=True, stop=True),
                    reads=[("kd8", sl)], writes=[("pb", bKK[h // 4])])
                S_.op("pe", lambda e, h=h, sl=sl, c0=c0, bQK=bQK: e.matmul(
                    P.pb[bQK[h // 4]][:, (h % 4) * 128:(h % 4 + 1) * 128], lhsT=kd8[sl][:, h, c0:c0 + 128],
                    rhs=qd[sl][:, h, c0:c0 + 128], start=True, stop=True),
                    reads=[("kd8", sl), ("qd", sl)], writes=[("pb", bQK[h // 4])])
            yield
            S_.op("pool", lambda e: e.tensor_tensor(out=DG[:], in0=identf[:].unsqueeze(1).to_broadcast([128, 8, 128]),
                                                   in1=bc8(gcl[:, 0:8])(128), op=ALU.mult), reads=["identf", "gcl"], writes=["DG"])
            yield
            for hh in range(2):
                bG = bankA()
                hs = slice(hh * 4, hh * 4 + 4)
                S_.op("pe", lambda e, bG=bG, hs=hs: e.matmul(P.pb[bG][:], lhsT=onesf[:], rhs=DG[:, hs, :].rearrange("p h c -> p (h c)"),
                                                             start=True, stop=True), reads=["onesf", "DG"], writes=[("pb", bG)])
                pG = P.pb[bG][:].rearrange("p (h c) -> p h c", h=4)
                S_.op("dve", lambda e, pG=pG, hs=hs: e.scalar_tensor_tensor(
                    out=tL[:, hs, :], in0=pG, scalar=-1.0, in1=offL[:, hs].unsqueeze(2).to_broadcast([128, 4, 128]),
                    op0=ALU.mult, op1=ALU.add), reads=[("pb", bG), "offL"], writes=[("tL", hh)])
                S_.op("dve", lambda e, pG=pG, hs=hs: e.tensor_tensor(
                    out=tU[:, hs, :], in0=pG, in1=gcl[:, hs].unsqueeze(2).to_broadcast([128, 4, 128]), op=ALU.subtract),
                    reads=[("pb", bG), "gcl"], writes=[("tU", hh)])
                S_.op("pool", lambda e, hs=hs: e.tensor_tensor(out=tL[:, hs, :], in0=tL[:, hs, :],
                                                               in1=maskL[:].unsqueeze(1).to_broadcast([128, 4, 128]), op=ALU.add),
                      reads=[("tL", hh), "maskL"], writes=[("tL", hh)])
                S_.op("pool", lambda e, hs=hs: e.tensor_tensor(out=tU[:, hs, :], in0=tU[:, hs, :],
                                                               in1=maskU[:].unsqueeze(1).to_broadcast([128, 4, 128]), op=ALU.add),
                      reads=[("tU", hh), "maskU"], writes=[("tU", hh)])
                S_.op("act", lambda e, hs=hs: e.activation(out=Lb[:, hs, :], in_=tL[:, hs, :], func=AF.Exp), reads=[("tL", hh)], writes=[("Lb", hh)])
                S_.op("act", lambda e, hs=hs: e.activation(out=Ui[:, hs, :], in_=tU[:, hs, :], func=AF.Exp), reads=[("tU", hh)], writes=[("Ui", hh)])
                S_.op("dve", lambda e, hh=hh, hs=hs, bKK=bKK: e.scalar_tensor_tensor(
                    out=X[0][:, hs, :], in0=P.pb[bKK[hh]][:].rearrange("p (h c) -> p h c", h=4), scalar=-1.0, in1=Lb[:, hs, :],
                    op0=ALU.mult, op1=ALU.mult), reads=[("pb", bKK[hh]), ("Lb", hh)], writes=[("X", 0)])
                S_.op("dve", lambda e, hh=hh, hs=hs, bQK=bQK: e.tensor_tensor(
                    out=qkdT[:, hs, :], in0=P.pb[bQK[hh]][:].rearrange("p (h c) -> p h c", h=4), in1=Ui[:, hs, :], op=ALU.mult),
                    reads=[("pb", bQK[hh]), ("Ui", hh)], writes=[K_qkdT])
            yield
            bB = [bankA(), bankA()]
            yield
            for h in range(8):
                S_.op("pe", lambda e, bB=bB, h=h: e.transpose(P.pb[bB[h // 4]][:, (h % 4) * 128:(h % 4 + 1) * 128], X[0][:, h, :], identf[:]),
                      reads=[("X", 0), "identf"], writes=[("pb", bB[h // 4])])
            yield
            for hh in range(2):
                S_.op("act", lambda e, bB=bB, hh=hh: e.activation(out=Y[0][:, hh * 4:hh * 4 + 4, :].rearrange("p h c -> p (h c)"), in_=P.pb[bB[hh]][:],
                                                                 func=AF.Identity), reads=[("pb", bB[hh])], writes=[("Y", 0)])
            yield
            S_.op("pool", lambda e: e.tensor_tensor(out=Q[0][:], in0=Y[0][:], in1=identf[:].unsqueeze(1).to_broadcast([128, 8, 128]), op=ALU.add),
                  reads=[("Y", 0), "identf"], writes=[("Q", 0)])
            yield
            for lv in range(1, 6):
                a, n = (lv - 1) % 2, lv % 2
                bX = [bankA(), bankA()]
                for h in range(8):
                    S_.op("pe", lambda e, h=h, a=a, bX=bX: e.matmul(P.pb[bX[h // 4]][:, (h % 4) * 128:(h % 4 + 1) * 128],
                                                                 lhsT=Y[a][:, h, :], rhs=X[a][:, h, :], start=True, stop=True),
                          reads=[("X", a), ("Y", a)], writes=[("pb", bX[h // 4])])
                for hh in range(2):
                    S_.op("act", lambda e, hh=hh, n=n, bX=bX: e.activation(
                        out=X[n][:, hh * 4:hh * 4 + 4, :].rearrange("p h c -> p (h c)"), in_=P.pb[bX[hh]][:], func=AF.Identity),
                        reads=[("pb", bX[hh])], writes=[("X", n)])
                if lv < 5:
                    bY = [bankA(), bankA()]
                    for h in range(8):
                        S_.op("pe", lambda e, h=h, a=a, bY=bY: e.matmul(P.pb[bY[h // 4]][:, (h % 4) * 128:(h % 4 + 1) * 128],
                                                                     lhsT=X[a][:, h, :], rhs=Y[a][:, h, :], start=True, stop=True),
                              reads=[("X", a), ("Y", a)], writes=[("pb", bY[h // 4])])
                    for hh in range(2):
                        S_.op("dve", lambda e, hh=hh, n=n, bY=bY: e.tensor_copy(
                            out=Y[n][:, hh * 4:hh * 4 + 4, :].rearrange("p h c -> p (h c)"), in_=P.pb[bY[hh]][:]),
                            reads=[("pb", bY[hh])], writes=[("Y", n)])
                bQ = [bankA(), bankA()]
                for h in range(8):
                    S_.op("pe", lambda e, h=h, a=a, n=n, bQ=bQ: e.matmul(P.pb[bQ[h // 4]][:, (h % 4) * 128:(h % 4 + 1) * 128],
                                                                      lhsT=X[n][:, h, :], rhs=Q[a][:, h, :], start=True, stop=True),
                          reads=[("X", n), ("Q", a)], writes=[("pb", bQ[h // 4])])
                for hh in range(2):
                    S_.op("dve", lambda e, hh=hh, n=n, a=a, bQ=bQ: e.tensor_tensor(
                        out=Q[n][:, hh * 4:hh * 4 + 4, :].rearrange("p h c -> p (h c)"), in0=P.pb[bQ[hh]][:],
                        in1=Q[a][:, hh * 4:hh * 4 + 4, :].rearrange("p h c -> p (h c)"), op=ALU.add),
                        reads=[("pb", bQ[hh]), ("Q", a)], writes=[("Q", n)])
            yield
            S_.op("act", lambda e: e.activation(out=TTb[:].rearrange("p h c -> p (h c)"), in_=Q[1][:].rearrange("p h c -> p (h c)"), func=AF.Identity),
                  reads=[("Q", 1)], writes=["TTb"])
            yield
            TT = TTb
            yield
            bu = bankA()
            yield
            bw = [bankA(), bankA()]
            yield
            for h in range(8):
                S_.op("pe", lambda e, h=h, bu=bu: e.matmul(P.pb[bu][:, h * 64:(h + 1) * 64], lhsT=TT[:, h, :], rhs=bv[:, h * 64:(h + 1) * 64],
                                                         start=True, stop=True), reads=["TTb", "bv"], writes=[("pb", bu)])
                S_.op("pe", lambda e, h=h, bw=bw: e.matmul(
                    P.pb[bw[h // 4]][0:64, (h % 4) * 128:(h % 4 + 1) * 128], lhsT=kbg[:, h * 64:(h + 1) * 64], rhs=TT[:, h, :],
                    start=True, stop=True), reads=["TTb", "kbg"], writes=[("pb", bw[h // 4])])
            yield
            S_.op("act", lambda e, bu=bu: e.activation(out=uu[:], in_=P.pb[bu][:], func=AF.Identity), reads=[("pb", bu)], writes=[K_uu])
            yield
            for hh in range(2):
                S_.op("dve", lambda e, bw=bw, hh=hh: e.tensor_copy(out=wT[:, hh * 4:hh * 4 + 4, :].rearrange("p a c -> p (a c)"), in_=P.pb[bw[hh]][0:64, :]),
                      reads=[("pb", bw[hh])], writes=[K_wT])

        def B_tile(g, sl, j, c0, par):
            uu, wT, kdec, qkdT, eg, decS = uu2[par], wT2[par], kdec2[par], qkdT2[par], eg2[par], decS2[par]
            K_uu = ("uu", par)
            K_wT = ("wT", par)
            K_kdec = ("kdec", par)
            K_qkdT = ("qkdT", par)
            K_eg = ("eg", par)
            K_decS = ("decS", par)
            for ch in range(2):
                p0 = ch * 64
                bvn, bo1, bo2, bs = bankB(), bankB(), bankB(), bankB()
                for h in range(8):
                    S_.op("pe", lambda e, h=h, p0=p0, ch=ch, bvn=bvn: e.matmul(
                        P.pb[bvn][p0:p0 + 64, h * 64:(h + 1) * 64], lhsT=wT[:, h, ch * 64:(ch + 1) * 64],
                        rhs=Sbf[:, h * 64:(h + 1) * 64], start=True, stop=True, tile_position=(0, p0)),
                        reads=[K_wT, "Sbf"], writes=[("pb", bvn)])
                for h in range(8):
                    S_.op("pe", lambda e, h=h, p0=p0, ch=ch, bo1=bo1, sl=sl, c0=c0: e.matmul(
                        P.pb[bo1][p0:p0 + 64, h * 64:(h + 1) * 64], lhsT=qd[sl][:, h, c0 + p0:c0 + p0 + 64],
                        rhs=Sbf[:, h * 64:(h + 1) * 64], start=True, stop=True, tile_position=(0, p0)),
                        reads=[("qd", sl), "Sbf"], writes=[("pb", bo1)])
                S_.op("dve", lambda e, p0=p0, bvn=bvn: e.tensor_tensor(out=vnew[p0:p0 + 64, :], in0=uu[p0:p0 + 64, :], in1=P.pb[bvn][p0:p0 + 64, :],
                                                                    op=ALU.subtract), reads=[K_uu, ("pb", bvn)], writes=[("vnew", ch)])
                S_.op("dve", lambda e, p0=p0, bo1=bo1: e.tensor_tensor(
                    out=o1s[p0:p0 + 64, :].rearrange("p (h d) -> p h d", h=8), in0=P.pb[bo1][p0:p0 + 64, :].rearrange("p (h d) -> p h d", h=8),
                    in1=eg[p0:p0 + 64, :].unsqueeze(2).to_broadcast([64, 8, 64]), op=ALU.mult),
                    reads=[K_eg, ("pb", bo1)], writes=[("o1s", ch)])
                S_.op("pool", lambda e, ch=ch: e.tensor_tensor(
                    out=S1[:].rearrange("p (a d) -> p a d", a=8), in0=Sf[:].rearrange("p (a d) -> p a d", a=8),
                    in1=decS[:, ch * 8:ch * 8 + 8].unsqueeze(2).to_broadcast([64, 8, 64]), op=ALU.mult),
                    reads=["Sf", K_decS], writes=["S1"])
                for h in range(8):
                    S_.op("pe", lambda e, h=h, p0=p0, ch=ch, bs=bs: e.matmul(
                        P.pb[bs][0:64, h * 64:(h + 1) * 64], lhsT=kdec[p0:p0 + 64, h * 64:(h + 1) * 64],
                        rhs=vnew[p0:p0 + 64, h * 64:(h + 1) * 64], start=True, stop=True, tile_position=(p0, 0)),
                        reads=[K_kdec, ("vnew", ch)], writes=[("pb", bs)])
                for h in range(8):
                    S_.op("pe", lambda e, h=h, p0=p0, ch=ch, bo2=bo2: e.matmul(
                        P.pb[bo2][p0:p0 + 64, h * 64:(h + 1) * 64], lhsT=qkdT[p0:p0 + 64, h, ch * 64:(ch + 1) * 64],
                        rhs=vnew[p0:p0 + 64, h * 64:(h + 1) * 64], start=True, stop=True, tile_position=(p0, p0)),
                        reads=[K_qkdT, ("vnew", ch)], writes=[("pb", bo2)])
                S_.op("dve", lambda e, bs=bs: e.tensor_tensor(out=Sbf[:], in0=S1[:], in1=P.pb[bs][0:64, :], op=ALU.add),
                      reads=["S1", ("pb", bs)], writes=["Sbf"])
                S_.op("dve", lambda e, bs=bs: e.tensor_tensor(out=Sf[:], in0=S1[:], in1=P.pb[bs][0:64, :], op=ALU.add),
                      reads=["S1", ("pb", bs)], writes=["Sf"])
                S_.op("dve", lambda e, p0=p0, bo2=bo2: e.tensor_tensor(out=otm[p0:p0 + 64, :], in0=o1s[p0:p0 + 64, :], in1=P.pb[bo2][p0:p0 + 64, :],
                                                                    op=ALU.add), reads=[("o1s", ch), ("pb", bo2)], writes=[("otm", ch)])
            yield
            OT = [("otm", 0), ("otm", 1)]
            yield
            S_.op("pool", lambda e: e.tensor_tensor(out=sqo[:], in0=otm[:], in1=otm[:], op=ALU.mult), reads=OT, writes=["sqo"])
            yield
            S_.op("dve", lambda e: e.tensor_reduce(out=ssq[:], in_=sqo[:].rearrange("p (h d) -> p h d", h=8), axis=mybir.AxisListType.X, op=ALU.add),
                  reads=["sqo"], writes=["ssq"])
            yield
            S_.op("act", lambda e: e.activation(out=ssq[:], in_=ssq[:], func=AF.Sqrt, bias=epsT[:], scale=1.0 / 64), reads=["ssq", "eps"], writes=["ssq"])
            yield
            S_.op("dve", lambda e: e.reciprocal(out=ssq[:], in_=ssq[:]), reads=["ssq"], writes=["ssq"])
            yield
            S_.op("dve", lambda e: e.tensor_tensor(out=sqo[:].rearrange("p (h d) -> p h d", h=8), in0=otm[:].rearrange("p (h d) -> p h d", h=8),
                                                  in1=bc8(ssq[:])(64), op=ALU.mult), reads=OT + ["ssq"], writes=["sqo"])
            yield
            S_.op("pool", lambda e: e.tensor_tensor(out=onb[:].rearrange("p (h d) -> p h d", h=8), in0=sqo[:].rearrange("p (h d) -> p h d", h=8),
                                                   in1=dng[:].unsqueeze(1).to_broadcast([128, 8, 64]), op=ALU.mult), reads=["sqo", "dng"], writes=["onb"])
            yield
            bo = bankB()
            yield
            pO = P.pb[bo][:].bitcast(BF16)
            yield
            for hp in range(8):
                S_.op("pe", lambda e, pO=pO, hp=hp: e.transpose(pO[:, hp * 128:(hp + 1) * 128], onb[:, (hp % 4) * 128:(hp % 4 + 1) * 128], ident[:]),
                      reads=["onb", "ident"], writes=[("pb", bo)])
            yield
            S_.op("dve", lambda e, pO=pO, sl=sl, c0=c0: e.tensor_tensor(
                out=mst[sl][:, :, c0:c0 + 128], in0=pO[:, 0:512].rearrange("p (a c) -> p a c", a=4), in1=zs[sl][:, :, c0:c0 + 128], op=ALU.mult),
                reads=[("pb", bo), ("zs", sl)], writes=[("mst", sl)])
            if j == 3:
                S_.op("sp", lambda e, g=g, sl=sl: e.dma_start(out=MIXT.rearrange("(c p) s -> p c s", p=128)[:, 4:8, g * 512:(g + 1) * 512], in_=mst[sl][:]),
                      reads=[("mst", sl)], dma=True, semkey="mst%d" % sl)

        tiles = [(g, g % 2, j, j * 128, (g * 4 + j) % 2) for g in range(NG) for j in range(4)]
        for _ in A_tile(*tiles[0]):
            pass
        for ti, tl in enumerate(tiles):
            gb = B_tile(*tl)
            ga = A_tile(*tiles[ti + 1]) if ti + 1 < len(tiles) else iter(())
            da = db = False
            while not (da and db):
                for _ in range(3):
                    if not da:
                        try:
                            next(ga)
                        except StopIteration:
                            da = True
                if not db:
                    try:
                        next(gb)
                    except StopIteration:
                        db = True
        P.finish()


def _host_all(inputs, b, S):
    f = np.float32
    col = lambda v: np.ascontiguousarray(np.asarray(v, f).reshape(-1, 128).T)
    d = {}
    d["x"] = np.ascontiguousarray(np.asarray(inputs["x"][b, :S], f))
    d["c_col"] = col(inputs["c"][b])
    d["w_ada"] = np.asarray(inputs["w_ada"][0], f)
    d["bada_col"] = col(inputs["b_ada"][0])
    d["bada_row"] = np.ascontiguousarray(np.asarray(inputs["b_ada"][0], f).reshape(6, 1024))
    d["gattn_col"] = col(inputs["norm_attn_g"][0])
    d["gffn_col"] = col(inputs["norm_ffn_g"][0])
    d["w_in"] = np.asarray(inputs["w_in"][0], f)
    cw = np.asarray(inputs["conv_w"][0], f)
    d["cw_col"] = np.ascontiguousarray(cw.T.reshape(12, 128, 4).transpose(1, 0, 2).reshape(128, 48))
    d["ident"] = np.eye(128, dtype=f)
    bo = np.zeros((128, 128), f)
    bo[:64, :64] = 1
    bo[64:, 64:] = 1
    d["bones"] = bo
    d["identf"] = np.eye(128, dtype=f)
    d["bonesf"] = bo
    idx = np.arange(128)
    same = (idx[:, None] // 64) == (idx[None, :] // 64)
    d["trif"] = (same & (idx[:, None] <= idx[None, :])).astype(f)
    d["maskL"] = np.where(same & (idx[None, :] < idx[:, None]), 0.0, -30000.0).astype(f)
    d["maskU"] = np.where(same & (idx[None, :] >= idx[:, None]), 0.0, -30000.0).astype(f)
    d["alog_row"] = np.asarray(inputs["a_log"][0], f).reshape(1, 8)
    d["dtb_row"] = np.asarray(inputs["dt_bias"][0], f).reshape(1, 8)
    d["dng_row"] = np.asarray(inputs["delta_norm_g"][0], f).reshape(1, 64)
    d["w_out"] = np.asarray(inputs["w_out"][0], f)
    d["w_gate"] = np.asarray(inputs["w_gate"][0], f)
    d["w_up"] = np.asarray(inputs["w_up"][0], f)
    d["w_down"] = np.asarray(inputs["w_down"][0], f)
    d["fg_row"] = np.asarray(inputs["final_norm_g"], f).reshape(1, 1024)
    return d


def kernel(**inputs):
    S = inputs["x"].shape[1]
    B = inputs["x"].shape[0]
    nc, semstack = build_program(S, dbg=False)
    consts = host_consts(inputs)
    in_maps = []
    for b in range(B):
        d = _host_all(inputs, b, S)
        d.update(consts)
        in_maps.append(d)
    res = run_bass_kernel_spmd(nc, in_maps, core_ids=list(range(B)))
    return np.stack([np.asarray(r["out"], np.float32) for r in res.results], axis=0)
```

```python
from contextlib import ExitStack
import numpy as np
import concourse.bass as bass
import concourse.mybir as mybir
from concourse.bass_utils import run_bass_kernel_spmd

F32 = mybir.dt.float32
BF16 = mybir.dt.bfloat16
ALU = mybir.AluOpType
AF = mybir.ActivationFunctionType

ENGS = ("pe", "act", "dve", "pool", "sp")
EPOCH = 20000


class _Op:
    __slots__ = ("eng", "fn", "deps", "dma", "semkey", "idx", "needs_inc", "sem", "val")

    def __init__(self, eng, fn, dma, semkey):
        self.eng, self.fn, self.dma, self.semkey = eng, fn, dma, semkey
        self.deps = []
        self.needs_inc = False
        self.sem = None
        self.val = 0


class Sched:
    def __init__(self, nc):
        self.nc = nc
        self.ops = {e: [] for e in ENGS}
        self.last_w = {}
        self.readers = {}
        self.last_dma_on_sem = {}
        self.n = 0

    def op(self, eng, fn, reads=(), writes=(), dma=False, semkey=None):
        o = _Op(eng, fn, dma, semkey)
        o.idx = self.n
        self.n += 1
        deps = {}

        def add(p):
            if p is None or p is o:
                return
            if (not p.dma) and (not dma) and p.eng == "pe" and eng == "pe":
                return
            deps[id(p)] = p

        for k in reads:
            add(self.last_w.get(k))
        for k in writes:
            add(self.last_w.get(k))
            for r in self.readers.get(k, ()):
                add(r)
        if dma:
            assert semkey is not None
            add(self.last_dma_on_sem.get(semkey))
            self.last_dma_on_sem[semkey] = o
        o.deps = list(deps.values())
        for p in o.deps:
            p.needs_inc = True
        for k in reads:
            self.readers.setdefault(k, []).append(o)
        for k in writes:
            self.last_w[k] = o
            self.readers[k] = []
        self.ops[eng].append(o)
        return o

    def emit(self, stack, final_wait_ops=()):
        nc = self.nc
        for o in final_wait_ops:
            o.needs_inc = True
        sems = {}

        def getsem(name):
            if name not in sems:
                sems[name] = stack.enter_context(nc.semaphore(name))
            return sems[name]

        dma_cnt = {}
        for e in ENGS:
            cnt = 0
            for o in self.ops[e]:
                if o.dma:
                    c = dma_cnt.get(o.semkey, 0) + 1
                    dma_cnt[o.semkey] = c
                    o.sem = getsem("d_" + str(o.semkey))
                    o.val = 16 * c
                    o.needs_inc = True
                elif o.needs_inc:
                    ep, v = divmod(cnt, EPOCH)
                    o.sem = getsem("c_%s_%d" % (e, ep))
                    o.val = v + 1
                    cnt += 1
        self.nsems = len(sems)
        block = stack.enter_context(nc.Block())
        engmap = {"pe": block.tensor, "act": block.scalar, "dve": block.vector,
                  "pool": block.gpsimd, "sp": block.sync}
        for e in ENGS:
            ops = self.ops[e]
            fw = [o for o in final_wait_ops] if e == "sp" else []

            def body(engine, ops=ops, fw=fw):
                waited = {}
                for o in ops:
                    for p in o.deps:
                        key = id(p.sem)
                        if waited.get(key, 0) >= p.val:
                            continue
                        engine.wait_ge(p.sem, p.val)
                        waited[key] = p.val
                    ins = o.fn(engine)
                    if o.needs_inc:
                        ins.then_inc(o.sem, 16 if o.dma else 1)
                for p in fw:
                    key = id(p.sem)
                    if waited.get(key, 0) >= p.val:
                        continue
                    engine.wait_ge(p.sem, p.val)
                    waited[key] = p.val

            engmap[e](body)


D = 1024
INW = 3600
DFF = 2816
EPS = 1e-6


class Phase:
    def __init__(self, nc, semstack, name):
        self.nc, self.semstack, self.name = nc, semstack, name
        self.st = ExitStack()
        self.S = Sched(nc)
        self.nb = 0
        self.pb = None
        self.cnt = 0

    def __enter__(self):
        self.st.__enter__()
        self.pb = [self.st.enter_context(self.nc.psum_tensor("%s_pb%d" % (self.name, i), [128, 512], F32))
                   for i in range(8)]
        return self

    def sb(self, name, shape, dt):
        return self.st.enter_context(self.nc.sbuf_tensor(self.name + "_" + name, shape, dt))

    def bank(self):
        i = self.nb % 8
        self.nb += 1
        return i

    def finish(self):
        S = self.S
        fw = [o for e in ENGS for o in S.ops[e] if o.dma]
        for e in ("pe", "act", "dve", "pool"):
            if S.ops[e]:
                fw.append(S.ops[e][-1])
        nc = self.nc
        ph = self

        class _SemStack:
            def enter_context(self_inner, cm):
                return cm

        _emit(S, nc, self.semstack, self.st, fw, self.name)

    def __exit__(self, *a):
        r = self.st.__exit__(*a)
        return r


def _emit(S, nc, semstack, blockstack, final_wait_ops, pname):
    for o in final_wait_ops:
        o.needs_inc = True
    sems = {}

    def getsem(name):
        name = pname + "_" + "".join(ch if ch.isalnum() else "_" for ch in name)
        if name not in sems:
            sems[name] = blockstack.enter_context(nc.semaphore(name))
        return sems[name]

    dma_cnt = {}
    for e in ENGS:
        cnt = 0
        for o in S.ops[e]:
            if o.dma:
                c = dma_cnt.get(o.semkey, 0) + 1
                dma_cnt[o.semkey] = c
                o.sem = getsem("d_" + str(o.semkey))
                o.val = 16 * c
                o.needs_inc = True
            elif o.needs_inc:
                ep, v = divmod(cnt, EPOCH)
                o.sem = getsem("c_%s_%d" % (e, ep))
                o.val = v + 1
                cnt += 1
    S.nsems = len(sems)
    for sm in sems.values():
        nc.sync.sem_clear(sm)
    nc.all_engine_barrier()
    block = blockstack.enter_context(nc.Block())
    engmap = {"pe": block.tensor, "act": block.scalar, "dve": block.vector,
              "pool": block.gpsimd, "sp": block.sync}
    for e in ENGS:
        ops = S.ops[e]
        fw = list(final_wait_ops) if e == "sp" else []

        def body(engine, ops=ops, fw=fw):
            waited = {}
            for o in ops:
                for p in o.deps:
                    key = id(p.sem)
                    if waited.get(key, 0) >= p.val:
                        continue
                    engine.wait_ge(p.sem, p.val)
                    waited[key] = p.val
                ins = o.fn(engine)
                if o.needs_inc:
                    ins.then_inc(o.sem, 16 if o.dma else 1)
            for p in fw:
                key = id(p.sem)
                if waited.get(key, 0) >= p.val:
                    continue
                engine.wait_ge(p.sem, p.val)
                waited[key] = p.val

        engmap[e](body)


def _kw(**k):
    return k


def build_program(S, dbg=False, upto=9):
    nc = bass.Bass("TRN2", target_bir_lowering=False)
    NG = S // 512
    OUTK = "ExternalOutput" if dbg else "Internal"

    def din(name, shape, dt=F32):
        return nc.dram_tensor(name, shape, dt, kind="ExternalInput").ap()

    def dsc(name, shape, dt):
        return nc.dram_tensor(name, shape, dt, kind=OUTK).ap()

    x = din("x", [S, D])
    c_col = din("c_col", [128, 8])
    w_ada = din("w_ada", [D, 6 * D])
    bada_col = din("bada_col", [128, 48])
    bada_row = din("bada_row", [6, D])
    gattn_col = din("gattn_col", [128, 8])
    gffn_col = din("gffn_col", [128, 8])
    w_in = din("w_in", [D, INW])
    cw_col = din("cw_col", [128, 48])
    ident_in = din("ident", [128, 128])
    bones_in = din("bones", [128, 128])
    tb_in = din("tb", [128, 24 * 256])
    out = nc.dram_tensor("out", [S, D], F32, kind="ExternalOutput").ap()

    MODC = dsc("MODC", [128, 32], F32)
    GROW = dsc("GROW", [2, 128, D], F32)
    QT = dsc("QT", [512, S], BF16)
    KT = dsc("KT", [512, S], BF16)
    VV = dsc("VV", [S, 512], BF16)
    QD = dsc("QD", [512, S], BF16)
    KD = dsc("KD", [512, S], BF16)
    VD = dsc("VD", [512, S], BF16)
    ZS = dsc("ZS", [512, S], BF16)
    BA = dsc("BA", [S, 16], F32)
    MIXT = dsc("MIXT", [D, S], BF16)

    semstack = ExitStack()
    semstack.__enter__()

    with Phase(nc, semstack, "p0") as P:
        S_ = P.S
        ccol = P.sb("ccol", [128, 8], F32)
        sbf = P.sb("sbf", [128, 8], BF16)
        sbc = P.sb("sbc", [128, 8, 128], BF16)
        bcol = P.sb("bcol", [128, 48], F32)
        gcol = P.sb("gcol", [128, 16], F32)
        modc = P.sb("modc", [128, 32], F32)
        wa = [P.sb("wa%d" % i, [128, 8, D], BF16) for i in range(2)]
        brow = [P.sb("brow%d" % i, [128, D], F32) for i in range(2)]
        grow = [P.sb("grow%d" % i, [128, D], F32) for i in range(2)]
        S_.op("sp", lambda e: e.dma_start(out=ccol[:], in_=c_col), writes=["ccol"], dma=True, semkey="ccol")
        S_.op("sp", lambda e: e.dma_start(out=bcol[:], in_=bada_col), writes=["bcol"], dma=True, semkey="bcol")
        S_.op("sp", lambda e: e.dma_start(out=gcol[:, 0:8], in_=gattn_col), writes=["gcol"], dma=True, semkey="gcol")
        S_.op("sp", lambda e: e.dma_start(out=gcol[:, 8:16], in_=gffn_col), writes=["gcol"], dma=True, semkey="gcol")
        S_.op("act", lambda e: e.activation(out=sbf[:], in_=ccol[:], func=AF.Silu), reads=["ccol"], writes=["sbf"])
        S_.op("dve", lambda e: e.tensor_copy(out=sbc[:], in_=sbf[:].unsqueeze(2).to_broadcast([128, 8, 128])),
              reads=["sbf"], writes=["sbc"])
        wav = w_ada.rearrange("(k p) f -> p k f", p=128)
        colidx = {0: 0, 1: 1, 3: 2, 4: 3}
        for j in range(6):
            sl = j % 2
            S_.op("pool", lambda e, j=j, sl=sl: e.dma_start(out=wa[sl][:], in_=wav[:, :, j * D:(j + 1) * D]),
                  writes=[("wa", sl)], dma=True, semkey="wa%d" % sl)
            if j in colidx:
                jj = colidx[j]
                b = P.bank()
                for fcn in range(8):
                    for k in range(8):
                        S_.op("pe", lambda e, b=b, fcn=fcn, k=k, sl=sl: e.matmul(
                            P.pb[b][:, fcn:fcn + 1], lhsT=wa[sl][:, k, fcn * 128:(fcn + 1) * 128], rhs=sbf[:, k:k + 1],
                            start=(k == 0), stop=(k == 7)), reads=[("wa", sl), "sbf"], writes=[("pb", b)])
                S_.op("dve", lambda e, b=b, jj=jj, j=j: e.tensor_tensor(
                    out=modc[:, jj * 8:(jj + 1) * 8], in0=P.pb[b][:, 0:8], in1=bcol[:, j * 8:(j + 1) * 8], op=ALU.add),
                    reads=[("pb", b), "bcol"], writes=["modc"])
            else:
                gi = 0 if j == 2 else 1
                S_.op("sp", lambda e, j=j, gi=gi: e.dma_start(out=brow[gi][:], in_=bada_row[j:j + 1, :].partition_broadcast(128)),
                      writes=[("brow", gi)], dma=True, semkey="brow%d" % gi)
                for half in range(2):
                    b = P.bank()
                    for k in range(8):
                        S_.op("pe", lambda e, b=b, k=k, sl=sl, half=half: e.matmul(
                            P.pb[b][:], lhsT=sbc[:, k, :], rhs=wa[sl][:, k, half * 512:(half + 1) * 512],
                            start=(k == 0), stop=(k == 7)), reads=[("wa", sl), "sbc"], writes=[("pb", b)])
                    S_.op("dve", lambda e, b=b, gi=gi, half=half: e.tensor_tensor(
                        out=grow[gi][:, half * 512:(half + 1) * 512], in0=P.pb[b][:], in1=brow[gi][:, half * 512:(half + 1) * 512],
                        op=ALU.add), reads=[("pb", b), ("brow", gi)], writes=[("grow", gi)])
                S_.op("sp", lambda e, gi=gi: e.dma_start(out=GROW[gi], in_=grow[gi][:]), reads=[("grow", gi)],
                      dma=True, semkey="grow%d" % gi)
        for jj, go in ((1, 0), (3, 8)):
            S_.op("dve", lambda e, jj=jj, go=go: e.scalar_tensor_tensor(
                out=modc[:, jj * 8:(jj + 1) * 8], in0=modc[:, jj * 8:(jj + 1) * 8], scalar=1.0, in1=gcol[:, go:go + 8],
                op0=ALU.add, op1=ALU.mult), reads=["modc", "gcol"], writes=["modc"])
        S_.op("sp", lambda e: e.dma_start(out=MODC, in_=modc[:]), reads=["modc"], dma=True, semkey="modc")
        P.finish()
    nc.all_engine_barrier()
    if upto < 1:
        return nc, semstack

    with Phase(nc, semstack, "p1") as P:
        S_ = P.S
        ident = P.sb("ident", [128, 128], BF16)
        bones = P.sb("bones", [128, 128], BF16)
        modc = P.sb("modc", [128, 32], F32)
        cw = P.sb("cw", [128, 48], F32)
        epsT = P.sb("eps", [128, 1], F32)
        win = P.sb("win", [128, 8, INW], BF16)
        S_.op("pool", lambda e: e.dma_start(out=ident[:], in_=ident_in), writes=["ident"], dma=True, semkey="ident")
        S_.op("pool", lambda e: e.dma_start(out=bones[:], in_=bones_in), writes=["bones"], dma=True, semkey="bones")
        S_.op("sp", lambda e: e.dma_start(out=modc[:], in_=MODC), writes=["modc"], dma=True, semkey="modc")
        S_.op("sp", lambda e: e.dma_start(out=cw[:], in_=cw_col), writes=["cw"], dma=True, semkey="cw")
        S_.op("dve", lambda e: e.memset(epsT[:], EPS), writes=["eps"])
        winv = w_in.rearrange("(k p) f -> p k f", p=128)
        for k in range(8):
            S_.op("pool", lambda e, k=k: e.dma_start(out=win[:, k, :], in_=winv[:, k, :]), writes=[("win", k)],
                  dma=True, semkey="win%d" % k)
        WIN = [("win", k) for k in range(8)]
        xt = [P.sb("xt%d" % i, [128, 4, D], F32) for i in range(2)]
        junk = P.sb("junk", [128, D], BF16)
        ss2 = [P.sb("ss%d" % i, [128, 4], F32) for i in range(2)]
        rstd2 = [P.sb("rstd%d" % i, [128, 4], F32) for i in range(2)]
        xs2 = [P.sb("xs%d" % i, [128, 4, D], BF16) for i in range(2)]
        hT2 = [P.sb("hT%d" % i, [128, 8, 512], BF16) for i in range(2)]
        NSTQ = 6
        stq = [P.sb("stq%d" % i, [128, 4, 512], BF16) for i in range(NSTQ)]
        cin = P.sb("cin", [128, 12, 515], F32)
        NROT = 5
        acc3 = [P.sb("acc%d" % i, [128, 512], F32) for i in range(NROT)]
        slu3 = [P.sb("slu%d" % i, [128, 512], F32) for i in range(NROT)]
        sq3 = [P.sb("sq%d" % i, [128, 512], BF16) for i in range(NROT)]
        rs3 = [P.sb("rs%d" % i, [128, 512], F32) for i in range(NROT)]
        rot = [0]
        stb = P.sb("stb", [128, 4, 16], F32)
        xv = x.rearrange("(g j p) d -> g p j d", j=4, p=128)
        S_.op("pool", lambda e: e.memset(cin[:], 0.0), writes=["cin"])
        nst = [0]

        def stage():
            i = nst[0] % NSTQ
            nst[0] += 1
            return i

        NROT2 = NROT

        def norm_item(g):
            xs_ = g % 2
            ss, rstd, xs, hT = ss2[xs_], rstd2[xs_], xs2[xs_], hT2[xs_]
            KSS, KRS = ("ss", xs_), ("rstd", xs_)
            S_.op("sp", lambda e: e.dma_start(out=xt[xs_][:], in_=xv[g]), writes=[("xt", xs_)], dma=True, semkey="xt%d" % xs_)
            S_.op("pool", lambda e: e.memset(ss[:], 0.0), writes=[KSS])
            yield
            for j in range(4):
                S_.op("act", lambda e, j=j: e.activation(out=junk[:], in_=xt[xs_][:, j, :], func=AF.Square, accum_out=ss[:, j:j + 1]),
                      reads=[("xt", xs_), KSS], writes=[KSS, "junk"])
            S_.op("act", lambda e: e.activation(out=rstd[:], in_=ss[:], func=AF.Sqrt, bias=epsT[:], scale=1.0 / D),
                  reads=[KSS, "eps"], writes=[KRS])
            yield
            S_.op("dve", lambda e: e.reciprocal(out=rstd[:], in_=rstd[:]), reads=[KRS], writes=[KRS])
            for j in range(4):
                S_.op("dve", lambda e, j=j: e.tensor_scalar(out=xs[:, j, :], in0=xt[xs_][:, j, :], scalar1=rstd[:, j:j + 1], scalar2=None, op0=ALU.mult),
                      reads=[("xt", xs_), KRS], writes=[("xs", xs_, j)])
            yield
            pend = None
            for c2 in range(5):
                if c2 < 4:
                    b = P.bank()
                    pbf = P.pb[b][:].bitcast(BF16)
                    for cc in range(2):
                        c = c2 * 2 + cc
                        for j in range(4):
                            S_.op("pe", lambda e, pbf=pbf, cc=cc, c=c, j=j: e.transpose(
                                pbf[:, cc * 512 + j * 128: cc * 512 + (j + 1) * 128], xs[:, j, c * 128:(c + 1) * 128], ident[:]),
                                reads=[("xs", xs_, j), "ident"], writes=[("pb", b)])
                if pend is not None:
                    pb_, pbf_, pc2 = pend
                    for cc in range(2):
                        c = pc2 * 2 + cc
                        S_.op("act", lambda e, pbf_=pbf_, cc=cc, c=c: e.activation(
                            out=hT[:, c, :], in_=pbf_[:, cc * 512:(cc + 1) * 512], func=AF.Identity,
                            bias=modc[:, c:c + 1], scale=modc[:, 8 + c:9 + c]),
                            reads=[("pb", pb_), "modc"], writes=[("hT", xs_, c)])
                pend = (b, pbf, c2) if c2 < 4 else None
                yield

        def proj_mm(g, fc):
            xs_ = g % 2
            hT = hT2[xs_]
            b = P.bank()
            for k in range(8):
                S_.op("pe", lambda e, k=k: e.matmul(
                    P.pb[b][:], lhsT=win[:, k, fc * 128:(fc + 1) * 128], rhs=hT[:, k, :], start=(k == 0), stop=(k == 7)),
                    reads=[("win", k), ("hT", xs_, k)], writes=[("pb", b)])
            return b

        def store(dst, si, g):
            S_.op("sp", lambda e: e.dma_start(
                out=dst.rearrange("(c p) s -> p c s", p=128)[:, :, g * 512:(g + 1) * 512], in_=stq[si][:]),
                reads=[("stq", si, i) for i in range(4)], dma=True, semkey="stq%d" % si)

        def qk_item(g, base, dst, si, i):
            b = proj_mm(g, base + i)
            yield
            if i % 2 == 0:
                S_.op("act", lambda e: e.activation(out=stq[si][:, i, :], in_=P.pb[b][:], func=AF.Identity),
                      reads=[("pb", b)], writes=[("stq", si, i)])
            else:
                S_.op("dve", lambda e: e.tensor_copy(out=stq[si][:, i, :], in_=P.pb[b][:]),
                      reads=[("pb", b)], writes=[("stq", si, i)])
            if i == 3:
                store(dst, si, g)

        def v_item(g, si, j):
            xs_ = g % 2
            hT = hT2[xs_]
            b = P.bank()
            for k in range(8):
                S_.op("pe", lambda e, k=k: e.matmul(
                    P.pb[b][:], lhsT=hT[:, k, j * 128:(j + 1) * 128], rhs=win[:, k, 1024:1536], start=(k == 0), stop=(k == 7)),
                    reads=[("win", k), ("hT", xs_, k)], writes=[("pb", b)])
            yield
            S_.op("act", lambda e: e.activation(out=stq[si][:, j, :], in_=P.pb[b][:], func=AF.Identity),
                  reads=[("pb", b)], writes=[("stq", si, j)])
            if j == 3:
                S_.op("sp", lambda e: e.dma_start(out=VV.rearrange("(g j p) f -> g p j f", j=4, p=128)[g], in_=stq[si][:]),
                      reads=[("stq", si, i) for i in range(4)], dma=True, semkey="stq%d" % si)

        def z_item(g, si, i):
            b = proj_mm(g, 24 + i)
            yield
            S_.op("act", lambda e: e.activation(out=stq[si][:, i, :], in_=P.pb[b][:], func=AF.Silu),
                  reads=[("pb", b)], writes=[("stq", si, i)])
            if i == 3:
                store(ZS, si, g)

        def ba_item(g):
            xs_ = g % 2
            hT = hT2[xs_]
            b = P.bank()
            for j in range(4):
                for k in range(8):
                    S_.op("pe", lambda e, k=k, j=j: e.matmul(
                        P.pb[b][:, j * 16:(j + 1) * 16], lhsT=hT[:, k, j * 128:(j + 1) * 128], rhs=win[:, k, 3584:3600],
                        start=(k == 0), stop=(k == 7)), reads=[("win", k), ("hT", xs_, k)], writes=[("pb", b)])
            yield
            S_.op("dve", lambda e: e.tensor_copy(out=stb[:].rearrange("p j f -> p (j f)"), in_=P.pb[b][:, 0:64]),
                  reads=[("pb", b)], writes=["stb"])
            S_.op("sp", lambda e: e.dma_start(out=BA.rearrange("(g j p) f -> g p j f", j=4, p=128)[g], in_=stb[:]),
                  reads=["stb"], dma=True, semkey="stb")

        def delta_item(g, grp, dst, si, i):
            ci = grp * 4 + i
            b = proj_mm(g, 12 + ci)
            yield
            S_.op("act", lambda e: e.activation(out=cin[:, ci, 3:515], in_=P.pb[b][:], func=AF.Identity),
                  reads=[("pb", b)], writes=[("cin", ci)])
            yield
            ri = rot[0] % NROT2
            rot[0] += 1
            acc, slu, sq, rs = acc3[ri], slu3[ri], sq3[ri], rs3[ri]
            KA, KSL, KSQ, KR = ("acc", ri), ("slu", ri), ("sq", ri), ("rs", ri)
            S_.op("dve", lambda e: e.tensor_scalar(out=acc[:], in0=cin[:, ci, 0:512], scalar1=cw[:, ci * 4:ci * 4 + 1], scalar2=None, op0=ALU.mult),
                  reads=[("cin", ci), "cw"], writes=[KA])
            for t in range(1, 4):
                S_.op("dve", lambda e, t=t: e.scalar_tensor_tensor(
                    out=acc[:], in0=cin[:, ci, t:t + 512], scalar=cw[:, ci * 4 + t:ci * 4 + t + 1], in1=acc[:],
                    op0=ALU.mult, op1=ALU.add), reads=[("cin", ci), "cw", KA], writes=[KA])
            S_.op("pool", lambda e: e.tensor_copy(out=cin[:, ci, 0:3], in_=cin[:, ci, 512:515]), reads=[("cin", ci)], writes=[("cin", ci)])
            yield
            if grp == 2:
                S_.op("act", lambda e: e.activation(out=stq[si][:, i, :], in_=acc[:], func=AF.Silu), reads=[KA], writes=[("stq", si, i)])
                if i == 3:
                    store(dst, si, g)
                return
            S_.op("act", lambda e: e.activation(out=slu[:], in_=acc[:], func=AF.Silu), reads=[KA], writes=[KSL])
            S_.op("pool", lambda e: e.tensor_tensor(out=sq[:], in0=slu[:], in1=slu[:], op=ALU.mult), reads=[KSL], writes=[KSQ])
            yield
            b2 = P.bank()
            S_.op("pe", lambda e: e.matmul(P.pb[b2][:], lhsT=bones[:], rhs=sq[:], start=True, stop=True), reads=["bones", KSQ], writes=[("pb", b2)])
            yield
            S_.op("act", lambda e: e.activation(out=rs[:], in_=P.pb[b2][:], func=AF.Ln, bias=epsT[:], scale=1.0), reads=[("pb", b2), "eps"], writes=[KR])
            S_.op("act", lambda e: e.activation(out=rs[:], in_=rs[:], func=AF.Exp, scale=-0.5), reads=[KR], writes=[KR])
            yield
            scl = 0.125 if grp == 0 else 1.0
            S_.op("dve", lambda e: e.scalar_tensor_tensor(out=stq[si][:, i, :], in0=slu[:], scalar=scl, in1=rs[:], op0=ALU.mult, op1=ALU.mult),
                  reads=[KSL, KR], writes=[("stq", si, i)])
            if i == 3:
                store(dst, si, g)

        def p1_items():
            for g in range(NG):
                if g + 1 < NG:
                    yield norm_item(g + 1)
                si = stage()
                for i in range(4):
                    yield qk_item(g, 0, QT, si, i)
                si = stage()
                for i in range(4):
                    yield qk_item(g, 4, KT, si, i)
                si = stage()
                for j in range(4):
                    yield v_item(g, si, j)
                for grp, dst in ((0, QD), (1, KD), (2, VD)):
                    si = stage()
                    for i in range(4):
                        yield delta_item(g, grp, dst, si, i)
                si = stage()
                for i in range(4):
                    yield z_item(g, si, i)
                yield ba_item(g)

        for _ in norm_item(0):
            pass
        run_skewed(p1_items())
        P.finish()
    nc.all_engine_barrier()
    if upto < 2:
        return nc, semstack
    _phase2(nc, semstack, S, QT, KT, VV, tb_in, MIXT)
    nc.all_engine_barrier()
    if upto < 3:
        return nc, semstack
    cst = dict(identf=din("identf", [128, 128]), trif=din("trif", [128, 128]), bonesf=din("bonesf", [128, 128]),
               maskL=din("maskL", [128, 128]), maskU=din("maskU", [128, 128]), ident=ident_in,
               alog=din("alog_row", [1, 8]), dtb=din("dtb_row", [1, 8]), dng=din("dng_row", [1, 64]))
    _phase3(nc, semstack, S, QD, KD, VD, ZS, BA, MIXT, cst)
    nc.all_engine_barrier()
    if upto < 4:
        return nc, semstack
    w_out = din("w_out", [D, D])
    w_gate = din("w_gate", [D, DFF])
    w_up = din("w_up", [D, DFF])
    w_down = din("w_down", [DFF, D])
    fg_row = din("fg_row", [1, D])
    X1 = dsc("X1", [S, D], F32)
    H2T = dsc("H2T", [D, S], BF16)
    _phase4a(nc, semstack, S, x, MIXT, w_out, GROW, MODC, ident_in, X1, H2T)
    nc.all_engine_barrier()
    if upto < 5:
        return nc, semstack
    _phase4b(nc, semstack, S, X1, H2T, w_gate, w_up, w_down, GROW, fg_row, out)
    return nc, semstack


P2DBG = {'mode': 9, 'strided': True}


def run_skewed(items):
    live = []
    it = iter(items)
    while True:
        nxt = next(it, None)
        if nxt is not None:
            live.append(nxt)
        if not live:
            break
        for gen in list(live):
            try:
                next(gen)
            except StopIteration:
                live.remove(gen)


def _phase2(nc, semstack, S, QT, KT, VV, tb_in, MIXT):
    NSB = S // 2048
    import os
    mode = int(os.environ.get('P2MODE', '9'))
    with Phase(nc, semstack, "p2") as P:
        S_ = P.S
        EB = P.sb("EB", [128, 24 * 256], BF16)
        ones = P.sb("ones", [128, 64], BF16)
        qt = P.sb("qt", [64, 8, 2048], BF16)
        kt = [P.sb("kt%d" % i, [64, 8, 2048], BF16) for i in range(2)]
        v1 = P.sb("v1", [128, 16, 512], BF16)
        v1p = P.sb("v1p", [128, 1, 512], BF16)
        v2 = P.sb("v2", [128, 16, 512], BF16)
        v2p = P.sb("v2p", [128, 4, 512], BF16)
        v3 = [P.sb("v3_%d" % i, [128, 16, 512], BF16) for i in range(2)]
        Et = [P.sb("E%d" % i, [128, 512], BF16) for i in range(4)]
        PT = [P.sb("PT%d" % i, [128, 512], BF16) for i in range(4)]
        accn = P.sb("accn", [128, 2048], F32)
        accd = P.sb("accd", [128, 2048], F32)
        mst = P.sb("mst", [128, 2048], BF16)
        S_.op("pool", lambda e: e.dma_start(out=EB[:], in_=tb_in), writes=["EB"], dma=True, semkey="tb")
        S_.op("act", lambda e: e.activation(out=EB[:], in_=EB[:], func=AF.Exp), reads=["EB"], writes=["EB"])
        S_.op("pool", lambda e: e.memset(ones[:], 1.0), writes=["ones"])
        cnt = [0, 0, 0]
        for N in range(NSB):
            cur, prv = N % 2, (N + 1) % 2
            t0 = N * 2048
            S_.op("sp", lambda e, t0=t0: e.dma_start(out=qt[:], in_=QT.rearrange("(c p) s -> p c s", p=64)[:, :, t0:t0 + 2048]),
                  writes=["qt"], dma=True, semkey="qt")
            S_.op("sp", lambda e, t0=t0, cur=cur: e.dma_start(out=kt[cur][:], in_=KT.rearrange("(c p) s -> p c s", p=64)[:, :, t0:t0 + 2048]),
                  writes=[("kt", cur)], dma=True, semkey="kt%d" % cur)
            Vsb = VV[t0:t0 + 2048, :]
            S_.op("sp", lambda e, Vsb=Vsb: e.dma_start(out=v1[:], in_=Vsb.rearrange("(n p) f -> p n f", p=128)),
                  writes=["v1"], dma=True, semkey="v1")
            for n_ in range(4):
                S_.op("sp", lambda e, Vsb=Vsb, n_=n_: e.dma_start(
                    out=v2[:, n_ * 4:(n_ + 1) * 4, :],
                    in_=Vsb[n_ * 512:(n_ + 1) * 512, :].rearrange("(p r) f -> p r f", r=4)),
                    writes=["v2"], dma=True, semkey="v2")
            S_.op("sp", lambda e, Vsb=Vsb, cur=cur: e.dma_start(out=v3[cur][:], in_=Vsb.rearrange("(p r) f -> p r f", r=16)),
                  writes=[("v3", cur)], dma=True, semkey="v3_%d" % cur)
            def unit(hp, br, gq, jj, nbk, dbk, N=N, cur=cur, prv=prv):
                if br == 0:
                    n = 4 * gq + jj
                    qs, st = n * 128, 1
                    vcur = (v1, n, "v1")
                    if n >= 1:
                        pk = (cur, (n - 1) * 128, (v1, n - 1, "v1"))
                    elif N >= 1:
                        pk = (prv, 15 * 128, (v1p, 0, "v1p"))
                    else:
                        pk = None
                elif br == 1:
                    n_, r = gq, jj
                    qs, st = n_ * 512 + r, 4
                    vcur = (v2, n_ * 4 + r, "v2")
                    if n_ >= 1:
                        pk = (cur, (n_ - 1) * 512 + r, (v2, (n_ - 1) * 4 + r, "v2"))
                    elif N >= 1:
                        pk = (prv, 3 * 512 + r, (v2p, r, "v2p"))
                    else:
                        pk = None
                else:
                    r = 4 * gq + jj
                    qs, st = r, 16
                    vcur = (v3[cur], r, ("v3", cur))
                    pk = (prv, r, (v3[prv], r, ("v3", prv))) if N >= 1 else None
                sbk = cnt[0] % 3
                ei = cnt[0] % 4
                cnt[0] += 1
                blks = ([(0,) + pk] if pk else []) + [(1, cur, qs, vcur)]
                for hl in range(2):
                    for (blk, slot, ks, _v) in blks:
                        S_.op("pe", lambda e, sbk=sbk, hl=hl, blk=blk, slot=slot, ks=ks, qs=qs, st=st, hp=hp: e.matmul(
                            P.pb[sbk][:, hl * 256 + blk * 128: hl * 256 + (blk + 1) * 128],
                            lhsT=kt[slot][:, 2 * hp + hl, ks:ks + 127 * st + 1:st],
                            rhs=qt[:, 2 * hp + hl, qs:qs + 127 * st + 1:st],
                            start=True, stop=True),
                            reads=[("kt", slot), "qt"], writes=[("pb", sbk)])
                yield
                c0 = 0 if pk else 128
                vw = lambda ap, c0=c0: ap.rearrange("p (h c) -> p h c", h=2)[:, :, c0:256]
                S_.op("act", lambda e, sbk=sbk, ei=ei, vw=vw: e.activation(
                    out=vw(Et[ei][:]), in_=vw(P.pb[sbk][:]), func=AF.Exp, scale=0.125),
                    reads=[("pb", sbk)], writes=[("E", ei)])
                yield
                eoff = (br * 8 + 2 * hp) * 256
                eng = "dve" if ei % 2 == 0 else "pool"
                S_.op(eng, lambda e, ei=ei, eoff=eoff, vw=vw: e.tensor_tensor(
                    out=vw(PT[ei][:]), in0=vw(Et[ei][:]), in1=vw(EB[:, eoff:eoff + 512]), op=ALU.mult),
                    reads=[("E", ei), "EB"], writes=[("PT", ei)])
                yield
                for hl in range(2):
                    h = 2 * hp + hl
                    for bi, (blk, slot, ks, (vt, vi, vkey)) in enumerate(blks):
                        fl = _kw(start=(bi == 0), stop=(bi == len(blks) - 1), tile_position=(0, hl * 64))
                        S_.op("pe", lambda e, nbk=nbk, hl=hl, jj=jj, vt=vt, vi=vi, h=h, ei=ei, blk=blk, fl=fl: e.matmul(
                            P.pb[nbk][hl * 64:(hl + 1) * 64, jj * 128:(jj + 1) * 128],
                            lhsT=vt[:, vi, h * 64:(h + 1) * 64],
                            rhs=PT[ei][:, hl * 256 + blk * 128: hl * 256 + (blk + 1) * 128], **fl),
                            reads=[vkey, ("PT", ei)], writes=[("pb", nbk)])
                        S_.op("pe", lambda e, dbk=dbk, hl=hl, jj=jj, ei=ei, blk=blk, fl=fl: e.matmul(
                            P.pb[dbk][hl * 64:(hl + 1) * 64, jj * 128:(jj + 1) * 128],
                            lhsT=ones[:, 0:64],
                            rhs=PT[ei][:, hl * 256 + blk * 128: hl * 256 + (blk + 1) * 128], **fl),
                            reads=["ones", ("PT", ei)], writes=[("pb", dbk)])
                if jj < 3:
                    return
                yield
                for bk, acc, akey in ((nbk, accn, "accn"), (dbk, accd, "accd")):
                    if br == 0:
                        S_.op("act", lambda e, bk=bk, acc=acc, gq=gq: e.activation(
                            out=acc[:, gq * 512:(gq + 1) * 512], in_=P.pb[bk][:], func=AF.Identity),
                            reads=[("pb", bk)], writes=[akey])
                    else:
                        if br == 1:
                            oap = acc[:, gq * 512:(gq + 1) * 512].rearrange("p (i r) -> p r i", r=4)
                        else:
                            oap = acc[:].rearrange("p (i r) -> p r i", r=16)[:, 4 * gq:4 * gq + 4, :]
                        S_.op("dve", lambda e, bk=bk, oap=oap: e.tensor_tensor(
                            out=oap, in0=P.pb[bk][:].rearrange("p (r i) -> p r i", r=4), in1=oap, op=ALU.add),
                            reads=[("pb", bk), akey], writes=[akey])

            def finalize(hp, t0=t0):
                for _ in range(6):
                    yield
                S_.op("dve", lambda e: e.reciprocal(out=accd[:], in_=accd[:]), reads=["accd"], writes=["accd"])
                S_.op("dve", lambda e: e.tensor_tensor(out=mst[:], in0=accn[:], in1=accd[:], op=ALU.mult),
                      reads=["accn", "accd"], writes=["mst"])
                S_.op("sp", lambda e, hp=hp, t0=t0: e.dma_start(out=MIXT[hp * 128:(hp + 1) * 128, t0:t0 + 2048], in_=mst[:]),
                      reads=["mst"], dma=True, semkey="mst")

            def items():
                for hp in range(4):
                    for br in range(3):
                        for gq in range(4):
                            nbk = 3 + cnt[1] % 2
                            dbk = 5 + cnt[1] % 2
                            cnt[1] += 1
                            for jj in range(4):
                                yield unit(hp, br, gq, jj, nbk, dbk)
                    yield finalize(hp)

            run_skewed(items())
            if N + 1 < NSB:
                S_.op("pool", lambda e: e.tensor_copy(out=v1p[:, 0, :], in_=v1[:, 15, :]), reads=["v1"], writes=["v1p"])
                S_.op("pool", lambda e: e.tensor_copy(out=v2p[:], in_=v2[:, 12:16, :]), reads=["v2"], writes=["v2p"])
        P.finish()


def host_consts(inputs):
    import math
    rel_bias = np.asarray(inputs["rel_bias"], np.float32)
    k = np.arange(128)[:, None]
    q = np.arange(128)[None, :]
    steps_prev = q + 128 - k
    steps_cur = q - k
    tb = np.full((128, 3, 8, 2, 128), -30000.0, np.float32)

    def bucket(dist):
        dist = np.asarray(dist, np.int64)
        max_exact = 16
        dist_f = np.maximum(dist, 1).astype(np.float32)
        lg = (np.log(dist_f / np.float32(max_exact)) / np.float32(math.log(2048 / max_exact))
              * np.float32(32 - max_exact)).astype(np.float32)
        large = max_exact + lg.astype(np.int32)
        return np.where(dist < max_exact, dist, np.minimum(large, 31)).astype(np.int64)

    for br, d in enumerate((1, 4, 16)):
        for blk, steps in ((0, steps_prev), (1, steps_cur)):
            valid = (steps >= 0) & (steps <= 128)
            bk = bucket(np.maximum(steps, 0) * d)
            for h in range(8):
                vals = rel_bias[bk, h]
                tb[:, br, h, blk, :] = np.where(valid, vals, np.float32(-30000.0))
    return {"tb": np.ascontiguousarray(tb.reshape(128, 24 * 256))}


def _phase4a(nc, semstack, S, x, MIXT, w_out, GROW, MODC, ident_in, X1, H2T):
    NG = S // 512
    with Phase(nc, semstack, "p4a") as P:
        S_ = P.S
        ident = P.sb("ident", [128, 128], BF16)
        modc = P.sb("modc", [128, 32], F32)
        epsT = P.sb("eps", [128, 1], F32)
        g1 = P.sb("g1", [128, D], F32)
        wo = P.sb("wo", [128, 8, D], BF16)
        S_.op("pool", lambda e: e.dma_start(out=ident[:], in_=ident_in), writes=["ident"], dma=True, semkey="ident")
        S_.op("sp", lambda e: e.dma_start(out=modc[:], in_=MODC), writes=["modc"], dma=True, semkey="modc")
        S_.op("sp", lambda e: e.dma_start(out=g1[:], in_=GROW[0]), writes=["g1"], dma=True, semkey="g1")
        S_.op("pool", lambda e: e.dma_start(out=wo[:], in_=w_out.rearrange("(k p) f -> p k f", p=128)), writes=["wo"],
              dma=True, semkey="wo")
        S_.op("dve", lambda e: e.memset(epsT[:], EPS), writes=["eps"])
        for k in range(8):
            S_.op("dve", lambda e, k=k: e.tensor_tensor(out=wo[:, k, :], in0=wo[:, k, :], in1=g1[:], op=ALU.mult),
                  reads=["wo", "g1"], writes=["wo"])
        xt = [P.sb("xt%d" % i, [128, 4, D], F32) for i in range(2)]
        mt = [P.sb("mt%d" % i, [128, 8, 512], BF16) for i in range(2)]
        junk = P.sb("junk", [128, D], BF16)
        ss2 = [P.sb("ss%d" % i, [128, 4], F32) for i in range(2)]
        rstd2 = [P.sb("rstd%d" % i, [128, 4], F32) for i in range(2)]
        xs2 = [P.sb("xs%d" % i, [128, 4, D], BF16) for i in range(2)]
        hst = [P.sb("hst%d" % i, [128, 8, 512], BF16) for i in range(2)]
        xv = x.rearrange("(g j p) d -> g p j d", j=4, p=128)
        x1v = X1.rearrange("(g j p) d -> g p j d", j=4, p=128)

        def load_item(g):
            sl = g % 2
            S_.op("sp", lambda e: e.dma_start(out=xt[sl][:], in_=xv[g]), writes=[("xt", sl)], dma=True, semkey="xt%d" % sl)
            S_.op("sp", lambda e: e.dma_start(out=mt[sl][:], in_=MIXT.rearrange("(c p) s -> p c s", p=128)[:, :, g * 512:(g + 1) * 512]),
                  writes=[("mt", sl)], dma=True, semkey="mt%d" % sl)
            yield

        def y_item(g, j, half):
            sl = g % 2
            b = P.bank()
            for k in range(8):
                S_.op("pe", lambda e, k=k: e.matmul(
                    P.pb[b][:], lhsT=mt[sl][:, k, j * 128:(j + 1) * 128], rhs=wo[:, k, half * 512:(half + 1) * 512],
                    start=(k == 0), stop=(k == 7)), reads=[("mt", sl), "wo"], writes=[("pb", b)])
            yield
            S_.op("dve", lambda e: e.tensor_tensor(
                out=xt[sl][:, j, half * 512:(half + 1) * 512], in0=P.pb[b][:], in1=xt[sl][:, j, half * 512:(half + 1) * 512],
                op=ALU.add), reads=[("pb", b), ("xt", sl)], writes=[("xt", sl)])

        def norm_item(g):
            sl = g % 2
            ss, rstd, xs = ss2[sl], rstd2[sl], xs2[sl]
            KSS, KRS = ("ss", sl), ("rstd", sl)
            S_.op("sp", lambda e: e.dma_start(out=x1v[g], in_=xt[sl][:]), reads=[("xt", sl)], dma=True, semkey="xt%d" % sl)
            S_.op("pool", lambda e: e.memset(ss[:], 0.0), writes=[KSS])
            for j in range(4):
                S_.op("act", lambda e, j=j: e.activation(out=junk[:], in_=xt[sl][:, j, :], func=AF.Square, accum_out=ss[:, j:j + 1]),
                      reads=[("xt", sl), KSS], writes=[KSS, "junk"])
            S_.op("act", lambda e: e.activation(out=rstd[:], in_=ss[:], func=AF.Sqrt, bias=epsT[:], scale=1.0 / D),
                  reads=[KSS, "eps"], writes=[KRS])
            yield
            S_.op("dve", lambda e: e.reciprocal(out=rstd[:], in_=rstd[:]), reads=[KRS], writes=[KRS])
            for j in range(4):
                S_.op("dve", lambda e, j=j: e.tensor_scalar(out=xs[:, j, :], in0=xt[sl][:, j, :], scalar1=rstd[:, j:j + 1], scalar2=None, op0=ALU.mult),
                      reads=[("xt", sl), KRS], writes=[("xs", sl, j)])
            yield
            pend = None
            for c2 in range(5):
                if c2 < 4:
                    b = P.bank()
                    pbf = P.pb[b][:].bitcast(BF16)
                    for cc in range(2):
                        c = c2 * 2 + cc
                        for j in range(4):
                            S_.op("pe", lambda e, pbf=pbf, cc=cc, c=c, j=j: e.transpose(
                                pbf[:, cc * 512 + j * 128: cc * 512 + (j + 1) * 128], xs[:, j, c * 128:(c + 1) * 128], ident[:]),
                                reads=[("xs", sl, j), "ident"], writes=[("pb", b)])
                if pend is not None:
                    pb_, pbf_, pc2 = pend
                    for cc in range(2):
                        c = pc2 * 2 + cc
                        S_.op("act", lambda e, pbf_=pbf_, cc=cc, c=c: e.activation(
                            out=hst[sl][:, c, :], in_=pbf_[:, cc * 512:(cc + 1) * 512], func=AF.Identity,
                            bias=modc[:, 16 + c:17 + c], scale=modc[:, 24 + c:25 + c]),
                            reads=[("pb", pb_), "modc"], writes=[("hst", sl)])
                pend = (b, pbf, c2) if c2 < 4 else None
                yield
            S_.op("sp", lambda e: e.dma_start(out=H2T.rearrange("(c p) s -> p c s", p=128)[:, :, g * 512:(g + 1) * 512], in_=hst[sl][:]),
                  reads=[("hst", sl)], dma=True, semkey="hst%d" % sl)

        def p4a_items():
            for g in range(NG):
                if g + 1 < NG:
                    yield load_item(g + 1)
                for j in range(4):
                    for half in range(2):
                        yield y_item(g, j, half)
                yield norm_item(g)

        for _ in load_item(0):
            pass
        run_skewed(p4a_items())
        P.finish()


def _phase4b(nc, semstack, S, X1, H2T, w_gate, w_up, w_down, GROW, fg_row, out):
    G = 512
    NJ = G // 128
    NG = S // G
    NF = DFF // 128
    with Phase(nc, semstack, "p4b") as P:
        S_ = P.S
        epsT = P.sb("eps", [128, 1], F32)
        fg = P.sb("fg", [128, D], F32)
        wg = P.sb("wg", [128, 8, DFF], BF16)
        wu = P.sb("wu", [128, 8, DFF], BF16)
        wd = P.sb("wd", [128, NF, D], BF16)
        S_.op("dve", lambda e: e.memset(epsT[:], EPS), writes=["eps"])
        S_.op("sp", lambda e: e.dma_start(out=fg[:], in_=fg_row.partition_broadcast(128)), writes=["fg"], dma=True, semkey="fg")
        for k in range(8):
            S_.op("pool", lambda e, k=k: e.dma_start(out=wg[:, k, :], in_=w_gate[k * 128:(k + 1) * 128, :]), writes=["wg"], dma=True, semkey="wg")
            S_.op("pool", lambda e, k=k: e.dma_start(out=wu[:, k, :], in_=w_up[k * 128:(k + 1) * 128, :]), writes=["wu"], dma=True, semkey="wu")
        xq = [P.sb("xq%d" % i, [128, NJ, D], F32) for i in range(2)]
        g2 = xq[1]
        S_.op("sp", lambda e: e.dma_start(out=g2[:, 0, :], in_=GROW[1]), writes=[("xq", 1)], dma=True, semkey="xq1")
        S_.op("pool", lambda e: e.dma_start(out=wd[:], in_=w_down.rearrange("(k p) f -> p k f", p=128)), writes=["wd"], dma=True, semkey="wd")
        for k in range(NF):
            S_.op("dve", lambda e, k=k: e.tensor_tensor(out=wd[:, k, :], in0=wd[:, k, :], in1=g2[:, 0, :], op=ALU.mult),
                  reads=["wd", ("xq", 1)], writes=["wd"])
        ht = [P.sb("ht%d" % i, [128, 8, G], BF16) for i in range(2)]
        aT = P.sb("aT", [128, NF, G], BF16)
        ss = P.sb("ss", [128, NJ], F32)
        rstd = P.sb("rstd", [128, NJ], F32)
        x1v = X1.rearrange("(g j p) d -> g p j d", j=NJ, p=128)
        ov = out.rearrange("(g j p) d -> g p j d", j=NJ, p=128)
        for g in range(NG):
            sl = g % 2
            S_.op("sp", lambda e, g=g, sl=sl: e.dma_start(out=xq[sl][:], in_=x1v[g]), writes=[("xq", sl)], dma=True, semkey="xq%d" % sl)
            S_.op("sp", lambda e, g=g, sl=sl: e.dma_start(out=ht[sl][:], in_=H2T.rearrange("(c p) s -> p c s", p=128)[:, :, g * G:(g + 1) * G]),
                  writes=[("ht", sl)], dma=True, semkey="ht%d" % sl)
            for fc in range(NF):
                ba, bb = P.bank(), P.bank()
                for (bk, w, wk) in ((ba, wg, "wg"), (bb, wu, "wu")):
                    for k in range(8):
                        S_.op("pe", lambda e, bk=bk, w=w, k=k, fc=fc, sl=sl: e.matmul(
                            P.pb[bk][:, 0:G], lhsT=w[:, k, fc * 128:(fc + 1) * 128], rhs=ht[sl][:, k, :], start=(k == 0), stop=(k == 7)),
                            reads=[wk, ("ht", sl)], writes=[("pb", bk)])
                S_.op("act", lambda e, ba=ba, fc=fc: e.activation(out=aT[:, fc, :], in_=P.pb[ba][:, 0:G], func=AF.Silu),
                      reads=[("pb", ba)], writes=[("aT", fc)])
                S_.op("dve", lambda e, bb=bb, fc=fc: e.tensor_tensor(out=aT[:, fc, :], in0=P.pb[bb][:, 0:G], in1=aT[:, fc, :], op=ALU.mult),
                      reads=[("pb", bb), ("aT", fc)], writes=[("aT", fc)])
            for j in range(NJ):
                for half in range(2):
                    b = P.bank()
                    for k in range(NF):
                        S_.op("pe", lambda e, b=b, k=k, j=j, half=half: e.matmul(
                            P.pb[b][:], lhsT=aT[:, k, j * 128:(j + 1) * 128], rhs=wd[:, k, half * 512:(half + 1) * 512],
                            start=(k == 0), stop=(k == NF - 1)), reads=[("aT", k), "wd"], writes=[("pb", b)])
                    S_.op("dve", lambda e, b=b, j=j, half=half, sl=sl: e.tensor_tensor(
                        out=xq[sl][:, j, half * 512:(half + 1) * 512], in0=P.pb[b][:], in1=xq[sl][:, j, half * 512:(half + 1) * 512],
                        op=ALU.add), reads=[("pb", b), ("xq", sl)], writes=[("xq", sl)])
            S_.op("pool", lambda e: e.memset(ss[:], 0.0), writes=["ss"])
            for j in range(NJ):
                S_.op("act", lambda e, j=j, sl=sl: e.activation(out=aT[:, 0:2, :].rearrange("p a c -> p (a c)"), in_=xq[sl][:, j, :], func=AF.Square, accum_out=ss[:, j:j + 1]),
                      reads=[("xq", sl), "ss"], writes=["ss", ("aT", 0), ("aT", 1)])
            S_.op("act", lambda e: e.activation(out=rstd[:], in_=ss[:], func=AF.Sqrt, bias=epsT[:], scale=1.0 / D),
                  reads=["ss", "eps"], writes=["rstd"])
            S_.op("dve", lambda e: e.reciprocal(out=rstd[:], in_=rstd[:]), reads=["rstd"], writes=["rstd"])
            for j in range(NJ):
                S_.op("dve", lambda e, j=j, sl=sl: e.scalar_tensor_tensor(
                    out=xq[sl][:, j, :], in0=xq[sl][:, j, :], scalar=rstd[:, j:j + 1], in1=fg[:], op0=ALU.mult, op1=ALU.mult),
                    reads=[("xq", sl), "rstd", "fg"], writes=[("xq", sl)])
            S_.op("sp", lambda e, g=g, sl=sl: e.dma_start(out=ov[g], in_=xq[sl][:]), reads=[("xq", sl)], dma=True, semkey="xq%d" % sl)
        P.finish()


def _phase3(nc, semstack, S, QD, KD, VD, ZS, BA, MIXT, cst):
    NG = S // 512
    with Phase(nc, semstack, "p3") as P:
        S_ = P.S
        sb = P.sb
        ident = sb("ident", [128, 128], BF16)
        identf = sb("identf", [128, 128], F32)
        trif = sb("trif", [128, 128], F32)
        bonesf = sb("bonesf", [128, 128], F32)
        onesf = sb("onesf", [128, 128], F32)
        maskL = sb("maskL", [128, 128], F32)
        maskU = sb("maskU", [128, 128], F32)
        negA = sb("negA", [128, 8], F32)
        dtb = sb("dtb", [128, 8], F32)
        dng = sb("dng", [128, 64], F32)
        one1 = sb("one1", [128, 1], F32)
        epsT = sb("eps", [128, 1], F32)
        S_.op("pool", lambda e: e.dma_start(out=ident[:], in_=cst["ident"]), writes=["ident"], dma=True, semkey="ident")
        for nm, t in (("identf", identf), ("trif", trif), ("bonesf", bonesf), ("maskL", maskL), ("maskU", maskU)):
            S_.op("sp", lambda e, nm=nm, t=t: e.dma_start(out=t[:], in_=cst[nm]), writes=[nm], dma=True, semkey=nm)
        S_.op("sp", lambda e: e.dma_start(out=negA[:], in_=cst["alog"].partition_broadcast(128)), writes=["negA"], dma=True, semkey="negA")
        S_.op("sp", lambda e: e.dma_start(out=dtb[:], in_=cst["dtb"].partition_broadcast(128)), writes=["dtb"], dma=True, semkey="dtb")
        S_.op("sp", lambda e: e.dma_start(out=dng[:], in_=cst["dng"].partition_broadcast(128)), writes=["dng"], dma=True, semkey="dng")
        S_.op("pool", lambda e: e.memset(onesf[:], 1.0), writes=["onesf"])
        S_.op("pool", lambda e: e.memset(one1[:], 1.0), writes=["one1"])
        S_.op("pool", lambda e: e.memset(epsT[:], EPS), writes=["eps"])
        S_.op("act", lambda e: e.activation(out=negA[:], in_=negA[:], func=AF.Exp), reads=["negA"], writes=["negA"])
        S_.op("dve", lambda e: e.tensor_scalar(out=negA[:], in0=negA[:], scalar1=-1.0, scalar2=None, op0=ALU.mult),
              reads=["negA"], writes=["negA"])
        kd = [sb("kd%d" % i, [128, 4, 512], BF16) for i in range(2)]
        qd = [sb("qd%d" % i, [64, 8, 512], BF16) for i in range(2)]
        kd8 = [sb("kd8_%d" % i, [64, 8, 512], BF16) for i in range(2)]
        vd = [sb("vd%d" % i, [128, 4, 512], BF16) for i in range(2)]
        zs = [sb("zs%d" % i, [128, 4, 512], BF16) for i in range(2)]
        ba = [sb("ba%d" % i, [128, 4, 16], F32) for i in range(2)]
        mst = [sb("mst%d" % i, [128, 4, 512], BF16) for i in range(2)]
        y16 = sb("y16", [128, 16], F32)
        u16 = sb("u16", [128, 16], F32)
        beta = sb("beta", [128, 8], F32)
        gg = sb("gg", [128, 8], F32)
        gcl = sb("gcl", [128, 16], F32)
        eg2 = [sb("eg%d" % i, [128, 8], F32) for i in range(2)]
        kdsc = sb("kdsc", [128, 8], F32)
        be = sb("be", [128, 8], F32)
        offL = sb("offL", [128, 8], F32)
        decS2 = [sb("decS%d" % i, [64, 16], F32) for i in range(2)]
        kbg = sb("kbg", [128, 512], BF16)
        kdec2 = [sb("kdec%d" % i, [128, 512], BF16) for i in range(2)]
        bv = sb("bv", [128, 512], BF16)
        DG = sb("DG", [128, 8, 128], F32)
        tL = sb("tL", [128, 8, 128], F32)
        tU = sb("tU", [128, 8, 128], F32)
        Lb = sb("Lb", [128, 8, 128], F32)
        TTb = sb("TTb", [128, 8, 128], BF16)
        Ui = sb("Ui", [128, 8, 128], BF16)
        qkdT2 = [sb("qkdT%d" % i, [128, 8, 128], BF16) for i in range(2)]
        X = [sb("X%d" % i, [128, 8, 128], F32) for i in range(2)]
        Y = [sb("Y%d" % i, [128, 8, 128], F32) for i in range(2)]
        Q = [sb("Q%d" % i, [128, 8, 128], F32) for i in range(2)]
        uu2 = [sb("uu%d" % i, [128, 512], F32) for i in range(2)]
        wT2 = [sb("wT%d" % i, [64, 8, 128], BF16) for i in range(2)]
        vnew = sb("vnew", [128, 512], BF16)
        o1s = sb("o1s", [128, 512], F32)
        otm = sb("otm", [128, 512], F32)
        sqo = sb("sqo", [128, 512], F32)
        ssq = sb("ssq", [128, 8], F32)
        onb = sb("onb", [128, 512], BF16)
        Sf = sb("Sf", [64, 512], F32)
        S1 = sb("S1", [64, 512], F32)
        Sbf = sb("Sbf", [64, 512], BF16)
        S_.op("pool", lambda e: e.memset(Sf[:], 0.0), writes=["Sf"])
        S_.op("pool", lambda e: e.memset(Sbf[:], 0.0), writes=["Sbf"])

        def bc8(t):
            return lambda n: t.unsqueeze(2).to_broadcast([128, 8, n])

        bcnt = [0, 0]

        def bankA():
            bcnt[0] += 1
            return (bcnt[0] - 1) % 5

        def bankB():
            bcnt[1] += 1
            return 5 + (bcnt[1] - 1) % 3

        def A_tile(g, sl, j, c0, par):
            uu, wT, kdec, qkdT, eg, decS = uu2[par], wT2[par], kdec2[par], qkdT2[par], eg2[par], decS2[par]
            K_uu = ("uu", par)
            K_wT = ("wT", par)
            K_kdec = ("kdec", par)
            K_qkdT = ("qkdT", par)
            K_eg = ("eg", par)
            K_decS = ("decS", par)
            if j == 0:
                for nm, t, src, pp in (("kd", kd, KD, 128), ("qd", qd, QD, 64), ("kd8", kd8, KD, 64), ("vd", vd, VD, 128), ("zs", zs, ZS, 128)):
                    S_.op("sp", lambda e, t=t, src=src, g=g, sl=sl, pp=pp: e.dma_start(
                        out=t[sl][:], in_=src.rearrange("(c p) s -> p c s", p=pp)[:, :, g * 512:(g + 1) * 512]),
                        writes=[(nm, sl)], dma=True, semkey="%s%d" % (nm, sl))
                S_.op("sp", lambda e, g=g, sl=sl: e.dma_start(out=ba[sl][:], in_=BA.rearrange("(g j p) f -> g p j f", j=4, p=128)[g]),
                      writes=[("ba", sl)], dma=True, semkey="ba%d" % sl)
            S_.op("dve", lambda e, j=j, sl=sl: e.tensor_scalar(out=y16[:, 0:8], in0=ba[sl][:, j, 0:8], scalar1=-1.0, scalar2=None, op0=ALU.mult),
                  reads=[("ba", sl)], writes=["y16a"])
            yield
            S_.op("dve", lambda e, j=j, sl=sl: e.tensor_tensor(out=y16[:, 8:16], in0=ba[sl][:, j, 8:16], in1=dtb[:], op=ALU.add),
                  reads=[("ba", sl), "dtb"], writes=["y16b"])
            yield
            S_.op("act", lambda e: e.activation(out=u16[:], in_=y16[:], func=AF.Exp), reads=["y16a", "y16b"], writes=["u16"])
            yield
            S_.op("act", lambda e: e.activation(out=u16[:], in_=u16[:], func=AF.Ln, bias=one1[:], scale=1.0), reads=["u16", "one1"], writes=["u16"])
            yield
            S_.op("act", lambda e: e.activation(out=beta[:], in_=u16[:, 0:8], func=AF.Exp, scale=-1.0), reads=["u16"], writes=["beta"])
            yield
            S_.op("dve", lambda e: e.tensor_tensor(out=gg[:], in0=u16[:, 8:16], in1=negA[:], op=ALU.mult), reads=["u16", "negA"], writes=["gg"])
            yield
            b = bankA()
            yield
            S_.op("pe", lambda e, b=b: e.matmul(P.pb[b][:, 0:8], lhsT=trif[:], rhs=gg[:], start=True, stop=True),
                  reads=["trif", "gg"], writes=[("pb", b)])
            yield
            S_.op("pe", lambda e, b=b: e.matmul(P.pb[b][:, 8:16], lhsT=bonesf[:], rhs=gg[:], start=True, stop=True),
                  reads=["bonesf", "gg"], writes=[("pb", b)])
            yield
            for ch in range(2):
                S_.op("pe", lambda e, b=b, ch=ch: e.matmul(
                    P.pb[b][0:64, 16 + ch * 8:24 + ch * 8], lhsT=bonesf[:, ch * 64:(ch + 1) * 64],
                    rhs=gg[:], start=True, stop=True), reads=["bonesf", "gg"], writes=[("pb", b)])
            yield
            S_.op("dve", lambda e, b=b: e.tensor_copy(out=gcl[:], in_=P.pb[b][:, 0:16]), reads=[("pb", b)], writes=["gcl"])
            yield
            S_.op("act", lambda e, b=b: e.activation(out=decS[:], in_=P.pb[b][0:64, 16:32], func=AF.Exp), reads=[("pb", b)], writes=[K_decS])
            yield
            S_.op("act", lambda e: e.activation(out=eg[:], in_=gcl[:, 0:8], func=AF.Exp), reads=["gcl"], writes=[K_eg])
            yield
            S_.op("dve", lambda e: e.tensor_tensor(out=kdsc[:], in0=gcl[:, 8:16], in1=gcl[:, 0:8], op=ALU.subtract), reads=["gcl"], writes=["kdsc"])
            yield
            S_.op("act", lambda e: e.activation(out=kdsc[:], in_=kdsc[:], func=AF.Exp), reads=["kdsc"], writes=["kdsc"])
            yield
            S_.op("dve", lambda e: e.tensor_tensor(out=be[:], in0=beta[:], in1=eg[:], op=ALU.mult), reads=["beta", K_eg], writes=["be"])
            yield
            S_.op("dve", lambda e: e.tensor_tensor(out=offL[:], in0=gcl[:, 0:8], in1=u16[:, 0:8], op=ALU.subtract), reads=["gcl", "u16"], writes=["offL"])
            yield
            bk_ = bankA()
            yield
            bv_ = bk_
            yield
            for (off, src, key) in ((0, kd, "kd"), (512, vd, "vd")):
                pbf = P.pb[bk_][:].bitcast(BF16)
                for hp in range(4):
                    S_.op("pe", lambda e, pbf=pbf, hp=hp, src=src, sl=sl, c0=c0, off=off: e.transpose(
                        pbf[:, off + hp * 128:off + (hp + 1) * 128], src[sl][:, hp, c0:c0 + 128], ident[:]),
                        reads=[(key, sl), "ident"], writes=[("pb", bk_)])
            yield
            pk = P.pb[bk_][:].bitcast(BF16)[:, 0:512].rearrange("p (h d) -> p h d", h=8)
            yield
            pv = P.pb[bv_][:].bitcast(BF16)[:, 512:1024].rearrange("p (h d) -> p h d", h=8)
            yield
            S_.op("dve", lambda e, pk=pk: e.tensor_tensor(out=kbg[:].rearrange("p (h d) -> p h d", h=8), in0=pk, in1=bc8(be[:])(64), op=ALU.mult),
                  reads=[("pb", bk_), "be"], writes=["kbg"])
            yield
            S_.op("dve", lambda e, pk=pk: e.tensor_tensor(out=kdec[:].rearrange("p (h d) -> p h d", h=8), in0=pk, in1=bc8(kdsc[:])(64), op=ALU.mult),
                  reads=[("pb", bk_), "kdsc"], writes=[K_kdec])
            yield
            S_.op("dve", lambda e, pv=pv: e.tensor_tensor(out=bv[:].rearrange("p (h d) -> p h d", h=8), in0=pv, in1=bc8(beta[:])(64), op=ALU.mult),
                  reads=[("pb", bv_), "beta"], writes=["bv"])
            yield
            bKK = [bankA(), bankA()]
            yield
            bQK = [bankA(), bankA()]
            yield
            for h in range(8):
                hp, hl = h // 2, h % 2
                S_.op("pe", lambda e, h=h, sl=sl, c0=c0, bKK=bKK: e.matmul(
                    P.pb[bKK[h // 4]][:, (h % 4) * 128:(h % 4 + 1) * 128], lhsT=kd8[sl][:, h, c0:c0 + 128],
                    rhs=kd8[sl][:, h, c0:c0 + 128], start=True, stop=True),
                    reads=[("kd8", sl)], writes=[("pb", bKK[h // 4])])
                S_.op("pe", lambda e, h=h, sl=sl, c0=c0, bQK=bQK: e.matmul(
                    P.pb[bQK[h // 4]][:, (h % 4) * 128:(h % 4 + 1) * 128], lhsT=kd8[sl][:, h, c0:c0 + 128],
                    rhs=qd[sl][:, h, c0:c0 + 128], start=True, stop=True),
                    reads=[("kd8", sl), ("qd", sl)], writes=[("pb", bQK[h // 4])])
            yield
            S_.op("pool", lambda e: e.tensor_tensor(out=DG[:], in0=identf[:].unsqueeze(1).to_broadcast([128, 8, 128]),
                                                   in1=bc8(gcl[:, 0:8])(128), op=ALU.mult), reads=["identf", "gcl"], writes=["DG"])
            yield
            for hh in range(2):
                bG = bankA()
                hs = slice(hh * 4, hh * 4 + 4)
                S_.op("pe", lambda e, bG=bG, hs=hs: e.matmul(P.pb[bG][:], lhsT=onesf[:], rhs=DG[:, hs, :].rearrange("p h c -> p (h c)"),
                                                             start=True, stop=True), reads=["onesf", "DG"], writes=[("pb", bG)])
                pG = P.pb[bG][:].rearrange("p (h c) -> p h c", h=4)
                S_.op("dve", lambda e, pG=pG, hs=hs: e.scalar_tensor_tensor(
                    out=tL[:, hs, :], in0=pG, scalar=-1.0, in1=offL[:, hs].unsqueeze(2).to_broadcast([128, 4, 128]),
                    op0=ALU.mult, op1=ALU.add), reads=[("pb", bG), "offL"], writes=[("tL", hh)])
                S_.op("dve", lambda e, pG=pG, hs=hs: e.tensor_tensor(
                    out=tU[:, hs, :], in0=pG, in1=gcl[:, hs].unsqueeze(2).to_broadcast([128, 4, 128]), op=ALU.subtract),
                    reads=[("pb", bG), "gcl"], writes=[("tU", hh)])
                S_.op("pool", lambda e, hs=hs: e.tensor_tensor(out=tL[:, hs, :], in0=tL[:, hs, :],
                                                               in1=maskL[:].unsqueeze(1).to_broadcast([128, 4, 128]), op=ALU.add),
                      reads=[("tL", hh), "maskL"], writes=[("tL", hh)])
                S_.op("pool", lambda e, hs=hs: e.tensor_tensor(out=tU[:, hs, :], in0=tU[:, hs, :],
                                                               in1=maskU[:].unsqueeze(1).to_broadcast([128, 4, 128]), op=ALU.add),
                      reads=[("tU", hh), "maskU"], writes=[("tU", hh)])
                S_.op("act", lambda e, hs=hs: e.activation(out=Lb[:, hs, :], in_=tL[:, hs, :], func=AF.Exp), reads=[("tL", hh)], writes=[("Lb", hh)])
                S_.op("act", lambda e, hs=hs: e.activation(out=Ui[:, hs, :], in_=tU[:, hs, :], func=AF.Exp), reads=[("tU", hh)], writes=[("Ui", hh)])
                S_.op("dve", lambda e, hh=hh, hs=hs, bKK=bKK: e.scalar_tensor_tensor(
                    out=X[0][:, hs, :], in0=P.pb[bKK[hh]][:].rearrange("p (h c) -> p h c", h=4), scalar=-1.0, in1=Lb[:, hs, :],
                    op0=ALU.mult, op1=ALU.mult), reads=[("pb", bKK[hh]), ("Lb", hh)], writes=[("X", 0)])
                S_.op("dve", lambda e, hh=hh, hs=hs, bQK=bQK: e.tensor_tensor(
                    out=qkdT[:, hs, :], in0=P.pb[bQK[hh]][:].rearrange("p (h c) -> p h c", h=4), in1=Ui[:, hs, :], op=ALU.mult),
                    reads=[("pb", bQK[hh]), ("Ui", hh)], writes=[K_qkdT])
            yield
            bB = [bankA(), bankA()]
            yield
            for h in range(8):
                S_.op("pe", lambda e, bB=bB, h=h: e.transpose(P.pb[bB[h // 4]][:, (h % 4) * 128:(h % 4 + 1) * 128], X[0][:, h, :], identf[:]),
                      reads=[("X", 0), "identf"], writes=[("pb", bB[h // 4])])
            yield
            for hh in range(2):
                S_.op("act", lambda e, bB=bB, hh=hh: e.activation(out=Y[0][:, hh * 4:hh * 4 + 4, :].rearrange("p h c -> p (h c)"), in_=P.pb[bB[hh]][:],
                                                                 func=AF.Identity), reads=[("pb", bB[hh])], writes=[("Y", 0)])
            yield
            S_.op("pool", lambda e: e.tensor_tensor(out=Q[0][:], in0=Y[0][:], in1=identf[:].unsqueeze(1).to_broadcast([128, 8, 128]), op=ALU.add),
                  reads=[("Y", 0), "identf"], writes=[("Q", 0)])
            yield
            for lv in range(1, 6):
                a, n = (lv - 1) % 2, lv % 2
                bX = [bankA(), bankA()]
                for h in range(8):
                    S_.op("pe", lambda e, h=h, a=a, bX=bX: e.matmul(P.pb[bX[h // 4]][:, (h % 4) * 128:(h % 4 + 1) * 128],
                                                                 lhsT=Y[a][:, h, :], rhs=X[a][:, h, :], start=True, stop=True),
                          reads=[("X", a), ("Y", a)], writes=[("pb", bX[h // 4])])
                for hh in range(2):
                    S_.op("act", lambda e, hh=hh, n=n, bX=bX: e.activation(
                        out=X[n][:, hh * 4:hh * 4 + 4, :].rearrange("p h c -> p (h c)"), in_=P.pb[bX[hh]][:], func=AF.Identity),
                        reads=[("pb", bX[hh])], writes=[("X", n)])
                if lv < 5:
                    bY = [bankA(), bankA()]
                    for h in range(8):
                        S_.op("pe", lambda e, h=h, a=a, bY=bY: e.matmul(P.pb[bY[h // 4]][:, (h % 4) * 128:(h % 4 + 1) * 128],
                                                                     lhsT=X[a][:, h, :], rhs=Y[a][:, h, :], start=True, stop=True),
                              reads=[("X", a), ("Y", a)], writes=[("pb", bY[h // 4])])
                    for hh in range(2):
                        S_.op("dve", lambda e, hh=hh, n=n, bY=bY: e.tensor_copy(
                            out=Y[n][:, hh * 4:hh * 4 + 4, :].rearrange("p h c -> p (h c)"), in_=P.pb[bY[hh]][:]),
                            reads=[("pb", bY[hh])], writes=[("Y", n)])
                bQ = [bankA(), bankA()]
                for h in range(8):
                    S_.op("pe", lambda e, h=h, a=a, n=n, bQ=bQ: e.matmul(P.pb[bQ[h // 4]][:, (h % 4) * 128:(h % 4 + 1) * 128],
                                                                      lhsT=X[n][:, h, :], rhs=Q[a][:, h, :], start=True, stop=True),
                          reads=[("X", n), ("Q", a)], writes=[("pb", bQ[h // 4])])
                for hh in range(2):
                    S_.op("dve", lambda e, hh=hh, n=n, a=a, bQ=bQ: e.tensor_tensor(
                        out=Q[n][:, hh * 4:hh * 4 + 4, :].rearrange("p h c -> p (h c)"), in0=P.pb[bQ[hh]][:],
                        in1=Q[a][:, hh * 4:hh * 4 + 4, :].rearrange("p h c -> p (h c)"), op=ALU.add),
                        reads=[("pb", bQ[hh]), ("Q", a)], writes=[("Q", n)])
            yield
            S_.op("act", lambda e: e.activation(out=TTb[:].rearrange("p h c -> p (h c)"), in_=Q[1][:].rearrange("p h c -> p (h c)"), func=AF.Identity),
                  reads=[("Q", 1)], writes=["TTb"])
            yield
            TT = TTb
            yield
            bu = bankA()
            yield
            bw = [bankA(), bankA()]
            yield
            for h in range(8):
                S_.op("pe", lambda e, h=h, bu=bu: e.matmul(P.pb[bu][:, h * 64:(h + 1) * 64], lhsT=TT[:, h, :], rhs=bv[:, h * 64:(h + 1) * 64],
                                                         start=True, stop=True), reads=["TTb", "bv"], writes=[("pb", bu)])
                S_.op("pe", lambda e, h=h, bw=bw: e.matmul(
                    P.pb[bw[h // 4]][0:64, (h % 4) * 128:(h % 4 + 1) * 128], lhsT=kbg[:, h * 64:(h + 1) * 64], rhs=TT[:, h, :],
                    start=True, stop=True), reads=["TTb", "kbg"], writes=[("pb", bw[h // 4])])
            yield
            S_.op("act", lambda e, bu=bu: e.activation(out=uu[:], in_=P.pb[bu][:], func=AF.Identity), reads=[("pb", bu)], writes=[K_uu])
            yield
            for hh in range(2):
                S_.op("dve", lambda e, bw=bw, hh=hh: e.tensor_copy(out=wT[:, hh * 4:hh * 4 + 4, :].rearrange("p a c -> p (a c)"), in_=P.pb[bw[hh]][0:64, :]),
                      reads=[("pb", bw[hh])], writes=[K_wT])

        def B_tile(g, sl, j, c0, par):
            uu, wT, kdec, qkdT, eg, decS = uu2[par], wT2[par], kdec2[par], qkdT2[par], eg2[par], decS2[par]
            K_uu = ("uu", par)
            K_wT = ("wT", par)
            K_kdec = ("kdec", par)
            K_qkdT = ("qkdT", par)
            K_eg = ("eg", par)
            K_decS = ("decS", par)
            for ch in range(2):
                p0 = ch * 64
                bvn, bo1, bo2, bs = bankB(), bankB(), bankB(), bankB()
                for h in range(8):
                    S_.op("pe", lambda e, h=h, p0=p0, ch=ch, bvn=bvn: e.matmul(
                        P.pb[bvn][p0:p0 + 64, h * 64:(h + 1) * 64], lhsT=wT[:, h, ch * 64:(ch + 1) * 64],
                        rhs=Sbf[:, h * 64:(h + 1) * 64], start=True, stop=True, tile_position=(0, p0)),
                        reads=[K_wT, "Sbf"], writes=[("pb", bvn)])
                for h in range(8):
                    S_.op("pe", lambda e, h=h, p0=p0, ch=ch, bo1=bo1, sl=sl, c0=c0: e.matmul(
                        P.pb[bo1][p0:p0 + 64, h * 64:(h + 1) * 64], lhsT=qd[sl][:, h, c0 + p0:c0 + p0 + 64],
                        rhs=Sbf[:, h * 64:(h + 1) * 64], start=True, stop=True, tile_position=(0, p0)),
                        reads=[("qd", sl), "Sbf"], writes=[("pb", bo1)])
                S_.op("dve", lambda e, p0=p0, bvn=bvn: e.tensor_tensor(out=vnew[p0:p0 + 64, :], in0=uu[p0:p0 + 64, :], in1=P.pb[bvn][p0:p0 + 64, :],
                                                                    op=ALU.subtract), reads=[K_uu, ("pb", bvn)], writes=[("vnew", ch)])
                S_.op("dve", lambda e, p0=p0, bo1=bo1: e.tensor_tensor(
                    out=o1s[p0:p0 + 64, :].rearrange("p (h d) -> p h d", h=8), in0=P.pb[bo1][p0:p0 + 64, :].rearrange("p (h d) -> p h d", h=8),
                    in1=eg[p0:p0 + 64, :].unsqueeze(2).to_broadcast([64, 8, 64]), op=ALU.mult),
                    reads=[K_eg, ("pb", bo1)], writes=[("o1s", ch)])
                S_.op("pool", lambda e, ch=ch: e.tensor_tensor(
                    out=S1[:].rearrange("p (a d) -> p a d", a=8), in0=Sf[:].rearrange("p (a d) -> p a d", a=8),
                    in1=decS[:, ch * 8:ch * 8 + 8].unsqueeze(2).to_broadcast([64, 8, 64]), op=ALU.mult),
                    reads=["Sf", K_decS], writes=["S1"])
                for h in range(8):
                    S_.op("pe", lambda e, h=h, p0=p0, ch=ch, bs=bs: e.matmul(
                        P.pb[bs][0:64, h * 64:(h + 1) * 64], lhsT=kdec[p0:p0 + 64, h * 64:(h + 1) * 64],
                        rhs=vnew[p0:p0 + 64, h * 64:(h + 1) * 64], start=True, stop=True, tile_position=(p0, 0)),
                        reads=[K_kdec, ("vnew", ch)], writes=[("pb", bs)])
                for h in range(8):
                    S_.op("pe", lambda e, h=h, p0=p0, ch=ch, bo2=bo2: e.matmul(
                        P.pb[bo2][p0:p0 + 64, h * 64:(h + 1) * 64], lhsT=qkdT[p0:p0 + 64, h, ch * 64:(ch + 1) * 64],
                        rhs=vnew[p0:p0 + 64, h * 64:(h + 1) * 64], start=True, stop=True, tile_position=(p0, p0)),
                        reads=[K_qkdT, ("vnew", ch)], writes=[("pb", bo2)])
                S_.op("dve", lambda e, bs=bs: e.tensor_tensor(out=Sbf[:], in0=S1[:], in1=P.pb[bs][0:64, :], op=ALU.add),
                      reads=["S1", ("pb", bs)], writes=["Sbf"])
                S_.op("dve", lambda e, bs=bs: e.tensor_tensor(out=Sf[:], in0=S1[:], in1=P.pb[bs][0:64, :], op=ALU.add),
                      reads=["S1", ("pb", bs)], writes=["Sf"])
                S_.op("dve", lambda e, p0=p0, bo2=bo2: e.tensor_tensor(out=otm[p0:p0 + 64, :], in0=o1s[p0:p0 + 64, :], in1=P.pb[bo2][p0:p0 + 64, :],
                                                                    op=ALU.add), reads=[("o1s", ch), ("pb", bo2)], writes=[("otm", ch)])
            yield
            OT = [("otm", 0), ("otm", 1)]
            yield
            S_.op("pool", lambda e: e.tensor_tensor(out=sqo[:], in0=otm[:], in1=otm[:], op=ALU.mult), reads=OT, writes=["sqo"])
            yield
            S_.op("dve", lambda e: e.tensor_reduce(out=ssq[:], in_=sqo[:].rearrange("p (h d) -> p h d", h=8), axis=mybir.AxisListType.X, op=ALU.add),
                  reads=["sqo"], writes=["ssq"])
            yield
            S_.op("act", lambda e: e.activation(out=ssq[:], in_=ssq[:], func=AF.Sqrt, bias=epsT[:], scale=1.0 / 64), reads=["ssq", "eps"], writes=["ssq"])
            yield
            S_.op("dve", lambda e: e.reciprocal(out=ssq[:], in_=ssq[:]), reads=["ssq"], writes=["ssq"])
            yield
            S_.op("dve", lambda e: e.tensor_tensor(out=sqo[:].rearrange("p (h d) -> p h d", h=8), in0=otm[:].rearrange("p (h d) -> p h d", h=8),
                                                  in1=bc8(ssq[:])(64), op=ALU.mult), reads=OT + ["ssq"], writes=["sqo"])
            yield
            S_.op("pool", lambda e: e.tensor_tensor(out=onb[:].rearrange("p (h d) -> p h d", h=8), in0=sqo[:].rearrange("p (h d) -> p h d", h=8),
                                                   in1=dng[:].unsqueeze(1).to_broadcast([128, 8, 64]), op=ALU.mult), reads=["sqo", "dng"], writes=["onb"])
            yield
            bo = bankB()
            yield
            pO = P.pb[bo][:].bitcast(BF16)
            yield
            for hp in range(8):
                S_.op("pe", lambda e, pO=pO, hp=hp: e.transpose(pO[:, hp * 128:(hp + 1) * 128], onb[:, (hp % 4) * 128:(hp % 4 + 1) * 128], ident[:]),
                      reads=["onb", "ident"], writes=[("pb", bo)])
            yield
            S_.op("dve", lambda e, pO=pO, sl=sl, c0=c0: e.tensor_tensor(
                out=mst[sl][:, :, c0:c0 + 128], in0=pO[:, 0:512].rearrange("p (a c) -> p a c", a=4), in1=zs[sl][:, :, c0:c0 + 128], op=ALU.mult),
                reads=[("pb", bo), ("zs", sl)], writes=[("mst", sl)])
            if j == 3:
                S_.op("sp", lambda e, g=g, sl=sl: e.dma_start(out=MIXT.rearrange("(c p) s -> p c s", p=128)[:, 4:8, g * 512:(g + 1) * 512], in_=mst[sl][:]),
                      reads=[("mst", sl)], dma=True, semkey="mst%d" % sl)

        tiles = [(g, g % 2, j, j * 128, (g * 4 + j) % 2) for g in range(NG) for j in range(4)]
        for _ in A_tile(*tiles[0]):
            pass
        for ti, tl in enumerate(tiles):
            gb = B_tile(*tl)
            ga = A_tile(*tiles[ti + 1]) if ti + 1 < len(tiles) else iter(())
            da = db = False
            while not (da and db):
                for _ in range(3):
                    if not da:
                        try:
                            next(ga)
                        except StopIteration:
                            da = True
                if not db:
                    try:
                        next(gb)
                    except StopIteration:
                        db = True
        P.finish()


def _host_all(inputs, b, S):
    f = np.float32
    col = lambda v: np.ascontiguousarray(np.asarray(v, f).reshape(-1, 128).T)
    d = {}
    d["x"] = np.ascontiguousarray(np.asarray(inputs["x"][b, :S], f))
    d["c_col"] = col(inputs["c"][b])
    d["w_ada"] = np.asarray(inputs["w_ada"][0], f)
    d["bada_col"] = col(inputs["b_ada"][0])
    d["bada_row"] = np.ascontiguousarray(np.asarray(inputs["b_ada"][0], f).reshape(6, 1024))
    d["gattn_col"] = col(inputs["norm_attn_g"][0])
    d["gffn_col"] = col(inputs["norm_ffn_g"][0])
    d["w_in"] = np.asarray(inputs["w_in"][0], f)
    cw = np.asarray(inputs["conv_w"][0], f)
    d["cw_col"] = np.ascontiguousarray(cw.T.reshape(12, 128, 4).transpose(1, 0, 2).reshape(128, 48))
    d["ident"] = np.eye(128, dtype=f)
    bo = np.zeros((128, 128), f)
    bo[:64, :64] = 1
    bo[64:, 64:] = 1
    d["bones"] = bo
    d["identf"] = np.eye(128, dtype=f)
    d["bonesf"] = bo
    idx = np.arange(128)
    same = (idx[:, None] // 64) == (idx[None, :] // 64)
    d["trif"] = (same & (idx[:, None] <= idx[None, :])).astype(f)
    d["maskL"] = np.where(same & (idx[None, :] < idx[:, None]), 0.0, -30000.0).astype(f)
    d["maskU"] = np.where(same & (idx[None, :] >= idx[:, None]), 0.0, -30000.0).astype(f)
    d["alog_row"] = np.asarray(inputs["a_log"][0], f).reshape(1, 8)
    d["dtb_row"] = np.asarray(inputs["dt_bias"][0], f).reshape(1, 8)
    d["dng_row"] = np.asarray(inputs["delta_norm_g"][0], f).reshape(1, 64)
    d["w_out"] = np.asarray(inputs["w_out"][0], f)
    d["w_gate"] = np.asarray(inputs["w_gate"][0], f)
    d["w_up"] = np.asarray(inputs["w_up"][0], f)
    d["w_down"] = np.asarray(inputs["w_down"][0], f)
    d["fg_row"] = np.asarray(inputs["final_norm_g"], f).reshape(1, 1024)
    return d


def kernel(**inputs):
    S = inputs["x"].shape[1]
    B = inputs["x"].shape[0]
    nc, semstack = build_program(S, dbg=False)
    consts = host_consts(inputs)
    in_maps = []
    for b in range(B):
        d = _host_all(inputs, b, S)
        d.update(consts)
        in_maps.append(d)
    res = run_bass_kernel_spmd(nc, in_maps, core_ids=list(range(B)))
    return np.stack([np.asarray(r["out"], np.float32) for r in res.results], axis=0)
```

```python
from contextlib import ExitStack
import numpy as np
import concourse.bass as bass
import concourse.mybir as mybir
from concourse.bass_utils import run_bass_kernel_spmd

F32 = mybir.dt.float32
BF16 = mybir.dt.bfloat16
ALU = mybir.AluOpType
AF = mybir.ActivationFunctionType

ENGS = ("pe", "act", "dve", "pool", "sp")
EPOCH = 20000


class _Op:
    __slots__ = ("eng", "fn", "deps", "dma", "semkey", "idx", "needs_inc", "sem", "val")

    def __init__(self, eng, fn, dma, semkey):
        self.eng, self.fn, self.dma, self.semkey = eng, fn, dma, semkey
        self.deps = []
        self.needs_inc = False
        self.sem = None
        self.val = 0


class Sched:
    def __init__(self, nc):
        self.nc = nc
        self.ops = {e: [] for e in ENGS}
        self.last_w = {}
        self.readers = {}
        self.last_dma_on_sem = {}
        self.n = 0

    def op(self, eng, fn, reads=(), writes=(), dma=False, semkey=None):
        o = _Op(eng, fn, dma, semkey)
        o.idx = self.n
        self.n += 1
        deps = {}

        def add(p):
            if p is None or p is o:
                return
            if (not p.dma) and (not dma) and p.eng == "pe" and eng == "pe":
                return
            deps[id(p)] = p

        for k in reads:
            add(self.last_w.get(k))
        for k in writes:
            add(self.last_w.get(k))
            for r in self.readers.get(k, ()):
                add(r)
        if dma:
            assert semkey is not None
            add(self.last_dma_on_sem.get(semkey))
            self.last_dma_on_sem[semkey] = o
        o.deps = list(deps.values())
        for p in o.deps:
            p.needs_inc = True
        for k in reads:
            self.readers.setdefault(k, []).append(o)
        for k in writes:
            self.last_w[k] = o
            self.readers[k] = []
        self.ops[eng].append(o)
        return o

    def emit(self, stack, final_wait_ops=()):
        nc = self.nc
        for o in final_wait_ops:
            o.needs_inc = True
        sems = {}

        def getsem(name):
            if name not in sems:
                sems[name] = stack.enter_context(nc.semaphore(name))
            return sems[name]

        dma_cnt = {}
        for e in ENGS:
            cnt = 0
            for o in self.ops[e]:
                if o.dma:
                    c = dma_cnt.get(o.semkey, 0) + 1
                    dma_cnt[o.semkey] = c
                    o.sem = getsem("d_" + str(o.semkey))
                    o.val = 16 * c
                    o.needs_inc = True
                elif o.needs_inc:
                    ep, v = divmod(cnt, EPOCH)
                    o.sem = getsem("c_%s_%d" % (e, ep))
                    o.val = v + 1
                    cnt += 1
        self.nsems = len(sems)
        block = stack.enter_context(nc.Block())
        engmap = {"pe": block.tensor, "act": block.scalar, "dve": block.vector,
                  "pool": block.gpsimd, "sp": block.sync}
        for e in ENGS:
            ops = self.ops[e]
            fw = [o for o in final_wait_ops] if e == "sp" else []

            def body(engine, ops=ops, fw=fw):
                waited = {}
                for o in ops:
                    for p in o.deps:
                        key = id(p.sem)
                        if waited.get(key, 0) >= p.val:
                            continue
                        engine.wait_ge(p.sem, p.val)
                        waited[key] = p.val
                    ins = o.fn(engine)
                    if o.needs_inc:
                        ins.then_inc(o.sem, 16 if o.dma else 1)
                for p in fw:
                    key = id(p.sem)
                    if waited.get(key, 0) >= p.val:
                        continue
                    engine.wait_ge(p.sem, p.val)
                    waited[key] = p.val

            engmap[e](body)


D = 1024
INW = 3600
DFF = 2816
EPS = 1e-6


class Phase:
    def __init__(self, nc, semstack, name):
        self.nc, self.semstack, self.name = nc, semstack, name
        self.st = ExitStack()
        self.S = Sched(nc)
        self.nb = 0
        self.pb = None
        self.cnt = 0

    def __enter__(self):
        self.st.__enter__()
        self.pb = [self.st.enter_context(self.nc.psum_tensor("%s_pb%d" % (self.name, i), [128, 512], F32))
                   for i in range(8)]
        return self

    def sb(self, name, shape, dt):
        return self.st.enter_context(self.nc.sbuf_tensor(self.name + "_" + name, shape, dt))

    def bank(self):
        i = self.nb % 8
        self.nb += 1
        return i

    def finish(self):
        S = self.S
        fw = [o for e in ENGS for o in S.ops[e] if o.dma]
        for e in ("pe", "act", "dve", "pool"):
            if S.ops[e]:
                fw.append(S.ops[e][-1])
        nc = self.nc
        ph = self

        class _SemStack:
            def enter_context(self_inner, cm):
                return cm

        _emit(S, nc, self.semstack, self.st, fw, self.name)

    def __exit__(self, *a):
        r = self.st.__exit__(*a)
        return r


def _emit(S, nc, semstack, blockstack, final_wait_ops, pname):
    for o in final_wait_ops:
        o.needs_inc = True
    sems = {}

    def getsem(name):
        name = pname + "_" + "".join(ch if ch.isalnum() else "_" for ch in name)
        if name not in sems:
            sems[name] = blockstack.enter_context(nc.semaphore(name))
        return sems[name]

    dma_cnt = {}
    for e in ENGS:
        cnt = 0
        for o in S.ops[e]:
            if o.dma:
                c = dma_cnt.get(o.semkey, 0) + 1
                dma_cnt[o.semkey] = c
                o.sem = getsem("d_" + str(o.semkey))
                o.val = 16 * c
                o.needs_inc = True
            elif o.needs_inc:
                ep, v = divmod(cnt, EPOCH)
                o.sem = getsem("c_%s_%d" % (e, ep))
                o.val = v + 1
                cnt += 1
    S.nsems = len(sems)
    for sm in sems.values():
        nc.sync.sem_clear(sm)
    nc.all_engine_barrier()
    block = blockstack.enter_context(nc.Block())
    engmap = {"pe": block.tensor, "act": block.scalar, "dve": block.vector,
              "pool": block.gpsimd, "sp": block.sync}
    for e in ENGS:
        ops = S.ops[e]
        fw = list(final_wait_ops) if e == "sp" else []

        def body(engine, ops=ops, fw=fw):
            waited = {}
            for o in ops:
                for p in o.deps:
                    key = id(p.sem)
                    if waited.get(key, 0) >= p.val:
                        continue
                    engine.wait_ge(p.sem, p.val)
                    waited[key] = p.val
                ins = o.fn(engine)
                if o.needs_inc:
                    ins.then_inc(o.sem, 16 if o.dma else 1)
            for p in fw:
                key = id(p.sem)
                if waited.get(key, 0) >= p.val:
                    continue
                engine.wait_ge(p.sem, p.val)
                waited[key] = p.val

        engmap[e](body)


def _kw(**k):
    return k


def build_program(S, dbg=False, upto=9):
    nc = bass.Bass("TRN2", target_bir_lowering=False)
    NG = S // 512
    OUTK = "ExternalOutput" if dbg else "Internal"

    def din(name, shape, dt=F32):
        return nc.dram_tensor(name, shape, dt, kind="ExternalInput").ap()

    def dsc(name, shape, dt):
        return nc.dram_tensor(name, shape, dt, kind=OUTK).ap()

    x = din("x", [S, D])
    c_col = din("c_col", [128, 8])
    w_ada = din("w_ada", [D, 6 * D])
    bada_col = din("bada_col", [128, 48])
    bada_row = din("bada_row", [6, D])
    gattn_col = din("gattn_col", [128, 8])
    gffn_col = din("gffn_col", [128, 8])
    w_in = din("w_in", [D, INW])
    cw_col = din("cw_col", [128, 48])
    ident_in = din("ident", [128, 128])
    bones_in = din("bones", [128, 128])
    tb_in = din("tb", [128, 24 * 256])
    out = nc.dram_tensor("out", [S, D], F32, kind="ExternalOutput").ap()

    MODC = dsc("MODC", [128, 32], F32)
    GROW = dsc("GROW", [2, 128, D], F32)
    QT = dsc("QT", [512, S], BF16)
    KT = dsc("KT", [512, S], BF16)
    VV = dsc("VV", [S, 512], BF16)
    QD = dsc("QD", [512, S], BF16)
    KD = dsc("KD", [512, S], BF16)
    VD = dsc("VD", [512, S], BF16)
    ZS = dsc("ZS", [512, S], BF16)
    BA = dsc("BA", [S, 16], F32)
    MIXT = dsc("MIXT", [D, S], BF16)

    semstack = ExitStack()
    semstack.__enter__()

    with Phase(nc, semstack, "p0") as P:
        S_ = P.S
        ccol = P.sb("ccol", [128, 8], F32)
        sbf = P.sb("sbf", [128, 8], BF16)
        sbc = P.sb("sbc", [128, 8, 128], BF16)
        bcol = P.sb("bcol", [128, 48], F32)
        gcol = P.sb("gcol", [128, 16], F32)
        modc = P.sb("modc", [128, 32], F32)
        wa = [P.sb("wa%d" % i, [128, 8, D], BF16) for i in range(2)]
        brow = [P.sb("brow%d" % i, [128, D], F32) for i in range(2)]
        grow = [P.sb("grow%d" % i, [128, D], F32) for i in range(2)]
        S_.op("sp", lambda e: e.dma_start(out=ccol[:], in_=c_col), writes=["ccol"], dma=True, semkey="ccol")
        S_.op("sp", lambda e: e.dma_start(out=bcol[:], in_=bada_col), writes=["bcol"], dma=True, semkey="bcol")
        S_.op("sp", lambda e: e.dma_start(out=gcol[:, 0:8], in_=gattn_col), writes=["gcol"], dma=True, semkey="gcol")
        S_.op("sp", lambda e: e.dma_start(out=gcol[:, 8:16], in_=gffn_col), writes=["gcol"], dma=True, semkey="gcol")
        S_.op("act", lambda e: e.activation(out=sbf[:], in_=ccol[:], func=AF.Silu), reads=["ccol"], writes=["sbf"])
        S_.op("dve", lambda e: e.tensor_copy(out=sbc[:], in_=sbf[:].unsqueeze(2).to_broadcast([128, 8, 128])),
              reads=["sbf"], writes=["sbc"])
        wav = w_ada.rearrange("(k p) f -> p k f", p=128)
        colidx = {0: 0, 1: 1, 3: 2, 4: 3}
        for j in range(6):
            sl = j % 2
            S_.op("pool", lambda e, j=j, sl=sl: e.dma_start(out=wa[sl][:], in_=wav[:, :, j * D:(j + 1) * D]),
                  writes=[("wa", sl)], dma=True, semkey="wa%d" % sl)
            if j in colidx:
                jj = colidx[j]
                b = P.bank()
                for fcn in range(8):
                    for k in range(8):
                        S_.op("pe", lambda e, b=b, fcn=fcn, k=k, sl=sl: e.matmul(
                            P.pb[b][:, fcn:fcn + 1], lhsT=wa[sl][:, k, fcn * 128:(fcn + 1) * 128], rhs=sbf[:, k:k + 1],
                            start=(k == 0), stop=(k == 7)), reads=[("wa", sl), "sbf"], writes=[("pb", b)])
                S_.op("dve", lambda e, b=b, jj=jj, j=j: e.tensor_tensor(
                    out=modc[:, jj * 8:(jj + 1) * 8], in0=P.pb[b][:, 0:8], in1=bcol[:, j * 8:(j + 1) * 8], op=ALU.add),
                    reads=[("pb", b), "bcol"], writes=["modc"])
            else:
                gi = 0 if j == 2 else 1
                S_.op("sp", lambda e, j=j, gi=gi: e.dma_start(out=brow[gi][:], in_=bada_row[j:j + 1, :].partition_broadcast(128)),
                      writes=[("brow", gi)], dma=True, semkey="brow%d" % gi)
                for half in range(2):
                    b = P.bank()
                    for k in range(8):
                        S_.op("pe", lambda e, b=b, k=k, sl=sl, half=half: e.matmul(
                            P.pb[b][:], lhsT=sbc[:, k, :], rhs=wa[sl][:, k, half * 512:(half + 1) * 512],
                            start=(k == 0), stop=(k == 7)), reads=[("wa", sl), "sbc"], writes=[("pb", b)])
                    S_.op("dve", lambda e, b=b, gi=gi, half=half: e.tensor_tensor(
                        out=grow[gi][:, half * 512:(half + 1) * 512], in0=P.pb[b][:], in1=brow[gi][:, half * 512:(half + 1) * 512],
                        op=ALU.add), reads=[("pb", b), ("brow", gi)], writes=[("grow", gi)])
                S_.op("sp", lambda e, gi=gi: e.dma_start(out=GROW[gi], in_=grow[gi][:]), reads=[("grow", gi)],
                      dma=True, semkey="grow%d" % gi)
        for jj, go in ((1, 0), (3, 8)):
            S_.op("dve", lambda e, jj=jj, go=go: e.scalar_tensor_tensor(
                out=modc[:, jj * 8:(jj + 1) * 8], in0=modc[:, jj * 8:(jj + 1) * 8], scalar=1.0, in1=gcol[:, go:go + 8],
                op0=ALU.add, op1=ALU.mult), reads=["modc", "gcol"], writes=["modc"])
        S_.op("sp", lambda e: e.dma_start(out=MODC, in_=modc[:]), reads=["modc"], dma=True, semkey="modc")
        P.finish()
    nc.all_engine_barrier()
    if upto < 1:
        return nc, semstack

    with Phase(nc, semstack, "p1") as P:
        S_ = P.S
        ident = P.sb("ident", [128, 128], BF16)
        bones = P.sb("bones", [128, 128], BF16)
        modc = P.sb("modc", [128, 32], F32)
        cw = P.sb("cw", [128, 48], F32)
        epsT = P.sb("eps", [128, 1], F32)
        win = P.sb("win", [128, 8, INW], BF16)
        S_.op("pool", lambda e: e.dma_start(out=ident[:], in_=ident_in), writes=["ident"], dma=True, semkey="ident")
        S_.op("pool", lambda e: e.dma_start(out=bones[:], in_=bones_in), writes=["bones"], dma=True, semkey="bones")
        S_.op("sp", lambda e: e.dma_start(out=modc[:], in_=MODC), writes=["modc"], dma=True, semkey="modc")
        S_.op("sp", lambda e: e.dma_start(out=cw[:], in_=cw_col), writes=["cw"], dma=True, semkey="cw")
        S_.op("dve", lambda e: e.memset(epsT[:], EPS), writes=["eps"])
        winv = w_in.rearrange("(k p) f -> p k f", p=128)
        for k in range(8):
            S_.op("pool", lambda e, k=k: e.dma_start(out=win[:, k, :], in_=winv[:, k, :]), writes=[("win", k)],
                  dma=True, semkey="win%d" % k)
        WIN = [("win", k) for k in range(8)]
        xt = [P.sb("xt%d" % i, [128, 4, D], F32) for i in range(2)]
        junk = P.sb("junk", [128, D], BF16)
        ss2 = [P.sb("ss%d" % i, [128, 4], F32) for i in range(2)]
        rstd2 = [P.sb("rstd%d" % i, [128, 4], F32) for i in range(2)]
        xs2 = [P.sb("xs%d" % i, [128, 4, D], BF16) for i in range(2)]
        hT2 = [P.sb("hT%d" % i, [128, 8, 512], BF16) for i in range(2)]
        NSTQ = 6
        stq = [P.sb("stq%d" % i, [128, 4, 512], BF16) for i in range(NSTQ)]
        cin = P.sb("cin", [128, 12, 515], F32)
        NROT = 5
        acc3 = [P.sb("acc%d" % i, [128, 512], F32) for i in range(NROT)]
        slu3 = [P.sb("slu%d" % i, [128, 512], F32) for i in range(NROT)]
        sq3 = [P.sb("sq%d" % i, [128, 512], BF16) for i in range(NROT)]
        rs3 = [P.sb("rs%d" % i, [128, 512], F32) for i in range(NROT)]
        rot = [0]
        stb = P.sb("stb", [128, 4, 16], F32)
        xv = x.rearrange("(g j p) d -> g p j d", j=4, p=128)
        S_.op("pool", lambda e: e.memset(cin[:], 0.0), writes=["cin"])
        nst = [0]

        def stage():
            i = nst[0] % NSTQ
            nst[0] += 1
            return i

        NROT2 = NROT

        def norm_item(g):
            xs_ = g % 2
            ss, rstd, xs, hT = ss2[xs_], rstd2[xs_], xs2[xs_], hT2[xs_]
            KSS, KRS = ("ss", xs_), ("rstd", xs_)
            S_.op("sp", lambda e: e.dma_start(out=xt[xs_][:], in_=xv[g]), writes=[("xt", xs_)], dma=True, semkey="xt%d" % xs_)
            S_.op("pool", lambda e: e.memset(ss[:], 0.0), writes=[KSS])
            yield
            for j in range(4):
                S_.op("act", lambda e, j=j: e.activation(out=junk[:], in_=xt[xs_][:, j, :], func=AF.Square, accum_out=ss[:, j:j + 1]),
                      reads=[("xt", xs_), KSS], writes=[KSS, "junk"])
            S_.op("act", lambda e: e.activation(out=rstd[:], in_=ss[:], func=AF.Sqrt, bias=epsT[:], scale=1.0 / D),
                  reads=[KSS, "eps"], writes=[KRS])
            yield
            S_.op("dve", lambda e: e.reciprocal(out=rstd[:], in_=rstd[:]), reads=[KRS], writes=[KRS])
            for j in range(4):
                S_.op("dve", lambda e, j=j: e.tensor_scalar(out=xs[:, j, :], in0=xt[xs_][:, j, :], scalar1=rstd[:, j:j + 1], scalar2=None, op0=ALU.mult),
                      reads=[("xt", xs_), KRS], writes=[("xs", xs_, j)])
            yield
            pend = None
            for c2 in range(5):
                if c2 < 4:
                    b = P.bank()
                    pbf = P.pb[b][:].bitcast(BF16)
                    for cc in range(2):
                        c = c2 * 2 + cc
                        for j in range(4):
                            S_.op("pe", lambda e, pbf=pbf, cc=cc, c=c, j=j: e.transpose(
                                pbf[:, cc * 512 + j * 128: cc * 512 + (j + 1) * 128], xs[:, j, c * 128:(c + 1) * 128], ident[:]),
                                reads=[("xs", xs_, j), "ident"], writes=[("pb", b)])
                if pend is not None:
                    pb_, pbf_, pc2 = pend
                    for cc in range(2):
                        c = pc2 * 2 + cc
                        S_.op("act", lambda e, pbf_=pbf_, cc=cc, c=c: e.activation(
                            out=hT[:, c, :], in_=pbf_[:, cc * 512:(cc + 1) * 512], func=AF.Identity,
                            bias=modc[:, c:c + 1], scale=modc[:, 8 + c:9 + c]),
                            reads=[("pb", pb_), "modc"], writes=[("hT", xs_, c)])
                pend = (b, pbf, c2) if c2 < 4 else None
                yield

        def proj_mm(g, fc):
            xs_ = g % 2
            hT = hT2[xs_]
            b = P.bank()
            for k in range(8):
                S_.op("pe", lambda e, k=k: e.matmul(
                    P.pb[b][:], lhsT=win[:, k, fc * 128:(fc + 1) * 128], rhs=hT[:, k, :], start=(k == 0), stop=(k == 7)),
                    reads=[("win", k), ("hT", xs_, k)], writes=[("pb", b)])
            return b

        def store(dst, si, g):
            S_.op("sp", lambda e: e.dma_start(
                out=dst.rearrange("(c p) s -> p c s", p=128)[:, :, g * 512:(g + 1) * 512], in_=stq[si][:]),
                reads=[("stq", si, i) for i in range(4)], dma=True, semkey="stq%d" % si)

        def qk_item(g, base, dst, si, i):
            b = proj_mm(g, base + i)
            yield
            if i % 2 == 0:
                S_.op("act", lambda e: e.activation(out=stq[si][:, i, :], in_=P.pb[b][:], func=AF.Identity),
                      reads=[("pb", b)], writes=[("stq", si, i)])
            else:
                S_.op("dve", lambda e: e.tensor_copy(out=stq[si][:, i, :], in_=P.pb[b][:]),
                      reads=[("pb", b)], writes=[("stq", si, i)])
            if i == 3:
                store(dst, si, g)

        def v_item(g, si, j):
            xs_ = g % 2
            hT = hT2[xs_]
            b = P.bank()
            for k in range(8):
                S_.op("pe", lambda e, k=k: e.matmul(
                    P.pb[b][:], lhsT=hT[:, k, j * 128:(j + 1) * 128], rhs=win[:, k, 1024:1536], start=(k == 0), stop=(k == 7)),
                    reads=[("win", k), ("hT", xs_, k)], writes=[("pb", b)])
            yield
            S_.op("act", lambda e: e.activation(out=stq[si][:, j, :], in_=P.pb[b][:], func=AF.Identity),
                  reads=[("pb", b)], writes=[("stq", si, j)])
            if j == 3:
                S_.op("sp", lambda e: e.dma_start(out=VV.rearrange("(g j p) f -> g p j f", j=4, p=128)[g], in_=stq[si][:]),
                      reads=[("stq", si, i) for i in range(4)], dma=True, semkey="stq%d" % si)

        def z_item(g, si, i):
            b = proj_mm(g, 24 + i)
            yield
            S_.op("act", lambda e: e.activation(out=stq[si][:, i, :], in_=P.pb[b][:], func=AF.Silu),
                  reads=[("pb", b)], writes=[("stq", si, i)])
            if i == 3:
                store(ZS, si, g)

        def ba_item(g):
            xs_ = g % 2
            hT = hT2[xs_]
            b = P.bank()
            for j in range(4):
                for k in range(8):
                    S_.op("pe", lambda e, k=k, j=j: e.matmul(
                        P.pb[b][:, j * 16:(j + 1) * 16], lhsT=hT[:, k, j * 128:(j + 1) * 128], rhs=win[:, k, 3584:3600],
                        start=(k == 0), stop=(k == 7)), reads=[("win", k), ("hT", xs_, k)], writes=[("pb", b)])
            yield
            S_.op("dve", lambda e: e.tensor_copy(out=stb[:].rearrange("p j f -> p (j f)"), in_=P.pb[b][:, 0:64]),
                  reads=[("pb", b)], writes=["stb"])
            S_.op("sp", lambda e: e.dma_start(out=BA.rearrange("(g j p) f -> g p j f", j=4, p=128)[g], in_=stb[:]),
                  reads=["stb"], dma=True, semkey="stb")

        def delta_item(g, grp, dst, si, i):
            ci = grp * 4 + i
            b = proj_mm(g, 12 + ci)
            yield
            S_.op("act", lambda e: e.activation(out=cin[:, ci, 3:515], in_=P.pb[b][:], func=AF.Identity),
                  reads=[("pb", b)], writes=[("cin", ci)])
            yield
            ri = rot[0] % NROT2
            rot[0] += 1
            acc, slu, sq, rs = acc3[ri], slu3[ri], sq3[ri], rs3[ri]
            KA, KSL, KSQ, KR = ("acc", ri), ("slu", ri), ("sq", ri), ("rs", ri)
            S_.op("dve", lambda e: e.tensor_scalar(out=acc[:], in0=cin[:, ci, 0:512], scalar1=cw[:, ci * 4:ci * 4 + 1], scalar2=None, op0=ALU.mult),
                  reads=[("cin", ci), "cw"], writes=[KA])
            for t in range(1, 4):
                S_.op("dve", lambda e, t=t: e.scalar_tensor_tensor(
                    out=acc[:], in0=cin[:, ci, t:t + 512], scalar=cw[:, ci * 4 + t:ci * 4 + t + 1], in1=acc[:],
                    op0=ALU.mult, op1=ALU.add), reads=[("cin", ci), "cw", KA], writes=[KA])
            S_.op("pool", lambda e: e.tensor_copy(out=cin[:, ci, 0:3], in_=cin[:, ci, 512:515]), reads=[("cin", ci)], writes=[("cin", ci)])
            yield
            if grp == 2:
                S_.op("act", lambda e: e.activation(out=stq[si][:, i, :], in_=acc[:], func=AF.Silu), reads=[KA], writes=[("stq", si, i)])
                if i == 3:
                    store(dst, si, g)
                return
            S_.op("act", lambda e: e.activation(out=slu[:], in_=acc[:], func=AF.Silu), reads=[KA], writes=[KSL])
            S_.op("pool", lambda e: e.tensor_tensor(out=sq[:], in0=slu[:], in1=slu[:], op=ALU.mult), reads=[KSL], writes=[KSQ])
            yield
            b2 = P.bank()
            S_.op("pe", lambda e: e.matmul(P.pb[b2][:], lhsT=bones[:], rhs=sq[:], start=True, stop=True), reads=["bones", KSQ], writes=[("pb", b2)])
            yield
            S_.op("act", lambda e: e.activation(out=rs[:], in_=P.pb[b2][:], func=AF.Ln, bias=epsT[:], scale=1.0), reads=[("pb", b2), "eps"], writes=[KR])
            S_.op("act", lambda e: e.activation(out=rs[:], in_=rs[:], func=AF.Exp, scale=-0.5), reads=[KR], writes=[KR])
            yield
            scl = 0.125 if grp == 0 else 1.0
            S_.op("dve", lambda e: e.scalar_tensor_tensor(out=stq[si][:, i, :], in0=slu[:], scalar=scl, in1=rs[:], op0=ALU.mult, op1=ALU.mult),
                  reads=[KSL, KR], writes=[("stq", si, i)])
            if i == 3:
                store(dst, si, g)

        def p1_items():
            for g in range(NG):
                if g + 1 < NG:
                    yield norm_item(g + 1)
                si = stage()
                for i in range(4):
                    yield qk_item(g, 0, QT, si, i)
                si = stage()
                for i in range(4):
                    yield qk_item(g, 4, KT, si, i)
                si = stage()
                for j in range(4):
                    yield v_item(g, si, j)
                for grp, dst in ((0, QD), (1, KD), (2, VD)):
                    si = stage()
                    for i in range(4):
                        yield delta_item(g, grp, dst, si, i)
                si = stage()
                for i in range(4):
                    yield z_item(g, si, i)
                yield ba_item(g)

        for _ in norm_item(0):
            pass
        run_skewed(p1_items())
        P.finish()
    nc.all_engine_barrier()
    if upto < 2:
        return nc, semstack
    _phase2(nc, semstack, S, QT, KT, VV, tb_in, MIXT)
    nc.all_engine_barrier()
    if upto < 3:
        return nc, semstack
    cst = dict(identf=din("identf", [128, 128]), trif=din("trif", [128, 128]), bonesf=din("bonesf", [128, 128]),
               maskL=din("maskL", [128, 64]), maskU=din("maskU", [128, 64]), identc=din("identc", [128, 64]), ident=ident_in,
               alog=din("alog_row", [1, 8]), dtb=din("dtb_row", [1, 8]), dng=din("dng_row", [1, 64]))
    _phase3(nc, semstack, S, QD, KD, VD, ZS, BA, MIXT, cst)
    nc.all_engine_barrier()
    if upto < 4:
        return nc, semstack
    w_out = din("w_out", [D, D])
    w_gate = din("w_gate", [D, DFF])
    w_up = din("w_up", [D, DFF])
    w_down = din("w_down", [DFF, D])
    fg_row = din("fg_row", [1, D])
    X1 = dsc("X1", [S, D], F32)
    H2T = dsc("H2T", [D, S], BF16)
    _phase4a(nc, semstack, S, x, MIXT, w_out, GROW, MODC, ident_in, X1, H2T)
    nc.all_engine_barrier()
    if upto < 5:
        return nc, semstack
    _phase4b(nc, semstack, S, X1, H2T, w_gate, w_up, w_down, GROW, fg_row, out)
    return nc, semstack


P2DBG = {'mode': 9, 'strided': True}


def run_skewed(items):
    live = []
    it = iter(items)
    while True:
        nxt = next(it, None)
        if nxt is not None:
            live.append(nxt)
        if not live:
            break
        for gen in list(live):
            try:
                next(gen)
            except StopIteration:
                live.remove(gen)


def _phase2(nc, semstack, S, QT, KT, VV, tb_in, MIXT):
    NSB = S // 2048
    import os
    mode = int(os.environ.get('P2MODE', '9'))
    with Phase(nc, semstack, "p2") as P:
        S_ = P.S
        EB = P.sb("EB", [128, 24 * 256], BF16)
        ones = P.sb("ones", [128, 64], BF16)
        qt = P.sb("qt", [64, 8, 2048], BF16)
        kt = [P.sb("kt%d" % i, [64, 8, 2048], BF16) for i in range(2)]
        v1 = P.sb("v1", [128, 16, 512], BF16)
        v1p = P.sb("v1p", [128, 1, 512], BF16)
        v2 = P.sb("v2", [128, 16, 512], BF16)
        v2p = P.sb("v2p", [128, 4, 512], BF16)
        v3 = [P.sb("v3_%d" % i, [128, 16, 512], BF16) for i in range(2)]
        Et = [P.sb("E%d" % i, [128, 512], BF16) for i in range(4)]
        PT = [P.sb("PT%d" % i, [128, 512], BF16) for i in range(4)]
        accn = P.sb("accn", [128, 2048], F32)
        accd = P.sb("accd", [128, 2048], F32)
        mst = P.sb("mst", [128, 2048], BF16)
        S_.op("pool", lambda e: e.dma_start(out=EB[:], in_=tb_in), writes=["EB"], dma=True, semkey="tb")
        S_.op("act", lambda e: e.activation(out=EB[:], in_=EB[:], func=AF.Exp), reads=["EB"], writes=["EB"])
        S_.op("pool", lambda e: e.memset(ones[:], 1.0), writes=["ones"])
        cnt = [0, 0, 0]
        for N in range(NSB):
            cur, prv = N % 2, (N + 1) % 2
            t0 = N * 2048
            S_.op("sp", lambda e, t0=t0: e.dma_start(out=qt[:], in_=QT.rearrange("(c p) s -> p c s", p=64)[:, :, t0:t0 + 2048]),
                  writes=["qt"], dma=True, semkey="qt")
            S_.op("sp", lambda e, t0=t0, cur=cur: e.dma_start(out=kt[cur][:], in_=KT.rearrange("(c p) s -> p c s", p=64)[:, :, t0:t0 + 2048]),
                  writes=[("kt", cur)], dma=True, semkey="kt%d" % cur)
            Vsb = VV[t0:t0 + 2048, :]
            S_.op("sp", lambda e, Vsb=Vsb: e.dma_start(out=v1[:], in_=Vsb.rearrange("(n p) f -> p n f", p=128)),
                  writes=["v1"], dma=True, semkey="v1")
            for n_ in range(4):
                S_.op("sp", lambda e, Vsb=Vsb, n_=n_: e.dma_start(
                    out=v2[:, n_ * 4:(n_ + 1) * 4, :],
                    in_=Vsb[n_ * 512:(n_ + 1) * 512, :].rearrange("(p r) f -> p r f", r=4)),
                    writes=["v2"], dma=True, semkey="v2")
            S_.op("sp", lambda e, Vsb=Vsb, cur=cur: e.dma_start(out=v3[cur][:], in_=Vsb.rearrange("(p r) f -> p r f", r=16)),
                  writes=[("v3", cur)], dma=True, semkey="v3_%d" % cur)
            def unit(hp, br, gq, jj, nbk, dbk, N=N, cur=cur, prv=prv):
                if br == 0:
                    n = 4 * gq + jj
                    qs, st = n * 128, 1
                    vcur = (v1, n, "v1")
                    if n >= 1:
                        pk = (cur, (n - 1) * 128, (v1, n - 1, "v1"))
                    elif N >= 1:
                        pk = (prv, 15 * 128, (v1p, 0, "v1p"))
                    else:
                        pk = None
                elif br == 1:
                    n_, r = gq, jj
                    qs, st = n_ * 512 + r, 4
                    vcur = (v2, n_ * 4 + r, "v2")
                    if n_ >= 1:
                        pk = (cur, (n_ - 1) * 512 + r, (v2, (n_ - 1) * 4 + r, "v2"))
                    elif N >= 1:
                        pk = (prv, 3 * 512 + r, (v2p, r, "v2p"))
                    else:
                        pk = None
                else:
                    r = 4 * gq + jj
                    qs, st = r, 16
                    vcur = (v3[cur], r, ("v3", cur))
                    pk = (prv, r, (v3[prv], r, ("v3", prv))) if N >= 1 else None
                sbk = cnt[0] % 3
                ei = cnt[0] % 4
                cnt[0] += 1
                blks = ([(0,) + pk] if pk else []) + [(1, cur, qs, vcur)]
                for hl in range(2):
                    for (blk, slot, ks, _v) in blks:
                        S_.op("pe", lambda e, sbk=sbk, hl=hl, blk=blk, slot=slot, ks=ks, qs=qs, st=st, hp=hp: e.matmul(
                            P.pb[sbk][:, hl * 256 + blk * 128: hl * 256 + (blk + 1) * 128],
                            lhsT=kt[slot][:, 2 * hp + hl, ks:ks + 127 * st + 1:st],
                            rhs=qt[:, 2 * hp + hl, qs:qs + 127 * st + 1:st],
                            start=True, stop=True),
                            reads=[("kt", slot), "qt"], writes=[("pb", sbk)])
                yield
                c0 = 0 if pk else 128
                vw = lambda ap, c0=c0: ap.rearrange("p (h c) -> p h c", h=2)[:, :, c0:256]
                S_.op("act", lambda e, sbk=sbk, ei=ei, vw=vw: e.activation(
                    out=vw(Et[ei][:]), in_=vw(P.pb[sbk][:]), func=AF.Exp, scale=0.125),
                    reads=[("pb", sbk)], writes=[("E", ei)])
                yield
                eoff = (br * 8 + 2 * hp) * 256
                eng = "dve" if ei % 2 == 0 else "pool"
                S_.op(eng, lambda e, ei=ei, eoff=eoff, vw=vw: e.tensor_tensor(
                    out=vw(PT[ei][:]), in0=vw(Et[ei][:]), in1=vw(EB[:, eoff:eoff + 512]), op=ALU.mult),
                    reads=[("E", ei), "EB"], writes=[("PT", ei)])
                yield
                for hl in range(2):
                    h = 2 * hp + hl
                    for bi, (blk, slot, ks, (vt, vi, vkey)) in enumerate(blks):
                        fl = _kw(start=(bi == 0), stop=(bi == len(blks) - 1), tile_position=(0, hl * 64))
                        S_.op("pe", lambda e, nbk=nbk, hl=hl, jj=jj, vt=vt, vi=vi, h=h, ei=ei, blk=blk, fl=fl: e.matmul(
                            P.pb[nbk][hl * 64:(hl + 1) * 64, jj * 128:(jj + 1) * 128],
                            lhsT=vt[:, vi, h * 64:(h + 1) * 64],
                            rhs=PT[ei][:, hl * 256 + blk * 128: hl * 256 + (blk + 1) * 128], **fl),
                            reads=[vkey, ("PT", ei)], writes=[("pb", nbk)])
                        S_.op("pe", lambda e, dbk=dbk, hl=hl, jj=jj, ei=ei, blk=blk, fl=fl: e.matmul(
                            P.pb[dbk][hl * 64:(hl + 1) * 64, jj * 128:(jj + 1) * 128],
                            lhsT=ones[:, 0:64],
                            rhs=PT[ei][:, hl * 256 + blk * 128: hl * 256 + (blk + 1) * 128], **fl),
                            reads=["ones", ("PT", ei)], writes=[("pb", dbk)])
                if jj < 3:
                    return
                yield
                for bk, acc, akey in ((nbk, accn, "accn"), (dbk, accd, "accd")):
                    if br == 0:
                        S_.op("act", lambda e, bk=bk, acc=acc, gq=gq: e.activation(
                            out=acc[:, gq * 512:(gq + 1) * 512], in_=P.pb[bk][:], func=AF.Identity),
                            reads=[("pb", bk)], writes=[akey])
                    else:
                        if br == 1:
                            oap = acc[:, gq * 512:(gq + 1) * 512].rearrange("p (i r) -> p r i", r=4)
                        else:
                            oap = acc[:].rearrange("p (i r) -> p r i", r=16)[:, 4 * gq:4 * gq + 4, :]
                        S_.op("dve", lambda e, bk=bk, oap=oap: e.tensor_tensor(
                            out=oap, in0=P.pb[bk][:].rearrange("p (r i) -> p r i", r=4), in1=oap, op=ALU.add),
                            reads=[("pb", bk), akey], writes=[akey])

            def finalize(hp, t0=t0):
                for _ in range(6):
                    yield
                S_.op("dve", lambda e: e.reciprocal(out=accd[:], in_=accd[:]), reads=["accd"], writes=["accd"])
                S_.op("dve", lambda e: e.tensor_tensor(out=mst[:], in0=accn[:], in1=accd[:], op=ALU.mult),
                      reads=["accn", "accd"], writes=["mst"])
                S_.op("sp", lambda e, hp=hp, t0=t0: e.dma_start(out=MIXT[hp * 128:(hp + 1) * 128, t0:t0 + 2048], in_=mst[:]),
                      reads=["mst"], dma=True, semkey="mst")

            def items():
                for hp in range(4):
                    for br in range(3):
                        for gq in range(4):
                            nbk = 3 + cnt[1] % 2
                            dbk = 5 + cnt[1] % 2
                            cnt[1] += 1
                            for jj in range(4):
                                yield unit(hp, br, gq, jj, nbk, dbk)
                    yield finalize(hp)

            run_skewed(items())
            if N + 1 < NSB:
                S_.op("pool", lambda e: e.tensor_copy(out=v1p[:, 0, :], in_=v1[:, 15, :]), reads=["v1"], writes=["v1p"])
                S_.op("pool", lambda e: e.tensor_copy(out=v2p[:], in_=v2[:, 12:16, :]), reads=["v2"], writes=["v2p"])
        P.finish()


def host_consts(inputs):
    import math
    rel_bias = np.asarray(inputs["rel_bias"], np.float32)
    k = np.arange(128)[:, None]
    q = np.arange(128)[None, :]
    steps_prev = q + 128 - k
    steps_cur = q - k
    tb = np.full((128, 3, 8, 2, 128), -30000.0, np.float32)

    def bucket(dist):
        dist = np.asarray(dist, np.int64)
        max_exact = 16
        dist_f = np.maximum(dist, 1).astype(np.float32)
        lg = (np.log(dist_f / np.float32(max_exact)) / np.float32(math.log(2048 / max_exact))
              * np.float32(32 - max_exact)).astype(np.float32)
        large = max_exact + lg.astype(np.int32)
        return np.where(dist < max_exact, dist, np.minimum(large, 31)).astype(np.int64)

    for br, d in enumerate((1, 4, 16)):
        for blk, steps in ((0, steps_prev), (1, steps_cur)):
            valid = (steps >= 0) & (steps <= 128)
            bk = bucket(np.maximum(steps, 0) * d)
            for h in range(8):
                vals = rel_bias[bk, h]
                tb[:, br, h, blk, :] = np.where(valid, vals, np.float32(-30000.0))
    return {"tb": np.ascontiguousarray(tb.reshape(128, 24 * 256))}


def _phase4a(nc, semstack, S, x, MIXT, w_out, GROW, MODC, ident_in, X1, H2T):
    NG = S // 512
    with Phase(nc, semstack, "p4a") as P:
        S_ = P.S
        ident = P.sb("ident", [128, 128], BF16)
        modc = P.sb("modc", [128, 32], F32)
        epsT = P.sb("eps", [128, 1], F32)
        g1 = P.sb("g1", [128, D], F32)
        wo = P.sb("wo", [128, 8, D], BF16)
        S_.op("pool", lambda e: e.dma_start(out=ident[:], in_=ident_in), writes=["ident"], dma=True, semkey="ident")
        S_.op("sp", lambda e: e.dma_start(out=modc[:], in_=MODC), writes=["modc"], dma=True, semkey="modc")
        S_.op("sp", lambda e: e.dma_start(out=g1[:], in_=GROW[0]), writes=["g1"], dma=True, semkey="g1")
        S_.op("pool", lambda e: e.dma_start(out=wo[:], in_=w_out.rearrange("(k p) f -> p k f", p=128)), writes=["wo"],
              dma=True, semkey="wo")
        S_.op("dve", lambda e: e.memset(epsT[:], EPS), writes=["eps"])
        for k in range(8):
            S_.op("dve", lambda e, k=k: e.tensor_tensor(out=wo[:, k, :], in0=wo[:, k, :], in1=g1[:], op=ALU.mult),
                  reads=["wo", "g1"], writes=["wo"])
        xt = [P.sb("xt%d" % i, [128, 4, D], F32) for i in range(2)]
        mt = [P.sb("mt%d" % i, [128, 8, 512], BF16) for i in range(2)]
        junk = P.sb("junk", [128, D], BF16)
        ss2 = [P.sb("ss%d" % i, [128, 4], F32) for i in range(2)]
        rstd2 = [P.sb("rstd%d" % i, [128, 4], F32) for i in range(2)]
        xs2 = [P.sb("xs%d" % i, [128, 4, D], BF16) for i in range(2)]
        hst = [P.sb("hst%d" % i, [128, 8, 512], BF16) for i in range(2)]
        xv = x.rearrange("(g j p) d -> g p j d", j=4, p=128)
        x1v = X1.rearrange("(g j p) d -> g p j d", j=4, p=128)

        def load_item(g):
            sl = g % 2
            S_.op("sp", lambda e: e.dma_start(out=xt[sl][:], in_=xv[g]), writes=[("xt", sl)], dma=True, semkey="xt%d" % sl)
            S_.op("sp", lambda e: e.dma_start(out=mt[sl][:], in_=MIXT.rearrange("(c p) s -> p c s", p=128)[:, :, g * 512:(g + 1) * 512]),
                  writes=[("mt", sl)], dma=True, semkey="mt%d" % sl)
            yield

        def y_item(g, j, half):
            sl = g % 2
            b = P.bank()
            for k in range(8):
                S_.op("pe", lambda e, k=k: e.matmul(
                    P.pb[b][:], lhsT=mt[sl][:, k, j * 128:(j + 1) * 128], rhs=wo[:, k, half * 512:(half + 1) * 512],
                    start=(k == 0), stop=(k == 7)), reads=[("mt", sl), "wo"], writes=[("pb", b)])
            yield
            S_.op("dve", lambda e: e.tensor_tensor(
                out=xt[sl][:, j, half * 512:(half + 1) * 512], in0=P.pb[b][:], in1=xt[sl][:, j, half * 512:(half + 1) * 512],
                op=ALU.add), reads=[("pb", b), ("xt", sl)], writes=[("xt", sl)])

        def norm_item(g):
            sl = g % 2
            ss, rstd, xs = ss2[sl], rstd2[sl], xs2[sl]
            KSS, KRS = ("ss", sl), ("rstd", sl)
            S_.op("sp", lambda e: e.dma_start(out=x1v[g], in_=xt[sl][:]), reads=[("xt", sl)], dma=True, semkey="xt%d" % sl)
            S_.op("pool", lambda e: e.memset(ss[:], 0.0), writes=[KSS])
            for j in range(4):
                S_.op("act", lambda e, j=j: e.activation(out=junk[:], in_=xt[sl][:, j, :], func=AF.Square, accum_out=ss[:, j:j + 1]),
                      reads=[("xt", sl), KSS], writes=[KSS, "junk"])
            S_.op("act", lambda e: e.activation(out=rstd[:], in_=ss[:], func=AF.Sqrt, bias=epsT[:], scale=1.0 / D),
                  reads=[KSS, "eps"], writes=[KRS])
            yield
            S_.op("dve", lambda e: e.reciprocal(out=rstd[:], in_=rstd[:]), reads=[KRS], writes=[KRS])
            for j in range(4):
                S_.op("dve", lambda e, j=j: e.tensor_scalar(out=xs[:, j, :], in0=xt[sl][:, j, :], scalar1=rstd[:, j:j + 1], scalar2=None, op0=ALU.mult),
                      reads=[("xt", sl), KRS], writes=[("xs", sl, j)])
            yield
            pend = None
            for c2 in range(5):
                if c2 < 4:
                    b = P.bank()
                    pbf = P.pb[b][:].bitcast(BF16)
                    for cc in range(2):
                        c = c2 * 2 + cc
                        for j in range(4):
                            S_.op("pe", lambda e, pbf=pbf, cc=cc, c=c, j=j: e.transpose(
                                pbf[:, cc * 512 + j * 128: cc * 512 + (j + 1) * 128], xs[:, j, c * 128:(c + 1) * 128], ident[:]),
                                reads=[("xs", sl, j), "ident"], writes=[("pb", b)])
                if pend is not None:
                    pb_, pbf_, pc2 = pend
                    for cc in range(2):
                        c = pc2 * 2 + cc
                        S_.op("act", lambda e, pbf_=pbf_, cc=cc, c=c: e.activation(
                            out=hst[sl][:, c, :], in_=pbf_[:, cc * 512:(cc + 1) * 512], func=AF.Identity,
                            bias=modc[:, 16 + c:17 + c], scale=modc[:, 24 + c:25 + c]),
                            reads=[("pb", pb_), "modc"], writes=[("hst", sl)])
                pend = (b, pbf, c2) if c2 < 4 else None
                yield
            S_.op("sp", lambda e: e.dma_start(out=H2T.rearrange("(c p) s -> p c s", p=128)[:, :, g * 512:(g + 1) * 512], in_=hst[sl][:]),
                  reads=[("hst", sl)], dma=True, semkey="hst%d" % sl)

        def p4a_items():
            for g in range(NG):
                if g + 1 < NG:
                    yield load_item(g + 1)
                for j in range(4):
                    for half in range(2):
                        yield y_item(g, j, half)
                yield norm_item(g)

        for _ in load_item(0):
            pass
        run_skewed(p4a_items())
        P.finish()


def _phase4b(nc, semstack, S, X1, H2T, w_gate, w_up, w_down, GROW, fg_row, out):
    G = 512
    NJ = G // 128
    NG = S // G
    NF = DFF // 128
    with Phase(nc, semstack, "p4b") as P:
        S_ = P.S
        epsT = P.sb("eps", [128, 1], F32)
        fg = P.sb("fg", [128, D], F32)
        wg = P.sb("wg", [128, 8, DFF], BF16)
        wu = P.sb("wu", [128, 8, DFF], BF16)
        wd = P.sb("wd", [128, NF, D], BF16)
        S_.op("dve", lambda e: e.memset(epsT[:], EPS), writes=["eps"])
        S_.op("sp", lambda e: e.dma_start(out=fg[:], in_=fg_row.partition_broadcast(128)), writes=["fg"], dma=True, semkey="fg")
        for k in range(8):
            S_.op("pool", lambda e, k=k: e.dma_start(out=wg[:, k, :], in_=w_gate[k * 128:(k + 1) * 128, :]), writes=["wg"], dma=True, semkey="wg")
            S_.op("pool", lambda e, k=k: e.dma_start(out=wu[:, k, :], in_=w_up[k * 128:(k + 1) * 128, :]), writes=["wu"], dma=True, semkey="wu")
        xq = [P.sb("xq%d" % i, [128, NJ, D], F32) for i in range(2)]
        g2 = xq[1]
        S_.op("sp", lambda e: e.dma_start(out=g2[:, 0, :], in_=GROW[1]), writes=[("xq", 1)], dma=True, semkey="xq1")
        S_.op("pool", lambda e: e.dma_start(out=wd[:], in_=w_down.rearrange("(k p) f -> p k f", p=128)), writes=["wd"], dma=True, semkey="wd")
        for k in range(NF):
            S_.op("dve", lambda e, k=k: e.tensor_tensor(out=wd[:, k, :], in0=wd[:, k, :], in1=g2[:, 0, :], op=ALU.mult),
                  reads=["wd", ("xq", 1)], writes=["wd"])
        ht = [P.sb("ht%d" % i, [128, 8, G], BF16) for i in range(2)]
        aT = P.sb("aT", [128, NF, G], BF16)
        ss = P.sb("ss", [128, NJ], F32)
        rstd = P.sb("rstd", [128, NJ], F32)
        x1v = X1.rearrange("(g j p) d -> g p j d", j=NJ, p=128)
        ov = out.rearrange("(g j p) d -> g p j d", j=NJ, p=128)
        for g in range(NG):
            sl = g % 2
            S_.op("sp", lambda e, g=g, sl=sl: e.dma_start(out=xq[sl][:], in_=x1v[g]), writes=[("xq", sl)], dma=True, semkey="xq%d" % sl)
            S_.op("sp", lambda e, g=g, sl=sl: e.dma_start(out=ht[sl][:], in_=H2T.rearrange("(c p) s -> p c s", p=128)[:, :, g * G:(g + 1) * G]),
                  writes=[("ht", sl)], dma=True, semkey="ht%d" % sl)
            for fc in range(NF):
                ba, bb = P.bank(), P.bank()
                for (bk, w, wk) in ((ba, wg, "wg"), (bb, wu, "wu")):
                    for k in range(8):
                        S_.op("pe", lambda e, bk=bk, w=w, k=k, fc=fc, sl=sl: e.matmul(
                            P.pb[bk][:, 0:G], lhsT=w[:, k, fc * 128:(fc + 1) * 128], rhs=ht[sl][:, k, :], start=(k == 0), stop=(k == 7)),
                            reads=[wk, ("ht", sl)], writes=[("pb", bk)])
                S_.op("act", lambda e, ba=ba, fc=fc: e.activation(out=aT[:, fc, :], in_=P.pb[ba][:, 0:G], func=AF.Silu),
                      reads=[("pb", ba)], writes=[("aT", fc)])
                S_.op("dve", lambda e, bb=bb, fc=fc: e.tensor_tensor(out=aT[:, fc, :], in0=P.pb[bb][:, 0:G], in1=aT[:, fc, :], op=ALU.mult),
                      reads=[("pb", bb), ("aT", fc)], writes=[("aT", fc)])
            for j in range(NJ):
                for half in range(2):
                    b = P.bank()
                    for k in range(NF):
                        S_.op("pe", lambda e, b=b, k=k, j=j, half=half: e.matmul(
                            P.pb[b][:], lhsT=aT[:, k, j * 128:(j + 1) * 128], rhs=wd[:, k, half * 512:(half + 1) * 512],
                            start=(k == 0), stop=(k == NF - 1)), reads=[("aT", k), "wd"], writes=[("pb", b)])
                    S_.op("dve", lambda e, b=b, j=j, half=half, sl=sl: e.tensor_tensor(
                        out=xq[sl][:, j, half * 512:(half + 1) * 512], in0=P.pb[b][:], in1=xq[sl][:, j, half * 512:(half + 1) * 512],
                        op=ALU.add), reads=[("pb", b), ("xq", sl)], writes=[("xq", sl)])
            S_.op("pool", lambda e: e.memset(ss[:], 0.0), writes=["ss"])
            for j in range(NJ):
                S_.op("act", lambda e, j=j, sl=sl: e.activation(out=aT[:, 0:2, :].rearrange("p a c -> p (a c)"), in_=xq[sl][:, j, :], func=AF.Square, accum_out=ss[:, j:j + 1]),
                      reads=[("xq", sl), "ss"], writes=["ss", ("aT", 0), ("aT", 1)])
            S_.op("act", lambda e: e.activation(out=rstd[:], in_=ss[:], func=AF.Sqrt, bias=epsT[:], scale=1.0 / D),
                  reads=["ss", "eps"], writes=["rstd"])
            S_.op("dve", lambda e: e.reciprocal(out=rstd[:], in_=rstd[:]), reads=["rstd"], writes=["rstd"])
            for j in range(NJ):
                S_.op("dve", lambda e, j=j, sl=sl: e.scalar_tensor_tensor(
                    out=xq[sl][:, j, :], in0=xq[sl][:, j, :], scalar=rstd[:, j:j + 1], in1=fg[:], op0=ALU.mult, op1=ALU.mult),
                    reads=[("xq", sl), "rstd", "fg"], writes=[("xq", sl)])
            S_.op("sp", lambda e, g=g, sl=sl: e.dma_start(out=ov[g], in_=xq[sl][:]), reads=[("xq", sl)], dma=True, semkey="xq%d" % sl)
        P.finish()


def _phase3(nc, semstack, S, QD, KD, VD, ZS, BA, MIXT, cst):
    NG = S // 512
    with Phase(nc, semstack, "p3") as P:
        S_ = P.S
        sb = P.sb
        ident = sb("ident", [128, 128], BF16)
        identf = sb("identf", [128, 128], F32)
        trif = sb("trif", [128, 128], F32)
        bonesf = sb("bonesf", [128, 128], F32)
        onesf = sb("onesf", [128, 128], F32)
        maskL = sb("maskL", [128, 64], F32)
        identc = sb("identc", [128, 64], F32)
        maskU = sb("maskU", [128, 64], F32)
        negA = sb("negA", [128, 8], F32)
        dtb = sb("dtb", [128, 8], F32)
        dng = sb("dng", [128, 64], F32)
        one1 = sb("one1", [128, 1], F32)
        epsT = sb("eps", [128, 1], F32)
        S_.op("pool", lambda e: e.dma_start(out=ident[:], in_=cst["ident"]), writes=["ident"], dma=True, semkey="ident")
        for nm, t in (("identf", identf), ("trif", trif), ("bonesf", bonesf), ("maskL", maskL), ("maskU", maskU), ("identc", identc)):
            S_.op("sp", lambda e, nm=nm, t=t: e.dma_start(out=t[:], in_=cst[nm]), writes=[nm], dma=True, semkey=nm)
        S_.op("sp", lambda e: e.dma_start(out=negA[:], in_=cst["alog"].partition_broadcast(128)), writes=["negA"], dma=True, semkey="negA")
        S_.op("sp", lambda e: e.dma_start(out=dtb[:], in_=cst["dtb"].partition_broadcast(128)), writes=["dtb"], dma=True, semkey="dtb")
        S_.op("sp", lambda e: e.dma_start(out=dng[:], in_=cst["dng"].partition_broadcast(128)), writes=["dng"], dma=True, semkey="dng")
        S_.op("pool", lambda e: e.memset(onesf[:], 1.0), writes=["onesf"])
        S_.op("pool", lambda e: e.memset(one1[:], 1.0), writes=["one1"])
        S_.op("pool", lambda e: e.memset(epsT[:], EPS), writes=["eps"])
        S_.op("act", lambda e: e.activation(out=negA[:], in_=negA[:], func=AF.Exp), reads=["negA"], writes=["negA"])
        S_.op("dve", lambda e: e.tensor_scalar(out=negA[:], in0=negA[:], scalar1=-1.0, scalar2=None, op0=ALU.mult),
              reads=["negA"], writes=["negA"])
        kd = [sb("kd%d" % i, [128, 4, 512], BF16) for i in range(2)]
        qd = [sb("qd%d" % i, [64, 8, 512], BF16) for i in range(2)]
        kd8 = [sb("kd8_%d" % i, [64, 8, 512], BF16) for i in range(2)]
        vd = [sb("vd%d" % i, [128, 4, 512], BF16) for i in range(2)]
        zs = [sb("zs%d" % i, [128, 4, 512], BF16) for i in range(2)]
        ba = [sb("ba%d" % i, [128, 4, 16], F32) for i in range(2)]
        mst = [sb("mst%d" % i, [128, 4, 512], BF16) for i in range(2)]
        y16 = sb("y16", [128, 16], F32)
        u16 = sb("u16", [128, 16], F32)
        beta = sb("beta", [128, 8], F32)
        gg = sb("gg", [128, 8], F32)
        gcl = sb("gcl", [128, 16], F32)
        eg2 = [sb("eg%d" % i, [128, 8], F32) for i in range(2)]
        kdsc = sb("kdsc", [128, 8], F32)
        be = sb("be", [128, 8], F32)
        offL = sb("offL", [128, 8], F32)
        decS2 = [sb("decS%d" % i, [64, 16], F32) for i in range(2)]
        kbg = sb("kbg", [128, 512], BF16)
        kdec2 = [sb("kdec%d" % i, [128, 512], BF16) for i in range(2)]
        bv = sb("bv", [128, 512], BF16)
        DG = sb("DG", [128, 8, 64], F32)
        tL = sb("tL", [128, 8, 64], F32)
        tU = sb("tU", [128, 8, 64], F32)
        Lb = sb("Lb", [128, 8, 64], F32)
        TTb = sb("TTb", [128, 8, 64], BF16)
        Ui = sb("Ui", [128, 8, 64], BF16)
        qkdT2 = [sb("qkdT%d" % i, [128, 8, 64], BF16) for i in range(2)]
        X = [sb("X%d" % i, [128, 8, 64], F32) for i in range(2)]
        Y = [sb("Y%d" % i, [128, 8, 64], F32) for i in range(2)]
        Q = [sb("Q%d" % i, [128, 8, 64], F32) for i in range(2)]
        uu2 = [sb("uu%d" % i, [128, 512], F32) for i in range(2)]
        wT2 = [sb("wT%d" % i, [64, 8, 128], BF16) for i in range(2)]
        vnew = sb("vnew", [128, 512], BF16)
        o1s = sb("o1s", [128, 512], F32)
        otm = sb("otm", [128, 512], F32)
        sqo = sb("sqo", [128, 512], F32)
        ssq = sb("ssq", [128, 8], F32)
        onb = sb("onb", [128, 512], BF16)
        Sf = sb("Sf", [64, 512], F32)
        S1 = sb("S1", [64, 512], F32)
        Sbf = sb("Sbf", [64, 512], BF16)
        S_.op("pool", lambda e: e.memset(Sf[:], 0.0), writes=["Sf"])
        S_.op("pool", lambda e: e.memset(Sbf[:], 0.0), writes=["Sbf"])

        def bc8(t):
            return lambda n: t.unsqueeze(2).to_broadcast([128, 8, n])

        bcnt = [0, 0]

        def bankA():
            bcnt[0] += 1
            return (bcnt[0] - 1) % 5

        def bankB():
            bcnt[1] += 1
            return 5 + (bcnt[1] - 1) % 3

        def A_tile(g, sl, j, c0, par):
            uu, wT, kdec, qkdT, eg, decS = uu2[par], wT2[par], kdec2[par], qkdT2[par], eg2[par], decS2[par]
            K_uu = ("uu", par)
            K_wT = ("wT", par)
            K_kdec = ("kdec", par)
            K_qkdT = ("qkdT", par)
            K_eg = ("eg", par)
            K_decS = ("decS", par)
            if j == 0:
                for nm, t, src, pp in (("kd", kd, KD, 128), ("qd", qd, QD, 64), ("kd8", kd8, KD, 64), ("vd", vd, VD, 128), ("zs", zs, ZS, 128)):
                    S_.op("sp", lambda e, t=t, src=src, g=g, sl=sl, pp=pp: e.dma_start(
                        out=t[sl][:], in_=src.rearrange("(c p) s -> p c s", p=pp)[:, :, g * 512:(g + 1) * 512]),
                        writes=[(nm, sl)], dma=True, semkey="%s%d" % (nm, sl))
                S_.op("sp", lambda e, g=g, sl=sl: e.dma_start(out=ba[sl][:], in_=BA.rearrange("(g j p) f -> g p j f", j=4, p=128)[g]),
                      writes=[("ba", sl)], dma=True, semkey="ba%d" % sl)
            S_.op("dve", lambda e, j=j, sl=sl: e.tensor_scalar(out=y16[:, 0:8], in0=ba[sl][:, j, 0:8], scalar1=-1.0, scalar2=None, op0=ALU.mult),
                  reads=[("ba", sl)], writes=["y16a"])
            yield
            S_.op("dve", lambda e, j=j, sl=sl: e.tensor_tensor(out=y16[:, 8:16], in0=ba[sl][:, j, 8:16], in1=dtb[:], op=ALU.add),
                  reads=[("ba", sl), "dtb"], writes=["y16b"])
            yield
            S_.op("act", lambda e: e.activation(out=u16[:], in_=y16[:], func=AF.Exp), reads=["y16a", "y16b"], writes=["u16"])
            yield
            S_.op("act", lambda e: e.activation(out=u16[:], in_=u16[:], func=AF.Ln, bias=one1[:], scale=1.0), reads=["u16", "one1"], writes=["u16"])
            yield
            S_.op("act", lambda e: e.activation(out=beta[:], in_=u16[:, 0:8], func=AF.Exp, scale=-1.0), reads=["u16"], writes=["beta"])
            yield
            S_.op("dve", lambda e: e.tensor_tensor(out=gg[:], in0=u16[:, 8:16], in1=negA[:], op=ALU.mult), reads=["u16", "negA"], writes=["gg"])
            yield
            b = bankA()
            yield
            S_.op("pe", lambda e, b=b: e.matmul(P.pb[b][:, 0:8], lhsT=trif[:], rhs=gg[:], start=True, stop=True),
                  reads=["trif", "gg"], writes=[("pb", b)])
            yield
            S_.op("pe", lambda e, b=b: e.matmul(P.pb[b][:, 8:16], lhsT=bonesf[:], rhs=gg[:], start=True, stop=True),
                  reads=["bonesf", "gg"], writes=[("pb", b)])
            yield
            for ch in range(2):
                S_.op("pe", lambda e, b=b, ch=ch: e.matmul(
                    P.pb[b][0:64, 16 + ch * 8:24 + ch * 8], lhsT=bonesf[:, ch * 64:(ch + 1) * 64],
                    rhs=gg[:], start=True, stop=True), reads=["bonesf", "gg"], writes=[("pb", b)])
            yield
            S_.op("dve", lambda e, b=b: e.tensor_copy(out=gcl[:], in_=P.pb[b][:, 0:16]), reads=[("pb", b)], writes=["gcl"])
            yield
            S_.op("act", lambda e, b=b: e.activation(out=decS[:], in_=P.pb[b][0:64, 16:32], func=AF.Exp), reads=[("pb", b)], writes=[K_decS])
            yield
            S_.op("act", lambda e: e.activation(out=eg[:], in_=gcl[:, 0:8], func=AF.Exp), reads=["gcl"], writes=[K_eg])
            yield
            S_.op("dve", lambda e: e.tensor_tensor(out=kdsc[:], in0=gcl[:, 8:16], in1=gcl[:, 0:8], op=ALU.subtract), reads=["gcl"], writes=["kdsc"])
            yield
            S_.op("act", lambda e: e.activation(out=kdsc[:], in_=kdsc[:], func=AF.Exp), reads=["kdsc"], writes=["kdsc"])
            yield
            S_.op("dve", lambda e: e.tensor_tensor(out=be[:], in0=beta[:], in1=eg[:], op=ALU.mult), reads=["beta", K_eg], writes=["be"])
            yield
            S_.op("dve", lambda e: e.tensor_tensor(out=offL[:], in0=gcl[:, 0:8], in1=u16[:, 0:8], op=ALU.subtract), reads=["gcl", "u16"], writes=["offL"])
            yield
            bk_ = bankA()
            yield
            bv_ = bk_
            yield
            for (off, src, key) in ((0, kd, "kd"), (512, vd, "vd")):
                pbf = P.pb[bk_][:].bitcast(BF16)
                for hp in range(4):
                    S_.op("pe", lambda e, pbf=pbf, hp=hp, src=src, sl=sl, c0=c0, off=off: e.transpose(
                        pbf[:, off + hp * 128:off + (hp + 1) * 128], src[sl][:, hp, c0:c0 + 128], ident[:]),
                        reads=[(key, sl), "ident"], writes=[("pb", bk_)])
            yield
            pk = P.pb[bk_][:].bitcast(BF16)[:, 0:512].rearrange("p (h d) -> p h d", h=8)
            yield
            pv = P.pb[bv_][:].bitcast(BF16)[:, 512:1024].rearrange("p (h d) -> p h d", h=8)
            yield
            S_.op("dve", lambda e, pk=pk: e.tensor_tensor(out=kbg[:].rearrange("p (h d) -> p h d", h=8), in0=pk, in1=bc8(be[:])(64), op=ALU.mult),
                  reads=[("pb", bk_), "be"], writes=["kbg"])
            yield
            S_.op("dve", lambda e, pk=pk: e.tensor_tensor(out=kdec[:].rearrange("p (h d) -> p h d", h=8), in0=pk, in1=bc8(kdsc[:])(64), op=ALU.mult),
                  reads=[("pb", bk_), "kdsc"], writes=[K_kdec])
            yield
            S_.op("dve", lambda e, pv=pv: e.tensor_tensor(out=bv[:].rearrange("p (h d) -> p h d", h=8), in0=pv, in1=bc8(beta[:])(64), op=ALU.mult),
                  reads=[("pb", bv_), "beta"], writes=["bv"])
            bKK, bQK = bankA(), bankA()
            for h in range(8):
                for ch in range(2):
                    kk = kd8[sl][:, h, c0 + ch * 64:c0 + ch * 64 + 64]
                    S_.op("pe", lambda e, h=h, ch=ch, kk=kk: e.matmul(
                        P.pb[bKK][ch * 64:(ch + 1) * 64, h * 64:(h + 1) * 64], lhsT=kk, rhs=kk, start=True, stop=True,
                        tile_position=(0, ch * 64)), reads=[("kd8", sl)], writes=[("pb", bKK)])
            yield
            for h in range(8):
                for ch in range(2):
                    kk = kd8[sl][:, h, c0 + ch * 64:c0 + ch * 64 + 64]
                    qq = qd[sl][:, h, c0 + ch * 64:c0 + ch * 64 + 64]
                    S_.op("pe", lambda e, h=h, ch=ch, kk=kk, qq=qq: e.matmul(
                        P.pb[bQK][ch * 64:(ch + 1) * 64, h * 64:(h + 1) * 64], lhsT=kk, rhs=qq, start=True, stop=True,
                        tile_position=(0, ch * 64)), reads=[("kd8", sl), ("qd", sl)], writes=[("pb", bQK)])
            yield
            S_.op("pool", lambda e: e.tensor_tensor(out=DG[:], in0=identc[:].unsqueeze(1).to_broadcast([128, 8, 64]),
                                                   in1=bc8(gcl[:, 0:8])(64), op=ALU.mult), reads=["identc", "gcl"], writes=["DG"])
            yield
            bG = bankA()
            S_.op("pe", lambda e: e.matmul(P.pb[bG][:], lhsT=bonesf[:], rhs=DG[:].rearrange("p h c -> p (h c)"), start=True, stop=True),
                  reads=["bonesf", "DG"], writes=[("pb", bG)])
            yield
            pG = P.pb[bG][:].rearrange("p (h c) -> p h c", h=8)
            S_.op("dve", lambda e: e.scalar_tensor_tensor(out=tL[:], in0=pG, scalar=-1.0, in1=bc8(offL[:])(64), op0=ALU.mult, op1=ALU.add),
                  reads=[("pb", bG), "offL"], writes=["tL"])
            S_.op("dve", lambda e: e.tensor_tensor(out=tU[:], in0=pG, in1=bc8(gcl[:, 0:8])(64), op=ALU.subtract),
                  reads=[("pb", bG), "gcl"], writes=["tU"])
            yield
            S_.op("pool", lambda e: e.tensor_tensor(out=tL[:], in0=tL[:], in1=maskL[:].unsqueeze(1).to_broadcast([128, 8, 64]), op=ALU.add),
                  reads=["tL", "maskL"], writes=["tL"])
            S_.op("pool", lambda e: e.tensor_tensor(out=tU[:], in0=tU[:], in1=maskU[:].unsqueeze(1).to_broadcast([128, 8, 64]), op=ALU.add),
                  reads=["tU", "maskU"], writes=["tU"])
            yield
            S_.op("act", lambda e: e.activation(out=Lb[:], in_=tL[:], func=AF.Exp), reads=["tL"], writes=["Lb"])
            S_.op("act", lambda e: e.activation(out=Ui[:], in_=tU[:], func=AF.Exp), reads=["tU"], writes=["Ui"])
            yield
            S_.op("dve", lambda e: e.scalar_tensor_tensor(out=X[0][:], in0=P.pb[bKK][:].rearrange("p (h c) -> p h c", h=8), scalar=-1.0, in1=Lb[:],
                                                         op0=ALU.mult, op1=ALU.mult), reads=[("pb", bKK), "Lb"], writes=[("X", 0)])
            S_.op("dve", lambda e: e.tensor_tensor(out=qkdT[:], in0=P.pb[bQK][:].rearrange("p (h c) -> p h c", h=8), in1=Ui[:], op=ALU.mult),
                  reads=[("pb", bQK), "Ui"], writes=[K_qkdT])
            yield
            bB = bankA()
            for h in range(8):
                for ch in range(2):
                    S_.op("pe", lambda e, h=h, ch=ch: e.matmul(
                        P.pb[bB][ch * 64:(ch + 1) * 64, h * 64:(h + 1) * 64], lhsT=X[0][ch * 64:(ch + 1) * 64, h, :],
                        rhs=identf[ch * 64:(ch + 1) * 64, ch * 64:(ch + 1) * 64], start=True, stop=True, tile_position=(ch * 64, ch * 64)),
                        reads=[("X", 0), "identf"], writes=[("pb", bB)])
            yield
            S_.op("act", lambda e: e.activation(out=Y[0][:].rearrange("p h c -> p (h c)"), in_=P.pb[bB][:], func=AF.Identity),
                  reads=[("pb", bB)], writes=[("Y", 0)])
            yield
            S_.op("pool", lambda e: e.tensor_tensor(out=Q[0][:], in0=Y[0][:], in1=identc[:].unsqueeze(1).to_broadcast([128, 8, 64]), op=ALU.add),
                  reads=[("Y", 0), "identc"], writes=[("Q", 0)])
            yield

            def quad_mm(bk, L, R, rk):
                for h in range(8):
                    for ch in range(2):
                        S_.op("pe", lambda e, h=h, ch=ch: e.matmul(
                            P.pb[bk][ch * 64:(ch + 1) * 64, h * 64:(h + 1) * 64], lhsT=L[ch * 64:(ch + 1) * 64, h, :],
                            rhs=R[ch * 64:(ch + 1) * 64, h, :], start=True, stop=True, tile_position=(ch * 64, ch * 64)),
                            reads=rk, writes=[("pb", bk)])

            for lv in range(1, 6):
                a, n = (lv - 1) % 2, lv % 2
                bX = bankA()
                quad_mm(bX, Y[a], X[a], [("X", a), ("Y", a)])
                yield
                S_.op("act", lambda e, n=n, bX=bX: e.activation(out=X[n][:].rearrange("p h c -> p (h c)"), in_=P.pb[bX][:], func=AF.Identity),
                      reads=[("pb", bX)], writes=[("X", n)])
                if lv < 5:
                    bY = bankA()
                    quad_mm(bY, X[a], Y[a], [("X", a), ("Y", a)])
                    yield
                    S_.op("dve", lambda e, n=n, bY=bY: e.tensor_copy(out=Y[n][:].rearrange("p h c -> p (h c)"), in_=P.pb[bY][:]),
                          reads=[("pb", bY)], writes=[("Y", n)])
                yield
                bQ = bankA()
                quad_mm(bQ, X[n], Q[a], [("X", n), ("Q", a)])
                yield
                S_.op("dve", lambda e, n=n, a=a, bQ=bQ: e.tensor_tensor(
                    out=Q[n][:].rearrange("p h c -> p (h c)"), in0=P.pb[bQ][:], in1=Q[a][:].rearrange("p h c -> p (h c)"), op=ALU.add),
                    reads=[("pb", bQ), ("Q", a)], writes=[("Q", n)])
                yield
            S_.op("act", lambda e: e.activation(out=TTb[:].rearrange("p h c -> p (h c)"), in_=Q[1][:].rearrange("p h c -> p (h c)"), func=AF.Identity),
                  reads=[("Q", 1)], writes=["TTb"])
            yield
            bu = bankA()
            for h in range(8):
                for ch in range(2):
                    S_.op("pe", lambda e, h=h, ch=ch: e.matmul(
                        P.pb[bu][ch * 64:(ch + 1) * 64, h * 64:(h + 1) * 64], lhsT=TTb[ch * 64:(ch + 1) * 64, h, :],
                        rhs=bv[ch * 64:(ch + 1) * 64, h * 64:(h + 1) * 64], start=True, stop=True, tile_position=(ch * 64, ch * 64)),
                        reads=["TTb", "bv"], writes=[("pb", bu)])
            yield
            S_.op("act", lambda e: e.activation(out=uu[:], in_=P.pb[bu][:], func=AF.Identity), reads=[("pb", bu)], writes=[K_uu])
            bw = [bankA(), bankA()]
            for h in range(8):
                for ch in range(2):
                    S_.op("pe", lambda e, h=h, ch=ch, bw=bw: e.matmul(
                        P.pb[bw[ch]][0:64, h * 64:(h + 1) * 64],
                        lhsT=kbg[ch * 64:(ch + 1) * 64, h * 64:(h + 1) * 64], rhs=TTb[ch * 64:(ch + 1) * 64, h, :],
                        start=True, stop=True, tile_position=(ch * 64, 0)), reads=["TTb", "kbg"], writes=[("pb", bw[ch])])
            yield
            for ch in range(2):
                S_.op("dve", lambda e, bw=bw, ch=ch: e.tensor_copy(out=wT[:, :, ch * 64:(ch + 1) * 64],
                                                                 in_=P.pb[bw[ch]][0:64, :].rearrange("p (h c) -> p h c", h=8)),
                      reads=[("pb", bw[ch])], writes=[K_wT])

        def B_tile(g, sl, j, c0, par):
            uu, wT, kdec, qkdT, eg, decS = uu2[par], wT2[par], kdec2[par], qkdT2[par], eg2[par], decS2[par]
            K_uu = ("uu", par)
            K_wT = ("wT", par)
            K_kdec = ("kdec", par)
            K_qkdT = ("qkdT", par)
            K_eg = ("eg", par)
            K_decS = ("decS", par)
            for ch in range(2):
                p0 = ch * 64
                bvn, bo1, bo2, bs = bankB(), bankB(), bankB(), bankB()
                for h in range(8):
                    S_.op("pe", lambda e, h=h, p0=p0, ch=ch, bvn=bvn: e.matmul(
                        P.pb[bvn][p0:p0 + 64, h * 64:(h + 1) * 64], lhsT=wT[:, h, ch * 64:(ch + 1) * 64],
                        rhs=Sbf[:, h * 64:(h + 1) * 64], start=True, stop=True, tile_position=(0, p0)),
                        reads=[K_wT, "Sbf"], writes=[("pb", bvn)])
                for h in range(8):
                    S_.op("pe", lambda e, h=h, p0=p0, ch=ch, bo1=bo1, sl=sl, c0=c0: e.matmul(
                        P.pb[bo1][p0:p0 + 64, h * 64:(h + 1) * 64], lhsT=qd[sl][:, h, c0 + p0:c0 + p0 + 64],
                        rhs=Sbf[:, h * 64:(h + 1) * 64], start=True, stop=True, tile_position=(0, p0)),
                        reads=[("qd", sl), "Sbf"], writes=[("pb", bo1)])
                S_.op("dve", lambda e, p0=p0, bvn=bvn: e.tensor_tensor(out=vnew[p0:p0 + 64, :], in0=uu[p0:p0 + 64, :], in1=P.pb[bvn][p0:p0 + 64, :],
                                                                    op=ALU.subtract), reads=[K_uu, ("pb", bvn)], writes=[("vnew", ch)])
                S_.op("dve", lambda e, p0=p0, bo1=bo1: e.tensor_tensor(
                    out=o1s[p0:p0 + 64, :].rearrange("p (h d) -> p h d", h=8), in0=P.pb[bo1][p0:p0 + 64, :].rearrange("p (h d) -> p h d", h=8),
                    in1=eg[p0:p0 + 64, :].unsqueeze(2).to_broadcast([64, 8, 64]), op=ALU.mult),
                    reads=[K_eg, ("pb", bo1)], writes=[("o1s", ch)])
                S_.op("pool", lambda e, ch=ch: e.tensor_tensor(
                    out=S1[:].rearrange("p (a d) -> p a d", a=8), in0=Sf[:].rearrange("p (a d) -> p a d", a=8),
                    in1=decS[:, ch * 8:ch * 8 + 8].unsqueeze(2).to_broadcast([64, 8, 64]), op=ALU.mult),
                    reads=["Sf", K_decS], writes=["S1"])
                for h in range(8):
                    S_.op("pe", lambda e, h=h, p0=p0, ch=ch, bs=bs: e.matmul(
                        P.pb[bs][0:64, h * 64:(h + 1) * 64], lhsT=kdec[p0:p0 + 64, h * 64:(h + 1) * 64],
                        rhs=vnew[p0:p0 + 64, h * 64:(h + 1) * 64], start=True, stop=True, tile_position=(p0, 0)),
                        reads=[K_kdec, ("vnew", ch)], writes=[("pb", bs)])
                for h in range(8):
                    S_.op("pe", lambda e, h=h, p0=p0, ch=ch, bo2=bo2: e.matmul(
                        P.pb[bo2][p0:p0 + 64, h * 64:(h + 1) * 64], lhsT=qkdT[p0:p0 + 64, h, :],
                        rhs=vnew[p0:p0 + 64, h * 64:(h + 1) * 64], start=True, stop=True, tile_position=(p0, p0)),
                        reads=[K_qkdT, ("vnew", ch)], writes=[("pb", bo2)])
                S_.op("dve", lambda e, bs=bs: e.tensor_tensor(out=Sbf[:], in0=S1[:], in1=P.pb[bs][0:64, :], op=ALU.add),
                      reads=["S1", ("pb", bs)], writes=["Sbf"])
                S_.op("dve", lambda e, bs=bs: e.tensor_tensor(out=Sf[:], in0=S1[:], in1=P.pb[bs][0:64, :], op=ALU.add),
                      reads=["S1", ("pb", bs)], writes=["Sf"])
                S_.op("dve", lambda e, p0=p0, bo2=bo2: e.tensor_tensor(out=otm[p0:p0 + 64, :], in0=o1s[p0:p0 + 64, :], in1=P.pb[bo2][p0:p0 + 64, :],
                                                                    op=ALU.add), reads=[("o1s", ch), ("pb", bo2)], writes=[("otm", ch)])
            yield
            OT = [("otm", 0), ("otm", 1)]
            yield
            S_.op("pool", lambda e: e.tensor_tensor(out=sqo[:], in0=otm[:], in1=otm[:], op=ALU.mult), reads=OT, writes=["sqo"])
            yield
            S_.op("dve", lambda e: e.tensor_reduce(out=ssq[:], in_=sqo[:].rearrange("p (h d) -> p h d", h=8), axis=mybir.AxisListType.X, op=ALU.add),
                  reads=["sqo"], writes=["ssq"])
            yield
            S_.op("act", lambda e: e.activation(out=ssq[:], in_=ssq[:], func=AF.Sqrt, bias=epsT[:], scale=1.0 / 64), reads=["ssq", "eps"], writes=["ssq"])
            yield
            S_.op("dve", lambda e: e.reciprocal(out=ssq[:], in_=ssq[:]), reads=["ssq"], writes=["ssq"])
            yield
            S_.op("dve", lambda e: e.tensor_tensor(out=sqo[:].rearrange("p (h d) -> p h d", h=8), in0=otm[:].rearrange("p (h d) -> p h d", h=8),
                                                  in1=bc8(ssq[:])(64), op=ALU.mult), reads=OT + ["ssq"], writes=["sqo"])
            yield
            S_.op("pool", lambda e: e.tensor_tensor(out=onb[:].rearrange("p (h d) -> p h d", h=8), in0=sqo[:].rearrange("p (h d) -> p h d", h=8),
                                                   in1=dng[:].unsqueeze(1).to_broadcast([128, 8, 64]), op=ALU.mult), reads=["sqo", "dng"], writes=["onb"])
            yield
            bo = bankB()
            yield
            pO = P.pb[bo][:].bitcast(BF16)
            yield
            for hp in range(8):
                S_.op("pe", lambda e, pO=pO, hp=hp: e.transpose(pO[:, hp * 128:(hp + 1) * 128], onb[:, (hp % 4) * 128:(hp % 4 + 1) * 128], ident[:]),
                      reads=["onb", "ident"], writes=[("pb", bo)])
            yield
            S_.op("dve", lambda e, pO=pO, sl=sl, c0=c0: e.tensor_tensor(
                out=mst[sl][:, :, c0:c0 + 128], in0=pO[:, 0:512].rearrange("p (a c) -> p a c", a=4), in1=zs[sl][:, :, c0:c0 + 128], op=ALU.mult),
                reads=[("pb", bo), ("zs", sl)], writes=[("mst", sl)])
            if j == 3:
                S_.op("sp", lambda e, g=g, sl=sl: e.dma_start(out=MIXT.rearrange("(c p) s -> p c s", p=128)[:, 4:8, g * 512:(g + 1) * 512], in_=mst[sl][:]),
                      reads=[("mst", sl)], dma=True, semkey="mst%d" % sl)

        tiles = [(g, g % 2, j, j * 128, (g * 4 + j) % 2) for g in range(NG) for j in range(4)]
        for _ in A_tile(*tiles[0]):
            pass
        for ti, tl in enumerate(tiles):
            gb = B_tile(*tl)
            ga = A_tile(*tiles[ti + 1]) if ti + 1 < len(tiles) else iter(())
            da = db = False
            while not (da and db):
                for _ in range(3):
                    if not da:
                        try:
                            next(ga)
                        except StopIteration:
                            da = True
                if not db:
                    try:
                        next(gb)
                    except StopIteration:
                        db = True
        P.finish()


def _host_all(inputs, b, S):
    f = np.float32
    col = lambda v: np.ascontiguousarray(np.asarray(v, f).reshape(-1, 128).T)
    d = {}
    d["x"] = np.ascontiguousarray(np.asarray(inputs["x"][b, :S], f))
    d["c_col"] = col(inputs["c"][b])
    d["w_ada"] = np.asarray(inputs["w_ada"][0], f)
    d["bada_col"] = col(inputs["b_ada"][0])
    d["bada_row"] = np.ascontiguousarray(np.asarray(inputs["b_ada"][0], f).reshape(6, 1024))
    d["gattn_col"] = col(inputs["norm_attn_g"][0])
    d["gffn_col"] = col(inputs["norm_ffn_g"][0])
    d["w_in"] = np.asarray(inputs["w_in"][0], f)
    cw = np.asarray(inputs["conv_w"][0], f)
    d["cw_col"] = np.ascontiguousarray(cw.T.reshape(12, 128, 4).transpose(1, 0, 2).reshape(128, 48))
    d["ident"] = np.eye(128, dtype=f)
    bo = np.zeros((128, 128), f)
    bo[:64, :64] = 1
    bo[64:, 64:] = 1
    d["bones"] = bo
    d["identf"] = np.eye(128, dtype=f)
    d["bonesf"] = bo
    idx = np.arange(128)
    same = (idx[:, None] // 64) == (idx[None, :] // 64)
    d["trif"] = (same & (idx[:, None] <= idx[None, :])).astype(f)
    ii = np.arange(128) % 64
    cc = np.arange(64)
    d["maskL"] = np.where(cc[None, :] < ii[:, None], 0.0, -30000.0).astype(f)
    d["maskU"] = np.where(cc[None, :] >= ii[:, None], 0.0, -30000.0).astype(f)
    d["identc"] = (cc[None, :] == ii[:, None]).astype(f)
    d["alog_row"] = np.asarray(inputs["a_log"][0], f).reshape(1, 8)
    d["dtb_row"] = np.asarray(inputs["dt_bias"][0], f).reshape(1, 8)
    d["dng_row"] = np.asarray(inputs["delta_norm_g"][0], f).reshape(1, 64)
    d["w_out"] = np.asarray(inputs["w_out"][0], f)
    d["w_gate"] = np.asarray(inputs["w_gate"][0], f)
    d["w_up"] = np.asarray(inputs["w_up"][0], f)
    d["w_down"] = np.asarray(inputs["w_down"][0], f)
    d["fg_row"] = np.asarray(inputs["final_norm_g"], f).reshape(1, 1024)
    return d


def kernel(**inputs):
    S = inputs["x"].shape[1]
    B = inputs["x"].shape[0]
    nc, semstack = build_program(S, dbg=False)
    consts = host_consts(inputs)
    in_maps = []
    for b in range(B):
        d = _host_all(inputs, b, S)
        d.update(consts)
        in_maps.append(d)
    res = run_bass_kernel_spmd(nc, in_maps, core_ids=list(range(B)))
    return np.stack([np.asarray(r["out"], np.float32) for r in res.results], axis=0)
```

```python
from contextlib import ExitStack
import numpy as np
import concourse.bass as bass
import concourse.mybir as mybir
from concourse.bass_utils import run_bass_kernel_spmd

F32 = mybir.dt.float32
BF16 = mybir.dt.bfloat16
ALU = mybir.AluOpType
AF = mybir.ActivationFunctionType

ENGS = ("pe", "act", "dve", "pool", "sp")
EPOCH = 20000


class _Op:
    __slots__ = ("eng", "fn", "deps", "dma", "semkey", "idx", "needs_inc", "sem", "val")

    def __init__(self, eng, fn, dma, semkey):
        self.eng, self.fn, self.dma, self.semkey = eng, fn, dma, semkey
        self.deps = []
        self.needs_inc = False
        self.sem = None
        self.val = 0


class Sched:
    def __init__(self, nc):
        self.nc = nc
        self.ops = {e: [] for e in ENGS}
        self.last_w = {}
        self.readers = {}
        self.last_dma_on_sem = {}
        self.n = 0

    def op(self, eng, fn, reads=(), writes=(), dma=False, semkey=None):
        o = _Op(eng, fn, dma, semkey)
        o.idx = self.n
        self.n += 1
        deps = {}

        def add(p):
            if p is None or p is o:
                return
            if (not p.dma) and (not dma) and p.eng == "pe" and eng == "pe":
                return
            deps[id(p)] = p

        for k in reads:
            add(self.last_w.get(k))
        for k in writes:
            add(self.last_w.get(k))
            for r in self.readers.get(k, ()):
                add(r)
        if dma:
            assert semkey is not None
            add(self.last_dma_on_sem.get(semkey))
            self.last_dma_on_sem[semkey] = o
        o.deps = list(deps.values())
        for p in o.deps:
            p.needs_inc = True
        for k in reads:
            self.readers.setdefault(k, []).append(o)
        for k in writes:
            self.last_w[k] = o
            self.readers[k] = []
        self.ops[eng].append(o)
        return o

    def emit(self, stack, final_wait_ops=()):
        nc = self.nc
        for o in final_wait_ops:
            o.needs_inc = True
        sems = {}

        def getsem(name):
            if name not in sems:
                sems[name] = stack.enter_context(nc.semaphore(name))
            return sems[name]

        dma_cnt = {}
        for e in ENGS:
            cnt = 0
            for o in self.ops[e]:
                if o.dma:
                    c = dma_cnt.get(o.semkey, 0) + 1
                    dma_cnt[o.semkey] = c
                    o.sem = getsem("d_" + str(o.semkey))
                    o.val = 16 * c
                    o.needs_inc = True
                elif o.needs_inc:
                    ep, v = divmod(cnt, EPOCH)
                    o.sem = getsem("c_%s_%d" % (e, ep))
                    o.val = v + 1
                    cnt += 1
        self.nsems = len(sems)
        block = stack.enter_context(nc.Block())
        engmap = {"pe": block.tensor, "act": block.scalar, "dve": block.vector,
                  "pool": block.gpsimd, "sp": block.sync}
        for e in ENGS:
            ops = self.ops[e]
            fw = [o for o in final_wait_ops] if e == "sp" else []

            def body(engine, ops=ops, fw=fw):
                waited = {}
                for o in ops:
                    for p in o.deps:
                        key = id(p.sem)
                        if waited.get(key, 0) >= p.val:
                            continue
                        engine.wait_ge(p.sem, p.val)
                        waited[key] = p.val
                    ins = o.fn(engine)
                    if o.needs_inc:
                        ins.then_inc(o.sem, 16 if o.dma else 1)
                for p in fw:
                    key = id(p.sem)
                    if waited.get(key, 0) >= p.val:
                        continue
                    engine.wait_ge(p.sem, p.val)
                    waited[key] = p.val

            engmap[e](body)


D = 1024
INW = 3600
DFF = 2816
EPS = 1e-6


class Phase:
    def __init__(self, nc, semstack, name):
        self.nc, self.semstack, self.name = nc, semstack, name
        self.st = ExitStack()
        self.S = Sched(nc)
        self.nb = 0
        self.pb = None
        self.cnt = 0

    def __enter__(self):
        self.st.__enter__()
        self.pb = [self.st.enter_context(self.nc.psum_tensor("%s_pb%d" % (self.name, i), [128, 512], F32))
                   for i in range(8)]
        return self

    def sb(self, name, shape, dt):
        return self.st.enter_context(self.nc.sbuf_tensor(self.name + "_" + name, shape, dt))

    def bank(self):
        i = self.nb % 8
        self.nb += 1
        return i

    def finish(self):
        S = self.S
        fw = [o for e in ENGS for o in S.ops[e] if o.dma]
        for e in ("pe", "act", "dve", "pool"):
            if S.ops[e]:
                fw.append(S.ops[e][-1])
        nc = self.nc
        ph = self

        class _SemStack:
            def enter_context(self_inner, cm):
                return cm

        _emit(S, nc, self.semstack, self.st, fw, self.name)

    def __exit__(self, *a):
        r = self.st.__exit__(*a)
        return r


def _emit(S, nc, semstack, blockstack, final_wait_ops, pname):
    for o in final_wait_ops:
        o.needs_inc = True
    sems = {}

    def getsem(name):
        name = pname + "_" + "".join(ch if ch.isalnum() else "_" for ch in name)
        if name not in sems:
            sems[name] = blockstack.enter_context(nc.semaphore(name))
        return sems[name]

    dma_cnt = {}
    for e in ENGS:
        cnt = 0
        for o in S.ops[e]:
            if o.dma:
                c = dma_cnt.get(o.semkey, 0) + 1
                dma_cnt[o.semkey] = c
                o.sem = getsem("d_" + str(o.semkey))
                o.val = 16 * c
                o.needs_inc = True
            elif o.needs_inc:
                ep, v = divmod(cnt, EPOCH)
                o.sem = getsem("c_%s_%d" % (e, ep))
                o.val = v + 1
                cnt += 1
    S.nsems = len(sems)
    for sm in sems.values():
        nc.sync.sem_clear(sm)
    nc.all_engine_barrier()
    block = blockstack.enter_context(nc.Block())
    engmap = {"pe": block.tensor, "act": block.scalar, "dve": block.vector,
              "pool": block.gpsimd, "sp": block.sync}
    for e in ENGS:
        ops = S.ops[e]
        fw = list(final_wait_ops) if e == "sp" else []

        def body(engine, ops=ops, fw=fw):
            waited = {}
            for o in ops:
                for p in o.deps:
                    key = id(p.sem)
                    if waited.get(key, 0) >= p.val:
                        continue
                    engine.wait_ge(p.sem, p.val)
                    waited[key] = p.val
                ins = o.fn(engine)
                if o.needs_inc:
                    ins.then_inc(o.sem, 16 if o.dma else 1)
            for p in fw:
                key = id(p.sem)
                if waited.get(key, 0) >= p.val:
                    continue
                engine.wait_ge(p.sem, p.val)
                waited[key] = p.val

        engmap[e](body)


def _kw(**k):
    return k


def build_program(S, dbg=False, upto=9):
    nc = bass.Bass("TRN2", target_bir_lowering=False)
    NG = S // 512
    OUTK = "ExternalOutput" if dbg else "Internal"

    def din(name, shape, dt=F32):
        return nc.dram_tensor(name, shape, dt, kind="ExternalInput").ap()

    def dsc(name, shape, dt):
        return nc.dram_tensor(name, shape, dt, kind=OUTK).ap()

    x = din("x", [S, D])
    c_col = din("c_col", [128, 8])
    w_ada = din("w_ada", [D, 6 * D])
    bada_col = din("bada_col", [128, 48])
    bada_row = din("bada_row", [6, D])
    gattn_col = din("gattn_col", [128, 8])
    gffn_col = din("gffn_col", [128, 8])
    w_in = din("w_in", [D, INW])
    cw_col = din("cw_col", [128, 48])
    ident_in = din("ident", [128, 128])
    bones_in = din("bones", [128, 128])
    tb_in = din("tb", [128, 24 * 256])
    out = nc.dram_tensor("out", [S, D], F32, kind="ExternalOutput").ap()

    MODC = dsc("MODC", [128, 32], F32)
    GROW = dsc("GROW", [2, 128, D], F32)
    QT = dsc("QT", [512, S], BF16)
    KT = dsc("KT", [512, S], BF16)
    VV = dsc("VV", [S, 512], BF16)
    QD = dsc("QD", [512, S], BF16)
    KD = dsc("KD", [512, S], BF16)
    VD = dsc("VD", [512, S], BF16)
    ZS = dsc("ZS", [512, S], BF16)
    BA = dsc("BA", [S, 16], F32)
    MIXT = dsc("MIXT", [D, S], BF16)

    semstack = ExitStack()
    semstack.__enter__()

    with Phase(nc, semstack, "p0") as P:
        S_ = P.S
        ccol = P.sb("ccol", [128, 8], F32)
        sbf = P.sb("sbf", [128, 8], BF16)
        sbc = P.sb("sbc", [128, 8, 128], BF16)
        bcol = P.sb("bcol", [128, 48], F32)
        gcol = P.sb("gcol", [128, 16], F32)
        modc = P.sb("modc", [128, 32], F32)
        wa = [P.sb("wa%d" % i, [128, 8, D], BF16) for i in range(2)]
        brow = [P.sb("brow%d" % i, [128, D], F32) for i in range(2)]
        grow = [P.sb("grow%d" % i, [128, D], F32) for i in range(2)]
        S_.op("sp", lambda e: e.dma_start(out=ccol[:], in_=c_col), writes=["ccol"], dma=True, semkey="ccol")
        S_.op("sp", lambda e: e.dma_start(out=bcol[:], in_=bada_col), writes=["bcol"], dma=True, semkey="bcol")
        S_.op("sp", lambda e: e.dma_start(out=gcol[:, 0:8], in_=gattn_col), writes=["gcol"], dma=True, semkey="gcol")
        S_.op("sp", lambda e: e.dma_start(out=gcol[:, 8:16], in_=gffn_col), writes=["gcol"], dma=True, semkey="gcol")
        S_.op("act", lambda e: e.activation(out=sbf[:], in_=ccol[:], func=AF.Silu), reads=["ccol"], writes=["sbf"])
        S_.op("dve", lambda e: e.tensor_copy(out=sbc[:], in_=sbf[:].unsqueeze(2).to_broadcast([128, 8, 128])),
              reads=["sbf"], writes=["sbc"])
        wav = w_ada.rearrange("(k p) f -> p k f", p=128)
        colidx = {0: 0, 1: 1, 3: 2, 4: 3}
        for j in range(6):
            sl = j % 2
            S_.op("pool", lambda e, j=j, sl=sl: e.dma_start(out=wa[sl][:], in_=wav[:, :, j * D:(j + 1) * D]),
                  writes=[("wa", sl)], dma=True, semkey="wa%d" % sl)
            if j in colidx:
                jj = colidx[j]
                b = P.bank()
                for fcn in range(8):
                    for k in range(8):
                        S_.op("pe", lambda e, b=b, fcn=fcn, k=k, sl=sl: e.matmul(
                            P.pb[b][:, fcn:fcn + 1], lhsT=wa[sl][:, k, fcn * 128:(fcn + 1) * 128], rhs=sbf[:, k:k + 1],
                            start=(k == 0), stop=(k == 7)), reads=[("wa", sl), "sbf"], writes=[("pb", b)])
                S_.op("dve", lambda e, b=b, jj=jj, j=j: e.tensor_tensor(
                    out=modc[:, jj * 8:(jj + 1) * 8], in0=P.pb[b][:, 0:8], in1=bcol[:, j * 8:(j + 1) * 8], op=ALU.add),
                    reads=[("pb", b), "bcol"], writes=["modc"])
            else:
                gi = 0 if j == 2 else 1
                S_.op("sp", lambda e, j=j, gi=gi: e.dma_start(out=brow[gi][:], in_=bada_row[j:j + 1, :].partition_broadcast(128)),
                      writes=[("brow", gi)], dma=True, semkey="brow%d" % gi)
                for half in range(2):
                    b = P.bank()
                    for k in range(8):
                        S_.op("pe", lambda e, b=b, k=k, sl=sl, half=half: e.matmul(
                            P.pb[b][:], lhsT=sbc[:, k, :], rhs=wa[sl][:, k, half * 512:(half + 1) * 512],
                            start=(k == 0), stop=(k == 7)), reads=[("wa", sl), "sbc"], writes=[("pb", b)])
                    S_.op("dve", lambda e, b=b, gi=gi, half=half: e.tensor_tensor(
                        out=grow[gi][:, half * 512:(half + 1) * 512], in0=P.pb[b][:], in1=brow[gi][:, half * 512:(half + 1) * 512],
                        op=ALU.add), reads=[("pb", b), ("brow", gi)], writes=[("grow", gi)])
                S_.op("sp", lambda e, gi=gi: e.dma_start(out=GROW[gi], in_=grow[gi][:]), reads=[("grow", gi)],
                      dma=True, semkey="grow%d" % gi)
        for jj, go in ((1, 0), (3, 8)):
            S_.op("dve", lambda e, jj=jj, go=go: e.scalar_tensor_tensor(
                out=modc[:, jj * 8:(jj + 1) * 8], in0=modc[:, jj * 8:(jj + 1) * 8], scalar=1.0, in1=gcol[:, go:go + 8],
                op0=ALU.add, op1=ALU.mult), reads=["modc", "gcol"], writes=["modc"])
        S_.op("sp", lambda e: e.dma_start(out=MODC, in_=modc[:]), reads=["modc"], dma=True, semkey="modc")
        P.finish()
    nc.all_engine_barrier()
    if upto < 1:
        return nc, semstack

    with Phase(nc, semstack, "p1") as P:
        S_ = P.S
        ident = P.sb("ident", [128, 128], BF16)
        bones = P.sb("bones", [128, 128], BF16)
        modc = P.sb("modc", [128, 32], F32)
        cw = P.sb("cw", [128, 48], F32)
        epsT = P.sb("eps", [128, 1], F32)
        win = P.sb("win", [128, 8, INW], BF16)
        S_.op("pool", lambda e: e.dma_start(out=ident[:], in_=ident_in), writes=["ident"], dma=True, semkey="ident")
        S_.op("pool", lambda e: e.dma_start(out=bones[:], in_=bones_in), writes=["bones"], dma=True, semkey="bones")
        S_.op("sp", lambda e: e.dma_start(out=modc[:], in_=MODC), writes=["modc"], dma=True, semkey="modc")
        S_.op("sp", lambda e: e.dma_start(out=cw[:], in_=cw_col), writes=["cw"], dma=True, semkey="cw")
        S_.op("dve", lambda e: e.memset(epsT[:], EPS), writes=["eps"])
        winv = w_in.rearrange("(k p) f -> p k f", p=128)
        for k in range(8):
            S_.op("pool", lambda e, k=k: e.dma_start(out=win[:, k, :], in_=winv[:, k, :]), writes=[("win", k)],
                  dma=True, semkey="win%d" % k)
        WIN = [("win", k) for k in range(8)]
        xt = [P.sb("xt%d" % i, [128, 4, D], F32) for i in range(2)]
        junk = P.sb("junk", [128, D], BF16)
        ss2 = [P.sb("ss%d" % i, [128, 4], F32) for i in range(2)]
        rstd2 = [P.sb("rstd%d" % i, [128, 4], F32) for i in range(2)]
        xs2 = [P.sb("xs%d" % i, [128, 4, D], BF16) for i in range(2)]
        hT2 = [P.sb("hT%d" % i, [128, 8, 512], BF16) for i in range(2)]
        NSTQ = 6
        stq = [P.sb("stq%d" % i, [128, 4, 512], BF16) for i in range(NSTQ)]
        cin = P.sb("cin", [128, 12, 515], F32)
        NROT = 5
        acc3 = [P.sb("acc%d" % i, [128, 512], F32) for i in range(NROT)]
        slu3 = [P.sb("slu%d" % i, [128, 512], F32) for i in range(NROT)]
        sq3 = [P.sb("sq%d" % i, [128, 512], BF16) for i in range(NROT)]
        rs3 = [P.sb("rs%d" % i, [128, 512], F32) for i in range(NROT)]
        rot = [0]
        stb = P.sb("stb", [128, 4, 16], F32)
        xv = x.rearrange("(g j p) d -> g p j d", j=4, p=128)
        S_.op("pool", lambda e: e.memset(cin[:], 0.0), writes=["cin"])
        nst = [0]

        def stage():
            i = nst[0] % NSTQ
            nst[0] += 1
            return i

        NROT2 = NROT

        def norm_item(g):
            xs_ = g % 2
            ss, rstd, xs, hT = ss2[xs_], rstd2[xs_], xs2[xs_], hT2[xs_]
            KSS, KRS = ("ss", xs_), ("rstd", xs_)
            S_.op("sp", lambda e: e.dma_start(out=xt[xs_][:], in_=xv[g]), writes=[("xt", xs_)], dma=True, semkey="xt%d" % xs_)
            S_.op("pool", lambda e: e.memset(ss[:], 0.0), writes=[KSS])
            yield
            for j in range(4):
                S_.op("act", lambda e, j=j: e.activation(out=junk[:], in_=xt[xs_][:, j, :], func=AF.Square, accum_out=ss[:, j:j + 1]),
                      reads=[("xt", xs_), KSS], writes=[KSS, "junk"])
            S_.op("act", lambda e: e.activation(out=rstd[:], in_=ss[:], func=AF.Sqrt, bias=epsT[:], scale=1.0 / D),
                  reads=[KSS, "eps"], writes=[KRS])
            yield
            S_.op("dve", lambda e: e.reciprocal(out=rstd[:], in_=rstd[:]), reads=[KRS], writes=[KRS])
            for j in range(4):
                S_.op("dve", lambda e, j=j: e.tensor_scalar(out=xs[:, j, :], in0=xt[xs_][:, j, :], scalar1=rstd[:, j:j + 1], scalar2=None, op0=ALU.mult),
                      reads=[("xt", xs_), KRS], writes=[("xs", xs_, j)])
            yield
            pend = None
            for c2 in range(5):
                if c2 < 4:
                    b = P.bank()
                    pbf = P.pb[b][:].bitcast(BF16)
                    for cc in range(2):
                        c = c2 * 2 + cc
                        for j in range(4):
                            S_.op("pe", lambda e, pbf=pbf, cc=cc, c=c, j=j: e.transpose(
                                pbf[:, cc * 512 + j * 128: cc * 512 + (j + 1) * 128], xs[:, j, c * 128:(c + 1) * 128], ident[:]),
                                reads=[("xs", xs_, j), "ident"], writes=[("pb", b)])
                if pend is not None:
                    pb_, pbf_, pc2 = pend
                    for cc in range(2):
                        c = pc2 * 2 + cc
                        S_.op("act", lambda e, pbf_=pbf_, cc=cc, c=c: e.activation(
                            out=hT[:, c, :], in_=pbf_[:, cc * 512:(cc + 1) * 512], func=AF.Identity,
                            bias=modc[:, c:c + 1], scale=modc[:, 8 + c:9 + c]),
                            reads=[("pb", pb_), "modc"], writes=[("hT", xs_, c)])
                pend = (b, pbf, c2) if c2 < 4 else None
                yield

        def proj_mm(g, fc):
            xs_ = g % 2
            hT = hT2[xs_]
            b = P.bank()
            for k in range(8):
                S_.op("pe", lambda e, k=k: e.matmul(
                    P.pb[b][:], lhsT=win[:, k, fc * 128:(fc + 1) * 128], rhs=hT[:, k, :], start=(k == 0), stop=(k == 7)),
                    reads=[("win", k), ("hT", xs_, k)], writes=[("pb", b)])
            return b

        def store(dst, si, g):
            S_.op("sp", lambda e: e.dma_start(
                out=dst.rearrange("(c p) s -> p c s", p=128)[:, :, g * 512:(g + 1) * 512], in_=stq[si][:]),
                reads=[("stq", si, i) for i in range(4)], dma=True, semkey="stq%d" % si)

        def qk_item(g, base, dst, si, i):
            b = proj_mm(g, base + i)
            yield
            if i % 2 == 0:
                S_.op("act", lambda e: e.activation(out=stq[si][:, i, :], in_=P.pb[b][:], func=AF.Identity),
                      reads=[("pb", b)], writes=[("stq", si, i)])
            else:
                S_.op("dve", lambda e: e.tensor_copy(out=stq[si][:, i, :], in_=P.pb[b][:]),
                      reads=[("pb", b)], writes=[("stq", si, i)])
            if i == 3:
                store(dst, si, g)

        def v_item(g, si, j):
            xs_ = g % 2
            hT = hT2[xs_]
            b = P.bank()
            for k in range(8):
                S_.op("pe", lambda e, k=k: e.matmul(
                    P.pb[b][:], lhsT=hT[:, k, j * 128:(j + 1) * 128], rhs=win[:, k, 1024:1536], start=(k == 0), stop=(k == 7)),
                    reads=[("win", k), ("hT", xs_, k)], writes=[("pb", b)])
            yield
            S_.op("act", lambda e: e.activation(out=stq[si][:, j, :], in_=P.pb[b][:], func=AF.Identity),
                  reads=[("pb", b)], writes=[("stq", si, j)])
            if j == 3:
                S_.op("sp", lambda e: e.dma_start(out=VV.rearrange("(g j p) f -> g p j f", j=4, p=128)[g], in_=stq[si][:]),
                      reads=[("stq", si, i) for i in range(4)], dma=True, semkey="stq%d" % si)

        def z_item(g, si, i):
            b = proj_mm(g, 24 + i)
            yield
            S_.op("act", lambda e: e.activation(out=stq[si][:, i, :], in_=P.pb[b][:], func=AF.Silu),
                  reads=[("pb", b)], writes=[("stq", si, i)])
            if i == 3:
                store(ZS, si, g)

        def ba_item(g):
            xs_ = g % 2
            hT = hT2[xs_]
            b = P.bank()
            for j in range(4):
                for k in range(8):
                    S_.op("pe", lambda e, k=k, j=j: e.matmul(
                        P.pb[b][:, j * 16:(j + 1) * 16], lhsT=hT[:, k, j * 128:(j + 1) * 128], rhs=win[:, k, 3584:3600],
                        start=(k == 0), stop=(k == 7)), reads=[("win", k), ("hT", xs_, k)], writes=[("pb", b)])
            yield
            S_.op("dve", lambda e: e.tensor_copy(out=stb[:].rearrange("p j f -> p (j f)"), in_=P.pb[b][:, 0:64]),
                  reads=[("pb", b)], writes=["stb"])
            S_.op("sp", lambda e: e.dma_start(out=BA.rearrange("(g j p) f -> g p j f", j=4, p=128)[g], in_=stb[:]),
                  reads=["stb"], dma=True, semkey="stb")

        def delta_item(g, grp, dst, si, i):
            ci = grp * 4 + i
            b = proj_mm(g, 12 + ci)
            yield
            S_.op("act", lambda e: e.activation(out=cin[:, ci, 3:515], in_=P.pb[b][:], func=AF.Identity),
                  reads=[("pb", b)], writes=[("cin", ci)])
            yield
            ri = rot[0] % NROT2
            rot[0] += 1
            acc, slu, sq, rs = acc3[ri], slu3[ri], sq3[ri], rs3[ri]
            KA, KSL, KSQ, KR = ("acc", ri), ("slu", ri), ("sq", ri), ("rs", ri)
            S_.op("dve", lambda e: e.tensor_scalar(out=acc[:], in0=cin[:, ci, 0:512], scalar1=cw[:, ci * 4:ci * 4 + 1], scalar2=None, op0=ALU.mult),
                  reads=[("cin", ci), "cw"], writes=[KA])
            for t in range(1, 4):
                S_.op("dve", lambda e, t=t: e.scalar_tensor_tensor(
                    out=acc[:], in0=cin[:, ci, t:t + 512], scalar=cw[:, ci * 4 + t:ci * 4 + t + 1], in1=acc[:],
                    op0=ALU.mult, op1=ALU.add), reads=[("cin", ci), "cw", KA], writes=[KA])
            S_.op("pool", lambda e: e.tensor_copy(out=cin[:, ci, 0:3], in_=cin[:, ci, 512:515]), reads=[("cin", ci)], writes=[("cin", ci)])
            yield
            if grp == 2:
                S_.op("act", lambda e: e.activation(out=stq[si][:, i, :], in_=acc[:], func=AF.Silu), reads=[KA], writes=[("stq", si, i)])
                if i == 3:
                    store(dst, si, g)
                return
            S_.op("act", lambda e: e.activation(out=slu[:], in_=acc[:], func=AF.Silu), reads=[KA], writes=[KSL])
            S_.op("pool", lambda e: e.tensor_tensor(out=sq[:], in0=slu[:], in1=slu[:], op=ALU.mult), reads=[KSL], writes=[KSQ])
            yield
            b2 = P.bank()
            S_.op("pe", lambda e: e.matmul(P.pb[b2][:], lhsT=bones[:], rhs=sq[:], start=True, stop=True), reads=["bones", KSQ], writes=[("pb", b2)])
            yield
            S_.op("act", lambda e: e.activation(out=rs[:], in_=P.pb[b2][:], func=AF.Ln, bias=epsT[:], scale=1.0), reads=[("pb", b2), "eps"], writes=[KR])
            S_.op("act", lambda e: e.activation(out=rs[:], in_=rs[:], func=AF.Exp, scale=-0.5), reads=[KR], writes=[KR])
            yield
            scl = 0.125 if grp == 0 else 1.0
            S_.op("dve", lambda e: e.scalar_tensor_tensor(out=stq[si][:, i, :], in0=slu[:], scalar=scl, in1=rs[:], op0=ALU.mult, op1=ALU.mult),
                  reads=[KSL, KR], writes=[("stq", si, i)])
            if i == 3:
                store(dst, si, g)

        def p1_items():
            for g in range(NG):
                if g + 1 < NG:
                    yield norm_item(g + 1)
                si = stage()
                for i in range(4):
                    yield qk_item(g, 0, QT, si, i)
                si = stage()
                for i in range(4):
                    yield qk_item(g, 4, KT, si, i)
                si = stage()
                for j in range(4):
                    yield v_item(g, si, j)
                for grp, dst in ((0, QD), (1, KD), (2, VD)):
                    si = stage()
                    for i in range(4):
                        yield delta_item(g, grp, dst, si, i)
                si = stage()
                for i in range(4):
                    yield z_item(g, si, i)
                yield ba_item(g)

        for _ in norm_item(0):
            pass
        run_skewed(p1_items())
        P.finish()
    nc.all_engine_barrier()
    if upto < 2:
        return nc, semstack
    _phase2(nc, semstack, S, QT, KT, VV, tb_in, MIXT)
    nc.all_engine_barrier()
    if upto < 3:
        return nc, semstack
    cst = dict(identf=din("identf", [128, 128]), trif=din("trif", [128, 128]), bonesf=din("bonesf", [128, 128]),
               maskL=din("maskL", [128, 64]), maskU=din("maskU", [128, 64]), identc=din("identc", [128, 64]), ident=ident_in,
               alog=din("alog_row", [1, 8]), dtb=din("dtb_row", [1, 8]), dng=din("dng_row", [1, 64]))
    _phase3(nc, semstack, S, QD, KD, VD, ZS, BA, MIXT, cst)
    nc.all_engine_barrier()
    if upto < 4:
        return nc, semstack
    w_out = din("w_out", [D, D])
    w_gate = din("w_gate", [D, DFF])
    w_up = din("w_up", [D, DFF])
    w_down = din("w_down", [DFF, D])
    fg_row = din("fg_row", [1, D])
    X1 = dsc("X1", [S, D], F32)
    H2T = dsc("H2T", [D, S], BF16)
    _phase4a(nc, semstack, S, x, MIXT, w_out, GROW, MODC, ident_in, X1, H2T)
    nc.all_engine_barrier()
    if upto < 5:
        return nc, semstack
    _phase4b(nc, semstack, S, X1, H2T, w_gate, w_up, w_down, GROW, fg_row, out)
    return nc, semstack


P2DBG = {'mode': 9, 'strided': True}


def run_skewed(items):
    live = []
    it = iter(items)
    while True:
        nxt = next(it, None)
        if nxt is not None:
            live.append(nxt)
        if not live:
            break
        for gen in list(live):
            try:
                next(gen)
            except StopIteration:
                live.remove(gen)


def _phase2(nc, semstack, S, QT, KT, VV, tb_in, MIXT):
    NSB = S // 2048
    import os
    mode = int(os.environ.get('P2MODE', '9'))
    with Phase(nc, semstack, "p2") as P:
        S_ = P.S
        EB = P.sb("EB", [128, 24 * 256], BF16)
        ones = P.sb("ones", [128, 64], BF16)
        qt = P.sb("qt", [64, 8, 2048], BF16)
        kt = [P.sb("kt%d" % i, [64, 8, 2048], BF16) for i in range(2)]
        v1 = P.sb("v1", [128, 16, 512], BF16)
        v1p = P.sb("v1p", [128, 1, 512], BF16)
        v2 = P.sb("v2", [128, 16, 512], BF16)
        v2p = P.sb("v2p", [128, 4, 512], BF16)
        v3 = [P.sb("v3_%d" % i, [128, 16, 512], BF16) for i in range(2)]
        Et = [P.sb("E%d" % i, [128, 512], BF16) for i in range(4)]
        PT = [P.sb("PT%d" % i, [128, 512], BF16) for i in range(4)]
        accn = P.sb("accn", [128, 2048], F32)
        accd = P.sb("accd", [128, 2048], F32)
        mst = P.sb("mst", [128, 2048], BF16)
        S_.op("pool", lambda e: e.dma_start(out=EB[:], in_=tb_in), writes=["EB"], dma=True, semkey="tb")
        S_.op("act", lambda e: e.activation(out=EB[:], in_=EB[:], func=AF.Exp), reads=["EB"], writes=["EB"])
        S_.op("pool", lambda e: e.memset(ones[:], 1.0), writes=["ones"])
        cnt = [0, 0, 0]
        for N in range(NSB):
            cur, prv = N % 2, (N + 1) % 2
            t0 = N * 2048
            S_.op("sp", lambda e, t0=t0: e.dma_start(out=qt[:], in_=QT.rearrange("(c p) s -> p c s", p=64)[:, :, t0:t0 + 2048]),
                  writes=["qt"], dma=True, semkey="qt")
            S_.op("sp", lambda e, t0=t0, cur=cur: e.dma_start(out=kt[cur][:], in_=KT.rearrange("(c p) s -> p c s", p=64)[:, :, t0:t0 + 2048]),
                  writes=[("kt", cur)], dma=True, semkey="kt%d" % cur)
            Vsb = VV[t0:t0 + 2048, :]
            S_.op("sp", lambda e, Vsb=Vsb: e.dma_start(out=v1[:], in_=Vsb.rearrange("(n p) f -> p n f", p=128)),
                  writes=["v1"], dma=True, semkey="v1")
            for n_ in range(4):
                S_.op("sp", lambda e, Vsb=Vsb, n_=n_: e.dma_start(
                    out=v2[:, n_ * 4:(n_ + 1) * 4, :],
                    in_=Vsb[n_ * 512:(n_ + 1) * 512, :].rearrange("(p r) f -> p r f", r=4)),
                    writes=["v2"], dma=True, semkey="v2")
            S_.op("sp", lambda e, Vsb=Vsb, cur=cur: e.dma_start(out=v3[cur][:], in_=Vsb.rearrange("(p r) f -> p r f", r=16)),
                  writes=[("v3", cur)], dma=True, semkey="v3_%d" % cur)
            def unit(hp, br, gq, jj, nbk, dbk, N=N, cur=cur, prv=prv):
                if br == 0:
                    n = 4 * gq + jj
                    qs, st = n * 128, 1
                    vcur = (v1, n, "v1")
                    if n >= 1:
                        pk = (cur, (n - 1) * 128, (v1, n - 1, "v1"))
                    elif N >= 1:
                        pk = (prv, 15 * 128, (v1p, 0, "v1p"))
                    else:
                        pk = None
                elif br == 1:
                    n_, r = gq, jj
                    qs, st = n_ * 512 + r, 4
                    vcur = (v2, n_ * 4 + r, "v2")
                    if n_ >= 1:
                        pk = (cur, (n_ - 1) * 512 + r, (v2, (n_ - 1) * 4 + r, "v2"))
                    elif N >= 1:
                        pk = (prv, 3 * 512 + r, (v2p, r, "v2p"))
                    else:
                        pk = None
                else:
                    r = 4 * gq + jj
                    qs, st = r, 16
                    vcur = (v3[cur], r, ("v3", cur))
                    pk = (prv, r, (v3[prv], r, ("v3", prv))) if N >= 1 else None
                sbk = cnt[0] % 3
                ei = cnt[0] % 4
                cnt[0] += 1
                blks = ([(0,) + pk] if pk else []) + [(1, cur, qs, vcur)]
                for hl in range(2):
                    for (blk, slot, ks, _v) in blks:
                        S_.op("pe", lambda e, sbk=sbk, hl=hl, blk=blk, slot=slot, ks=ks, qs=qs, st=st, hp=hp: e.matmul(
                            P.pb[sbk][:, hl * 256 + blk * 128: hl * 256 + (blk + 1) * 128],
                            lhsT=kt[slot][:, 2 * hp + hl, ks:ks + 127 * st + 1:st],
                            rhs=qt[:, 2 * hp + hl, qs:qs + 127 * st + 1:st],
                            start=True, stop=True),
                            reads=[("kt", slot), "qt"], writes=[("pb", sbk)])
                yield
                c0 = 0 if pk else 128
                vw = lambda ap, c0=c0: ap.rearrange("p (h c) -> p h c", h=2)[:, :, c0:256]
                S_.op("act", lambda e, sbk=sbk, ei=ei, vw=vw: e.activation(
                    out=vw(Et[ei][:]), in_=vw(P.pb[sbk][:]), func=AF.Exp, scale=0.125),
                    reads=[("pb", sbk)], writes=[("E", ei)])
                yield
                eoff = (br * 8 + 2 * hp) * 256
                eng = "dve" if ei % 2 == 0 else "pool"
                S_.op(eng, lambda e, ei=ei, eoff=eoff, vw=vw: e.tensor_tensor(
                    out=vw(PT[ei][:]), in0=vw(Et[ei][:]), in1=vw(EB[:, eoff:eoff + 512]), op=ALU.mult),
                    reads=[("E", ei), "EB"], writes=[("PT", ei)])
                yield
                for hl in range(2):
                    h = 2 * hp + hl
                    for bi, (blk, slot, ks, (vt, vi, vkey)) in enumerate(blks):
                        fl = _kw(start=(bi == 0), stop=(bi == len(blks) - 1), tile_position=(0, hl * 64))
                        S_.op("pe", lambda e, nbk=nbk, hl=hl, jj=jj, vt=vt, vi=vi, h=h, ei=ei, blk=blk, fl=fl: e.matmul(
                            P.pb[nbk][hl * 64:(hl + 1) * 64, jj * 128:(jj + 1) * 128],
                            lhsT=vt[:, vi, h * 64:(h + 1) * 64],
                            rhs=PT[ei][:, hl * 256 + blk * 128: hl * 256 + (blk + 1) * 128], **fl),
                            reads=[vkey, ("PT", ei)], writes=[("pb", nbk)])
                        S_.op("pe", lambda e, dbk=dbk, hl=hl, jj=jj, ei=ei, blk=blk, fl=fl: e.matmul(
                            P.pb[dbk][hl * 64:(hl + 1) * 64, jj * 128:(jj + 1) * 128],
                            lhsT=ones[:, 0:64],
                            rhs=PT[ei][:, hl * 256 + blk * 128: hl * 256 + (blk + 1) * 128], **fl),
                            reads=["ones", ("PT", ei)], writes=[("pb", dbk)])
                if jj < 3:
                    return
                yield
                for bk, acc, akey in ((nbk, accn, "accn"), (dbk, accd, "accd")):
                    if br == 0:
                        S_.op("act", lambda e, bk=bk, acc=acc, gq=gq: e.activation(
                            out=acc[:, gq * 512:(gq + 1) * 512], in_=P.pb[bk][:], func=AF.Identity),
                            reads=[("pb", bk)], writes=[akey])
                    else:
                        if br == 1:
                            oap = acc[:, gq * 512:(gq + 1) * 512].rearrange("p (i r) -> p r i", r=4)
                        else:
                            oap = acc[:].rearrange("p (i r) -> p r i", r=16)[:, 4 * gq:4 * gq + 4, :]
                        S_.op("dve", lambda e, bk=bk, oap=oap: e.tensor_tensor(
                            out=oap, in0=P.pb[bk][:].rearrange("p (r i) -> p r i", r=4), in1=oap, op=ALU.add),
                            reads=[("pb", bk), akey], writes=[akey])

            def finalize(hp, t0=t0):
                for _ in range(6):
                    yield
                S_.op("dve", lambda e: e.reciprocal(out=accd[:], in_=accd[:]), reads=["accd"], writes=["accd"])
                S_.op("dve", lambda e: e.tensor_tensor(out=mst[:], in0=accn[:], in1=accd[:], op=ALU.mult),
                      reads=["accn", "accd"], writes=["mst"])
                S_.op("sp", lambda e, hp=hp, t0=t0: e.dma_start(out=MIXT[hp * 128:(hp + 1) * 128, t0:t0 + 2048], in_=mst[:]),
                      reads=["mst"], dma=True, semkey="mst")

            def items():
                for hp in range(4):
                    for br in range(3):
                        for gq in range(4):
                            nbk = 3 + cnt[1] % 2
                            dbk = 5 + cnt[1] % 2
                            cnt[1] += 1
                            for jj in range(4):
                                yield unit(hp, br, gq, jj, nbk, dbk)
                    yield finalize(hp)

            run_skewed(items())
            if N + 1 < NSB:
                S_.op("pool", lambda e: e.tensor_copy(out=v1p[:, 0, :], in_=v1[:, 15, :]), reads=["v1"], writes=["v1p"])
                S_.op("pool", lambda e: e.tensor_copy(out=v2p[:], in_=v2[:, 12:16, :]), reads=["v2"], writes=["v2p"])
        P.finish()


def host_consts(inputs):
    import math
    rel_bias = np.asarray(inputs["rel_bias"], np.float32)
    k = np.arange(128)[:, None]
    q = np.arange(128)[None, :]
    steps_prev = q + 128 - k
    steps_cur = q - k
    tb = np.full((128, 3, 8, 2, 128), -30000.0, np.float32)

    def bucket(dist):
        dist = np.asarray(dist, np.int64)
        max_exact = 16
        dist_f = np.maximum(dist, 1).astype(np.float32)
        lg = (np.log(dist_f / np.float32(max_exact)) / np.float32(math.log(2048 / max_exact))
              * np.float32(32 - max_exact)).astype(np.float32)
        large = max_exact + lg.astype(np.int32)
        return np.where(dist < max_exact, dist, np.minimum(large, 31)).astype(np.int64)

    for br, d in enumerate((1, 4, 16)):
        for blk, steps in ((0, steps_prev), (1, steps_cur)):
            valid = (steps >= 0) & (steps <= 128)
            bk = bucket(np.maximum(steps, 0) * d)
            for h in range(8):
                vals = rel_bias[bk, h]
                tb[:, br, h, blk, :] = np.where(valid, vals, np.float32(-30000.0))
    return {"tb": np.ascontiguousarray(tb.reshape(128, 24 * 256))}


def _phase4a(nc, semstack, S, x, MIXT, w_out, GROW, MODC, ident_in, X1, H2T):
    NG = S // 512
    with Phase(nc, semstack, "p4a") as P:
        S_ = P.S
        ident = P.sb("ident", [128, 128], BF16)
        modc = P.sb("modc", [128, 32], F32)
        epsT = P.sb("eps", [128, 1], F32)
        g1 = P.sb("g1", [128, D], F32)
        wo = P.sb("wo", [128, 8, D], BF16)
        S_.op("pool", lambda e: e.dma_start(out=ident[:], in_=ident_in), writes=["ident"], dma=True, semkey="ident")
        S_.op("sp", lambda e: e.dma_start(out=modc[:], in_=MODC), writes=["modc"], dma=True, semkey="modc")
        S_.op("sp", lambda e: e.dma_start(out=g1[:], in_=GROW[0]), writes=["g1"], dma=True, semkey="g1")
        S_.op("pool", lambda e: e.dma_start(out=wo[:], in_=w_out.rearrange("(k p) f -> p k f", p=128)), writes=["wo"],
              dma=True, semkey="wo")
        S_.op("dve", lambda e: e.memset(epsT[:], EPS), writes=["eps"])
        for k in range(8):
            S_.op("dve", lambda e, k=k: e.tensor_tensor(out=wo[:, k, :], in0=wo[:, k, :], in1=g1[:], op=ALU.mult),
                  reads=["wo", "g1"], writes=["wo"])
        xt = [P.sb("xt%d" % i, [128, 4, D], F32) for i in range(2)]
        mt = [P.sb("mt%d" % i, [128, 8, 512], BF16) for i in range(2)]
        junk = P.sb("junk", [128, D], BF16)
        ss2 = [P.sb("ss%d" % i, [128, 4], F32) for i in range(2)]
        rstd2 = [P.sb("rstd%d" % i, [128, 4], F32) for i in range(2)]
        xs2 = [P.sb("xs%d" % i, [128, 4, D], BF16) for i in range(2)]
        hst = [P.sb("hst%d" % i, [128, 8, 512], BF16) for i in range(2)]
        xv = x.rearrange("(g j p) d -> g p j d", j=4, p=128)
        x1v = X1.rearrange("(g j p) d -> g p j d", j=4, p=128)

        def load_item(g):
            sl = g % 2
            S_.op("sp", lambda e: e.dma_start(out=xt[sl][:], in_=xv[g]), writes=[("xt", sl)], dma=True, semkey="xt%d" % sl)
            S_.op("sp", lambda e: e.dma_start(out=mt[sl][:], in_=MIXT.rearrange("(c p) s -> p c s", p=128)[:, :, g * 512:(g + 1) * 512]),
                  writes=[("mt", sl)], dma=True, semkey="mt%d" % sl)
            yield

        def y_item(g, j, half):
            sl = g % 2
            b = P.bank()
            for k in range(8):
                S_.op("pe", lambda e, k=k: e.matmul(
                    P.pb[b][:], lhsT=mt[sl][:, k, j * 128:(j + 1) * 128], rhs=wo[:, k, half * 512:(half + 1) * 512],
                    start=(k == 0), stop=(k == 7)), reads=[("mt", sl), "wo"], writes=[("pb", b)])
            yield
            S_.op("dve", lambda e: e.tensor_tensor(
                out=xt[sl][:, j, half * 512:(half + 1) * 512], in0=P.pb[b][:], in1=xt[sl][:, j, half * 512:(half + 1) * 512],
                op=ALU.add), reads=[("pb", b), ("xt", sl)], writes=[("xt", sl)])

        def norm_item(g):
            sl = g % 2
            ss, rstd, xs = ss2[sl], rstd2[sl], xs2[sl]
            KSS, KRS = ("ss", sl), ("rstd", sl)
            S_.op("sp", lambda e: e.dma_start(out=x1v[g], in_=xt[sl][:]), reads=[("xt", sl)], dma=True, semkey="xt%d" % sl)
            S_.op("pool", lambda e: e.memset(ss[:], 0.0), writes=[KSS])
            for j in range(4):
                S_.op("act", lambda e, j=j: e.activation(out=junk[:], in_=xt[sl][:, j, :], func=AF.Square, accum_out=ss[:, j:j + 1]),
                      reads=[("xt", sl), KSS], writes=[KSS, "junk"])
            S_.op("act", lambda e: e.activation(out=rstd[:], in_=ss[:], func=AF.Sqrt, bias=epsT[:], scale=1.0 / D),
                  reads=[KSS, "eps"], writes=[KRS])
            yield
            S_.op("dve", lambda e: e.reciprocal(out=rstd[:], in_=rstd[:]), reads=[KRS], writes=[KRS])
            for j in range(4):
                S_.op("dve", lambda e, j=j: e.tensor_scalar(out=xs[:, j, :], in0=xt[sl][:, j, :], scalar1=rstd[:, j:j + 1], scalar2=None, op0=ALU.mult),
                      reads=[("xt", sl), KRS], writes=[("xs", sl, j)])
            yield
            pend = None
            for c2 in range(5):
                if c2 < 4:
                    b = P.bank()
                    pbf = P.pb[b][:].bitcast(BF16)
                    for cc in range(2):
                        c = c2 * 2 + cc
                        for j in range(4):
                            S_.op("pe", lambda e, pbf=pbf, cc=cc, c=c, j=j: e.transpose(
                                pbf[:, cc * 512 + j * 128: cc * 512 + (j + 1) * 128], xs[:, j, c * 128:(c + 1) * 128], ident[:]),
                                reads=[("xs", sl, j), "ident"], writes=[("pb", b)])
                if pend is not None:
                    pb_, pbf_, pc2 = pend
                    for cc in range(2):
                        c = pc2 * 2 + cc
                        S_.op("act", lambda e, pbf_=pbf_, cc=cc, c=c: e.activation(
                            out=hst[sl][:, c, :], in_=pbf_[:, cc * 512:(cc + 1) * 512], func=AF.Identity,
                            bias=modc[:, 16 + c:17 + c], scale=modc[:, 24 + c:25 + c]),
                            reads=[("pb", pb_), "modc"], writes=[("hst", sl)])
                pend = (b, pbf, c2) if c2 < 4 else None
                yield
            S_.op("sp", lambda e: e.dma_start(out=H2T.rearrange("(c p) s -> p c s", p=128)[:, :, g * 512:(g + 1) * 512], in_=hst[sl][:]),
                  reads=[("hst", sl)], dma=True, semkey="hst%d" % sl)

        def p4a_items():
            for g in range(NG):
                if g + 1 < NG:
                    yield load_item(g + 1)
                for j in range(4):
                    for half in range(2):
                        yield y_item(g, j, half)
                yield norm_item(g)

        for _ in load_item(0):
            pass
        run_skewed(p4a_items())
        P.finish()


def _phase4b(nc, semstack, S, X1, H2T, w_gate, w_up, w_down, GROW, fg_row, out):
    G = 512
    NJ = G // 128
    NG = S // G
    NF = DFF // 128
    with Phase(nc, semstack, "p4b") as P:
        S_ = P.S
        epsT = P.sb("eps", [128, 1], F32)
        fg = P.sb("fg", [128, D], F32)
        wg = P.sb("wg", [128, 8, DFF], BF16)
        wu = P.sb("wu", [128, 8, DFF], BF16)
        wd = P.sb("wd", [128, NF, D], BF16)
        S_.op("dve", lambda e: e.memset(epsT[:], EPS), writes=["eps"])
        S_.op("sp", lambda e: e.dma_start(out=fg[:], in_=fg_row.partition_broadcast(128)), writes=["fg"], dma=True, semkey="fg")
        for k in range(8):
            S_.op("pool", lambda e, k=k: e.dma_start(out=wg[:, k, :], in_=w_gate[k * 128:(k + 1) * 128, :]), writes=["wg"], dma=True, semkey="wg")
            S_.op("pool", lambda e, k=k: e.dma_start(out=wu[:, k, :], in_=w_up[k * 128:(k + 1) * 128, :]), writes=["wu"], dma=True, semkey="wu")
        xq = [P.sb("xq%d" % i, [128, NJ, D], F32) for i in range(2)]
        g2 = xq[1]
        S_.op("sp", lambda e: e.dma_start(out=g2[:, 0, :], in_=GROW[1]), writes=[("xq", 1)], dma=True, semkey="xq1")
        S_.op("pool", lambda e: e.dma_start(out=wd[:], in_=w_down.rearrange("(k p) f -> p k f", p=128)), writes=["wd"], dma=True, semkey="wd")
        for k in range(NF):
            S_.op("dve", lambda e, k=k: e.tensor_tensor(out=wd[:, k, :], in0=wd[:, k, :], in1=g2[:, 0, :], op=ALU.mult),
                  reads=["wd", ("xq", 1)], writes=["wd"])
        ht = [P.sb("ht%d" % i, [128, 8, G], BF16) for i in range(2)]
        aT = P.sb("aT", [128, NF, G], BF16)
        ss = P.sb("ss", [128, NJ], F32)
        rstd = P.sb("rstd", [128, NJ], F32)
        x1v = X1.rearrange("(g j p) d -> g p j d", j=NJ, p=128)
        ov = out.rearrange("(g j p) d -> g p j d", j=NJ, p=128)
        for g in range(NG):
            sl = g % 2
            S_.op("sp", lambda e, g=g, sl=sl: e.dma_start(out=xq[sl][:], in_=x1v[g]), writes=[("xq", sl)], dma=True, semkey="xq%d" % sl)
            S_.op("sp", lambda e, g=g, sl=sl: e.dma_start(out=ht[sl][:], in_=H2T.rearrange("(c p) s -> p c s", p=128)[:, :, g * G:(g + 1) * G]),
                  writes=[("ht", sl)], dma=True, semkey="ht%d" % sl)
            for fc in range(NF):
                ba, bb = P.bank(), P.bank()
                for (bk, w, wk) in ((ba, wg, "wg"), (bb, wu, "wu")):
                    for k in range(8):
                        S_.op("pe", lambda e, bk=bk, w=w, k=k, fc=fc, sl=sl: e.matmul(
                            P.pb[bk][:, 0:G], lhsT=w[:, k, fc * 128:(fc + 1) * 128], rhs=ht[sl][:, k, :], start=(k == 0), stop=(k == 7)),
                            reads=[wk, ("ht", sl)], writes=[("pb", bk)])
                S_.op("act", lambda e, ba=ba, fc=fc: e.activation(out=aT[:, fc, :], in_=P.pb[ba][:, 0:G], func=AF.Silu),
                      reads=[("pb", ba)], writes=[("aT", fc)])
                S_.op("dve", lambda e, bb=bb, fc=fc: e.tensor_tensor(out=aT[:, fc, :], in0=P.pb[bb][:, 0:G], in1=aT[:, fc, :], op=ALU.mult),
                      reads=[("pb", bb), ("aT", fc)], writes=[("aT", fc)])
            for j in range(NJ):
                for half in range(2):
                    b = P.bank()
                    for k in range(NF):
                        S_.op("pe", lambda e, b=b, k=k, j=j, half=half: e.matmul(
                            P.pb[b][:], lhsT=aT[:, k, j * 128:(j + 1) * 128], rhs=wd[:, k, half * 512:(half + 1) * 512],
                            start=(k == 0), stop=(k == NF - 1)), reads=[("aT", k), "wd"], writes=[("pb", b)])
                    S_.op("dve", lambda e, b=b, j=j, half=half, sl=sl: e.tensor_tensor(
                        out=xq[sl][:, j, half * 512:(half + 1) * 512], in0=P.pb[b][:], in1=xq[sl][:, j, half * 512:(half + 1) * 512],
                        op=ALU.add), reads=[("pb", b), ("xq", sl)], writes=[("xq", sl)])
            S_.op("pool", lambda e: e.memset(ss[:], 0.0), writes=["ss"])
            for j in range(NJ):
                S_.op("act", lambda e, j=j, sl=sl: e.activation(out=aT[:, 0:2, :].rearrange("p a c -> p (a c)"), in_=xq[sl][:, j, :], func=AF.Square, accum_out=ss[:, j:j + 1]),
                      reads=[("xq", sl), "ss"], writes=["ss", ("aT", 0), ("aT", 1)])
            S_.op("act", lambda e: e.activation(out=rstd[:], in_=ss[:], func=AF.Sqrt, bias=epsT[:], scale=1.0 / D),
                  reads=["ss", "eps"], writes=["rstd"])
            S_.op("dve", lambda e: e.reciprocal(out=rstd[:], in_=rstd[:]), reads=["rstd"], writes=["rstd"])
            for j in range(NJ):
                S_.op("dve", lambda e, j=j, sl=sl: e.scalar_tensor_tensor(
                    out=xq[sl][:, j, :], in0=xq[sl][:, j, :], scalar=rstd[:, j:j + 1], in1=fg[:], op0=ALU.mult, op1=ALU.mult),
                    reads=[("xq", sl), "rstd", "fg"], writes=[("xq", sl)])
            S_.op("sp", lambda e, g=g, sl=sl: e.dma_start(out=ov[g], in_=xq[sl][:]), reads=[("xq", sl)], dma=True, semkey="xq%d" % sl)
        P.finish()


def _phase3(nc, semstack, S, QD, KD, VD, ZS, BA, MIXT, cst):
    NG = S // 512
    with Phase(nc, semstack, "p3") as P:
        S_ = P.S
        sb = P.sb
        ident = sb("ident", [128, 128], BF16)
        identf = sb("identf", [128, 128], F32)
        trif = sb("trif", [128, 128], F32)
        bonesf = sb("bonesf", [128, 128], F32)
        onesf = sb("onesf", [128, 128], F32)
        maskL = sb("maskL", [128, 64], F32)
        identc = sb("identc", [128, 64], F32)
        maskU = sb("maskU", [128, 64], F32)
        negA = sb("negA", [128, 8], F32)
        dtb = sb("dtb", [128, 8], F32)
        dng = sb("dng", [128, 64], F32)
        one1 = sb("one1", [128, 1], F32)
        epsT = sb("eps", [128, 1], F32)
        S_.op("pool", lambda e: e.dma_start(out=ident[:], in_=cst["ident"]), writes=["ident"], dma=True, semkey="ident")
        for nm, t in (("identf", identf), ("trif", trif), ("bonesf", bonesf), ("maskL", maskL), ("maskU", maskU), ("identc", identc)):
            S_.op("sp", lambda e, nm=nm, t=t: e.dma_start(out=t[:], in_=cst[nm]), writes=[nm], dma=True, semkey=nm)
        S_.op("sp", lambda e: e.dma_start(out=negA[:], in_=cst["alog"].partition_broadcast(128)), writes=["negA"], dma=True, semkey="negA")
        S_.op("sp", lambda e: e.dma_start(out=dtb[:], in_=cst["dtb"].partition_broadcast(128)), writes=["dtb"], dma=True, semkey="dtb")
        S_.op("sp", lambda e: e.dma_start(out=dng[:], in_=cst["dng"].partition_broadcast(128)), writes=["dng"], dma=True, semkey="dng")
        S_.op("pool", lambda e: e.memset(onesf[:], 1.0), writes=["onesf"])
        S_.op("pool", lambda e: e.memset(one1[:], 1.0), writes=["one1"])
        S_.op("pool", lambda e: e.memset(epsT[:], EPS), writes=["eps"])
        S_.op("act", lambda e: e.activation(out=negA[:], in_=negA[:], func=AF.Exp), reads=["negA"], writes=["negA"])
        S_.op("dve", lambda e: e.tensor_scalar(out=negA[:], in0=negA[:], scalar1=-1.0, scalar2=None, op0=ALU.mult),
              reads=["negA"], writes=["negA"])
        kd = [sb("kd%d" % i, [128, 4, 512], BF16) for i in range(2)]
        qd = [sb("qd%d" % i, [64, 8, 512], BF16) for i in range(2)]
        kd8 = [sb("kd8_%d" % i, [64, 8, 512], BF16) for i in range(2)]
        vd = [sb("vd%d" % i, [128, 4, 512], BF16) for i in range(2)]
        zs = [sb("zs%d" % i, [128, 4, 512], BF16) for i in range(2)]
        ba = [sb("ba%d" % i, [128, 4, 16], F32) for i in range(2)]
        mst = [sb("mst%d" % i, [128, 4, 512], BF16) for i in range(2)]
        y16_2 = [sb("y16%d" % i, [128, 16], F32) for i in range(2)]
        u16_2 = [sb("u16%d" % i, [128, 16], F32) for i in range(2)]
        beta_2 = [sb("beta%d" % i, [128, 8], F32) for i in range(2)]
        gg_2 = [sb("gg%d" % i, [128, 8], F32) for i in range(2)]
        gcl_2 = [sb("gcl%d" % i, [128, 16], F32) for i in range(2)]
        eg2 = [sb("eg%d" % i, [128, 8], F32) for i in range(2)]
        kdsc_2 = [sb("kdsc%d" % i, [128, 8], F32) for i in range(2)]
        be_2 = [sb("be%d" % i, [128, 8], F32) for i in range(2)]
        offL_2 = [sb("offL%d" % i, [128, 8], F32) for i in range(2)]
        decS2 = [sb("decS%d" % i, [64, 16], F32) for i in range(2)]
        kbg_2 = [sb("kbg%d" % i, [128, 512], BF16) for i in range(2)]
        kdec2 = [sb("kdec%d" % i, [128, 512], BF16) for i in range(2)]
        bv_2 = [sb("bv%d" % i, [128, 512], BF16) for i in range(2)]
        DG_2 = [sb("DG%d" % i, [128, 8, 64], F32) for i in range(2)]
        tL_2 = [sb("tL%d" % i, [128, 8, 64], F32) for i in range(2)]
        tU_2 = [sb("tU%d" % i, [128, 8, 64], F32) for i in range(2)]
        Lb_2 = [sb("Lb%d" % i, [128, 8, 64], F32) for i in range(2)]
        TTb_2 = [sb("TTb%d" % i, [128, 8, 64], BF16) for i in range(2)]
        Ui_2 = [sb("Ui%d" % i, [128, 8, 64], BF16) for i in range(2)]
        qkdT2 = [sb("qkdT%d" % i, [128, 8, 64], BF16) for i in range(2)]
        X_2 = [[sb("X%d_%d" % (p_, i), [128, 8, 64], F32) for i in range(2)] for p_ in range(2)]
        Y_2 = [[sb("Y%d_%d" % (p_, i), [128, 8, 64], F32) for i in range(2)] for p_ in range(2)]
        Q_2 = [[sb("Q%d_%d" % (p_, i), [128, 8, 64], F32) for i in range(2)] for p_ in range(2)]
        uu2 = [sb("uu%d" % i, [128, 512], F32) for i in range(2)]
        wT2 = [sb("wT%d" % i, [64, 8, 128], BF16) for i in range(2)]
        vnew = sb("vnew", [128, 512], BF16)
        o1s = sb("o1s", [128, 512], F32)
        otm = sb("otm", [128, 512], F32)
        sqo = sb("sqo", [128, 512], F32)
        ssq = sb("ssq", [128, 8], F32)
        onb = sb("onb", [128, 512], BF16)
        Sf = sb("Sf", [64, 512], F32)
        S1 = sb("S1", [64, 512], F32)
        Sbf = sb("Sbf", [64, 512], BF16)
        S_.op("pool", lambda e: e.memset(Sf[:], 0.0), writes=["Sf"])
        S_.op("pool", lambda e: e.memset(Sbf[:], 0.0), writes=["Sbf"])

        def bc8(t):
            return lambda n: t.unsqueeze(2).to_broadcast([128, 8, n])

        bcnt = [0, 0, 0]

        def bankB():
            bcnt[2] += 1
            return 6 + (bcnt[2] - 1) % 2

        def A_tile(g, sl, j, c0, par):
            uu, wT, kdec, qkdT, eg, decS = uu2[par], wT2[par], kdec2[par], qkdT2[par], eg2[par], decS2[par]
            K_uu = ("uu", par)
            K_wT = ("wT", par)
            K_kdec = ("kdec", par)
            K_qkdT = ("qkdT", par)
            K_eg = ("eg", par)
            K_decS = ("decS", par)
            y16, u16, beta, gg, gcl, kdsc, be, offL, kbg, bv, DG, tL, tU, Lb, TTb, Ui = y16_2[par], u16_2[par], beta_2[par], gg_2[par], gcl_2[par], kdsc_2[par], be_2[par], offL_2[par], kbg_2[par], bv_2[par], DG_2[par], tL_2[par], tU_2[par], Lb_2[par], TTb_2[par], Ui_2[par]
            X, Y, Q = X_2[par], Y_2[par], Q_2[par]
            AINT = {"y16a", "y16b", "u16", "beta", "gg", "gcl", "kdsc", "be", "offL", "kbg", "bv", "DG", "tL", "tU", "Lb", "TTb", "Ui", "X", "Y", "Q"}

            def mk(k):
                if isinstance(k, str) and k in AINT:
                    return (k, "p", par)
                if isinstance(k, tuple) and k and k[0] in AINT:
                    return k + ("p", par)
                return k

            class _SW:
                def op(self, eng, fn, reads=(), writes=(), **kw):
                    return S_.op(eng, fn, reads=[mk(k) for k in reads], writes=[mk(k) for k in writes], **kw)
            SA = _SW()

            def bankA():
                bcnt[par] += 1
                return 3 * par + (bcnt[par] - 1) % 3
            if j == 0:
                for nm, t, src, pp in (("kd", kd, KD, 128), ("qd", qd, QD, 64), ("kd8", kd8, KD, 64), ("vd", vd, VD, 128), ("zs", zs, ZS, 128)):
                    S_.op("sp", lambda e, t=t, src=src, g=g, sl=sl, pp=pp: e.dma_start(
                        out=t[sl][:], in_=src.rearrange("(c p) s -> p c s", p=pp)[:, :, g * 512:(g + 1) * 512]),
                        writes=[(nm, sl)], dma=True, semkey="%s%d" % (nm, sl))
                S_.op("sp", lambda e, g=g, sl=sl: e.dma_start(out=ba[sl][:], in_=BA.rearrange("(g j p) f -> g p j f", j=4, p=128)[g]),
                      writes=[("ba", sl)], dma=True, semkey="ba%d" % sl)
            SA.op("dve", lambda e, j=j, sl=sl: e.tensor_scalar(out=y16[:, 0:8], in0=ba[sl][:, j, 0:8], scalar1=-1.0, scalar2=None, op0=ALU.mult),
                  reads=[("ba", sl)], writes=["y16a"])
            yield
            SA.op("dve", lambda e, j=j, sl=sl: e.tensor_tensor(out=y16[:, 8:16], in0=ba[sl][:, j, 8:16], in1=dtb[:], op=ALU.add),
                  reads=[("ba", sl), "dtb"], writes=["y16b"])
            yield
            SA.op("act", lambda e: e.activation(out=u16[:], in_=y16[:], func=AF.Exp), reads=["y16a", "y16b"], writes=["u16"])
            yield
            SA.op("act", lambda e: e.activation(out=u16[:], in_=u16[:], func=AF.Ln, bias=one1[:], scale=1.0), reads=["u16", "one1"], writes=["u16"])
            yield
            SA.op("act", lambda e: e.activation(out=beta[:], in_=u16[:, 0:8], func=AF.Exp, scale=-1.0), reads=["u16"], writes=["beta"])
            yield
            SA.op("dve", lambda e: e.tensor_tensor(out=gg[:], in0=u16[:, 8:16], in1=negA[:], op=ALU.mult), reads=["u16", "negA"], writes=["gg"])
            yield
            b = bankA()
            yield
            SA.op("pe", lambda e, b=b: e.matmul(P.pb[b][:, 0:8], lhsT=trif[:], rhs=gg[:], start=True, stop=True),
                  reads=["trif", "gg"], writes=[("pb", b)])
            yield
            SA.op("pe", lambda e, b=b: e.matmul(P.pb[b][:, 8:16], lhsT=bonesf[:], rhs=gg[:], start=True, stop=True),
                  reads=["bonesf", "gg"], writes=[("pb", b)])
            yield
            for ch in range(2):
                SA.op("pe", lambda e, b=b, ch=ch: e.matmul(
                    P.pb[b][0:64, 16 + ch * 8:24 + ch * 8], lhsT=bonesf[:, ch * 64:(ch + 1) * 64],
                    rhs=gg[:], start=True, stop=True), reads=["bonesf", "gg"], writes=[("pb", b)])
            yield
            SA.op("dve", lambda e, b=b: e.tensor_copy(out=gcl[:], in_=P.pb[b][:, 0:16]), reads=[("pb", b)], writes=["gcl"])
            yield
            SA.op("act", lambda e, b=b: e.activation(out=decS[:], in_=P.pb[b][0:64, 16:32], func=AF.Exp), reads=[("pb", b)], writes=[K_decS])
            yield
            SA.op("act", lambda e: e.activation(out=eg[:], in_=gcl[:, 0:8], func=AF.Exp), reads=["gcl"], writes=[K_eg])
            yield
            SA.op("dve", lambda e: e.tensor_tensor(out=kdsc[:], in0=gcl[:, 8:16], in1=gcl[:, 0:8], op=ALU.subtract), reads=["gcl"], writes=["kdsc"])
            yield
            SA.op("act", lambda e: e.activation(out=kdsc[:], in_=kdsc[:], func=AF.Exp), reads=["kdsc"], writes=["kdsc"])
            yield
            SA.op("dve", lambda e: e.tensor_tensor(out=be[:], in0=beta[:], in1=eg[:], op=ALU.mult), reads=["beta", K_eg], writes=["be"])
            yield
            SA.op("dve", lambda e: e.tensor_tensor(out=offL[:], in0=gcl[:, 0:8], in1=u16[:, 0:8], op=ALU.subtract), reads=["gcl", "u16"], writes=["offL"])
            yield
            bk_ = bankA()
            yield
            bv_ = bk_
            yield
            for (off, src, key) in ((0, kd, "kd"), (512, vd, "vd")):
                pbf = P.pb[bk_][:].bitcast(BF16)
                for hp in range(4):
                    SA.op("pe", lambda e, pbf=pbf, hp=hp, src=src, sl=sl, c0=c0, off=off: e.transpose(
                        pbf[:, off + hp * 128:off + (hp + 1) * 128], src[sl][:, hp, c0:c0 + 128], ident[:]),
                        reads=[(key, sl), "ident"], writes=[("pb", bk_)])
            yield
            pk = P.pb[bk_][:].bitcast(BF16)[:, 0:512].rearrange("p (h d) -> p h d", h=8)
            yield
            pv = P.pb[bv_][:].bitcast(BF16)[:, 512:1024].rearrange("p (h d) -> p h d", h=8)
            yield
            SA.op("dve", lambda e, pk=pk: e.tensor_tensor(out=kbg[:].rearrange("p (h d) -> p h d", h=8), in0=pk, in1=bc8(be[:])(64), op=ALU.mult),
                  reads=[("pb", bk_), "be"], writes=["kbg"])
            yield
            SA.op("dve", lambda e, pk=pk: e.tensor_tensor(out=kdec[:].rearrange("p (h d) -> p h d", h=8), in0=pk, in1=bc8(kdsc[:])(64), op=ALU.mult),
                  reads=[("pb", bk_), "kdsc"], writes=[K_kdec])
            yield
            SA.op("dve", lambda e, pv=pv: e.tensor_tensor(out=bv[:].rearrange("p (h d) -> p h d", h=8), in0=pv, in1=bc8(beta[:])(64), op=ALU.mult),
                  reads=[("pb", bv_), "beta"], writes=["bv"])
            bKK, bQK = bankA(), bankA()
            for h in range(8):
                for ch in range(2):
                    kk = kd8[sl][:, h, c0 + ch * 64:c0 + ch * 64 + 64]
                    SA.op("pe", lambda e, h=h, ch=ch, kk=kk: e.matmul(
                        P.pb[bKK][ch * 64:(ch + 1) * 64, h * 64:(h + 1) * 64], lhsT=kk, rhs=kk, start=True, stop=True,
                        tile_position=(0, ch * 64)), reads=[("kd8", sl)], writes=[("pb", bKK)])
            yield
            for h in range(8):
                for ch in range(2):
                    kk = kd8[sl][:, h, c0 + ch * 64:c0 + ch * 64 + 64]
                    qq = qd[sl][:, h, c0 + ch * 64:c0 + ch * 64 + 64]
                    SA.op("pe", lambda e, h=h, ch=ch, kk=kk, qq=qq: e.matmul(
                        P.pb[bQK][ch * 64:(ch + 1) * 64, h * 64:(h + 1) * 64], lhsT=kk, rhs=qq, start=True, stop=True,
                        tile_position=(0, ch * 64)), reads=[("kd8", sl), ("qd", sl)], writes=[("pb", bQK)])
            yield
            SA.op("pool", lambda e: e.tensor_tensor(out=DG[:], in0=identc[:].unsqueeze(1).to_broadcast([128, 8, 64]),
                                                   in1=bc8(gcl[:, 0:8])(64), op=ALU.mult), reads=["identc", "gcl"], writes=["DG"])
            yield
            bG = bankA()
            SA.op("pe", lambda e: e.matmul(P.pb[bG][:], lhsT=bonesf[:], rhs=DG[:].rearrange("p h c -> p (h c)"), start=True, stop=True),
                  reads=["bonesf", "DG"], writes=[("pb", bG)])
            yield
            pG = P.pb[bG][:].rearrange("p (h c) -> p h c", h=8)
            SA.op("dve", lambda e: e.scalar_tensor_tensor(out=tL[:], in0=pG, scalar=-1.0, in1=bc8(offL[:])(64), op0=ALU.mult, op1=ALU.add),
                  reads=[("pb", bG), "offL"], writes=["tL"])
            SA.op("dve", lambda e: e.tensor_tensor(out=tU[:], in0=pG, in1=bc8(gcl[:, 0:8])(64), op=ALU.subtract),
                  reads=[("pb", bG), "gcl"], writes=["tU"])
            yield
            SA.op("pool", lambda e: e.tensor_tensor(out=tL[:], in0=tL[:], in1=maskL[:].unsqueeze(1).to_broadcast([128, 8, 64]), op=ALU.add),
                  reads=["tL", "maskL"], writes=["tL"])
            SA.op("pool", lambda e: e.tensor_tensor(out=tU[:], in0=tU[:], in1=maskU[:].unsqueeze(1).to_broadcast([128, 8, 64]), op=ALU.add),
                  reads=["tU", "maskU"], writes=["tU"])
            yield
            SA.op("act", lambda e: e.activation(out=Lb[:], in_=tL[:], func=AF.Exp), reads=["tL"], writes=["Lb"])
            SA.op("act", lambda e: e.activation(out=Ui[:], in_=tU[:], func=AF.Exp), reads=["tU"], writes=["Ui"])
            yield
            SA.op("dve", lambda e: e.scalar_tensor_tensor(out=X[0][:], in0=P.pb[bKK][:].rearrange("p (h c) -> p h c", h=8), scalar=-1.0, in1=Lb[:],
                                                         op0=ALU.mult, op1=ALU.mult), reads=[("pb", bKK), "Lb"], writes=[("X", 0)])
            SA.op("dve", lambda e: e.tensor_tensor(out=qkdT[:], in0=P.pb[bQK][:].rearrange("p (h c) -> p h c", h=8), in1=Ui[:], op=ALU.mult),
                  reads=[("pb", bQK), "Ui"], writes=[K_qkdT])
            yield
            bB = bankA()
            for h in range(8):
                for ch in range(2):
                    SA.op("pe", lambda e, h=h, ch=ch: e.matmul(
                        P.pb[bB][ch * 64:(ch + 1) * 64, h * 64:(h + 1) * 64], lhsT=X[0][ch * 64:(ch + 1) * 64, h, :],
                        rhs=identf[ch * 64:(ch + 1) * 64, ch * 64:(ch + 1) * 64], start=True, stop=True, tile_position=(ch * 64, ch * 64)),
                        reads=[("X", 0), "identf"], writes=[("pb", bB)])
            yield
            SA.op("act", lambda e: e.activation(out=Y[0][:].rearrange("p h c -> p (h c)"), in_=P.pb[bB][:], func=AF.Identity),
                  reads=[("pb", bB)], writes=[("Y", 0)])
            yield
            SA.op("pool", lambda e: e.tensor_tensor(out=Q[0][:], in0=Y[0][:], in1=identc[:].unsqueeze(1).to_broadcast([128, 8, 64]), op=ALU.add),
                  reads=[("Y", 0), "identc"], writes=[("Q", 0)])
            yield

            def quad_mm(bk, L, R, rk):
                for h in range(8):
                    for ch in range(2):
                        SA.op("pe", lambda e, h=h, ch=ch: e.matmul(
                            P.pb[bk][ch * 64:(ch + 1) * 64, h * 64:(h + 1) * 64], lhsT=L[ch * 64:(ch + 1) * 64, h, :],
                            rhs=R[ch * 64:(ch + 1) * 64, h, :], start=True, stop=True, tile_position=(ch * 64, ch * 64)),
                            reads=rk, writes=[("pb", bk)])

            for lv in range(1, 6):
                a, n = (lv - 1) % 2, lv % 2
                bX = bankA()
                quad_mm(bX, Y[a], X[a], [("X", a), ("Y", a)])
                yield
                SA.op("act", lambda e, n=n, bX=bX: e.activation(out=X[n][:].rearrange("p h c -> p (h c)"), in_=P.pb[bX][:], func=AF.Identity),
                      reads=[("pb", bX)], writes=[("X", n)])
                if lv < 5:
                    bY = bankA()
                    quad_mm(bY, X[a], Y[a], [("X", a), ("Y", a)])
                    yield
                    SA.op("dve", lambda e, n=n, bY=bY: e.tensor_copy(out=Y[n][:].rearrange("p h c -> p (h c)"), in_=P.pb[bY][:]),
                          reads=[("pb", bY)], writes=[("Y", n)])
                yield
                bQ = bankA()
                quad_mm(bQ, X[n], Q[a], [("X", n), ("Q", a)])
                yield
                SA.op("dve", lambda e, n=n, a=a, bQ=bQ: e.tensor_tensor(
                    out=Q[n][:].rearrange("p h c -> p (h c)"), in0=P.pb[bQ][:], in1=Q[a][:].rearrange("p h c -> p (h c)"), op=ALU.add),
                    reads=[("pb", bQ), ("Q", a)], writes=[("Q", n)])
                yield
            SA.op("act", lambda e: e.activation(out=TTb[:].rearrange("p h c -> p (h c)"), in_=Q[1][:].rearrange("p h c -> p (h c)"), func=AF.Identity),
                  reads=[("Q", 1)], writes=["TTb"])
            yield
            bu = bankA()
            for h in range(8):
                for ch in range(2):
                    SA.op("pe", lambda e, h=h, ch=ch: e.matmul(
                        P.pb[bu][ch * 64:(ch + 1) * 64, h * 64:(h + 1) * 64], lhsT=TTb[ch * 64:(ch + 1) * 64, h, :],
                        rhs=bv[ch * 64:(ch + 1) * 64, h * 64:(h + 1) * 64], start=True, stop=True, tile_position=(ch * 64, ch * 64)),
                        reads=["TTb", "bv"], writes=[("pb", bu)])
            yield
            SA.op("act", lambda e: e.activation(out=uu[:], in_=P.pb[bu][:], func=AF.Identity), reads=[("pb", bu)], writes=[K_uu])
            bw = [bankA(), bankA()]
            for h in range(8):
                for ch in range(2):
                    SA.op("pe", lambda e, h=h, ch=ch, bw=bw: e.matmul(
                        P.pb[bw[ch]][0:64, h * 64:(h + 1) * 64],
                        lhsT=kbg[ch * 64:(ch + 1) * 64, h * 64:(h + 1) * 64], rhs=TTb[ch * 64:(ch + 1) * 64, h, :],
                        start=True, stop=True, tile_position=(ch * 64, 0)), reads=["TTb", "kbg"], writes=[("pb", bw[ch])])
            yield
            for ch in range(2):
                SA.op("dve", lambda e, bw=bw, ch=ch: e.tensor_copy(out=wT[:, :, ch * 64:(ch + 1) * 64],
                                                                 in_=P.pb[bw[ch]][0:64, :].rearrange("p (h c) -> p h c", h=8)),
                      reads=[("pb", bw[ch])], writes=[K_wT])

        def B_tile(g, sl, j, c0, par):
            uu, wT, kdec, qkdT, eg, decS = uu2[par], wT2[par], kdec2[par], qkdT2[par], eg2[par], decS2[par]
            K_uu = ("uu", par)
            K_wT = ("wT", par)
            K_kdec = ("kdec", par)
            K_qkdT = ("qkdT", par)
            K_eg = ("eg", par)
            K_decS = ("decS", par)
            for ch in range(2):
                p0 = ch * 64
                bvn, bo1, bo2, bs = bankB(), bankB(), bankB(), bankB()
                for h in range(8):
                    S_.op("pe", lambda e, h=h, p0=p0, ch=ch, bvn=bvn: e.matmul(
                        P.pb[bvn][p0:p0 + 64, h * 64:(h + 1) * 64], lhsT=wT[:, h, ch * 64:(ch + 1) * 64],
                        rhs=Sbf[:, h * 64:(h + 1) * 64], start=True, stop=True, tile_position=(0, p0)),
                        reads=[K_wT, "Sbf"], writes=[("pb", bvn)])
                for h in range(8):
                    S_.op("pe", lambda e, h=h, p0=p0, ch=ch, bo1=bo1, sl=sl, c0=c0: e.matmul(
                        P.pb[bo1][p0:p0 + 64, h * 64:(h + 1) * 64], lhsT=qd[sl][:, h, c0 + p0:c0 + p0 + 64],
                        rhs=Sbf[:, h * 64:(h + 1) * 64], start=True, stop=True, tile_position=(0, p0)),
                        reads=[("qd", sl), "Sbf"], writes=[("pb", bo1)])
                S_.op("dve", lambda e, p0=p0, bvn=bvn: e.tensor_tensor(out=vnew[p0:p0 + 64, :], in0=uu[p0:p0 + 64, :], in1=P.pb[bvn][p0:p0 + 64, :],
                                                                    op=ALU.subtract), reads=[K_uu, ("pb", bvn)], writes=[("vnew", ch)])
                S_.op("dve", lambda e, p0=p0, bo1=bo1: e.tensor_tensor(
                    out=o1s[p0:p0 + 64, :].rearrange("p (h d) -> p h d", h=8), in0=P.pb[bo1][p0:p0 + 64, :].rearrange("p (h d) -> p h d", h=8),
                    in1=eg[p0:p0 + 64, :].unsqueeze(2).to_broadcast([64, 8, 64]), op=ALU.mult),
                    reads=[K_eg, ("pb", bo1)], writes=[("o1s", ch)])
                S_.op("pool", lambda e, ch=ch: e.tensor_tensor(
                    out=S1[:].rearrange("p (a d) -> p a d", a=8), in0=Sf[:].rearrange("p (a d) -> p a d", a=8),
                    in1=decS[:, ch * 8:ch * 8 + 8].unsqueeze(2).to_broadcast([64, 8, 64]), op=ALU.mult),
                    reads=["Sf", K_decS], writes=["S1"])
                for h in range(8):
                    S_.op("pe", lambda e, h=h, p0=p0, ch=ch, bs=bs: e.matmul(
                        P.pb[bs][0:64, h * 64:(h + 1) * 64], lhsT=kdec[p0:p0 + 64, h * 64:(h + 1) * 64],
                        rhs=vnew[p0:p0 + 64, h * 64:(h + 1) * 64], start=True, stop=True, tile_position=(p0, 0)),
                        reads=[K_kdec, ("vnew", ch)], writes=[("pb", bs)])
                for h in range(8):
                    S_.op("pe", lambda e, h=h, p0=p0, ch=ch, bo2=bo2: e.matmul(
                        P.pb[bo2][p0:p0 + 64, h * 64:(h + 1) * 64], lhsT=qkdT[p0:p0 + 64, h, :],
                        rhs=vnew[p0:p0 + 64, h * 64:(h + 1) * 64], start=True, stop=True, tile_position=(p0, p0)),
                        reads=[K_qkdT, ("vnew", ch)], writes=[("pb", bo2)])
                S_.op("dve", lambda e, bs=bs: e.tensor_tensor(out=Sbf[:], in0=S1[:], in1=P.pb[bs][0:64, :], op=ALU.add),
                      reads=["S1", ("pb", bs)], writes=["Sbf"])
                S_.op("dve", lambda e, bs=bs: e.tensor_tensor(out=Sf[:], in0=S1[:], in1=P.pb[bs][0:64, :], op=ALU.add),
                      reads=["S1", ("pb", bs)], writes=["Sf"])
                S_.op("dve", lambda e, p0=p0, bo2=bo2: e.tensor_tensor(out=otm[p0:p0 + 64, :], in0=o1s[p0:p0 + 64, :], in1=P.pb[bo2][p0:p0 + 64, :],
                                                                    op=ALU.add), reads=[("o1s", ch), ("pb", bo2)], writes=[("otm", ch)])
            yield
            OT = [("otm", 0), ("otm", 1)]
            yield
            S_.op("pool", lambda e: e.tensor_tensor(out=sqo[:], in0=otm[:], in1=otm[:], op=ALU.mult), reads=OT, writes=["sqo"])
            yield
            S_.op("dve", lambda e: e.tensor_reduce(out=ssq[:], in_=sqo[:].rearrange("p (h d) -> p h d", h=8), axis=mybir.AxisListType.X, op=ALU.add),
                  reads=["sqo"], writes=["ssq"])
            yield
            S_.op("act", lambda e: e.activation(out=ssq[:], in_=ssq[:], func=AF.Sqrt, bias=epsT[:], scale=1.0 / 64), reads=["ssq", "eps"], writes=["ssq"])
            yield
            S_.op("dve", lambda e: e.reciprocal(out=ssq[:], in_=ssq[:]), reads=["ssq"], writes=["ssq"])
            yield
            S_.op("dve", lambda e: e.tensor_tensor(out=sqo[:].rearrange("p (h d) -> p h d", h=8), in0=otm[:].rearrange("p (h d) -> p h d", h=8),
                                                  in1=bc8(ssq[:])(64), op=ALU.mult), reads=OT + ["ssq"], writes=["sqo"])
            yield
            S_.op("pool", lambda e: e.tensor_tensor(out=onb[:].rearrange("p (h d) -> p h d", h=8), in0=sqo[:].rearrange("p (h d) -> p h d", h=8),
                                                   in1=dng[:].unsqueeze(1).to_broadcast([128, 8, 64]), op=ALU.mult), reads=["sqo", "dng"], writes=["onb"])
            yield
            bo = bankB()
            yield
            pO = P.pb[bo][:].bitcast(BF16)
            yield
            for hp in range(8):
                S_.op("pe", lambda e, pO=pO, hp=hp: e.transpose(pO[:, hp * 128:(hp + 1) * 128], onb[:, (hp % 4) * 128:(hp % 4 + 1) * 128], ident[:]),
                      reads=["onb", "ident"], writes=[("pb", bo)])
            yield
            S_.op("dve", lambda e, pO=pO, sl=sl, c0=c0: e.tensor_tensor(
                out=mst[sl][:, :, c0:c0 + 128], in0=pO[:, 0:512].rearrange("p (a c) -> p a c", a=4), in1=zs[sl][:, :, c0:c0 + 128], op=ALU.mult),
                reads=[("pb", bo), ("zs", sl)], writes=[("mst", sl)])
            if j == 3:
                S_.op("sp", lambda e, g=g, sl=sl: e.dma_start(out=MIXT.rearrange("(c p) s -> p c s", p=128)[:, 4:8, g * 512:(g + 1) * 512], in_=mst[sl][:]),
                      reads=[("mst", sl)], dma=True, semkey="mst%d" % sl)

        tiles = [(g, g % 2, j, j * 128, (g * 4 + j) % 2) for g in range(NG) for j in range(4)]
        NT_ = len(tiles)

        def adv(gen, n):
            for _ in range(n):
                try:
                    next(gen)
                except StopIteration:
                    return True
            return False

        gens = {}
        for t in (0, 1):
            if t < NT_:
                gens[t] = A_tile(*tiles[t])
        adv(gens[0], 10 ** 6)
        for ti in range(NT_):
            gb = B_tile(*tiles[ti])
            ga1 = gens.pop(ti + 1, None)
            if ti + 2 < NT_:
                gens[ti + 2] = A_tile(*tiles[ti + 2])
            ga2 = gens.get(ti + 2)
            d1 = ga1 is None
            d2 = ga2 is None
            db = False
            while not (d1 and db):
                if not d1:
                    d1 = adv(ga1, 2)
                if not d2:
                    d2 = adv(ga2, 2)
                if not db:
                    db = adv(gb, 1)
        P.finish()


def _host_all(inputs, b, S):
    f = np.float32
    col = lambda v: np.ascontiguousarray(np.asarray(v, f).reshape(-1, 128).T)
    d = {}
    d["x"] = np.ascontiguousarray(np.asarray(inputs["x"][b, :S], f))
    d["c_col"] = col(inputs["c"][b])
    d["w_ada"] = np.asarray(inputs["w_ada"][0], f)
    d["bada_col"] = col(inputs["b_ada"][0])
    d["bada_row"] = np.ascontiguousarray(np.asarray(inputs["b_ada"][0], f).reshape(6, 1024))
    d["gattn_col"] = col(inputs["norm_attn_g"][0])
    d["gffn_col"] = col(inputs["norm_ffn_g"][0])
    d["w_in"] = np.asarray(inputs["w_in"][0], f)
    cw = np.asarray(inputs["conv_w"][0], f)
    d["cw_col"] = np.ascontiguousarray(cw.T.reshape(12, 128, 4).transpose(1, 0, 2).reshape(128, 48))
    d["ident"] = np.eye(128, dtype=f)
    bo = np.zeros((128, 128), f)
    bo[:64, :64] = 1
    bo[64:, 64:] = 1
    d["bones"] = bo
    d["identf"] = np.eye(128, dtype=f)
    d["bonesf"] = bo
    idx = np.arange(128)
    same = (idx[:, None] // 64) == (idx[None, :] // 64)
    d["trif"] = (same & (idx[:, None] <= idx[None, :])).astype(f)
    ii = np.arange(128) % 64
    cc = np.arange(64)
    d["maskL"] = np.where(cc[None, :] < ii[:, None], 0.0, -30000.0).astype(f)
    d["maskU"] = np.where(cc[None, :] >= ii[:, None], 0.0, -30000.0).astype(f)
    d["identc"] = (cc[None, :] == ii[:, None]).astype(f)
    d["alog_row"] = np.asarray(inputs["a_log"][0], f).reshape(1, 8)
    d["dtb_row"] = np.asarray(inputs["dt_bias"][0], f).reshape(1, 8)
    d["dng_row"] = np.asarray(inputs["delta_norm_g"][0], f).reshape(1, 64)
    d["w_out"] = np.asarray(inputs["w_out"][0], f)
    d["w_gate"] = np.asarray(inputs["w_gate"][0], f)
    d["w_up"] = np.asarray(inputs["w_up"][0], f)
    d["w_down"] = np.asarray(inputs["w_down"][0], f)
    d["fg_row"] = np.asarray(inputs["final_norm_g"], f).reshape(1, 1024)
    return d


def kernel(**inputs):
    S = inputs["x"].shape[1]
    B = inputs["x"].shape[0]
    nc, semstack = build_program(S, dbg=False)
    consts = host_consts(inputs)
    in_maps = []
    for b in range(B):
        d = _host_all(inputs, b, S)
        d.update(consts)
        in_maps.append(d)
    res = run_bass_kernel_spmd(nc, in_maps, core_ids=list(range(B)))
    return np.stack([np.asarray(r["out"], np.float32) for r in res.results], axis=0)
```
